# Optimizing a Trainium2 kernel written in Bass

```python
import math
import jax, jax.numpy as jnp
from jax import lax
import numpy as np

D_MODEL = 2048
BATCH = 1
SEQ = 8192
DEPTH = 2

N_MIXERS = 2
N_HEADS = 16
HEAD_DIM = 128
N_KV_GROUPS = 4
HEADS_PER_GROUP = N_HEADS // N_KV_GROUPS
CMP_BLOCK = 32
CMP_STRIDE = 16
CMP_HIDDEN = 2 * HEAD_DIM
SEL_BLOCK = 64
SEL_TOPK = 16
WINDOW = 512
Q_BLOCK = 128
N_GATES = 3
KV_WIDTH = N_KV_GROUPS * HEAD_DIM
NSA_IN_COLS = N_HEADS * HEAD_DIM + 6 * KV_WIDTH + N_GATES * N_HEADS
REL_BUCKETS = 32
REL_MAX_DIST = 128
SSM_GROUP = 16
SSM_GROUPS = D_MODEL // SSM_GROUP
SSM_STATE = 64
D_FF = ((8 * D_MODEL + 2) // 3 + 255) // 256 * 256
N_NSA_LAYERS = (DEPTH + N_MIXERS - 1) // N_MIXERS
N_S5_LAYERS = DEPTH // N_MIXERS
RMS_EPS = 1e-6
NEG_INF = -1e30
FORCE_SCORE = 1e9

kernel_name = "nsa_s5_interleaved_hybrid"


def rmsnorm(x, g):
    x32 = x.astype(jnp.float32)
    y = x32 * lax.rsqrt(jnp.mean(x32 * x32, axis=-1, keepdims=True) + RMS_EPS)
    return (y * g.astype(jnp.float32)).astype(x.dtype)


def rel_bucket(dist):
    n = jnp.maximum(dist, 0)
    max_exact = REL_BUCKETS // 2
    large = max_exact + (jnp.log(jnp.maximum(n, 1).astype(jnp.float32) / max_exact)
                         / math.log(REL_MAX_DIST / max_exact)
                         * (REL_BUCKETS - max_exact)).astype(jnp.int32)
    large = jnp.minimum(large, REL_BUCKETS - 1)
    return jnp.where(n < max_exact, n, large)


def masked_softmax(s, mask):
    s = jnp.where(mask, s.astype(jnp.float32), NEG_INF)
    p = jax.nn.softmax(s, axis=-1)
    return jnp.where(mask, p, 0.0)


def nsa_mixer(h, rel_bias, w_in, w_out, pos_k, w1_k, w2_k, pos_v, w1_v, w2_v):
    Bsz, S, _ = h.shape
    G, HPG, DH = N_KV_GROUPS, HEADS_PER_GROUP, HEAD_DIM
    proj = h @ w_in
    qe = N_HEADS * DH
    bounds = [qe + i * KV_WIDTH for i in range(7)]
    q, kc, vc, ks, vs, kw, vw, gl = jnp.split(proj, bounds, axis=-1)
    q = q.reshape(Bsz, S, G, HPG, DH) * (DH ** -0.5)
    kc, vc, ks, vs, kw, vw = [a.reshape(Bsz, S, G, DH) for a in (kc, vc, ks, vs, kw, vw)]
    gates = jax.nn.sigmoid(gl.astype(jnp.float32)).astype(h.dtype).reshape(Bsz, S, N_GATES, G, HPG)
    t = jnp.arange(S, dtype=jnp.int32)

    n_cmp = (S - CMP_BLOCK) // CMP_STRIDE + 1
    starts = jnp.arange(n_cmp, dtype=jnp.int32) * CMP_STRIDE
    gidx = starts[:, None] + jnp.arange(CMP_BLOCK, dtype=jnp.int32)[None, :]

    def compress(kv, pos, w1, w2):
        blk = kv[:, gidx] + pos[None, None, :, None, :]
        blk = jnp.moveaxis(blk, 3, 2).reshape(Bsz, n_cmp, G, CMP_BLOCK * DH)
        return jax.nn.gelu(blk @ w1) @ w2

    kcc = compress(kc, pos_k, w1_k, w2_k)
    vcc = compress(vc, pos_v, w1_v, w2_v)
    dist_c = t[:, None] - (starts + CMP_BLOCK - 1)[None, :]
    mask_c = dist_c >= 0
    bias_c = rel_bias[rel_bucket(dist_c)].transpose(2, 0, 1).reshape(G, HPG, S, n_cmp)
    s_c = jnp.einsum('btghd,bcgd->bghtc', q, kcc).astype(jnp.float32) + bias_c
    p_c = masked_softmax(s_c, mask_c)
    o_c = jnp.einsum('bghtc,bcgd->btghd', p_c.astype(vcc.dtype), vcc)

    imp = p_c.sum(axis=2)
    n_sel = S // SEL_BLOCK
    ratio = SEL_BLOCK // CMP_STRIDE
    lo = CMP_BLOCK // CMP_STRIDE - 1
    count = ratio + lo
    right = max(0, (count - 1) + ratio * n_sel - lo - n_cmp)
    imp_pad = jnp.pad(imp, ((0, 0), (0, 0), (0, 0), (lo, right)))
    imp_slc = sum(imp_pad[..., k:k + ratio * n_sel:ratio] for k in range(count))
    j = jnp.arange(n_sel, dtype=jnp.int32)
    t_blk = t // SEL_BLOCK
    forced = (j[None, :] == 0) | (j[None, :] == t_blk[:, None]) | (j[None, :] == t_blk[:, None] - 1)
    causal_blk = (j * SEL_BLOCK)[None, :] <= t[:, None]
    score = jnp.where(forced, FORCE_SCORE, imp_slc)
    score = jnp.where(causal_blk, score, NEG_INF)
    k_top = min(SEL_TOPK, n_sel)
    _, sel_idx = lax.top_k(score, k_top)

    ks_blocks = ks.reshape(Bsz, n_sel, SEL_BLOCK, G, DH).transpose(0, 3, 1, 2, 4)
    vs_blocks = vs.reshape(Bsz, n_sel, SEL_BLOCK, G, DH).transpose(0, 3, 1, 2, 4)
    nq = S // Q_BLOCK
    q_chunks = q.reshape(Bsz, nq, Q_BLOCK, G, HPG, DH).transpose(1, 0, 2, 3, 4, 5)
    idx_chunks = sel_idx.reshape(Bsz, G, nq, Q_BLOCK, k_top).transpose(2, 0, 1, 3, 4)
    t0s = jnp.arange(nq, dtype=jnp.int32) * Q_BLOCK
    table_g = rel_bias.T.reshape(G, HPG, REL_BUCKETS)
    bi = jnp.arange(Bsz)[:, None, None, None]
    gi = jnp.arange(G)[None, :, None, None]
    gi5 = jnp.arange(G)[None, :, None, None, None]
    hi5 = jnp.arange(HPG)[None, None, :, None, None]
    in_blk = jnp.arange(SEL_BLOCK, dtype=jnp.int32)

    def sel_chunk(args):
        qc, ic, t0 = args
        kg = ks_blocks[bi, gi, ic].reshape(Bsz, G, Q_BLOCK, k_top * SEL_BLOCK, DH)
        vg = vs_blocks[bi, gi, ic].reshape(Bsz, G, Q_BLOCK, k_top * SEL_BLOCK, DH)
        kpos = (ic[..., None] * SEL_BLOCK + in_blk).reshape(Bsz, G, Q_BLOCK, k_top * SEL_BLOCK)
        tq = t0 + jnp.arange(Q_BLOCK, dtype=jnp.int32)
        dist = tq[None, None, :, None] - kpos
        bias = table_g[gi5, hi5, rel_bucket(dist)[:, :, None]]
        s = jnp.einsum('bqghd,bgqkd->bghqk', qc, kg).astype(jnp.float32) + bias
        p = masked_softmax(s, (dist >= 0)[:, :, None])
        return jnp.einsum('bghqk,bgqkd->bqghd', p.astype(vg.dtype), vg)

    o_s = lax.map(sel_chunk, (q_chunks, idx_chunks, t0s))
    o_s = o_s.transpose(1, 0, 2, 3, 4, 5).reshape(Bsz, S, G, HPG, DH)

    nb = S // Q_BLOCK
    n_band = WINDOW // Q_BLOCK

    def band(kv):
        padded = jnp.pad(kv, ((0, 0), (WINDOW, 0), (0, 0), (0, 0))).reshape(Bsz, nb + n_band, Q_BLOCK, G, DH)
        return jnp.concatenate([padded[:, i:i + nb] for i in range(n_band + 1)], axis=2)

    kwb, vwb = band(kw), band(vw)
    qi = jnp.arange(Q_BLOCK, dtype=jnp.int32)
    ki = jnp.arange((n_band + 1) * Q_BLOCK, dtype=jnp.int32)
    dist_w = qi[:, None] + WINDOW - ki[None, :]
    key_abs = jnp.arange(nb, dtype=jnp.int32)[:, None] * Q_BLOCK - WINDOW + ki[None, :]
    mask_w = ((dist_w >= 0) & (dist_w < WINDOW))[None] & (key_abs >= 0)[:, None, :]
    bias_w = rel_bias[rel_bucket(dist_w)].transpose(2, 0, 1).reshape(G, HPG, Q_BLOCK, -1)
    qw = q.reshape(Bsz, nb, Q_BLOCK, G, HPG, DH)
    s_w = jnp.einsum('bjqghd,bjkgd->bjghqk', qw, kwb).astype(jnp.float32) + bias_w
    p_w = masked_softmax(s_w, mask_w[:, None, None])
    o_w = jnp.einsum('bjghqk,bjkgd->bjqghd', p_w.astype(vwb.dtype), vwb).reshape(Bsz, S, G, HPG, DH)

    o = (gates[:, :, 0, :, :, None] * o_c + gates[:, :, 1, :, :, None] * o_s
         + gates[:, :, 2, :, :, None] * o_w)
    return o.reshape(Bsz, S, N_HEADS * DH) @ w_out


def s5_mixer(h, A_re, A_im, log_dt, B_re, B_im, C_re, C_im, D_skip, w_glu):
    Bsz, S, D = h.shape
    f32 = jnp.float32
    A_re, A_im, log_dt = A_re.astype(f32), A_im.astype(f32), log_dt.astype(f32)
    B_re, B_im, C_re, C_im = B_re.astype(f32), B_im.astype(f32), C_re.astype(f32), C_im.astype(f32)
    u = h.astype(f32).reshape(Bsz, S, SSM_GROUPS, SSM_GROUP)
    dt = jnp.exp(log_dt)[:, None]
    decay = jnp.exp(A_re * dt)
    ab_re = decay * jnp.cos(A_im * dt)
    ab_im = decay * jnp.sin(A_im * dt)
    pr, pi_ = ab_re - 1.0, ab_im
    den = A_re * A_re + A_im * A_im
    cf_re = (pr * A_re + pi_ * A_im) / den
    cf_im = (pi_ * A_re - pr * A_im) / den
    bb_re = cf_re[..., None] * B_re - cf_im[..., None] * B_im
    bb_im = cf_re[..., None] * B_im + cf_im[..., None] * B_re
    bu_re = jnp.einsum('bsgc,gnc->bsgn', u, bb_re)
    bu_im = jnp.einsum('bsgc,gnc->bsgn', u, bb_im)
    a_re = jnp.broadcast_to(ab_re, bu_re.shape)
    a_im = jnp.broadcast_to(ab_im, bu_im.shape)

    def combine(e1, e2):
        a1r, a1i, b1r, b1i = e1
        a2r, a2i, b2r, b2i = e2
        return (a1r * a2r - a1i * a2i, a1r * a2i + a1i * a2r,
                a2r * b1r - a2i * b1i + b2r, a2r * b1i + a2i * b1r + b2i)

    _, _, x_re, x_im = lax.associative_scan(combine, (a_re, a_im, bu_re, bu_im), axis=1)
    y = (jnp.einsum('bsgn,gcn->bsgc', x_re, C_re) - jnp.einsum('bsgn,gcn->bsgc', x_im, C_im)
         + D_skip.astype(f32).reshape(SSM_GROUPS, SSM_GROUP) * u)
    y = jax.nn.gelu(y).reshape(Bsz, S, D).astype(h.dtype)
    z = y @ w_glu
    return z[..., :D] * jax.nn.sigmoid(z[..., D:])


def swiglu(h, w_in, w_out):
    z = h @ w_in
    a, b = z[..., :D_FF], z[..., D_FF:]
    return (jax.nn.silu(a) * b) @ w_out


def setup_inputs(seed: int = 0) -> dict:
    key = jax.random.key(seed)
    ks = jax.random.split(key, 32)

    def nrm(k, shape, scale):
        return jax.random.normal(k, shape, jnp.float32) * scale

    D, H, DH, G, N, C = D_MODEL, N_HEADS, HEAD_DIM, SSM_GROUPS, SSM_STATE, SSM_GROUP
    n_idx = jnp.arange(N, dtype=jnp.float32)
    return {
        "x": nrm(ks[0], (BATCH, SEQ, D), 1.0),
        "rel_bias": nrm(ks[1], (REL_BUCKETS, H), 0.2),
        "mix_norm_g": 1.0 + nrm(ks[2], (DEPTH, D), 0.02),
        "ffn_norm_g": 1.0 + nrm(ks[3], (DEPTH, D), 0.02),
        "final_norm_g": 1.0 + nrm(ks[4], (D,), 0.02),
        "nsa_w_in": nrm(ks[5], (N_NSA_LAYERS, D, NSA_IN_COLS), D ** -0.5),
        "nsa_w_out": nrm(ks[6], (N_NSA_LAYERS, H * DH, D), (H * DH) ** -0.5),
        "cmp_pos_k": nrm(ks[7], (N_NSA_LAYERS, CMP_BLOCK, DH), 0.1),
        "cmp_w1_k": nrm(ks[8], (N_NSA_LAYERS, CMP_BLOCK * DH, CMP_HIDDEN), (CMP_BLOCK * DH) ** -0.5),
        "cmp_w2_k": nrm(ks[9], (N_NSA_LAYERS, CMP_HIDDEN, DH), CMP_HIDDEN ** -0.5),
        "cmp_pos_v": nrm(ks[10], (N_NSA_LAYERS, CMP_BLOCK, DH), 0.1),
        "cmp_w1_v": nrm(ks[11], (N_NSA_LAYERS, CMP_BLOCK * DH, CMP_HIDDEN), (CMP_BLOCK * DH) ** -0.5),
        "cmp_w2_v": nrm(ks[12], (N_NSA_LAYERS, CMP_HIDDEN, DH), CMP_HIDDEN ** -0.5),
        "s5_A_re": -0.5 + nrm(ks[13], (N_S5_LAYERS, G, N), 0.01),
        "s5_A_im": math.pi * n_idx + nrm(ks[14], (N_S5_LAYERS, G, N), 0.01),
        "s5_log_dt": jax.random.uniform(ks[15], (N_S5_LAYERS, G), jnp.float32,
                                        math.log(1e-3), math.log(1e-1)),
        "s5_B_re": nrm(ks[16], (N_S5_LAYERS, G, N, C), (2 * C) ** -0.5),
        "s5_B_im": nrm(ks[17], (N_S5_LAYERS, G, N, C), (2 * C) ** -0.5),
        "s5_C_re": nrm(ks[18], (N_S5_LAYERS, G, C, N), (2 * N) ** -0.5),
        "s5_C_im": nrm(ks[19], (N_S5_LAYERS, G, C, N), (2 * N) ** -0.5),
        "s5_D": nrm(ks[20], (N_S5_LAYERS, D), 1.0),
        "s5_w_glu": nrm(ks[21], (N_S5_LAYERS, D, 2 * D), D ** -0.5),
        "ffn_w_in": nrm(ks[22], (DEPTH, D, 2 * D_FF), D ** -0.5),
        "ffn_w_out": nrm(ks[23], (DEPTH, D_FF, D), D_FF ** -0.5),
    }


def reference(x, rel_bias, mix_norm_g, ffn_norm_g, final_norm_g, nsa_w_in, nsa_w_out,
              cmp_pos_k, cmp_w1_k, cmp_w2_k, cmp_pos_v, cmp_w1_v, cmp_w2_v,
              s5_A_re, s5_A_im, s5_log_dt, s5_B_re, s5_B_im, s5_C_re, s5_C_im, s5_D, s5_w_glu,
              ffn_w_in, ffn_w_out):
    h = x
    for i in range(DEPTH):
        hn = rmsnorm(h, mix_norm_g[i])
        li = i // N_MIXERS
        if i % N_MIXERS == 0:
            h = h + nsa_mixer(hn, rel_bias, nsa_w_in[li], nsa_w_out[li],
                              cmp_pos_k[li], cmp_w1_k[li], cmp_w2_k[li],
                              cmp_pos_v[li], cmp_w1_v[li], cmp_w2_v[li])
        else:
            h = h + s5_mixer(hn, s5_A_re[li], s5_A_im[li], s5_log_dt[li], s5_B_re[li], s5_B_im[li],
                             s5_C_re[li], s5_C_im[li], s5_D[li], s5_w_glu[li])
        h = h + swiglu(rmsnorm(h, ffn_norm_g[i]), ffn_w_in[i], ffn_w_out[i])
    return rmsnorm(h, final_norm_g)
```

```python
import math
from contextlib import ExitStack
import numpy as np
import ml_dtypes
import concourse.bass as bass
import concourse.mybir as mybir
from concourse.bass_utils import run_bass_kernel_spmd


F32 = mybir.dt.float32
BF16 = mybir.dt.bfloat16
AF = mybir.ActivationFunctionType
ALU = mybir.AluOpType
AX = mybir.AxisListType


class Res:
    __slots__ = ("name", "last_w", "readers", "dsem", "dcnt")

    def __init__(self, name):
        self.name = name
        self.last_w = None
        self.readers = []
        self.dsem = None
        self.dcnt = 0


class Prog:
    def __init__(self):
        self.nc = bass.Bass("TRN2", target_bir_lowering=False)
        self.es = ExitStack()
        nc = self.nc
        self.eng = {"pe": nc.tensor, "act": nc.scalar, "dve": nc.vector, "pool": nc.gpsimd, "sp": nc.sync}
        self.esem = {}
        self.ecnt = {}
        self.waited = {e: {} for e in self.eng}
        self.semid = {}
        for e in self.eng:
            s = self.es.enter_context(nc.semaphore("es_" + e))
            self.esem[e] = s
            self.ecnt[e] = 0
        self.nsem = len(self.eng)
        self.out_tokens = []
        self.dma_toks = {}
        self._n = 0
        self._stacks = [self.es]

    def uname(self, base):
        self._n += 1
        return f"{base}_{self._n}"

    def sbuf(self, name, shape, dt):
        return self._stacks[-1].enter_context(self.nc.sbuf_tensor(self.uname(name), list(shape), dt))

    def push_scope(self):
        st = ExitStack()
        self._stacks.append(st)
        return st

    def pop_scope(self):
        self.barrier()
        st = self._stacks.pop()
        st.close()

    def barrier(self):
        for e in self.eng:
            for e2 in self.eng:
                if e2 != e and self.ecnt[e2] > 0:
                    self._wait(e, (self.esem[e2], self.ecnt[e2]))
            for tok in self.dma_toks.values():
                self._wait(e, tok)

    def psum(self, name, shape, dt=F32):
        return self.es.enter_context(self.nc.psum_tensor(self.uname(name), list(shape), dt))

    def dram(self, name, shape, dt, kind):
        return self.nc.dram_tensor(name, list(shape), dt, kind=kind).ap()

    def res(self, name="r"):
        return Res(name)

    def _wait(self, e, tok):
        if tok is None:
            return
        sem, val = tok
        if e == "pe" and sem is self.esem["pe"]:
            return
        k = id(sem)
        w = self.waited[e]
        if w.get(k, 0) >= val:
            return
        self.eng[e].wait_ge(sem, val)
        w[k] = val

    def _deps(self, e, r, w):
        for x in r:
            self._wait(e, x.last_w)
        for x in w:
            self._wait(e, x.last_w)
            for t in x.readers:
                self._wait(e, t)

    def _commit(self, tok, r, w):
        for x in r:
            x.readers.append(tok)
            if len(x.readers) > 64:
                best = {}
                for s, v in x.readers:
                    if id(s) not in best or best[id(s)][1] < v:
                        best[id(s)] = (s, v)
                x.readers = list(best.values())
        for x in w:
            x.last_w = tok
            x.readers = []

    def op(self, e, fn, r=(), w=()):
        self._deps(e, r, w)
        inst = fn(self.eng[e])
        inst.then_inc(self.esem[e], 1)
        self.ecnt[e] += 1
        tok = (self.esem[e], self.ecnt[e])
        self._commit(tok, r, w)
        return tok

    def dma(self, q, out, in_, r=(), w=(), sres=None, is_output=False, **kw):
        self._deps(q, r, w)
        if sres is None:
            sres = (list(w) + list(r))[0]
        if sres.dsem is None:
            sres.dsem = self.es.enter_context(self.nc.semaphore(self.uname("ds")))
            self.nsem += 1
        inst = self.eng[q].dma_start(out=out, in_=in_, **kw)
        inst.then_inc(sres.dsem, 16)
        sres.dcnt += 16
        tok = (sres.dsem, sres.dcnt)
        self.dma_toks[id(sres.dsem)] = tok
        self._commit(tok, r, w)
        if is_output:
            self.out_tokens.append(tok)
        return tok

    def finish(self):
        best = {}
        for s, v in self.out_tokens:
            if id(s) not in best or best[id(s)][1] < v:
                best[id(s)] = (s, v)
        for s, v in best.values():
            self.eng["sp"].wait_ge(s, v)
        return self.nc

    def close(self):
        self.es.close()


BF = ml_dtypes.bfloat16
NEG = -30000.0


def own_idx(c):
    return np.concatenate([np.arange((8 * i + c) * 128, (8 * i + c) * 128 + 128) for i in range(8)])


def gcol(g):
    return np.ascontiguousarray(np.asarray(g, np.float32).reshape(16, 128).T)


def rel_bucket_np(dist):
    n = np.maximum(dist, 0).astype(np.int64)
    max_exact = 16
    nf = np.maximum(n, 1).astype(np.float32)
    large = max_exact + (np.log(nf / np.float32(max_exact)) / np.float32(np.log(128 / 16)) * np.float32(16)).astype(np.int32)
    large = np.minimum(large, 31)
    return np.where(n < max_exact, n, large).astype(np.int64)


def p2_tables(rel_bias, c):
    rel_bias = np.asarray(rel_bias, np.float32)
    kl = np.arange(128)[:, None]
    ql = np.arange(128)[None, :]
    tb_bias = np.zeros((9, 128, 16, 128), np.float32)
    tb_mask = np.zeros((128, 9, 128), np.float32)
    for jj in range(9):
        j = jj - 1
        r = c - j
        dist = r * 128 + ql - kl
        b = rel_bias[rel_bucket_np(dist)]
        tb_bias[jj] = b.transpose(0, 2, 1)
        tb_mask[:, jj, :] = np.where(dist < 0, NEG, 0.0)
    win_mask = np.zeros((128, 12, 128), np.float32)
    for jp in range(12):
        r = c + 4 - jp
        dist = r * 128 + ql - kl
        win_mask[:, jp, :] = np.where((dist < 0) | (dist >= 512), NEG, 0.0)
    qcol = np.arange(128)[:, None]
    e = np.arange(72)[None, :]
    dist_c = 128 * c + qcol - 16 * (e - 9) - 31
    ec_bias = rel_bias[rel_bucket_np(dist_c)].transpose(0, 2, 1)
    ec_mask = np.where(dist_c < 0, NEG, 0.0).astype(np.float32)
    fq = np.zeros((128, 8, 128), np.float32)
    blk = np.arange(128)[None, :]
    for i in range(8):
        t = (8 * i + c) * 128 + np.arange(128)[:, None]
        tb = t // 64
        f = np.zeros((128, 128), np.float32)
        f = np.where(blk > tb, -1e30, f)
        f = np.where(blk == tb - 1, 1e9, f)
        f = np.where(blk == tb, 2e9, f)
        f = np.where(blk == 0, 3e9, f)
        fq[:, i, :] = f
    return {
        "tb_bias": np.ascontiguousarray(tb_bias.reshape(9, 128, 2048)),
        "tb_mask": np.ascontiguousarray(tb_mask.reshape(128, 9 * 128)),
        "b31": np.ascontiguousarray(np.broadcast_to(rel_bias[31][None, :], (128, 16))),
        "win_mask": np.ascontiguousarray(win_mask.reshape(128, 12 * 128)),
        "ec_bias": np.ascontiguousarray(ec_bias.reshape(128, 16 * 72)),
        "ec_mask": ec_mask,
        "fq": np.ascontiguousarray(fq.reshape(128, 8 * 128)),
    }


def p2_consts():
    key = np.arange(8192)[None, :]
    j = np.arange(128)[:, None]
    R = (key // 64 == j).astype(np.float32).astype(BF)
    ident = np.eye(128, dtype=np.float32).astype(BF)
    return {"Rm": R, "ident": ident}


def p2_inputs(inp, l1, c, shared):
    r = l1[c]
    qT = np.asarray(r["qT"])
    q_l = qT.reshape(4, 4, 128, 8, 128).transpose(3, 0, 2, 1, 4).reshape(8, 4, 128, 512)
    gates = np.asarray(r["gates"]).reshape(8, 128, 48).transpose(1, 0, 2).reshape(128, 8 * 48)
    d = {"q_l": np.ascontiguousarray(q_l), "gates_l": np.ascontiguousarray(gates)}
    d.update(shared)
    d.update(p2_tables(inp["rel_bias"], c))
    return d


def p2_shared(inp, l1):
    sh = {}
    for nm in ("kcT", "vcT", "ksT", "kwT"):
        full = np.zeros((512, 8192), BF)
        for c in range(8):
            full[:, own_idx(c)] = np.asarray(l1[c][nm])
        sh[nm] = np.ascontiguousarray(full.reshape(4, 128, 8192))
    for nm, out in (("vs", "vs1"), ("vw", "vw1")):
        full = np.zeros((8192, 512), BF)
        for c in range(8):
            full[own_idx(c)] = np.asarray(l1[c][nm])
        v = full.reshape(64, 128, 4, 128).transpose(2, 1, 0, 3)
        v1 = np.ones((4, 128, 64, 129), BF)
        v1[..., :128] = v
        sh[out] = np.ascontiguousarray(v1.reshape(4, 128, 64 * 129))
    for kv in ("k", "v"):
        w1 = np.asarray(inp[f"cmp_w1_{kv}"][0], np.float32)
        sh[f"w1{kv}"] = np.ascontiguousarray(w1.reshape(32, 128, 256).transpose(1, 0, 2).reshape(128, 32 * 256))
        w2 = np.asarray(inp[f"cmp_w2_{kv}"][0], np.float32)
        sh[f"w2{kv}"] = np.ascontiguousarray(w2.reshape(2, 128, 128).transpose(1, 0, 2).reshape(128, 256))
        sh[f"posT{kv}"] = np.ascontiguousarray(np.asarray(inp[f"cmp_pos_{kv}"][0], np.float32).T)
    sh.update(p2_consts())
    return sh


def s5prep_inputs(inp):
    return {"A_re": np.ascontiguousarray(inp["s5_A_re"][0]), "A_im": np.ascontiguousarray(inp["s5_A_im"][0]),
            "log_dt": np.ascontiguousarray(inp["s5_log_dt"][0].reshape(128, 1)),
            "B_re": np.ascontiguousarray(inp["s5_B_re"][0].reshape(128, 1024)),
            "B_im": np.ascontiguousarray(inp["s5_B_im"][0].reshape(128, 1024))}


def s5main_inputs(inp, prep, uT_full, c):
    r = np.asarray(prep["o_r"]); th = np.asarray(prep["o_th"])
    bbre = np.asarray(prep["o_bbre"]).reshape(128, 64, 16); bbim = np.asarray(prep["o_bbim"]).reshape(128, 64, 16)
    C_re = np.asarray(inp["s5_C_re"][0]); C_im = np.asarray(inp["s5_C_im"][0])
    D = np.asarray(inp["s5_D"][0])
    BD = np.zeros((128, 8, 2, 128), np.float32)
    CT = np.zeros((128, 8, 2, 128), np.float32)
    rcol = np.zeros((128, 8), np.float32); thcol = np.zeros((128, 8), np.float32)
    for k in range(8):
        kg = 8 * c + k
        for gg in range(2):
            g = 2 * kg + gg
            ch0 = 32 * (k % 4) + 16 * gg
            st0 = 64 * gg
            BD[ch0:ch0 + 16, k, 0, st0:st0 + 64] = bbre[g].T
            BD[ch0:ch0 + 16, k, 1, st0:st0 + 64] = bbim[g].T
            CT[st0:st0 + 64, k, 0, ch0:ch0 + 16] = C_re[g].T
            CT[st0:st0 + 64, k, 1, ch0:ch0 + 16] = C_im[g].T
            rcol[st0:st0 + 64, k] = r[g]
            thcol[st0:st0 + 64, k] = th[g]
    Dcol = np.ascontiguousarray(D[256 * c:256 * c + 256].reshape(2, 128).T)
    iota = np.ascontiguousarray(np.broadcast_to(np.arange(512, dtype=np.float32)[None, :], (128, 512)))
    return {"uT": np.ascontiguousarray(uT_full[256 * c:256 * c + 256]), "BD": BD.reshape(128, -1), "CT": CT.reshape(128, -1),
            "rcol": rcol, "thcol": thcol, "iota": iota, "Dcol": Dcol}


D = 2048
NT = 1024
KC = 16
EPS = 1e-6


def rmsnorm_T(p, xT_dram, gcol_sb, gcol_res, hnT, hnT_res, ones_f, ones_res, out_dt_note=""):
    nc = p.nc
    xs = p.sbuf("xs", [128, KC, 512], F32)
    xs_r = p.res("xs")
    sq = [p.sbuf("sq", [128, 512], F32) for _ in range(2)]
    sq_r = [p.res("sq") for _ in range(2)]
    ss = p.psum("ss", [128, 512])
    ss_r = p.res("ss")
    rstd = p.sbuf("rstd", [128, 512], F32)
    rstd_r = p.res("rstd")
    for half in range(NT // 512):
        src = xT_dram[:, half * 512:(half + 1) * 512].rearrange("(k p) t -> p k t", p=128)
        for kk in range(0, KC, 4):
            p.dma("sp", xs[:, kk:kk + 4, :], src[:, kk:kk + 4, :], w=[xs_r])
        for k in range(KC):
            b = k % 2
            p.op("act", lambda e: e.activation(out=sq[b][:], in_=xs[:, k, :], func=AF.Square),
                 r=[xs_r], w=[sq_r[b]])
            p.op("pe", lambda e: e.matmul(ss[:], ones_f[:], sq[b][:], start=(k == 0), stop=(k == KC - 1)),
                 r=[sq_r[b], ones_res], w=[ss_r])
        p.op("dve", lambda e: e.tensor_scalar(out=rstd[:], in0=ss[:], scalar1=1.0 / D, scalar2=EPS,
                                              op0=ALU.mult, op1=ALU.add), r=[ss_r], w=[rstd_r])
        p.op("act", lambda e: e.activation(out=rstd[:], in_=rstd[:], func=AF.Sqrt), r=[rstd_r], w=[rstd_r])
        p.op("dve", lambda e: e.reciprocal(out=rstd[:], in_=rstd[:]), r=[rstd_r], w=[rstd_r])
        for k in range(KC):
            p.op("dve", lambda e: e.scalar_tensor_tensor(
                out=hnT[:, k, half * 512:(half + 1) * 512], in0=xs[:, k, :], scalar=gcol_sb[:, k:k + 1],
                in1=rstd[:], op0=ALU.mult, op1=ALU.mult), r=[xs_r, rstd_r, gcol_res], w=[hnT_res])


def build_p1(only=None):
    p = Prog()
    nc = p.nc
    xT = p.dram("xT", [D, NT], F32, "ExternalInput")
    gcol = p.dram("gcol", [128, KC], F32, "ExternalInput")
    w_in = p.dram("w_in", [D, 5168], F32, "ExternalInput")
    qT = p.dram("qT", [2048, NT], BF16, "ExternalOutput")
    kT = {n: p.dram(n, [512, NT], BF16, "ExternalOutput") for n in ("kcT", "vcT", "ksT", "kwT")}
    vv = {n: p.dram(n, [NT, 512], BF16, "ExternalOutput") for n in ("vs", "vw")}
    gates = p.dram("gates", [NT, 48], F32, "ExternalOutput")

    ones_f = p.sbuf("ones", [128, 128], F32)
    ones_r = p.res("ones")
    p.op("pool", lambda e: e.memset(ones_f[:], 1.0), w=[ones_r])
    gsb = p.sbuf("gsb", [128, KC], F32)
    g_r = p.res("g")
    p.dma("sp", gsb[:], gcol[:], w=[g_r])
    hnT = p.sbuf("hnT", [128, KC, NT], BF16)
    hnT_r = p.res("hnT")
    rmsnorm_T(p, xT, gsb, g_r, hnT, hnT_r, ones_f, ones_r)

    wst = [p.sbuf("wst", [128, KC, 512], F32) for _ in range(2)]
    wst_r = [p.res("wst") for _ in range(2)]
    wb = [p.sbuf("wb", [128, KC, 512], BF16) for _ in range(2)]
    wb_r = [p.res("wb") for _ in range(2)]
    ps = [p.psum("ps", [128, 512]) for _ in range(4)]
    ps_r = [p.res("ps") for _ in range(4)]
    ot = [p.sbuf("ot", [128, 512], BF16) for _ in range(4)]
    ot_r = [p.res("ot") for _ in range(4)]
    og = p.sbuf("og", [128, 48], F32)
    og_r = p.res("og")
    chunks = [("qT", 0), ("qT", 1), ("qT", 2), ("qT", 3), ("kcT", 0), ("vcT", 0), ("ksT", 0), ("vs", 0),
              ("kwT", 0), ("vw", 0), ("gates", 0)]
    pi = 0
    for ci, (nm, sub) in enumerate(chunks):
        if only is not None and ci not in only:
            continue
        c0 = ci * 512
        ncol = 512 if nm != "gates" else 48
        b = ci % 2
        src = w_in[:, c0:c0 + ncol].rearrange("(k p) c -> p k c", p=128)
        for kk in range(0, KC, 4):
            qn = "sp" if ((kk // 4) % 2 == 0 or nm == "gates") else "pool"
            p.dma(qn, wst[b][:, kk:kk + 4, :ncol], src[:, kk:kk + 4, :], w=[wst_r[b]])
        for kk in range(0, KC, 8):
            p.op("pool" if nm != "gates" else "dve", lambda e: e.tensor_copy(out=wb[b][:, kk:kk + 8, :ncol], in_=wst[b][:, kk:kk + 8, :ncol]),
                 r=[wst_r[b]], w=[wb_r[b]])
        if nm in ("qT", "kcT", "vcT", "ksT", "kwT"):
            dst = qT if nm == "qT" else kT[nm]
            scale = 128 ** -0.5 if nm == "qT" else 1.0
            for m in range(4):
                for n in range(NT // 512):
                    pb = pi % 4
                    pi += 1
                    for k in range(KC):
                        p.op("pe", lambda e: e.matmul(ps[pb][:], wb[b][:, k, m * 128:(m + 1) * 128],
                                                      hnT[:, k, n * 512:(n + 1) * 512], start=(k == 0), stop=(k == KC - 1)),
                             r=[wb_r[b], hnT_r], w=[ps_r[pb]])
                    p.op("act", lambda e: e.activation(out=ot[pb][:], in_=ps[pb][:], func=AF.Copy, scale=scale),
                         r=[ps_r[pb]], w=[ot_r[pb]])
                    row0 = sub * 512 + m * 128
                    p.dma("sp", dst[row0:row0 + 128, n * 512:(n + 1) * 512], ot[pb][:], r=[ot_r[pb]], is_output=True)
        else:
            for t in range(NT // 128):
                pb = pi % 4
                pi += 1
                for k in range(KC):
                    p.op("pe", lambda e: e.matmul(ps[pb][:, :ncol], hnT[:, k, t * 128:(t + 1) * 128],
                                                  wb[b][:, k, :ncol], start=(k == 0), stop=(k == KC - 1)),
                         r=[wb_r[b], hnT_r], w=[ps_r[pb]])
                if nm == "gates":
                    p.op("act", lambda e: e.activation(out=og[:], in_=ps[pb][:, :48], func=AF.Sigmoid),
                         r=[ps_r[pb]], w=[og_r])
                    p.dma("sp", gates[t * 128:(t + 1) * 128, :], og[:], r=[og_r], is_output=True)
                else:
                    p.op("act", lambda e: e.activation(out=ot[pb][:], in_=ps[pb][:], func=AF.Copy),
                         r=[ps_r[pb]], w=[ot_r[pb]])
                    p.dma("sp", vv[nm][t * 128:(t + 1) * 128, :], ot[pb][:], r=[ot_r[pb]], is_output=True)
    p.finish()
    p.close()
    return p.nc


NI = 8
G = 4
NCMP = 511


def build_p2(only_groups=None, do_sel=True, do_win=True):
    p = Prog()
    q_l = p.dram("q_l", [NI, G, 128, 512], BF16, "ExternalInput")
    gates_l = p.dram("gates_l", [128, NI * 48], F32, "ExternalInput")
    kcT = p.dram("kcT", [G, 128, 8192], BF16, "ExternalInput")
    vcT = p.dram("vcT", [G, 128, 8192], BF16, "ExternalInput")
    ksT = p.dram("ksT", [G, 128, 8192], BF16, "ExternalInput")
    kwT = p.dram("kwT", [G, 128, 8192], BF16, "ExternalInput")
    vs1 = p.dram("vs1", [G, 128, 64 * 129], BF16, "ExternalInput")
    vw1 = p.dram("vw1", [G, 128, 64 * 129], BF16, "ExternalInput")
    w1 = {"k": p.dram("w1k", [128, 32 * 256], F32, "ExternalInput"), "v": p.dram("w1v", [128, 32 * 256], F32, "ExternalInput")}
    w2 = {"k": p.dram("w2k", [128, 2 * 128], F32, "ExternalInput"), "v": p.dram("w2v", [128, 2 * 128], F32, "ExternalInput")}
    posT = {"k": p.dram("posTk", [128, 32], F32, "ExternalInput"), "v": p.dram("posTv", [128, 32], F32, "ExternalInput")}
    tb_bias = p.dram("tb_bias", [9, 128, 16 * 128], F32, "ExternalInput")
    tb_mask = p.dram("tb_mask", [128, 9 * 128], F32, "ExternalInput")
    b31 = p.dram("b31", [128, 16], F32, "ExternalInput")
    win_mask = p.dram("win_mask", [128, 12 * 128], F32, "ExternalInput")
    ec_bias = p.dram("ec_bias", [128, 16 * 72], F32, "ExternalInput")
    ec_mask = p.dram("ec_mask", [128, 72], F32, "ExternalInput")
    fq = p.dram("fq", [128, NI * 128], F32, "ExternalInput")
    Rm = p.dram("Rm", [128, 8192], BF16, "ExternalInput")
    ident = p.dram("ident", [128, 128], BF16, "ExternalInput")
    o_out = p.dram("o_out", [NI * 128, 2048], BF16, "ExternalOutput")

    def T(name, shape, dt):
        return p.sbuf(name, shape, dt), p.res(name)

    kccT, kccT_r = T("kccT", [128, G, 512], BF16)
    vcc, vcc_r = T("vcc", [128, G, 4, 128], BF16)
    R_sb, R_r = T("R", [128, 8192], BF16)
    id_sb, id_r = T("ident", [128, 128], BF16)
    Tt, Tt_r = T("Tt", [128, 9, 16 * 128], BF16)
    Wm4, Wm4_r = T("Wm4", [128, 12, 4, 128], BF16)
    Fq, Fq_r = T("Fq", [128, NI, 128], F32)
    Ec, Ec_r = T("Ec", [128, 16, 72], F32)
    b31s, b31_r = T("b31", [128, 16], F32)
    gat, gat_r = T("gat", [128, NI, 48], F32)
    psS = [p.psum("psS", [128, 512]) for _ in range(2)]
    psS_r = [p.res("psS") for _ in range(2)]
    psO = [p.psum("psO", [128, 512]) for _ in range(4)]
    psO_r = [p.res("psO") for _ in range(4)]
    psC = psS
    psC_r = psS_r
    psT = p.psum("psT", [128, 1024], BF16)
    psT_r = p.res("psT")
    psX = p.psum("psX", [128, 512])
    psX_r = p.res("psX")

    p.dma("sp", R_sb[:], Rm[:], w=[R_r])
    p.dma("sp", id_sb[:], ident[:], w=[id_r])
    p.dma("sp", Fq[:].rearrange("p i b -> p (i b)"), fq[:], w=[Fq_r])
    p.dma("sp", b31s[:], b31[:], w=[b31_r])
    p.dma("sp", gat[:].rearrange("p i c -> p (i c)"), gates_l[:], w=[gat_r])

    p.push_scope()
    stg = [T("stg", [128, 2048], F32) for _ in range(2)]
    msk, msk_r = T("msk", [128, 12 * 128], F32)
    p.dma("sp", msk[:, 0:9 * 128], tb_mask[:], w=[msk_r])
    for j in range(9):
        sb, sr = stg[j % 2]
        p.dma("sp", sb[:], tb_bias[j], w=[sr])
        v3 = sb[:].rearrange("p (h q) -> p h q", h=16)
        p.op("dve", lambda e: e.tensor_tensor(out=v3, in0=v3, in1=b31s[:].unsqueeze(2).to_broadcast([128, 16, 128]),
                                              op=ALU.subtract), r=[sr, b31_r], w=[sr])
        p.op("dve", lambda e: e.tensor_tensor(out=Tt[:, j, :].rearrange("p (h q) -> p h q", h=16), in0=v3,
                                              in1=msk[:, j * 128:(j + 1) * 128].unsqueeze(1).to_broadcast([128, 16, 128]),
                                              op=ALU.add), r=[sr, msk_r], w=[Tt_r])
    p.dma("sp", msk[:], win_mask[:], w=[msk_r])
    for j in range(12):
        p.op("dve", lambda e: e.tensor_copy(out=Wm4[:, j, :, :],
                                            in_=msk[:, j * 128:(j + 1) * 128].unsqueeze(1).to_broadcast([128, 4, 128])),
             r=[msk_r], w=[Wm4_r])
    sb, sr = stg[0]
    p.dma("sp", sb[:, 0:16 * 72], ec_bias[:], w=[sr])
    p.dma("sp", msk[:, 0:72], ec_mask[:], w=[msk_r])
    v3 = sb[:, 0:16 * 72].rearrange("p (h e) -> p h e", h=16)
    p.op("dve", lambda e: e.tensor_tensor(out=v3, in0=v3, in1=b31s[:].unsqueeze(2).to_broadcast([128, 16, 72]),
                                          op=ALU.subtract), r=[sr, b31_r], w=[sr])
    p.op("dve", lambda e: e.tensor_tensor(out=Ec[:], in0=v3, in1=msk[:, 0:72].unsqueeze(1).to_broadcast([128, 16, 72]),
                                          op=ALU.add), r=[sr, msk_r], w=[Ec_r])
    w1b, w1b_r = T("w1b", [128, 32, 256], BF16)
    w2b, w2b_r = T("w2b", [128, 2, 128], BF16)
    posb, posb_r = T("posb", [128, 32], BF16)
    pw1, pw1_r = T("pw1", [128, 2], F32)
    kvT, kvT_r = T("kvT", [128, 8192], BF16)
    hidT, hidT_r = T("hidT", [128, 2, 512], BF16)
    xg, xg_r = T("xg", [128, 512], F32)
    tg, tg_r = T("tg", [128, 512], F32)
    p.op("dve", lambda e: e.memset(vcc[:], 0.0), w=[vcc_r])
    p.op("dve", lambda e: e.memset(kccT[:], 0.0), w=[kccT_r])
    for kv in ("k", "v"):
        for q4 in range(4):
            sb, sr = stg[q4 % 2]
            p.dma("sp", sb[:], w1[kv][:, q4 * 2048:(q4 + 1) * 2048], w=[sr])
            p.op("dve", lambda e: e.tensor_copy(out=w1b[:, q4 * 8:(q4 + 1) * 8, :].rearrange("p j c -> p (j c)"), in_=sb[:]),
                 r=[sr], w=[w1b_r])
        sb, sr = stg[0]
        p.dma("sp", sb[:, 0:256], w2[kv][:], w=[sr])
        p.op("dve", lambda e: e.tensor_copy(out=w2b[:].rearrange("p a b -> p (a b)"), in_=sb[:, 0:256]), r=[sr], w=[w2b_r])
        sb, sr = stg[1]
        p.dma("sp", sb[:, 0:32], posT[kv][:], w=[sr])
        p.op("dve", lambda e: e.tensor_copy(out=posb[:], in_=sb[:, 0:32]), r=[sr], w=[posb_r])
        for hc in range(2):
            for j in range(32):
                p.op("pe", lambda e: e.matmul(psX[:, 0:1], w1b[:, j, hc * 128:(hc + 1) * 128], posb[:, j:j + 1],
                                              start=(j == 0), stop=(j == 31)), r=[w1b_r, posb_r], w=[psX_r])
            p.op("dve", lambda e: e.tensor_copy(out=pw1[:, hc:hc + 1], in_=psX[:, 0:1]), r=[psX_r], w=[pw1_r])
        src = kcT if kv == "k" else vcT
        for g in range(G):
            p.dma("sp", kvT[:], src[g], w=[kvT_r])
            for hc in range(2):
                pc = psC[hc]
                for j in range(32):
                    p.op("pe", lambda e: e.matmul(pc[:, 0:NCMP], w1b[:, j, hc * 128:(hc + 1) * 128],
                                                  kvT[:, j:j + 16 * (NCMP - 1) + 1:16], start=(j == 0), stop=(j == 31)),
                         r=[w1b_r, kvT_r], w=[psC_r[hc]])
                p.op("act", lambda e: e.activation(out=xg[:, 0:NCMP], in_=pc[:, 0:NCMP], func=AF.Identity,
                                                   bias=pw1[:, hc:hc + 1]), r=[psC_r[hc], pw1_r], w=[xg_r])
                p.op("dve", lambda e: e.tensor_tensor(out=tg[:, 0:NCMP], in0=xg[:, 0:NCMP], in1=xg[:, 0:NCMP], op=ALU.mult),
                     r=[xg_r], w=[tg_r])
                p.op("dve", lambda e: e.tensor_scalar(out=tg[:, 0:NCMP], in0=tg[:, 0:NCMP], scalar1=0.044715, scalar2=1.0,
                                                      op0=ALU.mult, op1=ALU.add), r=[tg_r], w=[tg_r])
                p.op("dve", lambda e: e.tensor_tensor(out=tg[:, 0:NCMP], in0=tg[:, 0:NCMP], in1=xg[:, 0:NCMP], op=ALU.mult),
                     r=[tg_r, xg_r], w=[tg_r])
                p.op("act", lambda e: e.activation(out=tg[:, 0:NCMP], in_=tg[:, 0:NCMP], func=AF.Sigmoid, scale=1.5957691216),
                     r=[tg_r], w=[tg_r])
                p.op("dve", lambda e: e.tensor_tensor(out=hidT[:, hc, 0:NCMP], in0=tg[:, 0:NCMP], in1=xg[:, 0:NCMP], op=ALU.mult),
                     r=[tg_r, xg_r], w=[hidT_r])
            if kv == "k":
                for hc in range(2):
                    p.op("pe", lambda e: e.matmul(psX[:, 0:NCMP], w2b[:, hc, :], hidT[:, hc, 0:NCMP],
                                                  start=(hc == 0), stop=(hc == 1)), r=[w2b_r, hidT_r], w=[psX_r])
                p.op("act", lambda e: e.activation(out=kccT[:, g, 0:NCMP], in_=psX[:, 0:NCMP], func=AF.Copy),
                     r=[psX_r], w=[kccT_r])
            else:
                for cb in range(4):
                    n = min(NCMP, (cb + 1) * 128) - cb * 128
                    for hc in range(2):
                        p.op("pe", lambda e: e.matmul(psX[0:n, cb * 128:(cb + 1) * 128], hidT[:, hc, cb * 128:cb * 128 + n],
                                                      w2b[:, hc, :], start=(hc == 0), stop=(hc == 1)),
                             r=[w2b_r, hidT_r], w=[psX_r])
                    p.op("act", lambda e: e.activation(out=vcc[0:n, g, cb, :], in_=psX[0:n, cb * 128:(cb + 1) * 128], func=AF.Copy),
                         r=[psX_r], w=[vcc_r])
    p.pop_scope()

    ksT_s, ksT_r = T("ksT", [128, 8192], BF16)
    kwT_s, kwT_r = T("kwT", [128, 8192], BF16)
    vs_s, vs_r = T("vs1", [128, 64, 129], BF16)
    vw_s, vw_r = T("vw1", [128, 64, 129], BF16)
    qg = [T("qg", [128, 512], BF16) for _ in range(2)]
    s_sb = [T("s_sb", [128, 512], F32) for _ in range(2)]
    e_sb = [T("e_sb", [128, 512], F32) for _ in range(2)]
    p_bf = [T("p_bf", [128, 512], BF16) for _ in range(2)]
    pT_sb, pT_r = T("pT", [128, 4, 128], BF16)
    imp, imp_r = T("imp", [128, 520], F32)
    sc, sc_r = T("sc", [128, 128], F32)
    wk, wk_r = T("wk", [128, 128], F32)
    m8, m8_r = T("m8", [128, 16], F32)
    st1, st1_r = T("st1", [128, 8], F32)
    negm, negm_r = T("negm", [128, 128], BF16)
    nmT, nmT_r = T("nmT", [128, 4, 128], BF16)
    PT = [T("PT", [128, 512], BF16) for _ in range(3)]
    OA = [T("oacc", [128, 512], F32) for _ in range(2)]
    obf = [T("obf", [128, 512], BF16) for _ in range(2)]
    coef, coef_r = T("coef", [128, 8], F32)
    cnt = {"S": 0, "P": 0, "q": 0, "c": 0}

    def attend(i, g, qv, q_r, kts, K_s, K_r, V_s, V_r, extra, gate_col0, init_acc, oacc, oacc_r):
        n = len(kts)
        for idx, kt in enumerate(kts):
            sb_ = cnt["S"] % 2
            cnt["S"] += 1
            mms = [(K_s[:, kt * 128:(kt + 1) * 128], K_r, qv[:], q_r)] + extra(kt)
            for mi, (lt, lr, rh, rr) in enumerate(mms):
                p.op("pe", lambda e: e.matmul(psS[sb_][:], lt, rh, start=(mi == 0), stop=(mi == len(mms) - 1)),
                     r=[lr, rr], w=[psS_r[sb_]])
            pb_ = cnt["P"] % 3
            cnt["P"] += 1
            Pt, Pt_r = PT[pb_]
            p.op("act", lambda e: e.activation(out=Pt[:], in_=psS[sb_][:], func=AF.Exp), r=[psS_r[sb_]], w=[Pt_r])
            for h in range(4):
                bank = h
                c0 = 0
                p.op("pe", lambda e: e.matmul(psO[bank][:, c0:c0 + 129], Pt[:, h * 128:(h + 1) * 128], V_s[:, kt, :],
                                              start=(idx == 0), stop=(idx == n - 1)), r=[Pt_r, V_r], w=[psO_r[bank]])
        for h in range(4):
            bank = h
            c0 = 0
            head = g * 4 + h
            p.op("dve", lambda e: e.reciprocal(out=coef[:, h:h + 1], in_=psO[bank][:, c0 + 128:c0 + 129]),
                 r=[psO_r[bank]], w=[coef_r])
            p.op("dve", lambda e: e.tensor_tensor(out=coef[:, h:h + 1], in0=coef[:, h:h + 1],
                                                  in1=gat[:, i, gate_col0 + head:gate_col0 + head + 1], op=ALU.mult),
                 r=[coef_r, gat_r], w=[coef_r])
            dst = oacc[:, h * 128:(h + 1) * 128]
            if init_acc:
                p.op("dve", lambda e: e.tensor_scalar(out=dst, in0=psO[bank][:, c0:c0 + 128], scalar1=coef[:, h:h + 1],
                                                      scalar2=None, op0=ALU.mult), r=[psO_r[bank], coef_r], w=[oacc_r])
            else:
                p.op("dve", lambda e: e.scalar_tensor_tensor(out=dst, in0=psO[bank][:, c0:c0 + 128], scalar=coef[:, h:h + 1],
                                                             in1=dst, op0=ALU.mult, op1=ALU.add),
                     r=[psO_r[bank], coef_r, oacc_r], w=[oacc_r])

    groups = list(range(G)) if only_groups is None else only_groups
    for g in groups:
        p.dma("sp", ksT_s[:], ksT[g], w=[ksT_r])
        p.dma("sp", kwT_s[:], kwT[g], w=[kwT_r])
        p.dma("sp", vs_s[:].rearrange("p k d -> p (k d)"), vs1[g], w=[vs_r])
        p.dma("sp", vw_s[:].rearrange("p k d -> p (k d)"), vw1[g], w=[vw_r])
        for i in range(NI):
            qb = cnt["q"] % 2
            cnt["q"] += 1
            qv, q_r = qg[qb]
            p.dma("sp", qv[:], q_l[i, g], w=[q_r])
            oacc, oacc_r = OA[qb]
            ncv = min(NCMP, 64 * i + 63)
            e_lo = 64 * i - 9
            c_lo = max(0, e_lo)
            c_hi = min(ncv, 64 * i + 63)
            ncb = (ncv + 127) // 128
            p.op("pool", lambda e: e.memset(imp[:], 0.0), w=[imp_r])
            for h in range(4):
                head = g * 4 + h
                cb_ = cnt["c"] % 2
                cnt["c"] += 1
                pc, pc_r = psC[cb_], psC_r[cb_]
                s_, s_r = s_sb[cb_]
                e_, e_r = e_sb[cb_]
                pb, pb_r = p_bf[cb_]
                p.op("pe", lambda e: e.matmul(pc[:, 0:ncv], qv[:, h * 128:(h + 1) * 128], kccT[:, g, 0:ncv], start=True, stop=True),
                     r=[q_r, kccT_r], w=[pc_r])
                p.op("act", lambda e: e.activation(out=s_[:, 0:ncv], in_=pc[:, 0:ncv], func=AF.Copy), r=[pc_r], w=[s_r])
                p.op("dve", lambda e: e.tensor_tensor(out=s_[:, c_lo:c_hi], in0=s_[:, c_lo:c_hi],
                                                      in1=Ec[:, head, c_lo - e_lo:c_hi - e_lo], op=ALU.add),
                     r=[s_r, Ec_r], w=[s_r])
                p.op("dve", lambda e: e.reduce_max(out=st1[:, 0:1], in_=s_[:, 0:ncv], axis=AX.X), r=[s_r], w=[st1_r])
                p.op("dve", lambda e: e.tensor_scalar(out=st1[:, 1:2], in0=st1[:, 0:1], scalar1=-1000.0, scalar2=-1.0,
                                                      op0=ALU.max, op1=ALU.mult), r=[st1_r], w=[st1_r])
                p.op("dve", lambda e: e.memset(st1[:, 2:3], 0.0), w=[st1_r])
                p.op("act", lambda e: e.activation(out=e_[:, 0:ncv], in_=s_[:, 0:ncv], func=AF.Exp, bias=st1[:, 1:2],
                                                   accum_out=st1[:, 2:3]), r=[s_r, st1_r], w=[e_r, st1_r])
                p.op("dve", lambda e: e.tensor_scalar(out=st1[:, 3:4], in0=st1[:, 2:3], scalar1=1e-30, scalar2=None, op0=ALU.add),
                     r=[st1_r], w=[st1_r])
                p.op("dve", lambda e: e.reciprocal(out=st1[:, 4:5], in_=st1[:, 3:4]), r=[st1_r], w=[st1_r])
                p.op("dve", lambda e: e.scalar_tensor_tensor(out=imp[:, 1:1 + ncv], in0=e_[:, 0:ncv], scalar=st1[:, 4:5],
                                                             in1=imp[:, 1:1 + ncv], op0=ALU.mult, op1=ALU.add),
                     r=[e_r, st1_r, imp_r], w=[imp_r])
                p.op("dve", lambda e: e.tensor_tensor(out=st1[:, 5:6], in0=st1[:, 4:5], in1=gat[:, i, head:head + 1], op=ALU.mult),
                     r=[st1_r, gat_r], w=[st1_r])
                p.op("pool", lambda e: e.memset(pb[:], 0.0), w=[pb_r])
                p.op("dve", lambda e: e.tensor_scalar(out=pb[:, 0:ncv], in0=e_[:, 0:ncv], scalar1=st1[:, 5:6], scalar2=None,
                                                      op0=ALU.mult), r=[e_r, st1_r], w=[pb_r])
                for cb in range(ncb):
                    p.op("pe", lambda e: e.transpose(psT[:, cb * 128:(cb + 1) * 128], pb[:, cb * 128:(cb + 1) * 128], id_sb[:]),
                         r=[pb_r, id_r], w=[psT_r])
                p.op("act", lambda e: e.activation(out=pT_sb[:, 0:ncb, :].rearrange("p a b -> p (a b)"), in_=psT[:, 0:ncb * 128],
                                                   func=AF.Copy), r=[psT_r], w=[pT_r])
                for cb in range(ncb):
                    p.op("pe", lambda e: e.matmul(psX[:, 0:128], pT_sb[:, cb, :], vcc[:, g, cb, :], start=(cb == 0), stop=(cb == ncb - 1)),
                         r=[pT_r, vcc_r], w=[psX_r])
                p.op("act", lambda e: e.activation(out=oacc[:, h * 128:(h + 1) * 128], in_=psX[:, 0:128], func=AF.Copy),
                     r=[psX_r], w=[oacc_r])
            if do_sel:
                iv = imp[:, 0:512].rearrange("p (j f) -> p j f", f=4)
                p.op("dve", lambda e: e.tensor_reduce(out=sc[:], in_=iv, axis=AX.X, op=ALU.add), r=[imp_r], w=[sc_r])
                p.op("dve", lambda e: e.tensor_tensor(out=sc[:], in0=sc[:], in1=imp[:, 4:516].rearrange("p (j f) -> p j f", f=4)[:, :, 0],
                                                      op=ALU.add), r=[sc_r, imp_r], w=[sc_r])
                p.op("dve", lambda e: e.tensor_tensor(out=sc[:], in0=sc[:], in1=Fq[:, i, :], op=ALU.add), r=[sc_r, Fq_r], w=[sc_r])
                p.op("dve", lambda e: e.max(out=m8[:, 0:8], in_=sc[:]), r=[sc_r], w=[m8_r])
                p.op("dve", lambda e: e.match_replace(out=wk[:], in_to_replace=m8[:, 0:8], in_values=sc[:], imm_value=-3.0e38),
                     r=[sc_r, m8_r], w=[wk_r])
                p.op("dve", lambda e: e.max(out=m8[:, 8:16], in_=wk[:]), r=[wk_r], w=[m8_r])
                p.op("dve", lambda e: e.tensor_scalar(out=wk[:], in0=sc[:], scalar1=m8[:, 15:16], scalar2=None, op0=ALU.is_ge),
                     r=[sc_r, m8_r], w=[wk_r])
                p.op("dve", lambda e: e.tensor_scalar(out=negm[:], in0=wk[:], scalar1=1.0, scalar2=-NEG, op0=ALU.subtract, op1=ALU.mult),
                     r=[wk_r], w=[negm_r])
                p.op("pe", lambda e: e.transpose(psT[:, 512:640], negm[:], id_sb[:]), r=[negm_r, id_r], w=[psT_r])
                p.op("act", lambda e: e.activation(out=nmT[:], in_=psT[:, 512:640].unsqueeze(1).to_broadcast([128, 4, 128]), func=AF.Copy),
                     r=[psT_r], w=[nmT_r])

                def extra_sel(kt, i=i, g=g):
                    ex = [(R_sb[:, kt * 128:(kt + 1) * 128], R_r, nmT[:].rearrange("p a b -> p (a b)"), nmT_r)]
                    j = kt - 8 * i
                    if j >= -1:
                        ex.append((id_sb[:], id_r, Tt[:, j + 1, g * 512:(g + 1) * 512], Tt_r))
                    return ex

                attend(i, g, qv, q_r, list(range(0, 8 * i + 8)), ksT_s, ksT_r, vs_s, vs_r, extra_sel, 16, False, oacc, oacc_r)
            if do_win:
                def extra_win(kt, i=i, g=g):
                    jp = kt - (8 * i - 4)
                    ex = [(id_sb[:], id_r, Wm4[:, jp, :, :].rearrange("p a b -> p (a b)"), Wm4_r)]
                    if jp >= 3:
                        ex.append((id_sb[:], id_r, Tt[:, jp - 3, g * 512:(g + 1) * 512], Tt_r))
                    return ex

                kts = [kt for kt in range(8 * i - 4, 8 * i + 8) if kt >= 0]
                attend(i, g, qv, q_r, kts, kwT_s, kwT_r, vw_s, vw_r, extra_win, 32, False, oacc, oacc_r)
            ob, ob_r = obf[qb]
            p.op("act", lambda e: e.activation(out=ob[:], in_=oacc[:], func=AF.Copy), r=[oacc_r], w=[ob_r])
            p.dma("sp", o_out[i * 128:(i + 1) * 128, g * 512:(g + 1) * 512], ob[:], r=[ob_r], is_output=True)
    p.finish()
    p.close()
    return p.nc


DFF = 5632
MC = DFF // 128


def max_tokens(toks):
    best = {}
    for s, v in toks:
        if id(s) not in best or best[id(s)][1] < v:
            best[id(s)] = (s, v)
    return list(best.values())


class Fence:
    def __init__(self):
        self.toks = []

    def add(self, tok):
        self.toks.append(tok)
        if len(self.toks) > 256:
            self.toks = max_tokens(self.toks)

    def wait(self, p, e):
        for t in max_tokens(self.toks):
            p._wait(e, t)


def rmsnorm_T2(p, src_dram, fence, gsb, g_r, ones_f, ones_r, st, out_sb=None, out_sb_r=None, out_dram=None,
               out_fence=None, is_output=False):
    xs, xs_r, sq, sq_r, ss, ss_r, rstd, rstd_r, uo, uo_r = st
    for half in range(NT // 512):
        src = src_dram[:, half * 512:(half + 1) * 512].rearrange("(k p) t -> p k t", p=128)
        if fence is not None:
            fence.wait(p, "sp")
        for kk in range(0, KC, 4):
            p.dma("sp", xs[:, kk:kk + 4, :], src[:, kk:kk + 4, :], w=[xs_r])
        for k in range(KC):
            b = k % 2
            p.op("act", lambda e: e.activation(out=sq[b][:], in_=xs[:, k, :], func=AF.Square),
                 r=[xs_r], w=[sq_r[b]])
            p.op("pe", lambda e: e.matmul(ss[:], ones_f[:], sq[b][:], start=(k == 0), stop=(k == KC - 1)),
                 r=[sq_r[b], ones_r], w=[ss_r])
        p.op("dve", lambda e: e.tensor_scalar(out=rstd[:], in0=ss[:], scalar1=1.0 / D, scalar2=EPS,
                                              op0=ALU.mult, op1=ALU.add), r=[ss_r], w=[rstd_r])
        p.op("act", lambda e: e.activation(out=rstd[:], in_=rstd[:], func=AF.Sqrt), r=[rstd_r], w=[rstd_r])
        p.op("dve", lambda e: e.reciprocal(out=rstd[:], in_=rstd[:]), r=[rstd_r], w=[rstd_r])
        for k in range(KC):
            if out_sb is not None:
                p.op("dve", lambda e: e.scalar_tensor_tensor(
                    out=out_sb[:, k, half * 512:(half + 1) * 512], in0=xs[:, k, :], scalar=gsb[:, k:k + 1],
                    in1=rstd[:], op0=ALU.mult, op1=ALU.mult), r=[xs_r, rstd_r, g_r], w=[out_sb_r])
            else:
                b = k % 2
                p.op("dve", lambda e: e.scalar_tensor_tensor(
                    out=uo[b][:], in0=xs[:, k, :], scalar=gsb[:, k:k + 1],
                    in1=rstd[:], op0=ALU.mult, op1=ALU.mult), r=[xs_r, rstd_r, g_r], w=[uo_r[b]])
                tok = p.dma("sp", out_dram[k * 128:(k + 1) * 128, half * 512:(half + 1) * 512], uo[b][:],
                            r=[uo_r[b]], is_output=is_output)
                if out_fence is not None:
                    out_fence.add(tok)


def build_tok(glu, final):
    p = Prog()
    resT = p.dram("resT", [D, NT], F32, "ExternalInput")
    aT = p.dram("aT", [D, NT], BF16, "ExternalInput")
    wmix = p.dram("wmix", [D, 4096 if glu else 2048], F32, "ExternalInput")
    gff = p.dram("gff", [128, KC], F32, "ExternalInput")
    gn = p.dram("gn", [128, KC], F32, "ExternalInput")
    w1 = p.dram("w1", [D, 2 * DFF], F32, "ExternalInput")
    w2 = p.dram("w2", [DFF, D], F32, "ExternalInput")
    hmidT = p.dram("hmidT", [D, NT], F32, "ExternalOutput")
    hT = p.dram("hT", [D, NT], F32, "ExternalOutput")
    normT = p.dram("normT", [D, NT], F32, "ExternalOutput")

    ones_f = p.sbuf("ones", [128, 128], F32)
    ones_r = p.res("ones")
    p.op("dve", lambda e: e.memset(ones_f[:], 1.0), w=[ones_r])
    gffs = p.sbuf("gffs", [128, KC], F32)
    gff_r = p.res("gff")
    p.dma("sp", gffs[:], gff[:], w=[gff_r])
    gns = p.sbuf("gns", [128, KC], F32)
    gn_r = p.res("gn")
    p.dma("sp", gns[:], gn[:], w=[gn_r])
    st = (p.sbuf("xs", [128, KC, 512], F32), p.res("xs"),
          [p.sbuf("sq", [128, 512], F32) for _ in range(2)], [p.res("sq") for _ in range(2)],
          p.psum("ss", [128, 512]), p.res("ss"),
          p.sbuf("rstd", [128, 512], F32), p.res("rstd"),
          [p.sbuf("uo", [128, 512], F32) for _ in range(2)], [p.res("uo") for _ in range(2)])
    big = p.sbuf("big", [128, MC * 512], BF16)
    big_r = p.res("big")
    a_sb = big[:, 0:KC * NT].rearrange("p (k t) -> p k t", k=KC)
    actT = big[:, :].rearrange("p (m t) -> p m t", m=MC)
    hnT = p.sbuf("hnT", [128, KC, NT], BF16)
    hnT_r = p.res("hnT")
    NS = 2
    wst = [p.sbuf("wst", [128, MC * 128], F32) for _ in range(NS)]
    wst_r = [p.res("wst") for _ in range(NS)]
    wbf = [p.sbuf("wbf", [128, MC * 128], BF16) for _ in range(3)]
    wbf_r = [p.res("wbf") for _ in range(3)]
    ps = [p.psum("ps", [128, 512]) for _ in range(6)]
    ps_r = [p.res("ps") for _ in range(6)]
    xc = [p.sbuf("xc", [128, 512], F32) for _ in range(2)]
    xc_r = [p.res("xc") for _ in range(2)]
    t1 = [p.sbuf("t1", [128, 512], F32) for _ in range(2)]
    t1_r = [p.res("t1") for _ in range(2)]
    cnt = {"w": 0, "b": 0, "ps": 0, "x": 0, "t": 0}

    def load_w(wd, col0, kc):
        i = cnt["w"] % NS
        cnt["w"] += 1
        j = cnt["b"] % 3
        cnt["b"] += 1
        sv = wst[i][:, 0:kc * 128].rearrange("p (k c) -> p k c", k=kc)
        bv = wbf[j][:, 0:kc * 128].rearrange("p (k c) -> p k c", k=kc)
        src = wd[:, col0:col0 + 128].rearrange("(k p) c -> p k c", p=128)
        step = 16 if kc > 16 else 8
        for kk in range(0, kc, step):
            ke = min(kc, kk + step)
            p.dma("sp", sv[:, kk:ke, :], src[:, kk:ke, :], w=[wst_r[i]])
        p.op("pool", lambda e: e.tensor_copy(out=wbf[j][:, 0:kc * 128], in_=wst[i][:, 0:kc * 128]),
             r=[wst_r[i]], w=[wbf_r[j]])
        return bv, wbf_r[j]

    def mm(wv, w_r, inT, in_r, kc, half):
        pb = cnt["ps"] % 6
        cnt["ps"] += 1
        for k in range(kc):
            p.op("pe", lambda e: e.matmul(ps[pb][:], wv[:, k, :], inT[:, k, half * 512:(half + 1) * 512],
                                          start=(k == 0), stop=(k == kc - 1)), r=[w_r, in_r], w=[ps_r[pb]])
        return pb

    srcA = aT.rearrange("(k p) t -> p k t", p=128)
    for kk in range(0, KC, 4):
        p.dma("sp", a_sb[:, kk:kk + 4, :], srcA[:, kk:kk + 4, :], w=[big_r])
    f_mid = Fence()
    for f in range(KC):
        wv, w_r = load_w(wmix, f * 128, KC)
        if glu:
            wv2, w2_r = load_w(wmix, 2048 + f * 128, KC)
        for half in range(2):
            xb = cnt["x"] % 2
            cnt["x"] += 1
            p.dma("sp", xc[xb][:], resT[f * 128:(f + 1) * 128, half * 512:(half + 1) * 512], w=[xc_r[xb]])
            pb = mm(wv, w_r, a_sb, big_r, KC, half)
            if glu:
                pb2 = mm(wv2, w2_r, a_sb, big_r, KC, half)
                tb = cnt["t"] % 2
                cnt["t"] += 1
                p.op("act", lambda e: e.activation(out=t1[tb][:], in_=ps[pb2][:], func=AF.Sigmoid),
                     r=[ps_r[pb2]], w=[t1_r[tb]])
                p.op("dve", lambda e: e.tensor_tensor(out=t1[tb][:], in0=ps[pb][:], in1=t1[tb][:], op=ALU.mult),
                     r=[ps_r[pb], t1_r[tb]], w=[t1_r[tb]])
                p.op("dve", lambda e: e.tensor_tensor(out=xc[xb][:], in0=xc[xb][:], in1=t1[tb][:], op=ALU.add),
                     r=[t1_r[tb], xc_r[xb]], w=[xc_r[xb]])
            else:
                p.op("dve", lambda e: e.tensor_tensor(out=xc[xb][:], in0=ps[pb][:], in1=xc[xb][:], op=ALU.add),
                     r=[ps_r[pb], xc_r[xb]], w=[xc_r[xb]])
            tok = p.dma("sp", hmidT[f * 128:(f + 1) * 128, half * 512:(half + 1) * 512], xc[xb][:],
                        r=[xc_r[xb]], is_output=True)
            f_mid.add(tok)
    rmsnorm_T2(p, hmidT, f_mid, gffs, gff_r, ones_f, ones_r, st, out_sb=hnT, out_sb_r=hnT_r)
    f_h = Fence()
    for half in range(2):
        for m in range(MC):
            wa, wa_r = load_w(w1, m * 128, KC)
            wb_, wb_r = load_w(w1, DFF + m * 128, KC)
            pa = mm(wa, wa_r, hnT, hnT_r, KC, half)
            pbb = mm(wb_, wb_r, hnT, hnT_r, KC, half)
            tb = cnt["t"] % 2
            cnt["t"] += 1
            p.op("act", lambda e: e.activation(out=t1[tb][:], in_=ps[pa][:], func=AF.Silu),
                 r=[ps_r[pa]], w=[t1_r[tb]])
            p.op("dve", lambda e: e.tensor_tensor(out=actT[:, m, :], in0=ps[pbb][:], in1=t1[tb][:], op=ALU.mult),
                 r=[ps_r[pbb], t1_r[tb]], w=[big_r])
        for f in range(KC):
            wo, wo_r = load_w(w2, f * 128, MC)
            pb = cnt["ps"] % 6
            cnt["ps"] += 1
            for m in range(MC):
                p.op("pe", lambda e: e.matmul(ps[pb][:], wo[:, m, :], actT[:, m, :], start=(m == 0), stop=(m == MC - 1)),
                     r=[wo_r, big_r], w=[ps_r[pb]])
            xb = cnt["x"] % 2
            cnt["x"] += 1
            f_mid.wait(p, "sp")
            p.dma("sp", xc[xb][:], hmidT[f * 128:(f + 1) * 128, half * 512:(half + 1) * 512], w=[xc_r[xb]])
            p.op("dve", lambda e: e.tensor_tensor(out=xc[xb][:], in0=ps[pb][:], in1=xc[xb][:], op=ALU.add),
                 r=[ps_r[pb], xc_r[xb]], w=[xc_r[xb]])
            tok = p.dma("sp", hT[f * 128:(f + 1) * 128, half * 512:(half + 1) * 512], xc[xb][:],
                        r=[xc_r[xb]], is_output=True)
            f_h.add(tok)
    rmsnorm_T2(p, hT, f_h, gns, gn_r, ones_f, ones_r, st, out_dram=normT, is_output=True)
    p.finish()
    p.close()
    return p.nc


TWO_PI = 2.0 * math.pi


I32 = mybir.dt.int32


def sincos(p, T, ang, ang_r, s_out, c_out, out_r, tmp, tmp_r, shape_sl, n):
    sl = shape_sl
    tl = (slice(None), slice(0, n))
    if not hasattr(p, "_sc_tmp"):
        p._sc_tmp = (T("sc_ki", [128, 512], I32), T("sc_kf", [128, 512]), T("sc_y", [128, 512]), T("sc_m", [128, 512]))
    (ki, ki_r), (kf, kf_r), (y, y_r), (m, m_r) = p._sc_tmp
    p.op("dve", lambda e: e.tensor_scalar(out=kf[tl], in0=ang[sl], scalar1=1.0 / TWO_PI, scalar2=None, op0=ALU.mult),
         r=[ang_r], w=[kf_r])
    p.op("dve", lambda e: e.tensor_copy(out=ki[tl], in_=kf[tl]), r=[kf_r], w=[ki_r])
    p.op("dve", lambda e: e.tensor_copy(out=kf[tl], in_=ki[tl]), r=[ki_r], w=[kf_r])
    p.op("dve", lambda e: e.scalar_tensor_tensor(out=y[tl], in0=kf[tl], scalar=-TWO_PI, in1=ang[sl], op0=ALU.mult, op1=ALU.add),
         r=[kf_r, ang_r], w=[y_r])

    def fold(v, v_r):
        p.op("dve", lambda e: e.tensor_scalar(out=m[tl], in0=v[tl], scalar1=math.pi, scalar2=None, op0=ALU.is_gt), r=[v_r], w=[m_r])
        p.op("dve", lambda e: e.scalar_tensor_tensor(out=v[tl], in0=m[tl], scalar=-TWO_PI, in1=v[tl], op0=ALU.mult, op1=ALU.add),
             r=[m_r, v_r], w=[v_r])
        p.op("dve", lambda e: e.tensor_scalar(out=m[tl], in0=v[tl], scalar1=-math.pi, scalar2=None, op0=ALU.is_lt), r=[v_r], w=[m_r])
        p.op("dve", lambda e: e.scalar_tensor_tensor(out=v[tl], in0=m[tl], scalar=TWO_PI, in1=v[tl], op0=ALU.mult, op1=ALU.add),
             r=[m_r, v_r], w=[v_r])

    fold(y, y_r)
    p.op("act", lambda e: e.activation(out=s_out[sl], in_=y[tl], func=AF.Sin), r=[y_r], w=[out_r])
    p.op("dve", lambda e: e.tensor_scalar(out=y[tl], in0=y[tl], scalar1=0.5 * math.pi, scalar2=None, op0=ALU.add), r=[y_r], w=[y_r])
    fold(y, y_r)
    p.op("act", lambda e: e.activation(out=c_out[sl], in_=y[tl], func=AF.Sin), r=[y_r], w=[out_r])


def build_s5prep():
    p = Prog()
    A_re = p.dram("A_re", [128, 64], F32, "ExternalInput")
    A_im = p.dram("A_im", [128, 64], F32, "ExternalInput")
    log_dt = p.dram("log_dt", [128, 1], F32, "ExternalInput")
    B_re = p.dram("B_re", [128, 1024], F32, "ExternalInput")
    B_im = p.dram("B_im", [128, 1024], F32, "ExternalInput")
    o_r = p.dram("o_r", [128, 64], F32, "ExternalOutput")
    o_th = p.dram("o_th", [128, 64], F32, "ExternalOutput")
    o_bbre = p.dram("o_bbre", [128, 1024], F32, "ExternalOutput")
    o_bbim = p.dram("o_bbim", [128, 1024], F32, "ExternalOutput")

    def T(name, shape, dt=F32):
        return p.sbuf(name, shape, dt), p.res(name)

    are, are_r = T("are", [128, 64])
    aim, aim_r = T("aim", [128, 64])
    ldt, ldt_r = T("ldt", [128, 1])
    bre, bre_r = T("bre", [128, 64, 16])
    bim, bim_r = T("bim", [128, 64, 16])
    p.dma("sp", are[:], A_re[:], w=[are_r])
    p.dma("sp", aim[:], A_im[:], w=[aim_r])
    p.dma("sp", ldt[:], log_dt[:], w=[ldt_r])
    p.dma("sp", bre[:].rearrange("p n c -> p (n c)"), B_re[:], w=[bre_r])
    p.dma("sp", bim[:].rearrange("p n c -> p (n c)"), B_im[:], w=[bim_r])
    dt, dt_r = T("dt", [128, 1])
    lre, lre_r = T("lre", [128, 64])
    th, th_r = T("th", [128, 64])
    rr, rr_r = T("rr", [128, 64])
    sn, sc_r = T("sn", [128, 64])
    cs, _ = T("cs", [128, 64])
    tmp, tmp_r = T("tmp", [128, 64])
    abre, abre_r = T("abre", [128, 64])
    abim, abim_r = T("abim", [128, 64])
    den, den_r = T("den", [128, 64])
    t2, t2_r = T("t2", [128, 64])
    cfre, cfre_r = T("cfre", [128, 64])
    cfim, cfim_r = T("cfim", [128, 64])
    obr, obr_r = T("obr", [128, 64, 16])
    obi, obi_r = T("obi", [128, 64, 16])
    t3, t3_r = T("t3", [128, 64, 16])
    p.op("act", lambda e: e.activation(out=dt[:], in_=ldt[:], func=AF.Exp), r=[ldt_r], w=[dt_r])
    p.op("dve", lambda e: e.tensor_scalar(out=lre[:], in0=are[:], scalar1=dt[:, 0:1], scalar2=None, op0=ALU.mult),
         r=[are_r, dt_r], w=[lre_r])
    p.op("dve", lambda e: e.tensor_scalar(out=th[:], in0=aim[:], scalar1=dt[:, 0:1], scalar2=None, op0=ALU.mult),
         r=[aim_r, dt_r], w=[th_r])
    p.op("act", lambda e: e.activation(out=rr[:], in_=lre[:], func=AF.Exp), r=[lre_r], w=[rr_r])
    sl = (slice(None), slice(None))
    sincos(p, T, th, th_r, sn, cs, sc_r, tmp, tmp_r, sl, 64)
    p.op("dve", lambda e: e.tensor_tensor(out=abre[:], in0=rr[:], in1=cs[:], op=ALU.mult), r=[rr_r, sc_r], w=[abre_r])
    p.op("dve", lambda e: e.tensor_tensor(out=abim[:], in0=rr[:], in1=sn[:], op=ALU.mult), r=[rr_r, sc_r], w=[abim_r])
    p.op("dve", lambda e: e.tensor_scalar(out=abre[:], in0=abre[:], scalar1=-1.0, scalar2=None, op0=ALU.add), r=[abre_r], w=[abre_r])
    p.op("dve", lambda e: e.tensor_tensor(out=den[:], in0=are[:], in1=are[:], op=ALU.mult), r=[are_r], w=[den_r])
    p.op("dve", lambda e: e.tensor_tensor(out=t2[:], in0=aim[:], in1=aim[:], op=ALU.mult), r=[aim_r], w=[t2_r])
    p.op("dve", lambda e: e.tensor_tensor(out=den[:], in0=den[:], in1=t2[:], op=ALU.add), r=[den_r, t2_r], w=[den_r])
    p.op("dve", lambda e: e.reciprocal(out=den[:], in_=den[:]), r=[den_r], w=[den_r])
    p.op("dve", lambda e: e.tensor_tensor(out=cfre[:], in0=abre[:], in1=are[:], op=ALU.mult), r=[abre_r, are_r], w=[cfre_r])
    p.op("dve", lambda e: e.tensor_tensor(out=t2[:], in0=abim[:], in1=aim[:], op=ALU.mult), r=[abim_r, aim_r], w=[t2_r])
    p.op("dve", lambda e: e.tensor_tensor(out=cfre[:], in0=cfre[:], in1=t2[:], op=ALU.add), r=[cfre_r, t2_r], w=[cfre_r])
    p.op("dve", lambda e: e.tensor_tensor(out=cfre[:], in0=cfre[:], in1=den[:], op=ALU.mult), r=[cfre_r, den_r], w=[cfre_r])
    p.op("dve", lambda e: e.tensor_tensor(out=cfim[:], in0=abim[:], in1=are[:], op=ALU.mult), r=[abim_r, are_r], w=[cfim_r])
    p.op("dve", lambda e: e.tensor_tensor(out=t2[:], in0=abre[:], in1=aim[:], op=ALU.mult), r=[abre_r, aim_r], w=[t2_r])
    p.op("dve", lambda e: e.tensor_tensor(out=cfim[:], in0=cfim[:], in1=t2[:], op=ALU.subtract), r=[cfim_r, t2_r], w=[cfim_r])
    p.op("dve", lambda e: e.tensor_tensor(out=cfim[:], in0=cfim[:], in1=den[:], op=ALU.mult), r=[cfim_r, den_r], w=[cfim_r])
    cre_b = cfre[:].unsqueeze(2).to_broadcast([128, 64, 16])
    cim_b = cfim[:].unsqueeze(2).to_broadcast([128, 64, 16])
    p.op("dve", lambda e: e.tensor_tensor(out=obr[:], in0=bre[:], in1=cre_b, op=ALU.mult), r=[bre_r, cfre_r], w=[obr_r])
    p.op("dve", lambda e: e.tensor_tensor(out=t3[:], in0=bim[:], in1=cim_b, op=ALU.mult), r=[bim_r, cfim_r], w=[t3_r])
    p.op("dve", lambda e: e.tensor_tensor(out=obr[:], in0=obr[:], in1=t3[:], op=ALU.subtract), r=[obr_r, t3_r], w=[obr_r])
    p.op("dve", lambda e: e.tensor_tensor(out=obi[:], in0=bim[:], in1=cre_b, op=ALU.mult), r=[bim_r, cfre_r], w=[obi_r])
    p.op("dve", lambda e: e.tensor_tensor(out=t3[:], in0=bre[:], in1=cim_b, op=ALU.mult), r=[bre_r, cfim_r, obr_r], w=[t3_r])
    p.op("dve", lambda e: e.tensor_tensor(out=obi[:], in0=obi[:], in1=t3[:], op=ALU.add), r=[obi_r, t3_r], w=[obi_r])
    p.dma("sp", o_r[:], rr[:], r=[rr_r], is_output=True)
    p.dma("sp", o_th[:], th[:], r=[th_r], is_output=True)
    p.dma("sp", o_bbre[:], obr[:].rearrange("p n c -> p (n c)"), r=[obr_r], is_output=True)
    p.dma("sp", o_bbim[:], obi[:].rearrange("p n c -> p (n c)"), r=[obi_r], is_output=True)
    p.finish()
    p.close()
    return p.nc


NPAIR = 8
NB = 16


def build_s5main(nblocks=NB):
    p = Prog()
    uT = p.dram("uT", [256, 8192], F32, "ExternalInput")
    BD = p.dram("BD", [128, NPAIR * 2 * 128], F32, "ExternalInput")
    CT = p.dram("CT", [128, NPAIR * 2 * 128], F32, "ExternalInput")
    rcol = p.dram("rcol", [128, NPAIR], F32, "ExternalInput")
    thcol = p.dram("thcol", [128, NPAIR], F32, "ExternalInput")
    iota = p.dram("iota", [128, 512], F32, "ExternalInput")
    Dcol = p.dram("Dcol", [128, 2], F32, "ExternalInput")
    yT = p.dram("yT", [256, 8192], BF16, "ExternalOutput")

    def T(name, shape, dt=F32):
        return p.sbuf(name, shape, dt), p.res(name)

    bd, bd_r = T("bd", [128, NPAIR, 2, 128])
    ct, ct_r = T("ct", [128, NPAIR, 2, 128])
    rc, rc_r = T("rc", [128, NPAIR])
    thc, thc_r = T("thc", [128, NPAIR])
    io, io_r = T("io", [128, 512])
    dc, dc_r = T("dc", [128, 2])
    p.dma("sp", bd[:].rearrange("p a b c -> p (a b c)"), BD[:], w=[bd_r])
    p.dma("sp", ct[:].rearrange("p a b c -> p (a b c)"), CT[:], w=[ct_r])
    p.dma("sp", rc[:], rcol[:], w=[rc_r])
    p.dma("sp", thc[:], thcol[:], w=[thc_r])
    p.dma("sp", io[:], iota[:], w=[io_r])
    p.dma("sp", dc[:], Dcol[:], w=[dc_r])
    cosT, tab_r = T("cosT", [128, NPAIR, 512])
    sinT, _ = T("sinT", [128, NPAIR, 512])
    nsinT, _ = T("nsinT", [128, NPAIR, 512])
    c512, c512_r = T("c512", [128, NPAIR])
    s512, _ = T("s512", [128, NPAIR])
    ang, ang_r = T("ang", [128, 512])
    tmp, tmp_r = T("tmp", [128, 512])
    for k in range(NPAIR):
        p.op("dve", lambda e: e.tensor_scalar(out=ang[:], in0=io[:], scalar1=thc[:, k:k + 1], scalar2=None, op0=ALU.mult),
             r=[io_r, thc_r], w=[ang_r])
        sincos(p, T, ang, ang_r, sinT[:, k, :], cosT[:, k, :], tab_r, tmp, tmp_r, (slice(None), slice(None)), 512)
        p.op("dve", lambda e: e.tensor_scalar(out=nsinT[:, k, :], in0=sinT[:, k, :], scalar1=-1.0, scalar2=None, op0=ALU.mult),
             r=[tab_r], w=[tab_r])
    p.op("dve", lambda e: e.tensor_scalar(out=ang[:, 0:NPAIR], in0=thc[:], scalar1=512.0, scalar2=None, op0=ALU.mult),
         r=[thc_r], w=[ang_r])
    sincos(p, T, ang, ang_r, s512, c512, c512_r, tmp, tmp_r, (slice(None), slice(0, NPAIR)), NPAIR)

    vin_re, vin_r = T("vin_re", [128, NPAIR])
    vin_im, _ = T("vin_im", [128, NPAIR])
    p.op("dve", lambda e: e.memset(vin_re[:], 0.0), w=[vin_r])
    p.op("dve", lambda e: e.memset(vin_im[:], 0.0), w=[vin_r])
    ub = [T("ub", [128, 2, 512]) for _ in range(2)]
    psB = [p.psum("psB", [128, 512]) for _ in range(4)]
    psB_r = [p.res("psB") for _ in range(4)]
    psY = [p.psum("psY", [128, 512]) for _ in range(2)]
    psY_r = [p.res("psY") for _ in range(2)]
    wre = [T("wre", [128, 512]) for _ in range(2)]
    wim = [T("wim", [128, 512]) for _ in range(2)]
    t1 = [T("t1", [128, 512]) for _ in range(2)]
    vre = [T("vre", [128, 512]) for _ in range(2)]
    vim = [T("vim", [128, 512]) for _ in range(2)]
    xre = [T("xre", [128, 512]) for _ in range(2)]
    xim = [T("xim", [128, 512]) for _ in range(2)]
    t2 = [T("t2", [128, 512]) for _ in range(2)]
    yg, yg_r = T("yg", [128, 512])
    tg, tg_r = T("tg", [128, 512])
    yo = [T("yo", [128, 512], BF16) for _ in range(2)]
    sm, sm_r = T("sm", [128, 4])
    cnt = 0
    for b in range(nblocks):
        u, u_r = ub[b % 2]
        p.dma("sp", u[:], uT[:, b * 512:(b + 1) * 512].rearrange("(k p) t -> p k t", p=128), w=[u_r])
        for k in range(NPAIR):
            kc = k // 4
            i2 = cnt % 2
            cnt += 1
            pre, pre_r = psB[2 * i2], psB_r[2 * i2]
            pim, pim_r = psB[2 * i2 + 1], psB_r[2 * i2 + 1]
            p.op("pe", lambda e: e.matmul(pre[:], bd[:, k, 0, :], u[:, kc, :], start=True, stop=True), r=[bd_r, u_r], w=[pre_r])
            p.op("pe", lambda e: e.matmul(pim[:], bd[:, k, 1, :], u[:, kc, :], start=True, stop=True), r=[bd_r, u_r], w=[pim_r])
            (wr, wr_r), (wi, wi_r), (ta, ta_r) = wre[i2], wim[i2], t1[i2]
            (vr, vr_r), (vi, vi_r) = vre[i2], vim[i2]
            (xr, xr_r), (xi, xi_r), (tb, tb_r) = xre[i2], xim[i2], t2[i2]
            cT, sT, nsT = cosT[:, k, :], sinT[:, k, :], nsinT[:, k, :]
            p.op("dve", lambda e: e.tensor_tensor(out=wr[:], in0=pre[:], in1=cT, op=ALU.mult), r=[pre_r, tab_r], w=[wr_r])
            p.op("dve", lambda e: e.tensor_tensor(out=ta[:], in0=pim[:], in1=sT, op=ALU.mult), r=[pim_r, tab_r], w=[ta_r])
            p.op("dve", lambda e: e.tensor_tensor(out=wr[:], in0=wr[:], in1=ta[:], op=ALU.add), r=[wr_r, ta_r], w=[wr_r])
            p.op("dve", lambda e: e.tensor_tensor(out=wi[:], in0=pim[:], in1=cT, op=ALU.mult), r=[pim_r, tab_r], w=[wi_r])
            p.op("dve", lambda e: e.tensor_tensor(out=ta[:], in0=pre[:], in1=sT, op=ALU.mult), r=[pre_r, tab_r, wr_r], w=[ta_r])
            p.op("dve", lambda e: e.tensor_tensor(out=wi[:], in0=wi[:], in1=ta[:], op=ALU.subtract), r=[wi_r, ta_r], w=[wi_r])
            rb = rc[:, k:k + 1].to_broadcast([128, 512])
            p.op("dve", lambda e: e.tensor_tensor_scan(out=vr[:], data0=rb, data1=wr[:], initial=vin_re[:, k:k + 1],
                                                       op0=ALU.mult, op1=ALU.add), r=[wr_r, rc_r, vin_r], w=[vr_r])
            p.op("dve", lambda e: e.tensor_tensor_scan(out=vi[:], data0=rb, data1=wi[:], initial=vin_im[:, k:k + 1],
                                                       op0=ALU.mult, op1=ALU.add), r=[wi_r, rc_r, vin_r], w=[vi_r])
            p.op("dve", lambda e: e.tensor_tensor(out=sm[:, 0:1], in0=vr[:, 511:512], in1=c512[:, k:k + 1], op=ALU.mult),
                 r=[vr_r, c512_r], w=[sm_r])
            p.op("dve", lambda e: e.tensor_tensor(out=sm[:, 1:2], in0=vi[:, 511:512], in1=s512[:, k:k + 1], op=ALU.mult),
                 r=[vi_r, c512_r], w=[sm_r])
            p.op("dve", lambda e: e.tensor_tensor(out=sm[:, 2:3], in0=vr[:, 511:512], in1=s512[:, k:k + 1], op=ALU.mult),
                 r=[vr_r, c512_r], w=[sm_r])
            p.op("dve", lambda e: e.tensor_tensor(out=sm[:, 3:4], in0=vi[:, 511:512], in1=c512[:, k:k + 1], op=ALU.mult),
                 r=[vi_r, c512_r], w=[sm_r])
            p.op("dve", lambda e: e.tensor_tensor(out=vin_re[:, k:k + 1], in0=sm[:, 0:1], in1=sm[:, 1:2], op=ALU.subtract),
                 r=[sm_r], w=[vin_r])
            p.op("dve", lambda e: e.tensor_tensor(out=vin_im[:, k:k + 1], in0=sm[:, 2:3], in1=sm[:, 3:4], op=ALU.add),
                 r=[sm_r], w=[vin_r])
            p.op("pool", lambda e: e.tensor_tensor(out=xr[:], in0=vr[:], in1=cT, op=ALU.mult), r=[vr_r, tab_r], w=[xr_r])
            p.op("pool", lambda e: e.tensor_tensor(out=tb[:], in0=vi[:], in1=sT, op=ALU.mult), r=[vi_r, tab_r], w=[tb_r])
            p.op("pool", lambda e: e.tensor_tensor(out=xr[:], in0=xr[:], in1=tb[:], op=ALU.subtract), r=[xr_r, tb_r], w=[xr_r])
            p.op("pool", lambda e: e.tensor_tensor(out=xi[:], in0=vr[:], in1=nsT, op=ALU.mult), r=[vr_r, tab_r], w=[xi_r])
            p.op("pool", lambda e: e.tensor_tensor(out=tb[:], in0=vi[:], in1=cT, op=ALU.mult), r=[vi_r, tab_r, xr_r], w=[tb_r])
            p.op("pool", lambda e: e.tensor_tensor(out=xi[:], in0=xi[:], in1=tb[:], op=ALU.subtract), r=[xi_r, tb_r], w=[xi_r])
            py, py_r = psY[kc], psY_r[kc]
            kk = k % 4
            p.op("pe", lambda e: e.matmul(py[:], ct[:, k, 0, :], xr[:], start=(kk == 0), stop=False), r=[ct_r, xr_r], w=[py_r])
            p.op("pe", lambda e: e.matmul(py[:], ct[:, k, 1, :], xi[:], start=False, stop=(kk == 3)), r=[ct_r, xi_r], w=[py_r])
            if kk == 3:
                yob, yob_r = yo[kc]
                p.op("dve", lambda e: e.scalar_tensor_tensor(out=yg[:], in0=u[:, kc, :], scalar=dc[:, kc:kc + 1], in1=py[:],
                                                             op0=ALU.mult, op1=ALU.add), r=[u_r, dc_r, py_r], w=[yg_r])
                p.op("pool", lambda e: e.tensor_tensor(out=tg[:], in0=yg[:], in1=yg[:], op=ALU.mult), r=[yg_r], w=[tg_r])
                p.op("pool", lambda e: e.tensor_scalar(out=tg[:], in0=tg[:], scalar1=0.044715, scalar2=1.0, op0=ALU.mult, op1=ALU.add),
                     r=[tg_r], w=[tg_r])
                p.op("pool", lambda e: e.tensor_tensor(out=tg[:], in0=tg[:], in1=yg[:], op=ALU.mult), r=[tg_r, yg_r], w=[tg_r])
                p.op("act", lambda e: e.activation(out=tg[:], in_=tg[:], func=AF.Sigmoid, scale=1.5957691216), r=[tg_r], w=[tg_r])
                p.op("pool", lambda e: e.tensor_tensor(out=yob[:], in0=tg[:], in1=yg[:], op=ALU.mult), r=[tg_r, yg_r], w=[yob_r])
                p.dma("sp", yT[kc * 128:(kc + 1) * 128, b * 512:(b + 1) * 512], yob[:], r=[yob_r], is_output=True)
    p.finish()
    p.close()
    return p.nc


def _run(nc, in_maps):
    res = run_bass_kernel_spmd(nc, in_maps, core_ids=list(range(8)))
    return res.results


def kernel(**inp):
    inp = {k: np.asarray(v) for k, v in inp.items()}
    x = inp["x"][0]
    idx = [own_idx(c) for c in range(8)]
    xT = [np.ascontiguousarray(x[idx[c]].T) for c in range(8)]
    w_in = np.ascontiguousarray(inp["nsa_w_in"][0])
    g0 = gcol(inp["mix_norm_g"][0])
    l1 = _run(build_p1(), [{"xT": xT[c], "gcol": g0, "w_in": w_in} for c in range(8)])
    shared = p2_shared(inp, l1)
    l2 = _run(build_p2(), [p2_inputs(inp, l1, c, shared) for c in range(8)])
    del shared
    m3 = [{"resT": xT[c], "aT": np.ascontiguousarray(np.asarray(l2[c]["o_out"]).T),
           "wmix": np.ascontiguousarray(inp["nsa_w_out"][0]), "gff": gcol(inp["ffn_norm_g"][0]),
           "gn": gcol(inp["mix_norm_g"][1]), "w1": np.ascontiguousarray(inp["ffn_w_in"][0]),
           "w2": np.ascontiguousarray(inp["ffn_w_out"][0])} for c in range(8)]
    l3 = _run(build_tok(False, False), m3)
    del m3
    prep = _run(build_s5prep(), [s5prep_inputs(inp) for _ in range(8)])[0]
    uT_full = np.zeros((2048, 8192), np.float32)
    for c in range(8):
        uT_full[:, idx[c]] = np.asarray(l3[c]["normT"])
    l4 = _run(build_s5main(), [s5main_inputs(inp, prep, uT_full, c) for c in range(8)])
    y_full = np.concatenate([np.asarray(l4[c]["yT"]) for c in range(8)], axis=0)
    m5 = [{"resT": np.ascontiguousarray(np.asarray(l3[c]["hT"])), "aT": np.ascontiguousarray(y_full[:, idx[c]]),
           "wmix": np.ascontiguousarray(inp["s5_w_glu"][0]), "gff": gcol(inp["ffn_norm_g"][1]),
           "gn": gcol(inp["final_norm_g"]), "w1": np.ascontiguousarray(inp["ffn_w_in"][1]),
           "w2": np.ascontiguousarray(inp["ffn_w_out"][1])} for c in range(8)]
    l5 = _run(build_tok(True, True), m5)
    out = np.zeros((1, 8192, 2048), np.float32)
    for c in range(8):
        out[0, idx[c]] = np.asarray(l5[c]["normT"]).T
    return out
```

```python
import math
from contextlib import ExitStack
import numpy as np
import ml_dtypes
import concourse.bass as bass
import concourse.mybir as mybir
from concourse.bass_utils import run_bass_kernel_spmd


F32 = mybir.dt.float32
BF16 = mybir.dt.bfloat16
AF = mybir.ActivationFunctionType
ALU = mybir.AluOpType
AX = mybir.AxisListType


NO_SELF_WAIT = ()


class Res:
    __slots__ = ("name", "last_w", "readers", "dsem", "dcnt")

    def __init__(self, name):
        self.name = name
        self.last_w = None
        self.readers = []
        self.dsem = None
        self.dcnt = 0


class Prog:
    def __init__(self, num_devices=None):
        self.nc = bass.Bass("TRN2", target_bir_lowering=False, num_devices=num_devices)
        self.es = ExitStack()
        nc = self.nc
        self.eng = {"pe": nc.tensor, "act": nc.scalar, "dve": nc.vector, "pool": nc.gpsimd, "sp": nc.sync}
        self.esem = {}
        self.ecnt = {}
        self.waited = {e: {} for e in self.eng}
        self.semid = {}
        for e in self.eng:
            s = self.es.enter_context(nc.semaphore("es_" + e))
            self.esem[e] = s
            self.ecnt[e] = 0
        self.nsem = len(self.eng)
        self.out_tokens = []
        self.no_self_wait = set(NO_SELF_WAIT)
        self.dma_toks = {}
        self._n = 0
        self._stacks = [self.es]

    def uname(self, base):
        self._n += 1
        return f"{base}_{self._n}"

    def sbuf(self, name, shape, dt):
        return self._stacks[-1].enter_context(self.nc.sbuf_tensor(self.uname(name), list(shape), dt))

    def push_scope(self):
        st = ExitStack()
        self._stacks.append(st)
        return st

    def pop_scope(self):
        self.barrier()
        st = self._stacks.pop()
        st.close()

    def barrier(self):
        for e in self.eng:
            for e2 in self.eng:
                if e2 != e and self.ecnt[e2] > 0:
                    self._wait(e, (self.esem[e2], self.ecnt[e2]))
            for tok in self.dma_toks.values():
                self._wait(e, tok)

    def psum(self, name, shape, dt=F32):
        return self.es.enter_context(self.nc.psum_tensor(self.uname(name), list(shape), dt))

    def dram(self, name, shape, dt, kind):
        return self.nc.dram_tensor(name, list(shape), dt, kind=kind).ap()

    def res(self, name="r"):
        return Res(name)

    def _wait(self, e, tok):
        if tok is None:
            return
        sem, val = tok
        if e == "pe" and sem is self.esem["pe"]:
            return
        if e in self.no_self_wait and sem is self.esem[e]:
            return
        k = id(sem)
        w = self.waited[e]
        if w.get(k, 0) >= val:
            return
        self.eng[e].wait_ge(sem, val)
        w[k] = val

    def _deps(self, e, r, w):
        for x in r:
            self._wait(e, x.last_w)
        for x in w:
            self._wait(e, x.last_w)
            for t in x.readers:
                self._wait(e, t)

    def _commit(self, tok, r, w):
        for x in r:
            x.readers.append(tok)
            if len(x.readers) > 64:
                best = {}
                for s, v in x.readers:
                    if id(s) not in best or best[id(s)][1] < v:
                        best[id(s)] = (s, v)
                x.readers = list(best.values())
        for x in w:
            x.last_w = tok
            x.readers = []

    def op(self, e, fn, r=(), w=()):
        self._deps(e, r, w)
        inst = fn(self.eng[e])
        inst.then_inc(self.esem[e], 1)
        self.ecnt[e] += 1
        tok = (self.esem[e], self.ecnt[e])
        self._commit(tok, r, w)
        return tok

    def dma(self, q, out, in_, r=(), w=(), sres=None, is_output=False, **kw):
        self._deps(q, r, w)
        if sres is None:
            sres = (list(w) + list(r))[0]
        if sres.dsem is None:
            sres.dsem = self.es.enter_context(self.nc.semaphore(self.uname("ds")))
            self.nsem += 1
        inst = self.eng[q].dma_start(out=out, in_=in_, **kw)
        inst.then_inc(sres.dsem, 16)
        sres.dcnt += 16
        tok = (sres.dsem, sres.dcnt)
        self.dma_toks[id(sres.dsem)] = tok
        self._commit(tok, r, w)
        if is_output:
            self.out_tokens.append(tok)
        return tok

    def finish(self):
        best = {}
        for s, v in self.out_tokens:
            if id(s) not in best or best[id(s)][1] < v:
                best[id(s)] = (s, v)
        for s, v in best.values():
            self.eng["sp"].wait_ge(s, v)
        return self.nc

    def close(self):
        self.es.close()


BF = ml_dtypes.bfloat16
NEG = -30000.0


def own_idx(c):
    return np.concatenate([np.arange((8 * i + c) * 128, (8 * i + c) * 128 + 128) for i in range(8)])


def gcol(g):
    return np.ascontiguousarray(np.asarray(g, np.float32).reshape(16, 128).T)


def rel_bucket_np(dist):
    n = np.maximum(dist, 0).astype(np.int64)
    max_exact = 16
    nf = np.maximum(n, 1).astype(np.float32)
    large = max_exact + (np.log(nf / np.float32(max_exact)) / np.float32(np.log(128 / 16)) * np.float32(16)).astype(np.int32)
    large = np.minimum(large, 31)
    return np.where(n < max_exact, n, large).astype(np.int64)


def p2_tables(rel_bias, c):
    rel_bias = np.asarray(rel_bias, np.float32)
    kl = np.arange(128)[:, None]
    ql = np.arange(128)[None, :]
    tb_bias = np.zeros((9, 128, 16, 128), np.float32)
    tb_mask = np.zeros((128, 9, 128), np.float32)
    for jj in range(9):
        j = jj - 1
        r = c - j
        dist = r * 128 + ql - kl
        b = rel_bias[rel_bucket_np(dist)]
        tb_bias[jj] = b.transpose(0, 2, 1)
        tb_mask[:, jj, :] = np.where(dist < 0, NEG, 0.0)
    win_mask = np.zeros((128, 12, 128), np.float32)
    for jp in range(12):
        r = c + 4 - jp
        dist = r * 128 + ql - kl
        win_mask[:, jp, :] = np.where((dist < 0) | (dist >= 512), NEG, 0.0)
    qcol = np.arange(128)[:, None]
    e = np.arange(72)[None, :]
    dist_c = 128 * c + qcol - 16 * (e - 9) - 31
    ec_bias = rel_bias[rel_bucket_np(dist_c)].transpose(0, 2, 1)
    ec_mask = np.where(dist_c < 0, NEG, 0.0).astype(np.float32)
    fq = np.zeros((128, 8, 128), np.float32)
    blk = np.arange(128)[None, :]
    for i in range(8):
        t = (8 * i + c) * 128 + np.arange(128)[:, None]
        tb = t // 64
        f = np.zeros((128, 128), np.float32)
        f = np.where(blk > tb, -1e30, f)
        f = np.where(blk == tb - 1, 1e9, f)
        f = np.where(blk == tb, 2e9, f)
        f = np.where(blk == 0, 3e9, f)
        fq[:, i, :] = f
    return {
        "tb_bias": np.ascontiguousarray(tb_bias.reshape(9, 128, 2048)),
        "tb_mask": np.ascontiguousarray(tb_mask.reshape(128, 9 * 128)),
        "b31": np.ascontiguousarray(np.broadcast_to(rel_bias[31][None, :], (128, 16))),
        "win_mask": np.ascontiguousarray(win_mask.reshape(128, 12 * 128)),
        "ec_bias": np.ascontiguousarray(ec_bias.reshape(128, 16 * 72)),
        "ec_mask": ec_mask,
        "fq": np.ascontiguousarray(fq.reshape(128, 8 * 128)),
    }


def p2_consts():
    key = np.arange(8192)[None, :]
    j = np.arange(128)[:, None]
    R = (key // 64 == j).astype(np.float32).astype(BF)
    ident = np.eye(128, dtype=np.float32).astype(BF)
    return {"Rm": R, "ident": ident}


def p2_inputs(inp, l1, c, shared):
    r = l1[c]
    qT = np.asarray(r["qT"])
    q_l = qT.reshape(4, 4, 128, 8, 128).transpose(3, 0, 2, 1, 4).reshape(8, 4, 128, 512)
    gates = np.asarray(r["gates"]).reshape(8, 128, 48).transpose(1, 0, 2).reshape(128, 8 * 48)
    d = {"q_l": np.ascontiguousarray(q_l), "gates_l": np.ascontiguousarray(gates)}
    d.update(shared)
    d.update(p2_tables(inp["rel_bias"], c))
    return d


def p2_shared(inp, l1):
    sh = {}
    for nm in ("kcT", "vcT", "ksT", "kwT"):
        full = np.zeros((512, 8192), BF)
        for c in range(8):
            full[:, own_idx(c)] = np.asarray(l1[c][nm])
        sh[nm] = np.ascontiguousarray(full.reshape(4, 128, 8192))
    for nm, out in (("vs", "vs1"), ("vw", "vw1")):
        full = np.zeros((8192, 512), BF)
        for c in range(8):
            full[own_idx(c)] = np.asarray(l1[c][nm])
        v = full.reshape(64, 128, 4, 128).transpose(2, 1, 0, 3)
        v1 = np.ones((4, 128, 64, 129), BF)
        v1[..., :128] = v
        sh[out] = np.ascontiguousarray(v1.reshape(4, 128, 64 * 129))
    for kv in ("k", "v"):
        w1 = np.asarray(inp[f"cmp_w1_{kv}"][0], np.float32)
        sh[f"w1{kv}"] = np.ascontiguousarray(w1.reshape(32, 128, 256).transpose(1, 0, 2).reshape(128, 32 * 256))
        w2 = np.asarray(inp[f"cmp_w2_{kv}"][0], np.float32)
        sh[f"w2{kv}"] = np.ascontiguousarray(w2.reshape(2, 128, 128).transpose(1, 0, 2).reshape(128, 256))
        sh[f"posT{kv}"] = np.ascontiguousarray(np.asarray(inp[f"cmp_pos_{kv}"][0], np.float32).T)
    sh.update(p2_consts())
    return sh


def s5prep_inputs(inp):
    return {"A_re": np.ascontiguousarray(inp["s5_A_re"][0]), "A_im": np.ascontiguousarray(inp["s5_A_im"][0]),
            "log_dt": np.ascontiguousarray(inp["s5_log_dt"][0].reshape(128, 1)),
            "B_re": np.ascontiguousarray(inp["s5_B_re"][0].reshape(128, 1024)),
            "B_im": np.ascontiguousarray(inp["s5_B_im"][0].reshape(128, 1024))}


def s5main_inputs(inp, prep, uT_full, c):
    r = np.asarray(prep["o_r"]); th = np.asarray(prep["o_th"])
    bbre = np.asarray(prep["o_bbre"]).reshape(128, 64, 16); bbim = np.asarray(prep["o_bbim"]).reshape(128, 64, 16)
    C_re = np.asarray(inp["s5_C_re"][0]); C_im = np.asarray(inp["s5_C_im"][0])
    D = np.asarray(inp["s5_D"][0])
    BD = np.zeros((128, 8, 2, 128), np.float32)
    CT = np.zeros((128, 8, 2, 128), np.float32)
    rcol = np.zeros((128, 8), np.float32); thcol = np.zeros((128, 8), np.float32)
    for k in range(8):
        kg = 8 * c + k
        for gg in range(2):
            g = 2 * kg + gg
            ch0 = 32 * (k % 4) + 16 * gg
            st0 = 64 * gg
            BD[ch0:ch0 + 16, k, 0, st0:st0 + 64] = bbre[g].T
            BD[ch0:ch0 + 16, k, 1, st0:st0 + 64] = bbim[g].T
            CT[st0:st0 + 64, k, 0, ch0:ch0 + 16] = C_re[g].T
            CT[st0:st0 + 64, k, 1, ch0:ch0 + 16] = C_im[g].T
            rcol[st0:st0 + 64, k] = r[g]
            thcol[st0:st0 + 64, k] = th[g]
    Dcol = np.ascontiguousarray(D[256 * c:256 * c + 256].reshape(2, 128).T)
    iota = np.ascontiguousarray(np.broadcast_to(np.arange(512, dtype=np.float32)[None, :], (128, 512)))
    return {"uT": np.ascontiguousarray(uT_full[256 * c:256 * c + 256]), "BD": BD.reshape(128, -1), "CT": CT.reshape(128, -1),
            "rcol": rcol, "thcol": thcol, "iota": iota, "Dcol": Dcol}


D = 2048
NT = 1024
KC = 16
EPS = 1e-6


def rmsnorm_T(p, xT_dram, gcol_sb, gcol_res, hnT, hnT_res, ones_f, ones_res, out_dt_note=""):
    nc = p.nc
    xs = p.sbuf("xs", [128, KC, 512], F32)
    xs_r = p.res("xs")
    sq = [p.sbuf("sq", [128, 512], F32) for _ in range(2)]
    sq_r = [p.res("sq") for _ in range(2)]
    ss = p.psum("ss", [128, 512])
    ss_r = p.res("ss")
    rstd = p.sbuf("rstd", [128, 512], F32)
    rstd_r = p.res("rstd")
    for half in range(NT // 512):
        src = xT_dram[:, half * 512:(half + 1) * 512].rearrange("(k p) t -> p k t", p=128)
        for kk in range(0, KC, 4):
            p.dma("sp", xs[:, kk:kk + 4, :], src[:, kk:kk + 4, :], w=[xs_r])
        for k in range(KC):
            b = k % 2
            p.op("act", lambda e: e.activation(out=sq[b][:], in_=xs[:, k, :], func=AF.Square),
                 r=[xs_r], w=[sq_r[b]])
            p.op("pe", lambda e: e.matmul(ss[:], ones_f[:], sq[b][:], start=(k == 0), stop=(k == KC - 1)),
                 r=[sq_r[b], ones_res], w=[ss_r])
        p.op("dve", lambda e: e.tensor_scalar(out=rstd[:], in0=ss[:], scalar1=1.0 / D, scalar2=EPS,
                                              op0=ALU.mult, op1=ALU.add), r=[ss_r], w=[rstd_r])
        p.op("act", lambda e: e.activation(out=rstd[:], in_=rstd[:], func=AF.Sqrt), r=[rstd_r], w=[rstd_r])
        p.op("dve", lambda e: e.reciprocal(out=rstd[:], in_=rstd[:]), r=[rstd_r], w=[rstd_r])
        for k in range(KC):
            p.op("dve", lambda e: e.scalar_tensor_tensor(
                out=hnT[:, k, half * 512:(half + 1) * 512], in0=xs[:, k, :], scalar=gcol_sb[:, k:k + 1],
                in1=rstd[:], op0=ALU.mult, op1=ALU.mult), r=[xs_r, rstd_r, gcol_res], w=[hnT_res])


def build_p1(only=None):
    p = Prog()
    nc = p.nc
    xT = p.dram("xT", [D, NT], F32, "ExternalInput")
    gcol = p.dram("gcol", [128, KC], F32, "ExternalInput")
    w_in = p.dram("w_in", [D, 5168], F32, "ExternalInput")
    qT = p.dram("qT", [2048, NT], BF16, "ExternalOutput")
    kT = {n: p.dram(n, [512, NT], BF16, "ExternalOutput") for n in ("kcT", "vcT", "ksT", "kwT")}
    vv = {n: p.dram(n, [NT, 512], BF16, "ExternalOutput") for n in ("vs", "vw")}
    gates = p.dram("gates", [NT, 48], F32, "ExternalOutput")

    ones_f = p.sbuf("ones", [128, 128], F32)
    ones_r = p.res("ones")
    p.op("pool", lambda e: e.memset(ones_f[:], 1.0), w=[ones_r])
    gsb = p.sbuf("gsb", [128, KC], F32)
    g_r = p.res("g")
    p.dma("sp", gsb[:], gcol[:], w=[g_r])
    hnT = p.sbuf("hnT", [128, KC, NT], BF16)
    hnT_r = p.res("hnT")
    rmsnorm_T(p, xT, gsb, g_r, hnT, hnT_r, ones_f, ones_r)

    wst = [p.sbuf("wst", [128, KC, 512], F32) for _ in range(2)]
    wst_r = [p.res("wst") for _ in range(2)]
    wb = [p.sbuf("wb", [128, KC, 512], BF16) for _ in range(2)]
    wb_r = [p.res("wb") for _ in range(2)]
    ps = [p.psum("ps", [128, 512]) for _ in range(4)]
    ps_r = [p.res("ps") for _ in range(4)]
    ot = [p.sbuf("ot", [128, 512], BF16) for _ in range(4)]
    ot_r = [p.res("ot") for _ in range(4)]
    og = p.sbuf("og", [128, 48], F32)
    og_r = p.res("og")
    chunks = [("qT", 0), ("qT", 1), ("qT", 2), ("qT", 3), ("kcT", 0), ("vcT", 0), ("ksT", 0), ("vs", 0),
              ("kwT", 0), ("vw", 0), ("gates", 0)]
    pi = 0
    for ci, (nm, sub) in enumerate(chunks):
        if only is not None and ci not in only:
            continue
        c0 = ci * 512
        ncol = 512 if nm != "gates" else 48
        b = ci % 2
        src = w_in[:, c0:c0 + ncol].rearrange("(k p) c -> p k c", p=128)
        for kk in range(0, KC, 4):
            qn = "sp" if ((kk // 4) % 2 == 0 or nm == "gates") else "pool"
            p.dma(qn, wst[b][:, kk:kk + 4, :ncol], src[:, kk:kk + 4, :], w=[wst_r[b]])
        for kk in range(0, KC, 8):
            p.op("pool" if nm != "gates" else "dve", lambda e: e.tensor_copy(out=wb[b][:, kk:kk + 8, :ncol], in_=wst[b][:, kk:kk + 8, :ncol]),
                 r=[wst_r[b]], w=[wb_r[b]])
        if nm in ("qT", "kcT", "vcT", "ksT", "kwT"):
            dst = qT if nm == "qT" else kT[nm]
            scale = 128 ** -0.5 if nm == "qT" else 1.0
            for m in range(4):
                for n in range(NT // 512):
                    pb = pi % 4
                    pi += 1
                    for k in range(KC):
                        p.op("pe", lambda e: e.matmul(ps[pb][:], wb[b][:, k, m * 128:(m + 1) * 128],
                                                      hnT[:, k, n * 512:(n + 1) * 512], start=(k == 0), stop=(k == KC - 1)),
                             r=[wb_r[b], hnT_r], w=[ps_r[pb]])
                    p.op("act", lambda e: e.activation(out=ot[pb][:], in_=ps[pb][:], func=AF.Copy, scale=scale),
                         r=[ps_r[pb]], w=[ot_r[pb]])
                    row0 = sub * 512 + m * 128
                    p.dma("act", dst[row0:row0 + 128, n * 512:(n + 1) * 512], ot[pb][:], r=[ot_r[pb]], is_output=True)
        else:
            for t in range(NT // 128):
                pb = pi % 4
                pi += 1
                for k in range(KC):
                    p.op("pe", lambda e: e.matmul(ps[pb][:, :ncol], hnT[:, k, t * 128:(t + 1) * 128],
                                                  wb[b][:, k, :ncol], start=(k == 0), stop=(k == KC - 1)),
                         r=[wb_r[b], hnT_r], w=[ps_r[pb]])
                if nm == "gates":
                    p.op("act", lambda e: e.activation(out=og[:], in_=ps[pb][:, :48], func=AF.Sigmoid),
                         r=[ps_r[pb]], w=[og_r])
                    p.dma("act", gates[t * 128:(t + 1) * 128, :], og[:], r=[og_r], is_output=True)
                else:
                    p.op("act", lambda e: e.activation(out=ot[pb][:], in_=ps[pb][:], func=AF.Copy),
                         r=[ps_r[pb]], w=[ot_r[pb]])
                    p.dma("act", vv[nm][t * 128:(t + 1) * 128, :], ot[pb][:], r=[ot_r[pb]], is_output=True)
    p.finish()
    p.close()
    return p.nc


NI = 8
G = 4
NCMP = 511


def build_p2(only_groups=None, do_sel=True, do_win=True):
    p = Prog()
    q_l = p.dram("q_l", [NI, G, 128, 512], BF16, "ExternalInput")
    gates_l = p.dram("gates_l", [128, NI * 48], F32, "ExternalInput")
    kcT = p.dram("kcT", [G, 128, 8192], BF16, "ExternalInput")
    vcT = p.dram("vcT", [G, 128, 8192], BF16, "ExternalInput")
    ksT = p.dram("ksT", [G, 128, 8192], BF16, "ExternalInput")
    kwT = p.dram("kwT", [G, 128, 8192], BF16, "ExternalInput")
    vs1 = p.dram("vs1", [G, 128, 64 * 129], BF16, "ExternalInput")
    vw1 = p.dram("vw1", [G, 128, 64 * 129], BF16, "ExternalInput")
    w1 = {"k": p.dram("w1k", [128, 32 * 256], F32, "ExternalInput"), "v": p.dram("w1v", [128, 32 * 256], F32, "ExternalInput")}
    w2 = {"k": p.dram("w2k", [128, 2 * 128], F32, "ExternalInput"), "v": p.dram("w2v", [128, 2 * 128], F32, "ExternalInput")}
    posT = {"k": p.dram("posTk", [128, 32], F32, "ExternalInput"), "v": p.dram("posTv", [128, 32], F32, "ExternalInput")}
    tb_bias = p.dram("tb_bias", [9, 128, 16 * 128], F32, "ExternalInput")
    tb_mask = p.dram("tb_mask", [128, 9 * 128], F32, "ExternalInput")
    b31 = p.dram("b31", [128, 16], F32, "ExternalInput")
    win_mask = p.dram("win_mask", [128, 12 * 128], F32, "ExternalInput")
    ec_bias = p.dram("ec_bias", [128, 16 * 72], F32, "ExternalInput")
    ec_mask = p.dram("ec_mask", [128, 72], F32, "ExternalInput")
    fq = p.dram("fq", [128, NI * 128], F32, "ExternalInput")
    Rm = p.dram("Rm", [128, 8192], BF16, "ExternalInput")
    ident = p.dram("ident", [128, 128], BF16, "ExternalInput")
    o_out = p.dram("o_out", [NI * 128, 2048], BF16, "ExternalOutput")

    def T(name, shape, dt):
        return p.sbuf(name, shape, dt), p.res(name)

    kccT, kccT_r = T("kccT", [128, G, 512], BF16)
    vcc, vcc_r = T("vcc", [128, G, 4, 128], BF16)
    R_sb, R_r = T("R", [128, 8192], BF16)
    id_sb, id_r = T("ident", [128, 128], BF16)
    Tt, Tt_r = T("Tt", [128, 9, 16 * 128], BF16)
    Wm4, Wm4_r = T("Wm4", [128, 12, 4, 128], BF16)
    Fq, Fq_r = T("Fq", [128, NI, 128], F32)
    Ec, Ec_r = T("Ec", [128, 16, 72], F32)
    b31s, b31_r = T("b31", [128, 16], F32)
    gat, gat_r = T("gat", [128, NI, 48], F32)
    psS = [p.psum("psS", [128, 512]) for _ in range(2)]
    psS_r = [p.res("psS") for _ in range(2)]
    psO = [p.psum("psO", [128, 512]) for _ in range(4)]
    psO_r = [p.res("psO") for _ in range(4)]
    psC = psS
    psC_r = psS_r
    psT = p.psum("psT", [128, 1024], BF16)
    psT_r = p.res("psT")
    psX = p.psum("psX", [128, 512])
    psX_r = p.res("psX")

    p.dma("sp", R_sb[:], Rm[:], w=[R_r])
    p.dma("sp", id_sb[:], ident[:], w=[id_r])
    p.dma("sp", Fq[:].rearrange("p i b -> p (i b)"), fq[:], w=[Fq_r])
    p.dma("sp", b31s[:], b31[:], w=[b31_r])
    p.dma("sp", gat[:].rearrange("p i c -> p (i c)"), gates_l[:], w=[gat_r])

    p.push_scope()
    stg = [T("stg", [128, 2048], F32) for _ in range(2)]
    msk, msk_r = T("msk", [128, 12 * 128], F32)
    p.dma("sp", msk[:, 0:9 * 128], tb_mask[:], w=[msk_r])
    for j in range(9):
        sb, sr = stg[j % 2]
        p.dma("sp", sb[:], tb_bias[j], w=[sr])
        v3 = sb[:].rearrange("p (h q) -> p h q", h=16)
        p.op("dve", lambda e: e.tensor_tensor(out=v3, in0=v3, in1=b31s[:].unsqueeze(2).to_broadcast([128, 16, 128]),
                                              op=ALU.subtract), r=[sr, b31_r], w=[sr])
        p.op("dve", lambda e: e.tensor_tensor(out=Tt[:, j, :].rearrange("p (h q) -> p h q", h=16), in0=v3,
                                              in1=msk[:, j * 128:(j + 1) * 128].unsqueeze(1).to_broadcast([128, 16, 128]),
                                              op=ALU.add), r=[sr, msk_r], w=[Tt_r])
    p.dma("sp", msk[:], win_mask[:], w=[msk_r])
    for j in range(12):
        p.op("dve", lambda e: e.tensor_copy(out=Wm4[:, j, :, :],
                                            in_=msk[:, j * 128:(j + 1) * 128].unsqueeze(1).to_broadcast([128, 4, 128])),
             r=[msk_r], w=[Wm4_r])
    sb, sr = stg[0]
    p.dma("sp", sb[:, 0:16 * 72], ec_bias[:], w=[sr])
    p.dma("sp", msk[:, 0:72], ec_mask[:], w=[msk_r])
    v3 = sb[:, 0:16 * 72].rearrange("p (h e) -> p h e", h=16)
    p.op("dve", lambda e: e.tensor_tensor(out=v3, in0=v3, in1=b31s[:].unsqueeze(2).to_broadcast([128, 16, 72]),
                                          op=ALU.subtract), r=[sr, b31_r], w=[sr])
    p.op("dve", lambda e: e.tensor_tensor(out=Ec[:], in0=v3, in1=msk[:, 0:72].unsqueeze(1).to_broadcast([128, 16, 72]),
                                          op=ALU.add), r=[sr, msk_r], w=[Ec_r])
    w1b, w1b_r = T("w1b", [128, 32, 256], BF16)
    w2b, w2b_r = T("w2b", [128, 2, 128], BF16)
    posb, posb_r = T("posb", [128, 32], BF16)
    pw1, pw1_r = T("pw1", [128, 2], F32)
    kvT, kvT_r = T("kvT", [128, 8192], BF16)
    hidT, hidT_r = T("hidT", [128, 2, 512], BF16)
    xg, xg_r = T("xg", [128, 512], F32)
    tg, tg_r = T("tg", [128, 512], F32)
    p.op("dve", lambda e: e.memset(vcc[:], 0.0), w=[vcc_r])
    p.op("dve", lambda e: e.memset(kccT[:], 0.0), w=[kccT_r])
    for kv in ("k", "v"):
        for q4 in range(4):
            sb, sr = stg[q4 % 2]
            p.dma("sp", sb[:], w1[kv][:, q4 * 2048:(q4 + 1) * 2048], w=[sr])
            p.op("dve", lambda e: e.tensor_copy(out=w1b[:, q4 * 8:(q4 + 1) * 8, :].rearrange("p j c -> p (j c)"), in_=sb[:]),
                 r=[sr], w=[w1b_r])
        sb, sr = stg[0]
        p.dma("sp", sb[:, 0:256], w2[kv][:], w=[sr])
        p.op("dve", lambda e: e.tensor_copy(out=w2b[:].rearrange("p a b -> p (a b)"), in_=sb[:, 0:256]), r=[sr], w=[w2b_r])
        sb, sr = stg[1]
        p.dma("sp", sb[:, 0:32], posT[kv][:], w=[sr])
        p.op("dve", lambda e: e.tensor_copy(out=posb[:], in_=sb[:, 0:32]), r=[sr], w=[posb_r])
        for hc in range(2):
            for j in range(32):
                p.op("pe", lambda e: e.matmul(psX[:, 0:1], w1b[:, j, hc * 128:(hc + 1) * 128], posb[:, j:j + 1],
                                              start=(j == 0), stop=(j == 31)), r=[w1b_r, posb_r], w=[psX_r])
            p.op("dve", lambda e: e.tensor_copy(out=pw1[:, hc:hc + 1], in_=psX[:, 0:1]), r=[psX_r], w=[pw1_r])
        src = kcT if kv == "k" else vcT
        for g in range(G):
            p.dma("sp", kvT[:], src[g], w=[kvT_r])
            for hc in range(2):
                pc = psC[hc]
                for j in range(32):
                    p.op("pe", lambda e: e.matmul(pc[:, 0:NCMP], w1b[:, j, hc * 128:(hc + 1) * 128],
                                                  kvT[:, j:j + 16 * (NCMP - 1) + 1:16], start=(j == 0), stop=(j == 31)),
                         r=[w1b_r, kvT_r], w=[psC_r[hc]])
                p.op("act", lambda e: e.activation(out=xg[:, 0:NCMP], in_=pc[:, 0:NCMP], func=AF.Identity,
                                                   bias=pw1[:, hc:hc + 1]), r=[psC_r[hc], pw1_r], w=[xg_r])
                p.op("dve", lambda e: e.tensor_tensor(out=tg[:, 0:NCMP], in0=xg[:, 0:NCMP], in1=xg[:, 0:NCMP], op=ALU.mult),
                     r=[xg_r], w=[tg_r])
                p.op("dve", lambda e: e.tensor_scalar(out=tg[:, 0:NCMP], in0=tg[:, 0:NCMP], scalar1=0.044715, scalar2=1.0,
                                                      op0=ALU.mult, op1=ALU.add), r=[tg_r], w=[tg_r])
                p.op("dve", lambda e: e.tensor_tensor(out=tg[:, 0:NCMP], in0=tg[:, 0:NCMP], in1=xg[:, 0:NCMP], op=ALU.mult),
                     r=[tg_r, xg_r], w=[tg_r])
                p.op("act", lambda e: e.activation(out=tg[:, 0:NCMP], in_=tg[:, 0:NCMP], func=AF.Sigmoid, scale=1.5957691216),
                     r=[tg_r], w=[tg_r])
                p.op("dve", lambda e: e.tensor_tensor(out=hidT[:, hc, 0:NCMP], in0=tg[:, 0:NCMP], in1=xg[:, 0:NCMP], op=ALU.mult),
                     r=[tg_r, xg_r], w=[hidT_r])
            if kv == "k":
                for hc in range(2):
                    p.op("pe", lambda e: e.matmul(psX[:, 0:NCMP], w2b[:, hc, :], hidT[:, hc, 0:NCMP],
                                                  start=(hc == 0), stop=(hc == 1)), r=[w2b_r, hidT_r], w=[psX_r])
                p.op("act", lambda e: e.activation(out=kccT[:, g, 0:NCMP], in_=psX[:, 0:NCMP], func=AF.Copy),
                     r=[psX_r], w=[kccT_r])
            else:
                for cb in range(4):
                    n = min(NCMP, (cb + 1) * 128) - cb * 128
                    for hc in range(2):
                        p.op("pe", lambda e: e.matmul(psX[0:n, cb * 128:(cb + 1) * 128], hidT[:, hc, cb * 128:cb * 128 + n],
                                                      w2b[:, hc, :], start=(hc == 0), stop=(hc == 1)),
                             r=[w2b_r, hidT_r], w=[psX_r])
                    p.op("act", lambda e: e.activation(out=vcc[0:n, g, cb, :], in_=psX[0:n, cb * 128:(cb + 1) * 128], func=AF.Copy),
                         r=[psX_r], w=[vcc_r])
    p.pop_scope()

    ksT_s, ksT_r = T("ksT", [128, 8192], BF16)
    kwT_s, kwT_r = T("kwT", [128, 8192], BF16)
    vs_s, vs_r = T("vs1", [128, 64, 129], BF16)
    vw_s, vw_r = T("vw1", [128, 64, 129], BF16)
    qg = [T("qg", [128, 512], BF16) for _ in range(2)]
    s4, s4_r = T("s4", [128, 4, 512], F32)
    e4, e4_r = T("e4", [128, 4, 512], F32)
    pb4, pb4_r = T("pb4", [128, 4, 512], BF16)
    pT4, pT4_r = T("pT4", [128, 4, 4, 128], BF16)
    st4, st4_r = T("st4", [128, 4, 8], F32)
    imp, imp_r = T("imp", [128, 520], F32)
    sc, sc_r = T("sc", [128, 128], F32)
    wk, wk_r = T("wk", [128, 128], F32)
    m8, m8_r = T("m8", [128, 16], F32)
    st1, st1_r = T("st1", [128, 8], F32)
    negm, negm_r = T("negm", [128, 128], BF16)
    nmT, nmT_r = T("nmT", [128, 4, 128], BF16)
    PT = [T("PT", [128, 512], BF16) for _ in range(3)]
    OA = [T("oacc", [128, 512], F32) for _ in range(2)]
    obf = [T("obf", [128, 512], BF16) for _ in range(2)]
    coef, coef_r = T("coef", [128, 8], F32)
    cnt = {"S": 0, "P": 0, "q": 0, "c": 0}

    def attend(i, g, qv, q_r, kts, K_s, K_r, V_s, V_r, extra, gate_col0, init_acc, oacc, oacc_r):
        n = len(kts)
        slots = {}

        def qk(idx):
            kt = kts[idx]
            sb_ = cnt["S"] % 2
            cnt["S"] += 1
            slots[idx] = sb_
            mms = [(K_s[:, kt * 128:(kt + 1) * 128], K_r, qv[:], q_r)] + extra(kt)
            for mi, (lt, lr, rh, rr) in enumerate(mms):
                p.op("pe", lambda e: e.matmul(psS[sb_][:], lt, rh, start=(mi == 0), stop=(mi == len(mms) - 1)),
                     r=[lr, rr], w=[psS_r[sb_]])

        qk(0)
        for idx, kt in enumerate(kts):
            if idx + 1 < n:
                qk(idx + 1)
            sb_ = slots.pop(idx)
            pb_ = cnt["P"] % 3
            cnt["P"] += 1
            Pt, Pt_r = PT[pb_]
            p.op("act", lambda e: e.activation(out=Pt[:], in_=psS[sb_][:], func=AF.Exp), r=[psS_r[sb_]], w=[Pt_r])
            for h in range(4):
                bank = h
                c0 = 0
                p.op("pe", lambda e: e.matmul(psO[bank][:, c0:c0 + 129], Pt[:, h * 128:(h + 1) * 128], V_s[:, kt, :],
                                              start=(idx == 0), stop=(idx == n - 1)), r=[Pt_r, V_r], w=[psO_r[bank]])
        for h in range(4):
            bank = h
            c0 = 0
            head = g * 4 + h
            p.op("dve", lambda e: e.reciprocal(out=coef[:, h:h + 1], in_=psO[bank][:, c0 + 128:c0 + 129]),
                 r=[psO_r[bank]], w=[coef_r])
            p.op("dve", lambda e: e.tensor_tensor(out=coef[:, h:h + 1], in0=coef[:, h:h + 1],
                                                  in1=gat[:, i, gate_col0 + head:gate_col0 + head + 1], op=ALU.mult),
                 r=[coef_r, gat_r], w=[coef_r])
            dst = oacc[:, h * 128:(h + 1) * 128]
            if init_acc:
                p.op("dve", lambda e: e.tensor_scalar(out=dst, in0=psO[bank][:, c0:c0 + 128], scalar1=coef[:, h:h + 1],
                                                      scalar2=None, op0=ALU.mult), r=[psO_r[bank], coef_r], w=[oacc_r])
            else:
                p.op("dve", lambda e: e.scalar_tensor_tensor(out=dst, in0=psO[bank][:, c0:c0 + 128], scalar=coef[:, h:h + 1],
                                                             in1=dst, op0=ALU.mult, op1=ALU.add),
                     r=[psO_r[bank], coef_r, oacc_r], w=[oacc_r])

    groups = list(range(G)) if only_groups is None else only_groups
    for g in groups:
        p.dma("sp", ksT_s[:], ksT[g], w=[ksT_r])
        p.dma("sp", kwT_s[:], kwT[g], w=[kwT_r])
        p.dma("sp", vs_s[:].rearrange("p k d -> p (k d)"), vs1[g], w=[vs_r])
        p.dma("sp", vw_s[:].rearrange("p k d -> p (k d)"), vw1[g], w=[vw_r])
        for i in range(NI):
            qb = cnt["q"] % 2
            cnt["q"] += 1
            qv, q_r = qg[qb]
            p.dma("sp", qv[:], q_l[i, g], w=[q_r])
            oacc, oacc_r = OA[qb]
            ncv = min(NCMP, 64 * i + 63)
            e_lo = 64 * i - 9
            c_lo = max(0, e_lo)
            c_hi = min(ncv, 64 * i + 63)
            ncb = (ncv + 127) // 128
            gsl = slice(g * 4, (g + 1) * 4)
            for h in range(4):
                p.op("pe", lambda e: e.matmul(psO[h][:, 0:ncv], qv[:, h * 128:(h + 1) * 128], kccT[:, g, 0:ncv], start=True, stop=True),
                     r=[q_r, kccT_r], w=[psO_r[h]])
            for h in range(4):
                p.op("act", lambda e: e.activation(out=s4[:, h, 0:ncv], in_=psO[h][:, 0:ncv], func=AF.Copy), r=[psO_r[h]], w=[s4_r])
            p.op("dve", lambda e: e.tensor_tensor(out=s4[:, :, c_lo:c_hi], in0=s4[:, :, c_lo:c_hi],
                                                  in1=Ec[:, gsl, c_lo - e_lo:c_hi - e_lo], op=ALU.add), r=[s4_r, Ec_r], w=[s4_r])
            p.op("dve", lambda e: e.tensor_reduce(out=st4[:, :, 0], in_=s4[:, :, 0:ncv], axis=AX.X, op=ALU.max), r=[s4_r], w=[st4_r])
            p.op("dve", lambda e: e.tensor_scalar(out=st4[:, :, 1], in0=st4[:, :, 0], scalar1=-1000.0, scalar2=-1.0,
                                                  op0=ALU.max, op1=ALU.mult), r=[st4_r], w=[st4_r])
            p.op("dve", lambda e: e.memset(st4[:, :, 2], 0.0), w=[st4_r])
            for h in range(4):
                p.op("act", lambda e: e.activation(out=e4[:, h, 0:ncv], in_=s4[:, h, 0:ncv], func=AF.Exp, bias=st4[:, h, 1:2],
                                                   accum_out=st4[:, h, 2:3]), r=[s4_r, st4_r], w=[e4_r, st4_r])
            p.op("dve", lambda e: e.tensor_scalar(out=st4[:, :, 3], in0=st4[:, :, 2], scalar1=1e-30, scalar2=None, op0=ALU.add),
                 r=[st4_r], w=[st4_r])
            p.op("dve", lambda e: e.reciprocal(out=st4[:, :, 4], in_=st4[:, :, 3]), r=[st4_r], w=[st4_r])
            p.op("dve", lambda e: e.tensor_tensor(out=st4[:, :, 5], in0=st4[:, :, 4], in1=gat[:, i, gsl], op=ALU.mult),
                 r=[st4_r, gat_r], w=[st4_r])
            p.op("dve", lambda e: e.tensor_tensor(out=s4[:, :, 0:ncv], in0=e4[:, :, 0:ncv],
                                                  in1=st4[:, :, 4:5].to_broadcast([128, 4, ncv]), op=ALU.mult),
                 r=[e4_r, st4_r], w=[s4_r])
            p.op("pool", lambda e: e.memset(imp[:], 0.0), w=[imp_r])
            p.op("dve", lambda e: e.tensor_reduce(out=imp[:, 1:1 + ncv], in_=s4[:, :, 0:ncv].rearrange("p h c -> p c h"),
                                                  axis=AX.X, op=ALU.add), r=[s4_r], w=[imp_r])
            p.op("pool", lambda e: e.memset(pb4[:], 0.0), w=[pb4_r])
            p.op("dve", lambda e: e.tensor_tensor(out=pb4[:, :, 0:ncv], in0=e4[:, :, 0:ncv],
                                                  in1=st4[:, :, 5:6].to_broadcast([128, 4, ncv]), op=ALU.mult),
                 r=[e4_r, st4_r], w=[pb4_r])
            for hp in range(2):
                for h in (2 * hp, 2 * hp + 1):
                    for cb in range(ncb):
                        c0 = (h % 2) * 512 + cb * 128
                        p.op("pe", lambda e: e.transpose(psT[:, c0:c0 + 128], pb4[:, h, cb * 128:(cb + 1) * 128], id_sb[:]),
                             r=[pb4_r, id_r], w=[psT_r])
                for h in (2 * hp, 2 * hp + 1):
                    c0 = (h % 2) * 512
                    p.op("act", lambda e: e.activation(out=pT4[:, h, 0:ncb, :].rearrange("p a b -> p (a b)"),
                                                       in_=psT[:, c0:c0 + ncb * 128], func=AF.Copy), r=[psT_r], w=[pT4_r])
            for h in range(4):
                for cb in range(ncb):
                    p.op("pe", lambda e: e.matmul(psO[h][:, 0:128], pT4[:, h, cb, :], vcc[:, g, cb, :], start=(cb == 0), stop=(cb == ncb - 1)),
                         r=[pT4_r, vcc_r], w=[psO_r[h]])
            for h in range(4):
                p.op("act", lambda e: e.activation(out=oacc[:, h * 128:(h + 1) * 128], in_=psO[h][:, 0:128], func=AF.Copy),
                     r=[psO_r[h]], w=[oacc_r])
            if do_sel:
                iv = imp[:, 0:512].rearrange("p (j f) -> p j f", f=4)
                p.op("dve", lambda e: e.tensor_reduce(out=sc[:], in_=iv, axis=AX.X, op=ALU.add), r=[imp_r], w=[sc_r])
                p.op("dve", lambda e: e.tensor_tensor(out=sc[:], in0=sc[:], in1=imp[:, 4:516].rearrange("p (j f) -> p j f", f=4)[:, :, 0],
                                                      op=ALU.add), r=[sc_r, imp_r], w=[sc_r])
                p.op("dve", lambda e: e.tensor_tensor(out=sc[:], in0=sc[:], in1=Fq[:, i, :], op=ALU.add), r=[sc_r, Fq_r], w=[sc_r])
                p.op("dve", lambda e: e.max(out=m8[:, 0:8], in_=sc[:]), r=[sc_r], w=[m8_r])
                p.op("dve", lambda e: e.match_replace(out=wk[:], in_to_replace=m8[:, 0:8], in_values=sc[:], imm_value=-3.0e38),
                     r=[sc_r, m8_r], w=[wk_r])
                p.op("dve", lambda e: e.max(out=m8[:, 8:16], in_=wk[:]), r=[wk_r], w=[m8_r])
                p.op("dve", lambda e: e.tensor_scalar(out=wk[:], in0=sc[:], scalar1=m8[:, 15:16], scalar2=None, op0=ALU.is_ge),
                     r=[sc_r, m8_r], w=[wk_r])
                p.op("dve", lambda e: e.tensor_scalar(out=negm[:], in0=wk[:], scalar1=1.0, scalar2=-NEG, op0=ALU.subtract, op1=ALU.mult),
                     r=[wk_r], w=[negm_r])
                p.op("pe", lambda e: e.transpose(psT[:, 512:640], negm[:], id_sb[:]), r=[negm_r, id_r], w=[psT_r])
                p.op("act", lambda e: e.activation(out=nmT[:], in_=psT[:, 512:640].unsqueeze(1).to_broadcast([128, 4, 128]), func=AF.Copy),
                     r=[psT_r], w=[nmT_r])

                def extra_sel(kt, i=i, g=g):
                    ex = [(R_sb[:, kt * 128:(kt + 1) * 128], R_r, nmT[:].rearrange("p a b -> p (a b)"), nmT_r)]
                    j = kt - 8 * i
                    if j >= -1:
                        ex.append((id_sb[:], id_r, Tt[:, j + 1, g * 512:(g + 1) * 512], Tt_r))
                    return ex

                attend(i, g, qv, q_r, list(range(0, 8 * i + 8)), ksT_s, ksT_r, vs_s, vs_r, extra_sel, 16, False, oacc, oacc_r)
            if do_win:
                def extra_win(kt, i=i, g=g):
                    jp = kt - (8 * i - 4)
                    ex = [(id_sb[:], id_r, Wm4[:, jp, :, :].rearrange("p a b -> p (a b)"), Wm4_r)]
                    if jp >= 3:
                        ex.append((id_sb[:], id_r, Tt[:, jp - 3, g * 512:(g + 1) * 512], Tt_r))
                    return ex

                kts = [kt for kt in range(8 * i - 4, 8 * i + 8) if kt >= 0]
                attend(i, g, qv, q_r, kts, kwT_s, kwT_r, vw_s, vw_r, extra_win, 32, False, oacc, oacc_r)
            ob, ob_r = obf[qb]
            p.op("act", lambda e: e.activation(out=ob[:], in_=oacc[:], func=AF.Copy), r=[oacc_r], w=[ob_r])
            p.dma("sp", o_out[i * 128:(i + 1) * 128, g * 512:(g + 1) * 512], ob[:], r=[ob_r], is_output=True)
    p.finish()
    p.close()
    return p.nc


DFF = 5632
MC = DFF // 128
MG = 2
MGC = MC // MG


def max_tokens(toks):
    best = {}
    for s, v in toks:
        if id(s) not in best or best[id(s)][1] < v:
            best[id(s)] = (s, v)
    return list(best.values())


class Fence:
    def __init__(self):
        self.toks = []

    def add(self, tok):
        self.toks.append(tok)
        if len(self.toks) > 256:
            self.toks = max_tokens(self.toks)

    def wait(self, p, e):
        for t in max_tokens(self.toks):
            p._wait(e, t)


def rmsnorm_T2(p, src_dram, fence, gsb, g_r, ones_f, ones_r, st, out_sb=None, out_sb_r=None, out_dram=None,
               out_fence=None, is_output=False):
    xq, xq_r, sq, sq_r, ss, ss_r, rstd, rstd_r, uo, uo_r = st
    n = 0
    for half in range(NT // 512):
        if fence is not None:
            fence.wait(p, "act")
        hs = slice(half * 512, (half + 1) * 512)
        for k in range(KC):
            b = n % 4
            n += 1
            p.dma("act", xq[b][:], src_dram[k * 128:(k + 1) * 128, hs], w=[xq_r[b]])
            b2 = k % 2
            p.op("act", lambda e: e.activation(out=sq[b2][:], in_=xq[b][:], func=AF.Square), r=[xq_r[b]], w=[sq_r[b2]])
            p.op("pe", lambda e: e.matmul(ss[:], ones_f[:], sq[b2][:], start=(k == 0), stop=(k == KC - 1)),
                 r=[sq_r[b2], ones_r], w=[ss_r])
        p.op("dve", lambda e: e.tensor_scalar(out=rstd[:], in0=ss[:], scalar1=1.0 / D, scalar2=EPS,
                                              op0=ALU.mult, op1=ALU.add), r=[ss_r], w=[rstd_r])
        p.op("act", lambda e: e.activation(out=rstd[:], in_=rstd[:], func=AF.Sqrt), r=[rstd_r], w=[rstd_r])
        p.op("dve", lambda e: e.reciprocal(out=rstd[:], in_=rstd[:]), r=[rstd_r], w=[rstd_r])
        for k in range(KC):
            b = n % 4
            n += 1
            p.dma("act", xq[b][:], src_dram[k * 128:(k + 1) * 128, hs], w=[xq_r[b]])
            if out_sb is not None:
                p.op("dve", lambda e: e.scalar_tensor_tensor(
                    out=out_sb[:, k, hs], in0=xq[b][:], scalar=gsb[:, k:k + 1],
                    in1=rstd[:], op0=ALU.mult, op1=ALU.mult), r=[xq_r[b], rstd_r, g_r], w=[out_sb_r])
            else:
                b2 = k % 2
                p.op("dve", lambda e: e.scalar_tensor_tensor(
                    out=uo[b2][:], in0=xq[b][:], scalar=gsb[:, k:k + 1],
                    in1=rstd[:], op0=ALU.mult, op1=ALU.mult), r=[xq_r[b], rstd_r, g_r], w=[uo_r[b2]])
                tok = p.dma("act", out_dram[k * 128:(k + 1) * 128, hs], uo[b2][:],
                            r=[uo_r[b2]], is_output=is_output)
                if out_fence is not None:
                    out_fence.add(tok)


def build_tok(glu, final):
    p = Prog()
    resT = p.dram("resT", [D, NT], F32, "ExternalInput")
    aT = p.dram("aT", [D, NT], BF16, "ExternalInput")
    wmix = p.dram("wmix", [D, 4096 if glu else 2048], F32, "ExternalInput")
    gff = p.dram("gff", [128, KC], F32, "ExternalInput")
    gn = p.dram("gn", [128, KC], F32, "ExternalInput")
    w1 = p.dram("w1", [D, 2 * DFF], F32, "ExternalInput")
    w2 = p.dram("w2", [DFF, D], F32, "ExternalInput")
    hmidT = p.dram("hmidT", [D, NT], F32, "ExternalOutput")
    hT = p.dram("hT", [D, NT], F32, "ExternalOutput")
    normT = p.dram("normT", [D, NT], F32, "ExternalOutput")

    def T(name, shape, dt=F32):
        return p.sbuf(name, shape, dt), p.res(name)

    ones_f, ones_r = T("ones", [128, 128])
    p.op("dve", lambda e: e.memset(ones_f[:], 1.0), w=[ones_r])
    gffs, gff_r = T("gffs", [128, KC])
    p.dma("sp", gffs[:], gff[:], w=[gff_r])
    gns, gn_r = T("gns", [128, KC])
    p.dma("sp", gns[:], gn[:], w=[gn_r])
    xq = [T("xq", [128, 512]) for _ in range(4)]
    st = ([t for t, _ in xq], [r for _, r in xq],
          [p.sbuf("sq", [128, 512], F32) for _ in range(2)], [p.res("sq") for _ in range(2)],
          p.psum("ss", [128, 512]), p.res("ss"),
          p.sbuf("rstd", [128, 512], F32), p.res("rstd"),
          [p.sbuf("uo", [128, 512], F32) for _ in range(2)], [p.res("uo") for _ in range(2)])
    big, big_r = T("big", [128, MGC * NT], BF16)
    a_sb = big[:, 0:KC * NT].rearrange("p (k t) -> p k t", k=KC)
    actT = big[:, :].rearrange("p (m t) -> p m t", m=MGC)
    hnT, hnT_r = T("hnT", [128, KC, NT], BF16)
    NS = 3
    wst = [T("wst", [128, KC * 256]) for _ in range(NS)]
    NBF = 4
    wbf = [T("wbf", [128, KC * 256], BF16) for _ in range(NBF)]
    ps = [p.psum("ps", [128, 512]) for _ in range(6)]
    ps_r = [p.res("ps") for _ in range(6)]
    xc = [T("xc", [128, 512]) for _ in range(3)]
    t1 = [T("t1", [128, 512]) for _ in range(2)]
    cnt = {"w": 0, "b": 0, "ps": 0, "x": 0, "t": 0}

    def load_w(wd, row0, kc, col0, ncol):
        i = cnt["w"] % NS
        cnt["w"] += 1
        j = cnt["b"] % NBF
        cnt["b"] += 1
        ws, ws_r = wst[i]
        wb, wb_r = wbf[j]
        sv = ws[:, 0:kc * ncol].rearrange("p (k c) -> p k c", k=kc)
        bv = wb[:, 0:kc * ncol].rearrange("p (k c) -> p k c", k=kc)
        src = wd[row0:row0 + kc * 128, col0:col0 + ncol].rearrange("(k p) c -> p k c", p=128)
        h = (kc + 1) // 2
        p.dma("sp", sv[:, 0:h, :], src[:, 0:h, :], w=[ws_r])
        p.dma("sp", sv[:, h:kc, :], src[:, h:kc, :], w=[ws_r])
        p.op("pool", lambda e: e.tensor_copy(out=wb[:, 0:kc * ncol], in_=ws[:, 0:kc * ncol]), r=[ws_r], w=[wb_r])
        return bv, wb_r

    def mm(wv, w_r, inT, in_r, kc, half):
        pb = cnt["ps"] % 6
        cnt["ps"] += 1
        for k in range(kc):
            p.op("pe", lambda e: e.matmul(ps[pb][:], wv[:, k, :], inT[:, k, half * 512:(half + 1) * 512],
                                          start=(k == 0), stop=(k == kc - 1)), r=[w_r, in_r], w=[ps_r[pb]])
        return pb

    def prefetched(reqs):
        nxt = load_w(*reqs[0]) if reqs else None
        for n_ in range(len(reqs)):
            cur = nxt
            nxt = load_w(*reqs[n_ + 1]) if n_ + 1 < len(reqs) else None
            yield cur

    srcA = aT.rearrange("(k p) t -> p k t", p=128)
    for kk in range(0, KC, 4):
        p.dma("sp", a_sb[:, kk:kk + 4, :], srcA[:, kk:kk + 4, :], w=[big_r])
    f_mid = Fence()
    reqs = []
    for f in range(KC):
        reqs.append((wmix, 0, KC, f * 128, 128))
        if glu:
            reqs.append((wmix, 0, KC, 2048 + f * 128, 128))
    it = prefetched(reqs)
    for f in range(KC):
        wv, w_r = next(it)
        if glu:
            wv2, w2_r = next(it)
        for half in range(2):
            hs = slice(half * 512, (half + 1) * 512)
            xb = cnt["x"] % 3
            cnt["x"] += 1
            xcb, xcb_r = xc[xb]
            p.dma("act", xcb[:], resT[f * 128:(f + 1) * 128, hs], w=[xcb_r])
            pb = mm(wv, w_r, a_sb, big_r, KC, half)
            if glu:
                pb2 = mm(wv2, w2_r, a_sb, big_r, KC, half)
                tb = cnt["t"] % 2
                cnt["t"] += 1
                tt, tt_r = t1[tb]
                p.op("act", lambda e: e.activation(out=tt[:], in_=ps[pb2][:], func=AF.Sigmoid), r=[ps_r[pb2]], w=[tt_r])
                p.op("dve", lambda e: e.tensor_tensor(out=tt[:], in0=ps[pb][:], in1=tt[:], op=ALU.mult),
                     r=[ps_r[pb], tt_r], w=[tt_r])
                p.op("dve", lambda e: e.tensor_tensor(out=xcb[:], in0=xcb[:], in1=tt[:], op=ALU.add),
                     r=[tt_r, xcb_r], w=[xcb_r])
            else:
                p.op("dve", lambda e: e.tensor_tensor(out=xcb[:], in0=ps[pb][:], in1=xcb[:], op=ALU.add),
                     r=[ps_r[pb], xcb_r], w=[xcb_r])
            tok = p.dma("act", hmidT[f * 128:(f + 1) * 128, hs], xcb[:], r=[xcb_r], is_output=True)
            f_mid.add(tok)
    rmsnorm_T2(p, hmidT, f_mid, gffs, gff_r, ones_f, ones_r, st, out_sb=hnT, out_sb_r=hnT_r)
    src_res, src_fence = hmidT, f_mid
    for mg in range(MG):
        reqs = []
        for m2 in range(0, MGC, 2):
            m = mg * MGC + m2
            reqs.append((w1, 0, KC, m * 128, 256))
            reqs.append((w1, 0, KC, DFF + m * 128, 256))
        it = prefetched(reqs)
        for m2 in range(0, MGC, 2):
            wa, wa_r = next(it)
            wb_, wb_r = next(it)
            for mm_ in range(2):
                ml = m2 + mm_
                for half in range(2):
                    pa = mm(wa[:, :, mm_ * 128:(mm_ + 1) * 128], wa_r, hnT, hnT_r, KC, half)
                    pbb = mm(wb_[:, :, mm_ * 128:(mm_ + 1) * 128], wb_r, hnT, hnT_r, KC, half)
                    tb = cnt["t"] % 2
                    cnt["t"] += 1
                    tt, tt_r = t1[tb]
                    p.op("act", lambda e: e.activation(out=tt[:], in_=ps[pa][:], func=AF.Silu), r=[ps_r[pa]], w=[tt_r])
                    p.op("dve", lambda e: e.tensor_tensor(out=actT[:, ml, half * 512:(half + 1) * 512], in0=ps[pbb][:],
                                                          in1=tt[:], op=ALU.mult), r=[ps_r[pbb], tt_r], w=[big_r])
        f_new = Fence()
        reqs = [(w2, mg * MGC * 128, MGC, f * 128, 128) for f in range(KC)]
        it = prefetched(reqs)
        for f in range(KC):
            wo, wo_r = next(it)
            for half in range(2):
                hs = slice(half * 512, (half + 1) * 512)
                pb = cnt["ps"] % 6
                cnt["ps"] += 1
                for ml in range(MGC):
                    p.op("pe", lambda e: e.matmul(ps[pb][:], wo[:, ml, :], actT[:, ml, hs], start=(ml == 0), stop=(ml == MGC - 1)),
                         r=[wo_r, big_r], w=[ps_r[pb]])
                xb = cnt["x"] % 3
                cnt["x"] += 1
                xcb, xcb_r = xc[xb]
                src_fence.wait(p, "act")
                p.dma("act", xcb[:], src_res[f * 128:(f + 1) * 128, hs], w=[xcb_r])
                p.op("dve", lambda e: e.tensor_tensor(out=xcb[:], in0=ps[pb][:], in1=xcb[:], op=ALU.add),
                     r=[ps_r[pb], xcb_r], w=[xcb_r])
                dst = hT if mg == MG - 1 else normT
                tok = p.dma("act", dst[f * 128:(f + 1) * 128, hs], xcb[:], r=[xcb_r], is_output=True)
                f_new.add(tok)
        src_res, src_fence = (hT if mg == MG - 1 else normT), f_new
    rmsnorm_T2(p, hT, src_fence, gns, gn_r, ones_f, ones_r, st, out_dram=normT, is_output=True)
    p.finish()
    p.close()
    return p.nc


TWO_PI = 2.0 * math.pi


I32 = mybir.dt.int32


def sincos(p, T, ang, ang_r, s_out, c_out, out_r, tmp, tmp_r, shape_sl, n):
    sl = shape_sl
    tl = (slice(None), slice(0, n))
    if not hasattr(p, "_sc_tmp"):
        p._sc_tmp = (T("sc_ki", [128, 512], I32), T("sc_kf", [128, 512]), T("sc_y", [128, 512]), T("sc_m", [128, 512]))
    (ki, ki_r), (kf, kf_r), (y, y_r), (m, m_r) = p._sc_tmp
    p.op("dve", lambda e: e.tensor_scalar(out=kf[tl], in0=ang[sl], scalar1=1.0 / TWO_PI, scalar2=None, op0=ALU.mult),
         r=[ang_r], w=[kf_r])
    p.op("dve", lambda e: e.tensor_copy(out=ki[tl], in_=kf[tl]), r=[kf_r], w=[ki_r])
    p.op("dve", lambda e: e.tensor_copy(out=kf[tl], in_=ki[tl]), r=[ki_r], w=[kf_r])
    p.op("dve", lambda e: e.scalar_tensor_tensor(out=y[tl], in0=kf[tl], scalar=-TWO_PI, in1=ang[sl], op0=ALU.mult, op1=ALU.add),
         r=[kf_r, ang_r], w=[y_r])

    def fold(v, v_r):
        p.op("dve", lambda e: e.tensor_scalar(out=m[tl], in0=v[tl], scalar1=math.pi, scalar2=None, op0=ALU.is_gt), r=[v_r], w=[m_r])
        p.op("dve", lambda e: e.scalar_tensor_tensor(out=v[tl], in0=m[tl], scalar=-TWO_PI, in1=v[tl], op0=ALU.mult, op1=ALU.add),
             r=[m_r, v_r], w=[v_r])
        p.op("dve", lambda e: e.tensor_scalar(out=m[tl], in0=v[tl], scalar1=-math.pi, scalar2=None, op0=ALU.is_lt), r=[v_r], w=[m_r])
        p.op("dve", lambda e: e.scalar_tensor_tensor(out=v[tl], in0=m[tl], scalar=TWO_PI, in1=v[tl], op0=ALU.mult, op1=ALU.add),
             r=[m_r, v_r], w=[v_r])

    fold(y, y_r)
    p.op("act", lambda e: e.activation(out=s_out[sl], in_=y[tl], func=AF.Sin), r=[y_r], w=[out_r])
    p.op("dve", lambda e: e.tensor_scalar(out=y[tl], in0=y[tl], scalar1=0.5 * math.pi, scalar2=None, op0=ALU.add), r=[y_r], w=[y_r])
    fold(y, y_r)
    p.op("act", lambda e: e.activation(out=c_out[sl], in_=y[tl], func=AF.Sin), r=[y_r], w=[out_r])


def build_s5prep():
    p = Prog()
    A_re = p.dram("A_re", [128, 64], F32, "ExternalInput")
    A_im = p.dram("A_im", [128, 64], F32, "ExternalInput")
    log_dt = p.dram("log_dt", [128, 1], F32, "ExternalInput")
    B_re = p.dram("B_re", [128, 1024], F32, "ExternalInput")
    B_im = p.dram("B_im", [128, 1024], F32, "ExternalInput")
    o_r = p.dram("o_r", [128, 64], F32, "ExternalOutput")
    o_th = p.dram("o_th", [128, 64], F32, "ExternalOutput")
    o_bbre = p.dram("o_bbre", [128, 1024], F32, "ExternalOutput")
    o_bbim = p.dram("o_bbim", [128, 1024], F32, "ExternalOutput")

    def T(name, shape, dt=F32):
        return p.sbuf(name, shape, dt), p.res(name)

    are, are_r = T("are", [128, 64])
    aim, aim_r = T("aim", [128, 64])
    ldt, ldt_r = T("ldt", [128, 1])
    bre, bre_r = T("bre", [128, 64, 16])
    bim, bim_r = T("bim", [128, 64, 16])
    p.dma("sp", are[:], A_re[:], w=[are_r])
    p.dma("sp", aim[:], A_im[:], w=[aim_r])
    p.dma("sp", ldt[:], log_dt[:], w=[ldt_r])
    p.dma("sp", bre[:].rearrange("p n c -> p (n c)"), B_re[:], w=[bre_r])
    p.dma("sp", bim[:].rearrange("p n c -> p (n c)"), B_im[:], w=[bim_r])
    dt, dt_r = T("dt", [128, 1])
    lre, lre_r = T("lre", [128, 64])
    th, th_r = T("th", [128, 64])
    rr, rr_r = T("rr", [128, 64])
    sn, sc_r = T("sn", [128, 64])
    cs, _ = T("cs", [128, 64])
    tmp, tmp_r = T("tmp", [128, 64])
    abre, abre_r = T("abre", [128, 64])
    abim, abim_r = T("abim", [128, 64])
    den, den_r = T("den", [128, 64])
    t2, t2_r = T("t2", [128, 64])
    cfre, cfre_r = T("cfre", [128, 64])
    cfim, cfim_r = T("cfim", [128, 64])
    obr, obr_r = T("obr", [128, 64, 16])
    obi, obi_r = T("obi", [128, 64, 16])
    t3, t3_r = T("t3", [128, 64, 16])
    p.op("act", lambda e: e.activation(out=dt[:], in_=ldt[:], func=AF.Exp), r=[ldt_r], w=[dt_r])
    p.op("dve", lambda e: e.tensor_scalar(out=lre[:], in0=are[:], scalar1=dt[:, 0:1], scalar2=None, op0=ALU.mult),
         r=[are_r, dt_r], w=[lre_r])
    p.op("dve", lambda e: e.tensor_scalar(out=th[:], in0=aim[:], scalar1=dt[:, 0:1], scalar2=None, op0=ALU.mult),
         r=[aim_r, dt_r], w=[th_r])
    p.op("act", lambda e: e.activation(out=rr[:], in_=lre[:], func=AF.Exp), r=[lre_r], w=[rr_r])
    sl = (slice(None), slice(None))
    sincos(p, T, th, th_r, sn, cs, sc_r, tmp, tmp_r, sl, 64)
    p.op("dve", lambda e: e.tensor_tensor(out=abre[:], in0=rr[:], in1=cs[:], op=ALU.mult), r=[rr_r, sc_r], w=[abre_r])
    p.op("dve", lambda e: e.tensor_tensor(out=abim[:], in0=rr[:], in1=sn[:], op=ALU.mult), r=[rr_r, sc_r], w=[abim_r])
    p.op("dve", lambda e: e.tensor_scalar(out=abre[:], in0=abre[:], scalar1=-1.0, scalar2=None, op0=ALU.add), r=[abre_r], w=[abre_r])
    p.op("dve", lambda e: e.tensor_tensor(out=den[:], in0=are[:], in1=are[:], op=ALU.mult), r=[are_r], w=[den_r])
    p.op("dve", lambda e: e.tensor_tensor(out=t2[:], in0=aim[:], in1=aim[:], op=ALU.mult), r=[aim_r], w=[t2_r])
    p.op("dve", lambda e: e.tensor_tensor(out=den[:], in0=den[:], in1=t2[:], op=ALU.add), r=[den_r, t2_r], w=[den_r])
    p.op("dve", lambda e: e.reciprocal(out=den[:], in_=den[:]), r=[den_r], w=[den_r])
    p.op("dve", lambda e: e.tensor_tensor(out=cfre[:], in0=abre[:], in1=are[:], op=ALU.mult), r=[abre_r, are_r], w=[cfre_r])
    p.op("dve", lambda e: e.tensor_tensor(out=t2[:], in0=abim[:], in1=aim[:], op=ALU.mult), r=[abim_r, aim_r], w=[t2_r])
    p.op("dve", lambda e: e.tensor_tensor(out=cfre[:], in0=cfre[:], in1=t2[:], op=ALU.add), r=[cfre_r, t2_r], w=[cfre_r])
    p.op("dve", lambda e: e.tensor_tensor(out=cfre[:], in0=cfre[:], in1=den[:], op=ALU.mult), r=[cfre_r, den_r], w=[cfre_r])
    p.op("dve", lambda e: e.tensor_tensor(out=cfim[:], in0=abim[:], in1=are[:], op=ALU.mult), r=[abim_r, are_r], w=[cfim_r])
    p.op("dve", lambda e: e.tensor_tensor(out=t2[:], in0=abre[:], in1=aim[:], op=ALU.mult), r=[abre_r, aim_r], w=[t2_r])
    p.op("dve", lambda e: e.tensor_tensor(out=cfim[:], in0=cfim[:], in1=t2[:], op=ALU.subtract), r=[cfim_r, t2_r], w=[cfim_r])
    p.op("dve", lambda e: e.tensor_tensor(out=cfim[:], in0=cfim[:], in1=den[:], op=ALU.mult), r=[cfim_r, den_r], w=[cfim_r])
    cre_b = cfre[:].unsqueeze(2).to_broadcast([128, 64, 16])
    cim_b = cfim[:].unsqueeze(2).to_broadcast([128, 64, 16])
    p.op("dve", lambda e: e.tensor_tensor(out=obr[:], in0=bre[:], in1=cre_b, op=ALU.mult), r=[bre_r, cfre_r], w=[obr_r])
    p.op("dve", lambda e: e.tensor_tensor(out=t3[:], in0=bim[:], in1=cim_b, op=ALU.mult), r=[bim_r, cfim_r], w=[t3_r])
    p.op("dve", lambda e: e.tensor_tensor(out=obr[:], in0=obr[:], in1=t3[:], op=ALU.subtract), r=[obr_r, t3_r], w=[obr_r])
    p.op("dve", lambda e: e.tensor_tensor(out=obi[:], in0=bim[:], in1=cre_b, op=ALU.mult), r=[bim_r, cfre_r], w=[obi_r])
    p.op("dve", lambda e: e.tensor_tensor(out=t3[:], in0=bre[:], in1=cim_b, op=ALU.mult), r=[bre_r, cfim_r, obr_r], w=[t3_r])
    p.op("dve", lambda e: e.tensor_tensor(out=obi[:], in0=obi[:], in1=t3[:], op=ALU.add), r=[obi_r, t3_r], w=[obi_r])
    p.dma("sp", o_r[:], rr[:], r=[rr_r], is_output=True)
    p.dma("sp", o_th[:], th[:], r=[th_r], is_output=True)
    p.dma("sp", o_bbre[:], obr[:].rearrange("p n c -> p (n c)"), r=[obr_r], is_output=True)
    p.dma("sp", o_bbim[:], obi[:].rearrange("p n c -> p (n c)"), r=[obi_r], is_output=True)
    p.finish()
    p.close()
    return p.nc


NPAIR = 8
NB = 16


def build_s5main(nblocks=NB):
    p = Prog()
    uT = p.dram("uT", [256, 8192], F32, "ExternalInput")
    BD = p.dram("BD", [128, NPAIR * 2 * 128], F32, "ExternalInput")
    CT = p.dram("CT", [128, NPAIR * 2 * 128], F32, "ExternalInput")
    rcol = p.dram("rcol", [128, NPAIR], F32, "ExternalInput")
    thcol = p.dram("thcol", [128, NPAIR], F32, "ExternalInput")
    iota = p.dram("iota", [128, 512], F32, "ExternalInput")
    Dcol = p.dram("Dcol", [128, 2], F32, "ExternalInput")
    yT = p.dram("yT", [256, 8192], BF16, "ExternalOutput")

    def T(name, shape, dt=F32):
        return p.sbuf(name, shape, dt), p.res(name)

    bd, bd_r = T("bd", [128, NPAIR, 2, 128])
    ct, ct_r = T("ct", [128, NPAIR, 2, 128])
    rc, rc_r = T("rc", [128, NPAIR])
    thc, thc_r = T("thc", [128, NPAIR])
    io, io_r = T("io", [128, 512])
    dc, dc_r = T("dc", [128, 2])
    p.dma("sp", bd[:].rearrange("p a b c -> p (a b c)"), BD[:], w=[bd_r])
    p.dma("sp", ct[:].rearrange("p a b c -> p (a b c)"), CT[:], w=[ct_r])
    p.dma("sp", rc[:], rcol[:], w=[rc_r])
    p.dma("sp", thc[:], thcol[:], w=[thc_r])
    p.dma("sp", io[:], iota[:], w=[io_r])
    p.dma("sp", dc[:], Dcol[:], w=[dc_r])
    cosT, tab_r = T("cosT", [128, NPAIR, 512])
    sinT, _ = T("sinT", [128, NPAIR, 512])
    nsinT, _ = T("nsinT", [128, NPAIR, 512])
    c512, c512_r = T("c512", [128, NPAIR])
    s512, _ = T("s512", [128, NPAIR])
    ang, ang_r = T("ang", [128, 512])
    tmp, tmp_r = T("tmp", [128, 512])
    for k in range(NPAIR):
        p.op("dve", lambda e: e.tensor_scalar(out=ang[:], in0=io[:], scalar1=thc[:, k:k + 1], scalar2=None, op0=ALU.mult),
             r=[io_r, thc_r], w=[ang_r])
        sincos(p, T, ang, ang_r, sinT[:, k, :], cosT[:, k, :], tab_r, tmp, tmp_r, (slice(None), slice(None)), 512)
        p.op("dve", lambda e: e.tensor_scalar(out=nsinT[:, k, :], in0=sinT[:, k, :], scalar1=-1.0, scalar2=None, op0=ALU.mult),
             r=[tab_r], w=[tab_r])
    p.op("dve", lambda e: e.tensor_scalar(out=ang[:, 0:NPAIR], in0=thc[:], scalar1=512.0, scalar2=None, op0=ALU.mult),
         r=[thc_r], w=[ang_r])
    sincos(p, T, ang, ang_r, s512, c512, c512_r, tmp, tmp_r, (slice(None), slice(0, NPAIR)), NPAIR)

    vin_re, vin_r = T("vin_re", [128, NPAIR])
    vin_im, _ = T("vin_im", [128, NPAIR])
    p.op("dve", lambda e: e.memset(vin_re[:], 0.0), w=[vin_r])
    p.op("dve", lambda e: e.memset(vin_im[:], 0.0), w=[vin_r])
    ub = [T("ub", [128, 2, 512]) for _ in range(2)]
    psB = [p.psum("psB", [128, 512]) for _ in range(4)]
    psB_r = [p.res("psB") for _ in range(4)]
    psY = [p.psum("psY", [128, 512]) for _ in range(2)]
    psY_r = [p.res("psY") for _ in range(2)]
    wre = [T("wre", [128, 512]) for _ in range(2)]
    wim = [T("wim", [128, 512]) for _ in range(2)]
    t1 = [T("t1", [128, 512]) for _ in range(2)]
    vre = [T("vre", [128, 512]) for _ in range(2)]
    vim = [T("vim", [128, 512]) for _ in range(2)]
    xre = [T("xre", [128, 512]) for _ in range(2)]
    xim = [T("xim", [128, 512]) for _ in range(2)]
    t2 = [T("t2", [128, 512]) for _ in range(2)]
    yg, yg_r = T("yg", [128, 512])
    tg, tg_r = T("tg", [128, 512])
    yo = [T("yo", [128, 512], BF16) for _ in range(2)]
    sm, sm_r = T("sm", [128, 4])
    cnt = 0
    for b in range(nblocks):
        u, u_r = ub[b % 2]
        p.dma("sp", u[:], uT[:, b * 512:(b + 1) * 512].rearrange("(k p) t -> p k t", p=128), w=[u_r])
        for k in range(NPAIR):
            kc = k // 4
            i2 = cnt % 2
            cnt += 1
            pre, pre_r = psB[2 * i2], psB_r[2 * i2]
            pim, pim_r = psB[2 * i2 + 1], psB_r[2 * i2 + 1]
            p.op("pe", lambda e: e.matmul(pre[:], bd[:, k, 0, :], u[:, kc, :], start=True, stop=True), r=[bd_r, u_r], w=[pre_r])
            p.op("pe", lambda e: e.matmul(pim[:], bd[:, k, 1, :], u[:, kc, :], start=True, stop=True), r=[bd_r, u_r], w=[pim_r])
            (wr, wr_r), (wi, wi_r), (ta, ta_r) = wre[i2], wim[i2], t1[i2]
            (vr, vr_r), (vi, vi_r) = vre[i2], vim[i2]
            (xr, xr_r), (xi, xi_r), (tb, tb_r) = xre[i2], xim[i2], t2[i2]
            cT, sT, nsT = cosT[:, k, :], sinT[:, k, :], nsinT[:, k, :]
            p.op("dve", lambda e: e.tensor_tensor(out=wr[:], in0=pre[:], in1=cT, op=ALU.mult), r=[pre_r, tab_r], w=[wr_r])
            p.op("dve", lambda e: e.tensor_tensor(out=ta[:], in0=pim[:], in1=sT, op=ALU.mult), r=[pim_r, tab_r], w=[ta_r])
            p.op("dve", lambda e: e.tensor_tensor(out=wr[:], in0=wr[:], in1=ta[:], op=ALU.add), r=[wr_r, ta_r], w=[wr_r])
            p.op("dve", lambda e: e.tensor_tensor(out=wi[:], in0=pim[:], in1=cT, op=ALU.mult), r=[pim_r, tab_r], w=[wi_r])
            p.op("dve", lambda e: e.tensor_tensor(out=ta[:], in0=pre[:], in1=sT, op=ALU.mult), r=[pre_r, tab_r, wr_r], w=[ta_r])
            p.op("dve", lambda e: e.tensor_tensor(out=wi[:], in0=wi[:], in1=ta[:], op=ALU.subtract), r=[wi_r, ta_r], w=[wi_r])
            rb = rc[:, k:k + 1].to_broadcast([128, 512])
            p.op("dve", lambda e: e.tensor_tensor_scan(out=vr[:], data0=rb, data1=wr[:], initial=vin_re[:, k:k + 1],
                                                       op0=ALU.mult, op1=ALU.add), r=[wr_r, rc_r, vin_r], w=[vr_r])
            p.op("dve", lambda e: e.tensor_tensor_scan(out=vi[:], data0=rb, data1=wi[:], initial=vin_im[:, k:k + 1],
                                                       op0=ALU.mult, op1=ALU.add), r=[wi_r, rc_r, vin_r], w=[vi_r])
            p.op("dve", lambda e: e.tensor_tensor(out=sm[:, 0:1], in0=vr[:, 511:512], in1=c512[:, k:k + 1], op=ALU.mult),
                 r=[vr_r, c512_r], w=[sm_r])
            p.op("dve", lambda e: e.tensor_tensor(out=sm[:, 1:2], in0=vi[:, 511:512], in1=s512[:, k:k + 1], op=ALU.mult),
                 r=[vi_r, c512_r], w=[sm_r])
            p.op("dve", lambda e: e.tensor_tensor(out=sm[:, 2:3], in0=vr[:, 511:512], in1=s512[:, k:k + 1], op=ALU.mult),
                 r=[vr_r, c512_r], w=[sm_r])
            p.op("dve", lambda e: e.tensor_tensor(out=sm[:, 3:4], in0=vi[:, 511:512], in1=c512[:, k:k + 1], op=ALU.mult),
                 r=[vi_r, c512_r], w=[sm_r])
            p.op("dve", lambda e: e.tensor_tensor(out=vin_re[:, k:k + 1], in0=sm[:, 0:1], in1=sm[:, 1:2], op=ALU.subtract),
                 r=[sm_r], w=[vin_r])
            p.op("dve", lambda e: e.tensor_tensor(out=vin_im[:, k:k + 1], in0=sm[:, 2:3], in1=sm[:, 3:4], op=ALU.add),
                 r=[sm_r], w=[vin_r])
            p.op("dve", lambda e: e.tensor_tensor(out=xr[:], in0=vr[:], in1=cT, op=ALU.mult), r=[vr_r, tab_r], w=[xr_r])
            p.op("dve", lambda e: e.tensor_tensor(out=tb[:], in0=vi[:], in1=sT, op=ALU.mult), r=[vi_r, tab_r], w=[tb_r])
            p.op("dve", lambda e: e.tensor_tensor(out=xr[:], in0=xr[:], in1=tb[:], op=ALU.subtract), r=[xr_r, tb_r], w=[xr_r])
            p.op("dve", lambda e: e.tensor_tensor(out=xi[:], in0=vr[:], in1=nsT, op=ALU.mult), r=[vr_r, tab_r], w=[xi_r])
            p.op("dve", lambda e: e.tensor_tensor(out=tb[:], in0=vi[:], in1=cT, op=ALU.mult), r=[vi_r, tab_r, xr_r], w=[tb_r])
            p.op("dve", lambda e: e.tensor_tensor(out=xi[:], in0=xi[:], in1=tb[:], op=ALU.subtract), r=[xi_r, tb_r], w=[xi_r])
            py, py_r = psY[kc], psY_r[kc]
            kk = k % 4
            p.op("pe", lambda e: e.matmul(py[:], ct[:, k, 0, :], xr[:], start=(kk == 0), stop=False), r=[ct_r, xr_r], w=[py_r])
            p.op("pe", lambda e: e.matmul(py[:], ct[:, k, 1, :], xi[:], start=False, stop=(kk == 3)), r=[ct_r, xi_r], w=[py_r])
            if kk == 3:
                yob, yob_r = yo[kc]
                p.op("dve", lambda e: e.scalar_tensor_tensor(out=yg[:], in0=u[:, kc, :], scalar=dc[:, kc:kc + 1], in1=py[:],
                                                             op0=ALU.mult, op1=ALU.add), r=[u_r, dc_r, py_r], w=[yg_r])
                p.op("pool", lambda e: e.tensor_tensor(out=tg[:], in0=yg[:], in1=yg[:], op=ALU.mult), r=[yg_r], w=[tg_r])
                p.op("pool", lambda e: e.tensor_scalar(out=tg[:], in0=tg[:], scalar1=0.044715, scalar2=1.0, op0=ALU.mult, op1=ALU.add),
                     r=[tg_r], w=[tg_r])
                p.op("pool", lambda e: e.tensor_tensor(out=tg[:], in0=tg[:], in1=yg[:], op=ALU.mult), r=[tg_r, yg_r], w=[tg_r])
                p.op("act", lambda e: e.activation(out=tg[:], in_=tg[:], func=AF.Sigmoid, scale=1.5957691216), r=[tg_r], w=[tg_r])
                p.op("pool", lambda e: e.tensor_tensor(out=yob[:], in0=tg[:], in1=yg[:], op=ALU.mult), r=[tg_r, yg_r], w=[yob_r])
                p.dma("sp", yT[kc * 128:(kc + 1) * 128, b * 512:(b + 1) * 512], yob[:], r=[yob_r], is_output=True)
    p.finish()
    p.close()
    return p.nc


def _run(nc, in_maps):
    res = run_bass_kernel_spmd(nc, in_maps, core_ids=list(range(8)))
    return res.results


def kernel(**inp):
    inp = {k: np.asarray(v) for k, v in inp.items()}
    x = inp["x"][0]
    idx = [own_idx(c) for c in range(8)]
    xT = [np.ascontiguousarray(x[idx[c]].T) for c in range(8)]
    w_in = np.ascontiguousarray(inp["nsa_w_in"][0])
    g0 = gcol(inp["mix_norm_g"][0])
    l1 = _run(build_p1(), [{"xT": xT[c], "gcol": g0, "w_in": w_in} for c in range(8)])
    shared = p2_shared(inp, l1)
    l2 = _run(build_p2(), [p2_inputs(inp, l1, c, shared) for c in range(8)])
    del shared
    m3 = [{"resT": xT[c], "aT": np.ascontiguousarray(np.asarray(l2[c]["o_out"]).T),
           "wmix": np.ascontiguousarray(inp["nsa_w_out"][0]), "gff": gcol(inp["ffn_norm_g"][0]),
           "gn": gcol(inp["mix_norm_g"][1]), "w1": np.ascontiguousarray(inp["ffn_w_in"][0]),
           "w2": np.ascontiguousarray(inp["ffn_w_out"][0])} for c in range(8)]
    l3 = _run(build_tok(False, False), m3)
    del m3
    prep = _run(build_s5prep(), [s5prep_inputs(inp) for _ in range(8)])[0]
    uT_full = np.zeros((2048, 8192), np.float32)
    for c in range(8):
        uT_full[:, idx[c]] = np.asarray(l3[c]["normT"])
    l4 = _run(build_s5main(), [s5main_inputs(inp, prep, uT_full, c) for c in range(8)])
    y_full = np.concatenate([np.asarray(l4[c]["yT"]) for c in range(8)], axis=0)
    m5 = [{"resT": np.ascontiguousarray(np.asarray(l3[c]["hT"])), "aT": np.ascontiguousarray(y_full[:, idx[c]]),
           "wmix": np.ascontiguousarray(inp["s5_w_glu"][0]), "gff": gcol(inp["ffn_norm_g"][1]),
           "gn": gcol(inp["final_norm_g"]), "w1": np.ascontiguousarray(inp["ffn_w_in"][1]),
           "w2": np.ascontiguousarray(inp["ffn_w_out"][1])} for c in range(8)]
    l5 = _run(build_tok(True, True), m5)
    out = np.zeros((1, 8192, 2048), np.float32)
    for c in range(8):
        out[0, idx[c]] = np.asarray(l5[c]["normT"]).T
    return out
```

```python
import math
from contextlib import ExitStack
import numpy as np
import ml_dtypes
import concourse.bass as bass
import concourse.mybir as mybir
from concourse.bass_utils import run_bass_kernel_spmd


F32 = mybir.dt.float32
BF16 = mybir.dt.bfloat16
AF = mybir.ActivationFunctionType
ALU = mybir.AluOpType
AX = mybir.AxisListType


NO_SELF_WAIT = ()


class Res:
    __slots__ = ("name", "last_w", "readers", "dsem", "dcnt")

    def __init__(self, name):
        self.name = name
        self.last_w = None
        self.readers = []
        self.dsem = None
        self.dcnt = 0


class Prog:
    def __init__(self, num_devices=None):
        self.nc = bass.Bass("TRN2", target_bir_lowering=False, num_devices=num_devices)
        self.es = ExitStack()
        nc = self.nc
        self.eng = {"pe": nc.tensor, "act": nc.scalar, "dve": nc.vector, "pool": nc.gpsimd, "sp": nc.sync}
        self.esem = {}
        self.ecnt = {}
        self.waited = {e: {} for e in self.eng}
        self.semid = {}
        for e in self.eng:
            s = self.es.enter_context(nc.semaphore("es_" + e))
            self.esem[e] = s
            self.ecnt[e] = 0
        self.nsem = len(self.eng)
        self.out_tokens = []
        self.no_self_wait = set(NO_SELF_WAIT)
        self.dma_toks = {}
        self._n = 0
        self._stacks = [self.es]

    def uname(self, base):
        self._n += 1
        return f"{base}_{self._n}"

    def sbuf(self, name, shape, dt):
        return self._stacks[-1].enter_context(self.nc.sbuf_tensor(self.uname(name), list(shape), dt))

    def push_scope(self):
        st = ExitStack()
        self._stacks.append(st)
        return st

    def pop_scope(self):
        self.barrier()
        st = self._stacks.pop()
        st.close()

    def barrier(self):
        for e in self.eng:
            for e2 in self.eng:
                if e2 != e and self.ecnt[e2] > 0:
                    self._wait(e, (self.esem[e2], self.ecnt[e2]))
            for tok in self.dma_toks.values():
                self._wait(e, tok)

    def psum(self, name, shape, dt=F32):
        return self.es.enter_context(self.nc.psum_tensor(self.uname(name), list(shape), dt))

    def dram(self, name, shape, dt, kind):
        return self.nc.dram_tensor(name, list(shape), dt, kind=kind).ap()

    def res(self, name="r"):
        return Res(name)

    def _wait(self, e, tok):
        if tok is None:
            return
        sem, val = tok
        if e == "pe" and sem is self.esem["pe"]:
            return
        if e in self.no_self_wait and sem is self.esem[e]:
            return
        k = id(sem)
        w = self.waited[e]
        if w.get(k, 0) >= val:
            return
        self.eng[e].wait_ge(sem, val)
        w[k] = val

    def _deps(self, e, r, w):
        for x in r:
            self._wait(e, x.last_w)
        for x in w:
            self._wait(e, x.last_w)
            for t in x.readers:
                self._wait(e, t)

    def _commit(self, tok, r, w):
        for x in r:
            x.readers.append(tok)
            if len(x.readers) > 64:
                best = {}
                for s, v in x.readers:
                    if id(s) not in best or best[id(s)][1] < v:
                        best[id(s)] = (s, v)
                x.readers = list(best.values())
        for x in w:
            x.last_w = tok
            x.readers = []

    def op(self, e, fn, r=(), w=()):
        self._deps(e, r, w)
        inst = fn(self.eng[e])
        inst.then_inc(self.esem[e], 1)
        self.ecnt[e] += 1
        tok = (self.esem[e], self.ecnt[e])
        self._commit(tok, r, w)
        return tok

    def dma(self, q, out, in_, r=(), w=(), sres=None, is_output=False, **kw):
        self._deps(q, r, w)
        if sres is None:
            sres = (list(w) + list(r))[0]
        if sres.dsem is None:
            sres.dsem = self.es.enter_context(self.nc.semaphore(self.uname("ds")))
            self.nsem += 1
        inst = self.eng[q].dma_start(out=out, in_=in_, **kw)
        inst.then_inc(sres.dsem, 16)
        sres.dcnt += 16
        tok = (sres.dsem, sres.dcnt)
        self.dma_toks[id(sres.dsem)] = tok
        self._commit(tok, r, w)
        if is_output:
            self.out_tokens.append(tok)
        return tok

    def finish(self):
        best = {}
        for s, v in self.out_tokens:
            if id(s) not in best or best[id(s)][1] < v:
                best[id(s)] = (s, v)
        for s, v in best.values():
            self.eng["sp"].wait_ge(s, v)
        return self.nc

    def close(self):
        self.es.close()


BF = ml_dtypes.bfloat16
NEG = -30000.0


def own_idx(c):
    return np.concatenate([np.arange((8 * i + c) * 128, (8 * i + c) * 128 + 128) for i in range(8)])


def gcol(g):
    return np.ascontiguousarray(np.asarray(g, np.float32).reshape(16, 128).T)


def rel_bucket_np(dist):
    n = np.maximum(dist, 0).astype(np.int64)
    max_exact = 16
    nf = np.maximum(n, 1).astype(np.float32)
    large = max_exact + (np.log(nf / np.float32(max_exact)) / np.float32(np.log(128 / 16)) * np.float32(16)).astype(np.int32)
    large = np.minimum(large, 31)
    return np.where(n < max_exact, n, large).astype(np.int64)


def p2_tables(rel_bias, c):
    rel_bias = np.asarray(rel_bias, np.float32)
    kl = np.arange(128)[:, None]
    ql = np.arange(128)[None, :]
    tb_bias = np.zeros((9, 128, 16, 128), np.float32)
    tb_mask = np.zeros((128, 9, 128), np.float32)
    for jj in range(9):
        j = jj - 1
        r = c - j
        dist = r * 128 + ql - kl
        b = rel_bias[rel_bucket_np(dist)]
        tb_bias[jj] = b.transpose(0, 2, 1)
        tb_mask[:, jj, :] = np.where(dist < 0, NEG, 0.0)
    win_mask = np.zeros((128, 12, 128), np.float32)
    for jp in range(12):
        r = c + 4 - jp
        dist = r * 128 + ql - kl
        win_mask[:, jp, :] = np.where((dist < 0) | (dist >= 512), NEG, 0.0)
    qcol = np.arange(128)[:, None]
    e = np.arange(72)[None, :]
    dist_c = 128 * c + qcol - 16 * (e - 9) - 31
    ec_bias = rel_bias[rel_bucket_np(dist_c)].transpose(0, 2, 1)
    ec_mask = np.where(dist_c < 0, NEG, 0.0).astype(np.float32)
    fq = np.zeros((128, 8, 128), np.float32)
    blk = np.arange(128)[None, :]
    for i in range(8):
        t = (8 * i + c) * 128 + np.arange(128)[:, None]
        tb = t // 64
        f = np.zeros((128, 128), np.float32)
        f = np.where(blk > tb, -1e30, f)
        f = np.where(blk == tb - 1, 1e9, f)
        f = np.where(blk == tb, 2e9, f)
        f = np.where(blk == 0, 3e9, f)
        fq[:, i, :] = f
    return {
        "tb_bias": np.ascontiguousarray(tb_bias.reshape(9, 128, 2048)),
        "tb_mask": np.ascontiguousarray(tb_mask.reshape(128, 9 * 128)),
        "b31": np.ascontiguousarray(np.broadcast_to(rel_bias[31][None, :], (128, 16))),
        "win_mask": np.ascontiguousarray(win_mask.reshape(128, 12 * 128)),
        "ec_bias": np.ascontiguousarray(ec_bias.reshape(128, 16 * 72)),
        "ec_mask": ec_mask,
        "fq": np.ascontiguousarray(fq.reshape(128, 8 * 128)),
    }


def p2_consts():
    key = np.arange(8192)[None, :]
    j = np.arange(128)[:, None]
    R = (key // 64 == j).astype(np.float32).astype(BF)
    ident = np.eye(128, dtype=np.float32).astype(BF)
    return {"Rm": R, "ident": ident}


def p2_inputs(inp, l1, c, shared):
    r = l1[c]
    qT = np.asarray(r["qT"])
    q_l = qT.reshape(4, 4, 128, 8, 128).transpose(3, 0, 2, 1, 4).reshape(8, 4, 128, 512)
    gates = np.asarray(r["gates"]).reshape(8, 128, 48).transpose(1, 0, 2).reshape(128, 8 * 48)
    d = {"q_l": np.ascontiguousarray(q_l), "gates_l": np.ascontiguousarray(gates)}
    d.update(shared)
    d.update(p2_tables(inp["rel_bias"], c))
    return d


def p2_shared(inp, l1):
    sh = {}
    for nm in ("kcT", "vcT", "ksT", "kwT"):
        full = np.zeros((512, 8192), BF)
        for c in range(8):
            full[:, own_idx(c)] = np.asarray(l1[c][nm])
        sh[nm] = np.ascontiguousarray(full.reshape(4, 128, 8192))
    for nm, out in (("vs", "vs1"), ("vw", "vw1")):
        full = np.zeros((8192, 512), BF)
        for c in range(8):
            full[own_idx(c)] = np.asarray(l1[c][nm])
        v = full.reshape(64, 128, 4, 128).transpose(2, 1, 0, 3)
        v1 = np.ones((4, 128, 64, 129), BF)
        v1[..., :128] = v
        sh[out] = np.ascontiguousarray(v1.reshape(4, 128, 64 * 129))
    for kv in ("k", "v"):
        w1 = np.asarray(inp[f"cmp_w1_{kv}"][0], np.float32)
        sh[f"w1{kv}"] = np.ascontiguousarray(w1.reshape(32, 128, 256).transpose(1, 0, 2).reshape(128, 32 * 256))
        w2 = np.asarray(inp[f"cmp_w2_{kv}"][0], np.float32)
        sh[f"w2{kv}"] = np.ascontiguousarray(w2.reshape(2, 128, 128).transpose(1, 0, 2).reshape(128, 256))
        sh[f"posT{kv}"] = np.ascontiguousarray(np.asarray(inp[f"cmp_pos_{kv}"][0], np.float32).T)
    sh.update(p2_consts())
    return sh


def s5prep_inputs(inp):
    return {"A_re": np.ascontiguousarray(inp["s5_A_re"][0]), "A_im": np.ascontiguousarray(inp["s5_A_im"][0]),
            "log_dt": np.ascontiguousarray(inp["s5_log_dt"][0].reshape(128, 1)),
            "B_re": np.ascontiguousarray(inp["s5_B_re"][0].reshape(128, 1024)),
            "B_im": np.ascontiguousarray(inp["s5_B_im"][0].reshape(128, 1024))}


def s5main_inputs(inp, prep, uT_full, c):
    r = np.asarray(prep["o_r"]); th = np.asarray(prep["o_th"])
    bbre = np.asarray(prep["o_bbre"]).reshape(128, 64, 16); bbim = np.asarray(prep["o_bbim"]).reshape(128, 64, 16)
    C_re = np.asarray(inp["s5_C_re"][0]); C_im = np.asarray(inp["s5_C_im"][0])
    D = np.asarray(inp["s5_D"][0])
    BD = np.zeros((128, 8, 2, 128), np.float32)
    CT = np.zeros((128, 8, 2, 128), np.float32)
    rcol = np.zeros((128, 8), np.float32); thcol = np.zeros((128, 8), np.float32)
    for k in range(8):
        kg = 8 * c + k
        for gg in range(2):
            g = 2 * kg + gg
            ch0 = 32 * (k % 4) + 16 * gg
            st0 = 64 * gg
            BD[ch0:ch0 + 16, k, 0, st0:st0 + 64] = bbre[g].T
            BD[ch0:ch0 + 16, k, 1, st0:st0 + 64] = bbim[g].T
            CT[st0:st0 + 64, k, 0, ch0:ch0 + 16] = C_re[g].T
            CT[st0:st0 + 64, k, 1, ch0:ch0 + 16] = C_im[g].T
            rcol[st0:st0 + 64, k] = r[g]
            thcol[st0:st0 + 64, k] = th[g]
    Dcol = np.ascontiguousarray(D[256 * c:256 * c + 256].reshape(2, 128).T)
    iota = np.ascontiguousarray(np.broadcast_to(np.arange(512, dtype=np.float32)[None, :], (128, 512)))
    return {"uT": np.ascontiguousarray(uT_full[256 * c:256 * c + 256]), "BD": BD.reshape(128, -1), "CT": CT.reshape(128, -1),
            "rcol": rcol, "thcol": thcol, "iota": iota, "Dcol": Dcol, "identf": np.eye(128, dtype=np.float32)}


D = 2048
NT = 1024
KC = 16
EPS = 1e-6


def rmsnorm_T(p, xT_dram, gcol_sb, gcol_res, hnT, hnT_res, ones_f, ones_res, out_dt_note=""):
    nc = p.nc
    xs = p.sbuf("xs", [128, KC, 512], F32)
    xs_r = p.res("xs")
    sq = [p.sbuf("sq", [128, 512], F32) for _ in range(2)]
    sq_r = [p.res("sq") for _ in range(2)]
    ss = p.psum("ss", [128, 512])
    ss_r = p.res("ss")
    rstd = p.sbuf("rstd", [128, 512], F32)
    rstd_r = p.res("rstd")
    for half in range(NT // 512):
        src = xT_dram[:, half * 512:(half + 1) * 512].rearrange("(k p) t -> p k t", p=128)
        for kk in range(0, KC, 4):
            p.dma("sp", xs[:, kk:kk + 4, :], src[:, kk:kk + 4, :], w=[xs_r])
        for k in range(KC):
            b = k % 2
            p.op("act", lambda e: e.activation(out=sq[b][:], in_=xs[:, k, :], func=AF.Square),
                 r=[xs_r], w=[sq_r[b]])
            p.op("pe", lambda e: e.matmul(ss[:], ones_f[:], sq[b][:], start=(k == 0), stop=(k == KC - 1)),
                 r=[sq_r[b], ones_res], w=[ss_r])
        p.op("dve", lambda e: e.tensor_scalar(out=rstd[:], in0=ss[:], scalar1=1.0 / D, scalar2=EPS,
                                              op0=ALU.mult, op1=ALU.add), r=[ss_r], w=[rstd_r])
        p.op("act", lambda e: e.activation(out=rstd[:], in_=rstd[:], func=AF.Sqrt), r=[rstd_r], w=[rstd_r])
        p.op("dve", lambda e: e.reciprocal(out=rstd[:], in_=rstd[:]), r=[rstd_r], w=[rstd_r])
        for k in range(KC):
            p.op("dve", lambda e: e.scalar_tensor_tensor(
                out=hnT[:, k, half * 512:(half + 1) * 512], in0=xs[:, k, :], scalar=gcol_sb[:, k:k + 1],
                in1=rstd[:], op0=ALU.mult, op1=ALU.mult), r=[xs_r, rstd_r, gcol_res], w=[hnT_res])


def build_p1(only=None):
    p = Prog()
    nc = p.nc
    xT = p.dram("xT", [D, NT], F32, "ExternalInput")
    gcol = p.dram("gcol", [128, KC], F32, "ExternalInput")
    w_in = p.dram("w_in", [D, 5168], F32, "ExternalInput")
    qT = p.dram("qT", [2048, NT], BF16, "ExternalOutput")
    kT = {n: p.dram(n, [512, NT], BF16, "ExternalOutput") for n in ("kcT", "vcT", "ksT", "kwT")}
    vv = {n: p.dram(n, [NT, 512], BF16, "ExternalOutput") for n in ("vs", "vw")}
    gates = p.dram("gates", [NT, 48], F32, "ExternalOutput")

    ones_f = p.sbuf("ones", [128, 128], F32)
    ones_r = p.res("ones")
    p.op("pool", lambda e: e.memset(ones_f[:], 1.0), w=[ones_r])
    gsb = p.sbuf("gsb", [128, KC], F32)
    g_r = p.res("g")
    p.dma("sp", gsb[:], gcol[:], w=[g_r])
    hnT = p.sbuf("hnT", [128, KC, NT], BF16)
    hnT_r = p.res("hnT")
    rmsnorm_T(p, xT, gsb, g_r, hnT, hnT_r, ones_f, ones_r)

    wst = [p.sbuf("wst", [128, KC, 512], F32) for _ in range(2)]
    wst_r = [p.res("wst") for _ in range(2)]
    wb = [p.sbuf("wb", [128, KC, 512], BF16) for _ in range(2)]
    wb_r = [p.res("wb") for _ in range(2)]
    ps = [p.psum("ps", [128, 512]) for _ in range(4)]
    ps_r = [p.res("ps") for _ in range(4)]
    ot = [p.sbuf("ot", [128, 512], BF16) for _ in range(4)]
    ot_r = [p.res("ot") for _ in range(4)]
    og = p.sbuf("og", [128, 48], F32)
    og_r = p.res("og")
    chunks = [("qT", 0), ("qT", 1), ("qT", 2), ("qT", 3), ("kcT", 0), ("vcT", 0), ("ksT", 0), ("vs", 0),
              ("kwT", 0), ("vw", 0), ("gates", 0)]
    pi = 0
    for ci, (nm, sub) in enumerate(chunks):
        if only is not None and ci not in only:
            continue
        c0 = ci * 512
        ncol = 512 if nm != "gates" else 48
        b = ci % 2
        src = w_in[:, c0:c0 + ncol].rearrange("(k p) c -> p k c", p=128)
        for kk in range(0, KC, 4):
            p.dma("sp", wst[b][:, kk:kk + 4, :ncol], src[:, kk:kk + 4, :], w=[wst_r[b]])
        for kk in range(0, KC, 8):
            p.op("dve", lambda e: e.tensor_copy(out=wb[b][:, kk:kk + 8, :ncol], in_=wst[b][:, kk:kk + 8, :ncol]),
                 r=[wst_r[b]], w=[wb_r[b]])
        if nm in ("qT", "kcT", "vcT", "ksT", "kwT"):
            dst = qT if nm == "qT" else kT[nm]
            scale = 128 ** -0.5 if nm == "qT" else 1.0
            for m in range(4):
                for n in range(NT // 512):
                    pb = pi % 4
                    pi += 1
                    for k in range(KC):
                        p.op("pe", lambda e: e.matmul(ps[pb][:], wb[b][:, k, m * 128:(m + 1) * 128],
                                                      hnT[:, k, n * 512:(n + 1) * 512], start=(k == 0), stop=(k == KC - 1)),
                             r=[wb_r[b], hnT_r], w=[ps_r[pb]])
                    p.op("act", lambda e: e.activation(out=ot[pb][:], in_=ps[pb][:], func=AF.Copy, scale=scale),
                         r=[ps_r[pb]], w=[ot_r[pb]])
                    row0 = sub * 512 + m * 128
                    p.dma("act", dst[row0:row0 + 128, n * 512:(n + 1) * 512], ot[pb][:], r=[ot_r[pb]], is_output=True)
        else:
            for t in range(NT // 128):
                pb = pi % 4
                pi += 1
                for k in range(KC):
                    p.op("pe", lambda e: e.matmul(ps[pb][:, :ncol], hnT[:, k, t * 128:(t + 1) * 128],
                                                  wb[b][:, k, :ncol], start=(k == 0), stop=(k == KC - 1)),
                         r=[wb_r[b], hnT_r], w=[ps_r[pb]])
                if nm == "gates":
                    p.op("act", lambda e: e.activation(out=og[:], in_=ps[pb][:, :48], func=AF.Sigmoid),
                         r=[ps_r[pb]], w=[og_r])
                    p.dma("act", gates[t * 128:(t + 1) * 128, :], og[:], r=[og_r], is_output=True)
                else:
                    p.op("act", lambda e: e.activation(out=ot[pb][:], in_=ps[pb][:], func=AF.Copy),
                         r=[ps_r[pb]], w=[ot_r[pb]])
                    p.dma("act", vv[nm][t * 128:(t + 1) * 128, :], ot[pb][:], r=[ot_r[pb]], is_output=True)
    p.finish()
    p.close()
    return p.nc


NI = 8
G = 4
NCMP = 511


def build_p2(only_groups=None, do_sel=True, do_win=True):
    p = Prog()
    q_l = p.dram("q_l", [NI, G, 128, 512], BF16, "ExternalInput")
    gates_l = p.dram("gates_l", [128, NI * 48], F32, "ExternalInput")
    kcT = p.dram("kcT", [G, 128, 8192], BF16, "ExternalInput")
    vcT = p.dram("vcT", [G, 128, 8192], BF16, "ExternalInput")
    ksT = p.dram("ksT", [G, 128, 8192], BF16, "ExternalInput")
    kwT = p.dram("kwT", [G, 128, 8192], BF16, "ExternalInput")
    vs1 = p.dram("vs1", [G, 128, 64 * 129], BF16, "ExternalInput")
    vw1 = p.dram("vw1", [G, 128, 64 * 129], BF16, "ExternalInput")
    w1 = {"k": p.dram("w1k", [128, 32 * 256], F32, "ExternalInput"), "v": p.dram("w1v", [128, 32 * 256], F32, "ExternalInput")}
    w2 = {"k": p.dram("w2k", [128, 2 * 128], F32, "ExternalInput"), "v": p.dram("w2v", [128, 2 * 128], F32, "ExternalInput")}
    posT = {"k": p.dram("posTk", [128, 32], F32, "ExternalInput"), "v": p.dram("posTv", [128, 32], F32, "ExternalInput")}
    tb_bias = p.dram("tb_bias", [9, 128, 16 * 128], F32, "ExternalInput")
    tb_mask = p.dram("tb_mask", [128, 9 * 128], F32, "ExternalInput")
    b31 = p.dram("b31", [128, 16], F32, "ExternalInput")
    win_mask = p.dram("win_mask", [128, 12 * 128], F32, "ExternalInput")
    ec_bias = p.dram("ec_bias", [128, 16 * 72], F32, "ExternalInput")
    ec_mask = p.dram("ec_mask", [128, 72], F32, "ExternalInput")
    fq = p.dram("fq", [128, NI * 128], F32, "ExternalInput")
    Rm = p.dram("Rm", [128, 8192], BF16, "ExternalInput")
    ident = p.dram("ident", [128, 128], BF16, "ExternalInput")
    o_out = p.dram("o_out", [NI * 128, 2048], BF16, "ExternalOutput")

    def T(name, shape, dt):
        return p.sbuf(name, shape, dt), p.res(name)

    kccT, kccT_r = T("kccT", [128, G, 512], BF16)
    vcc, vcc_r = T("vcc", [128, G, 4, 128], BF16)
    R_sb, R_r = T("R", [128, 8192], BF16)
    id_sb, id_r = T("ident", [128, 128], BF16)
    Tt, Tt_r = T("Tt", [128, 9, 16 * 128], BF16)
    Wm4, Wm4_r = T("Wm4", [128, 12, 4, 128], BF16)
    Fq, Fq_r = T("Fq", [128, NI, 128], F32)
    Ec, Ec_r = T("Ec", [128, 16, 72], F32)
    b31s, b31_r = T("b31", [128, 16], F32)
    gat, gat_r = T("gat", [128, NI, 48], F32)
    psS = [p.psum("psS", [128, 512]) for _ in range(2)]
    psS_r = [p.res("psS") for _ in range(2)]
    psO = [p.psum("psO", [128, 512]) for _ in range(4)]
    psO_r = [p.res("psO") for _ in range(4)]
    psC = psS
    psC_r = psS_r
    psT = p.psum("psT", [128, 1024], BF16)
    psT_r = p.res("psT")
    psX = p.psum("psX", [128, 512])
    psX_r = p.res("psX")

    p.dma("sp", R_sb[:], Rm[:], w=[R_r])
    p.dma("sp", id_sb[:], ident[:], w=[id_r])
    p.dma("sp", Fq[:].rearrange("p i b -> p (i b)"), fq[:], w=[Fq_r])
    p.dma("sp", b31s[:], b31[:], w=[b31_r])
    p.dma("sp", gat[:].rearrange("p i c -> p (i c)"), gates_l[:], w=[gat_r])

    p.push_scope()
    stg = [T("stg", [128, 2048], F32) for _ in range(2)]
    msk, msk_r = T("msk", [128, 12 * 128], F32)
    p.dma("sp", msk[:, 0:9 * 128], tb_mask[:], w=[msk_r])
    for j in range(9):
        sb, sr = stg[j % 2]
        p.dma("sp", sb[:], tb_bias[j], w=[sr])
        v3 = sb[:].rearrange("p (h q) -> p h q", h=16)
        p.op("dve", lambda e: e.tensor_tensor(out=v3, in0=v3, in1=b31s[:].unsqueeze(2).to_broadcast([128, 16, 128]),
                                              op=ALU.subtract), r=[sr, b31_r], w=[sr])
        p.op("dve", lambda e: e.tensor_tensor(out=Tt[:, j, :].rearrange("p (h q) -> p h q", h=16), in0=v3,
                                              in1=msk[:, j * 128:(j + 1) * 128].unsqueeze(1).to_broadcast([128, 16, 128]),
                                              op=ALU.add), r=[sr, msk_r], w=[Tt_r])
    p.dma("sp", msk[:], win_mask[:], w=[msk_r])
    for j in range(12):
        p.op("dve", lambda e: e.tensor_copy(out=Wm4[:, j, :, :],
                                            in_=msk[:, j * 128:(j + 1) * 128].unsqueeze(1).to_broadcast([128, 4, 128])),
             r=[msk_r], w=[Wm4_r])
    sb, sr = stg[0]
    p.dma("sp", sb[:, 0:16 * 72], ec_bias[:], w=[sr])
    p.dma("sp", msk[:, 0:72], ec_mask[:], w=[msk_r])
    v3 = sb[:, 0:16 * 72].rearrange("p (h e) -> p h e", h=16)
    p.op("dve", lambda e: e.tensor_tensor(out=v3, in0=v3, in1=b31s[:].unsqueeze(2).to_broadcast([128, 16, 72]),
                                          op=ALU.subtract), r=[sr, b31_r], w=[sr])
    p.op("dve", lambda e: e.tensor_tensor(out=Ec[:], in0=v3, in1=msk[:, 0:72].unsqueeze(1).to_broadcast([128, 16, 72]),
                                          op=ALU.add), r=[sr, msk_r], w=[Ec_r])
    w1b, w1b_r = T("w1b", [128, 32, 256], BF16)
    w2b, w2b_r = T("w2b", [128, 2, 128], BF16)
    posb, posb_r = T("posb", [128, 32], BF16)
    pw1, pw1_r = T("pw1", [128, 2], F32)
    kvT, kvT_r = T("kvT", [128, 8192], BF16)
    hidT, hidT_r = T("hidT", [128, 2, 512], BF16)
    xg, xg_r = T("xg", [128, 512], F32)
    tg, tg_r = T("tg", [128, 512], F32)
    p.op("dve", lambda e: e.memset(vcc[:], 0.0), w=[vcc_r])
    p.op("dve", lambda e: e.memset(kccT[:], 0.0), w=[kccT_r])
    for kv in ("k", "v"):
        for q4 in range(4):
            sb, sr = stg[q4 % 2]
            p.dma("sp", sb[:], w1[kv][:, q4 * 2048:(q4 + 1) * 2048], w=[sr])
            p.op("dve", lambda e: e.tensor_copy(out=w1b[:, q4 * 8:(q4 + 1) * 8, :].rearrange("p j c -> p (j c)"), in_=sb[:]),
                 r=[sr], w=[w1b_r])
        sb, sr = stg[0]
        p.dma("sp", sb[:, 0:256], w2[kv][:], w=[sr])
        p.op("dve", lambda e: e.tensor_copy(out=w2b[:].rearrange("p a b -> p (a b)"), in_=sb[:, 0:256]), r=[sr], w=[w2b_r])
        sb, sr = stg[1]
        p.dma("sp", sb[:, 0:32], posT[kv][:], w=[sr])
        p.op("dve", lambda e: e.tensor_copy(out=posb[:], in_=sb[:, 0:32]), r=[sr], w=[posb_r])
        for hc in range(2):
            for j in range(32):
                p.op("pe", lambda e: e.matmul(psX[:, 0:1], w1b[:, j, hc * 128:(hc + 1) * 128], posb[:, j:j + 1],
                                              start=(j == 0), stop=(j == 31)), r=[w1b_r, posb_r], w=[psX_r])
            p.op("dve", lambda e: e.tensor_copy(out=pw1[:, hc:hc + 1], in_=psX[:, 0:1]), r=[psX_r], w=[pw1_r])
        src = kcT if kv == "k" else vcT
        for g in range(G):
            p.dma("sp", kvT[:], src[g], w=[kvT_r])
            for hc in range(2):
                pc = psC[hc]
                for j in range(32):
                    p.op("pe", lambda e: e.matmul(pc[:, 0:NCMP], w1b[:, j, hc * 128:(hc + 1) * 128],
                                                  kvT[:, j:j + 16 * (NCMP - 1) + 1:16], start=(j == 0), stop=(j == 31)),
                         r=[w1b_r, kvT_r], w=[psC_r[hc]])
                p.op("act", lambda e: e.activation(out=xg[:, 0:NCMP], in_=pc[:, 0:NCMP], func=AF.Identity,
                                                   bias=pw1[:, hc:hc + 1]), r=[psC_r[hc], pw1_r], w=[xg_r])
                p.op("dve", lambda e: e.tensor_tensor(out=tg[:, 0:NCMP], in0=xg[:, 0:NCMP], in1=xg[:, 0:NCMP], op=ALU.mult),
                     r=[xg_r], w=[tg_r])
                p.op("dve", lambda e: e.tensor_scalar(out=tg[:, 0:NCMP], in0=tg[:, 0:NCMP], scalar1=0.044715, scalar2=1.0,
                                                      op0=ALU.mult, op1=ALU.add), r=[tg_r], w=[tg_r])
                p.op("dve", lambda e: e.tensor_tensor(out=tg[:, 0:NCMP], in0=tg[:, 0:NCMP], in1=xg[:, 0:NCMP], op=ALU.mult),
                     r=[tg_r, xg_r], w=[tg_r])
                p.op("act", lambda e: e.activation(out=tg[:, 0:NCMP], in_=tg[:, 0:NCMP], func=AF.Sigmoid, scale=1.5957691216),
                     r=[tg_r], w=[tg_r])
                p.op("dve", lambda e: e.tensor_tensor(out=hidT[:, hc, 0:NCMP], in0=tg[:, 0:NCMP], in1=xg[:, 0:NCMP], op=ALU.mult),
                     r=[tg_r, xg_r], w=[hidT_r])
            if kv == "k":
                for hc in range(2):
                    p.op("pe", lambda e: e.matmul(psX[:, 0:NCMP], w2b[:, hc, :], hidT[:, hc, 0:NCMP],
                                                  start=(hc == 0), stop=(hc == 1)), r=[w2b_r, hidT_r], w=[psX_r])
                p.op("act", lambda e: e.activation(out=kccT[:, g, 0:NCMP], in_=psX[:, 0:NCMP], func=AF.Copy),
                     r=[psX_r], w=[kccT_r])
            else:
                for cb in range(4):
                    n = min(NCMP, (cb + 1) * 128) - cb * 128
                    for hc in range(2):
                        p.op("pe", lambda e: e.matmul(psX[0:n, cb * 128:(cb + 1) * 128], hidT[:, hc, cb * 128:cb * 128 + n],
                                                      w2b[:, hc, :], start=(hc == 0), stop=(hc == 1)),
                             r=[w2b_r, hidT_r], w=[psX_r])
                    p.op("act", lambda e: e.activation(out=vcc[0:n, g, cb, :], in_=psX[0:n, cb * 128:(cb + 1) * 128], func=AF.Copy),
                         r=[psX_r], w=[vcc_r])
    p.pop_scope()

    ksT_s, ksT_r = T("ksT", [128, 8192], BF16)
    kwT_s, kwT_r = T("kwT", [128, 8192], BF16)
    vs_s, vs_r = T("vs1", [128, 64, 129], BF16)
    vw_s, vw_r = T("vw1", [128, 64, 129], BF16)
    qg = [T("qg", [128, 512], BF16) for _ in range(2)]
    s4, s4_r = T("s4", [128, 4, 512], F32)
    e4, e4_r = T("e4", [128, 4, 512], F32)
    pb4, pb4_r = T("pb4", [128, 4, 512], BF16)
    pT4, pT4_r = T("pT4", [128, 4, 4, 128], BF16)
    st4, st4_r = T("st4", [128, 4, 8], F32)
    imp, imp_r = T("imp", [128, 520], F32)
    sc, sc_r = T("sc", [128, 128], F32)
    wk, wk_r = T("wk", [128, 128], F32)
    m8, m8_r = T("m8", [128, 16], F32)
    st1, st1_r = T("st1", [128, 8], F32)
    negm, negm_r = T("negm", [128, 128], BF16)
    nmT, nmT_r = T("nmT", [128, 4, 128], BF16)
    PT = [T("PT", [128, 512], BF16) for _ in range(3)]
    OA = [T("oacc", [128, 512], F32) for _ in range(2)]
    obf = [T("obf", [128, 512], BF16) for _ in range(2)]
    coef, coef_r = T("coef", [128, 8], F32)
    cnt = {"S": 0, "P": 0, "q": 0, "c": 0}

    def attend(i, g, qv, q_r, kts, K_s, K_r, V_s, V_r, extra, gate_col0, init_acc, oacc, oacc_r):
        n = len(kts)
        slots = {}

        def qk(idx):
            kt = kts[idx]
            sb_ = cnt["S"] % 2
            cnt["S"] += 1
            slots[idx] = sb_
            mms = [(K_s[:, kt * 128:(kt + 1) * 128], K_r, qv[:], q_r)] + extra(kt)
            for mi, (lt, lr, rh, rr) in enumerate(mms):
                p.op("pe", lambda e: e.matmul(psS[sb_][:], lt, rh, start=(mi == 0), stop=(mi == len(mms) - 1)),
                     r=[lr, rr], w=[psS_r[sb_]])

        qk(0)
        for idx, kt in enumerate(kts):
            if idx + 1 < n:
                qk(idx + 1)
            sb_ = slots.pop(idx)
            pb_ = cnt["P"] % 3
            cnt["P"] += 1
            Pt, Pt_r = PT[pb_]
            p.op("act", lambda e: e.activation(out=Pt[:], in_=psS[sb_][:], func=AF.Exp), r=[psS_r[sb_]], w=[Pt_r])
            for h in range(4):
                bank = h
                c0 = 0
                p.op("pe", lambda e: e.matmul(psO[bank][:, c0:c0 + 129], Pt[:, h * 128:(h + 1) * 128], V_s[:, kt, :],
                                              start=(idx == 0), stop=(idx == n - 1)), r=[Pt_r, V_r], w=[psO_r[bank]])
        for h in range(4):
            bank = h
            c0 = 0
            head = g * 4 + h
            p.op("dve", lambda e: e.reciprocal(out=coef[:, h:h + 1], in_=psO[bank][:, c0 + 128:c0 + 129]),
                 r=[psO_r[bank]], w=[coef_r])
            p.op("dve", lambda e: e.tensor_tensor(out=coef[:, h:h + 1], in0=coef[:, h:h + 1],
                                                  in1=gat[:, i, gate_col0 + head:gate_col0 + head + 1], op=ALU.mult),
                 r=[coef_r, gat_r], w=[coef_r])
            dst = oacc[:, h * 128:(h + 1) * 128]
            if init_acc:
                p.op("dve", lambda e: e.tensor_scalar(out=dst, in0=psO[bank][:, c0:c0 + 128], scalar1=coef[:, h:h + 1],
                                                      scalar2=None, op0=ALU.mult), r=[psO_r[bank], coef_r], w=[oacc_r])
            else:
                p.op("dve", lambda e: e.scalar_tensor_tensor(out=dst, in0=psO[bank][:, c0:c0 + 128], scalar=coef[:, h:h + 1],
                                                             in1=dst, op0=ALU.mult, op1=ALU.add),
                     r=[psO_r[bank], coef_r, oacc_r], w=[oacc_r])

    groups = list(range(G)) if only_groups is None else only_groups
    for g in groups:
        p.dma("sp", ksT_s[:], ksT[g], w=[ksT_r])
        p.dma("sp", kwT_s[:], kwT[g], w=[kwT_r])
        p.dma("sp", vs_s[:].rearrange("p k d -> p (k d)"), vs1[g], w=[vs_r])
        p.dma("sp", vw_s[:].rearrange("p k d -> p (k d)"), vw1[g], w=[vw_r])
        for i in range(NI):
            qb = cnt["q"] % 2
            cnt["q"] += 1
            qv, q_r = qg[qb]
            p.dma("sp", qv[:], q_l[i, g], w=[q_r])
            oacc, oacc_r = OA[qb]
            ncv = min(NCMP, 64 * i + 63)
            e_lo = 64 * i - 9
            c_lo = max(0, e_lo)
            c_hi = min(ncv, 64 * i + 63)
            ncb = (ncv + 127) // 128
            gsl = slice(g * 4, (g + 1) * 4)
            for h in range(4):
                p.op("pe", lambda e: e.matmul(psO[h][:, 0:ncv], qv[:, h * 128:(h + 1) * 128], kccT[:, g, 0:ncv], start=True, stop=True),
                     r=[q_r, kccT_r], w=[psO_r[h]])
            for h in range(4):
                p.op("act", lambda e: e.activation(out=s4[:, h, 0:ncv], in_=psO[h][:, 0:ncv], func=AF.Copy), r=[psO_r[h]], w=[s4_r])
            p.op("dve", lambda e: e.tensor_tensor(out=s4[:, :, c_lo:c_hi], in0=s4[:, :, c_lo:c_hi],
                                                  in1=Ec[:, gsl, c_lo - e_lo:c_hi - e_lo], op=ALU.add), r=[s4_r, Ec_r], w=[s4_r])
            p.op("dve", lambda e: e.tensor_reduce(out=st4[:, :, 0], in_=s4[:, :, 0:ncv], axis=AX.X, op=ALU.max), r=[s4_r], w=[st4_r])
            p.op("dve", lambda e: e.tensor_scalar(out=st4[:, :, 1], in0=st4[:, :, 0], scalar1=-1000.0, scalar2=-1.0,
                                                  op0=ALU.max, op1=ALU.mult), r=[st4_r], w=[st4_r])
            p.op("dve", lambda e: e.memset(st4[:, :, 2], 0.0), w=[st4_r])
            for h in range(4):
                p.op("act", lambda e: e.activation(out=e4[:, h, 0:ncv], in_=s4[:, h, 0:ncv], func=AF.Exp, bias=st4[:, h, 1:2],
                                                   accum_out=st4[:, h, 2:3]), r=[s4_r, st4_r], w=[e4_r, st4_r])
            p.op("dve", lambda e: e.tensor_scalar(out=st4[:, :, 3], in0=st4[:, :, 2], scalar1=1e-30, scalar2=None, op0=ALU.add),
                 r=[st4_r], w=[st4_r])
            p.op("dve", lambda e: e.reciprocal(out=st4[:, :, 4], in_=st4[:, :, 3]), r=[st4_r], w=[st4_r])
            p.op("dve", lambda e: e.tensor_tensor(out=st4[:, :, 5], in0=st4[:, :, 4], in1=gat[:, i, gsl], op=ALU.mult),
                 r=[st4_r, gat_r], w=[st4_r])
            p.op("dve", lambda e: e.tensor_tensor(out=s4[:, :, 0:ncv], in0=e4[:, :, 0:ncv],
                                                  in1=st4[:, :, 4:5].to_broadcast([128, 4, ncv]), op=ALU.mult),
                 r=[e4_r, st4_r], w=[s4_r])
            p.op("pool", lambda e: e.memset(imp[:], 0.0), w=[imp_r])
            p.op("dve", lambda e: e.tensor_reduce(out=imp[:, 1:1 + ncv], in_=s4[:, :, 0:ncv].rearrange("p h c -> p c h"),
                                                  axis=AX.X, op=ALU.add), r=[s4_r], w=[imp_r])
            p.op("pool", lambda e: e.memset(pb4[:], 0.0), w=[pb4_r])
            p.op("dve", lambda e: e.tensor_tensor(out=pb4[:, :, 0:ncv], in0=e4[:, :, 0:ncv],
                                                  in1=st4[:, :, 5:6].to_broadcast([128, 4, ncv]), op=ALU.mult),
                 r=[e4_r, st4_r], w=[pb4_r])
            for hp in range(2):
                for h in (2 * hp, 2 * hp + 1):
                    for cb in range(ncb):
                        c0 = (h % 2) * 512 + cb * 128
                        p.op("pe", lambda e: e.transpose(psT[:, c0:c0 + 128], pb4[:, h, cb * 128:(cb + 1) * 128], id_sb[:]),
                             r=[pb4_r, id_r], w=[psT_r])
                for h in (2 * hp, 2 * hp + 1):
                    c0 = (h % 2) * 512
                    p.op("act", lambda e: e.activation(out=pT4[:, h, 0:ncb, :].rearrange("p a b -> p (a b)"),
                                                       in_=psT[:, c0:c0 + ncb * 128], func=AF.Copy), r=[psT_r], w=[pT4_r])
            for h in range(4):
                for cb in range(ncb):
                    p.op("pe", lambda e: e.matmul(psO[h][:, 0:128], pT4[:, h, cb, :], vcc[:, g, cb, :], start=(cb == 0), stop=(cb == ncb - 1)),
                         r=[pT4_r, vcc_r], w=[psO_r[h]])
            for h in range(4):
                p.op("act", lambda e: e.activation(out=oacc[:, h * 128:(h + 1) * 128], in_=psO[h][:, 0:128], func=AF.Copy),
                     r=[psO_r[h]], w=[oacc_r])
            if do_sel:
                iv = imp[:, 0:512].rearrange("p (j f) -> p j f", f=4)
                p.op("dve", lambda e: e.tensor_reduce(out=sc[:], in_=iv, axis=AX.X, op=ALU.add), r=[imp_r], w=[sc_r])
                p.op("dve", lambda e: e.tensor_tensor(out=sc[:], in0=sc[:], in1=imp[:, 4:516].rearrange("p (j f) -> p j f", f=4)[:, :, 0],
                                                      op=ALU.add), r=[sc_r, imp_r], w=[sc_r])
                p.op("dve", lambda e: e.tensor_tensor(out=sc[:], in0=sc[:], in1=Fq[:, i, :], op=ALU.add), r=[sc_r, Fq_r], w=[sc_r])
                p.op("dve", lambda e: e.max(out=m8[:, 0:8], in_=sc[:]), r=[sc_r], w=[m8_r])
                p.op("dve", lambda e: e.match_replace(out=wk[:], in_to_replace=m8[:, 0:8], in_values=sc[:], imm_value=-3.0e38),
                     r=[sc_r, m8_r], w=[wk_r])
                p.op("dve", lambda e: e.max(out=m8[:, 8:16], in_=wk[:]), r=[wk_r], w=[m8_r])
                p.op("dve", lambda e: e.tensor_scalar(out=wk[:], in0=sc[:], scalar1=m8[:, 15:16], scalar2=None, op0=ALU.is_ge),
                     r=[sc_r, m8_r], w=[wk_r])
                p.op("dve", lambda e: e.tensor_scalar(out=negm[:], in0=wk[:], scalar1=1.0, scalar2=-NEG, op0=ALU.subtract, op1=ALU.mult),
                     r=[wk_r], w=[negm_r])
                p.op("pe", lambda e: e.transpose(psT[:, 512:640], negm[:], id_sb[:]), r=[negm_r, id_r], w=[psT_r])
                p.op("act", lambda e: e.activation(out=nmT[:], in_=psT[:, 512:640].unsqueeze(1).to_broadcast([128, 4, 128]), func=AF.Copy),
                     r=[psT_r], w=[nmT_r])

                def extra_sel(kt, i=i, g=g):
                    ex = [(R_sb[:, kt * 128:(kt + 1) * 128], R_r, nmT[:].rearrange("p a b -> p (a b)"), nmT_r)]
                    j = kt - 8 * i
                    if j >= -1:
                        ex.append((id_sb[:], id_r, Tt[:, j + 1, g * 512:(g + 1) * 512], Tt_r))
                    return ex

                attend(i, g, qv, q_r, list(range(0, 8 * i + 8)), ksT_s, ksT_r, vs_s, vs_r, extra_sel, 16, False, oacc, oacc_r)
            if do_win:
                def extra_win(kt, i=i, g=g):
                    jp = kt - (8 * i - 4)
                    ex = [(id_sb[:], id_r, Wm4[:, jp, :, :].rearrange("p a b -> p (a b)"), Wm4_r)]
                    if jp >= 3:
                        ex.append((id_sb[:], id_r, Tt[:, jp - 3, g * 512:(g + 1) * 512], Tt_r))
                    return ex

                kts = [kt for kt in range(8 * i - 4, 8 * i + 8) if kt >= 0]
                attend(i, g, qv, q_r, kts, kwT_s, kwT_r, vw_s, vw_r, extra_win, 32, False, oacc, oacc_r)
            ob, ob_r = obf[qb]
            p.op("act", lambda e: e.activation(out=ob[:], in_=oacc[:], func=AF.Copy), r=[oacc_r], w=[ob_r])
            p.dma("sp", o_out[i * 128:(i + 1) * 128, g * 512:(g + 1) * 512], ob[:], r=[ob_r], is_output=True)
    p.finish()
    p.close()
    return p.nc


DFF = 5632
MC = DFF // 128
MG = 2
MGC = MC // MG


def max_tokens(toks):
    best = {}
    for s, v in toks:
        if id(s) not in best or best[id(s)][1] < v:
            best[id(s)] = (s, v)
    return list(best.values())


class Fence:
    def __init__(self):
        self.toks = []

    def add(self, tok):
        self.toks.append(tok)
        if len(self.toks) > 256:
            self.toks = max_tokens(self.toks)

    def wait(self, p, e):
        for t in max_tokens(self.toks):
            p._wait(e, t)


def rmsnorm_T2(p, src_dram, fence, gsb, g_r, ones_f, ones_r, st, out_sb=None, out_sb_r=None, out_dram=None,
               out_fence=None, is_output=False):
    xq, xq_r, sq, sq_r, ss, ss_r, rstd, rstd_r, uo, uo_r = st
    n = 0
    for half in range(NT // 512):
        if fence is not None:
            fence.wait(p, "act")
        hs = slice(half * 512, (half + 1) * 512)
        for k in range(KC):
            b = n % 4
            n += 1
            p.dma("act", xq[b][:], src_dram[k * 128:(k + 1) * 128, hs], w=[xq_r[b]])
            b2 = k % 2
            p.op("act", lambda e: e.activation(out=sq[b2][:], in_=xq[b][:], func=AF.Square), r=[xq_r[b]], w=[sq_r[b2]])
            p.op("pe", lambda e: e.matmul(ss[:], ones_f[:], sq[b2][:], start=(k == 0), stop=(k == KC - 1)),
                 r=[sq_r[b2], ones_r], w=[ss_r])
        p.op("dve", lambda e: e.tensor_scalar(out=rstd[:], in0=ss[:], scalar1=1.0 / D, scalar2=EPS,
                                              op0=ALU.mult, op1=ALU.add), r=[ss_r], w=[rstd_r])
        p.op("act", lambda e: e.activation(out=rstd[:], in_=rstd[:], func=AF.Sqrt), r=[rstd_r], w=[rstd_r])
        p.op("dve", lambda e: e.reciprocal(out=rstd[:], in_=rstd[:]), r=[rstd_r], w=[rstd_r])
        for k in range(KC):
            b = n % 4
            n += 1
            p.dma("act", xq[b][:], src_dram[k * 128:(k + 1) * 128, hs], w=[xq_r[b]])
            if out_sb is not None:
                p.op("dve", lambda e: e.scalar_tensor_tensor(
                    out=out_sb[:, k, hs], in0=xq[b][:], scalar=gsb[:, k:k + 1],
                    in1=rstd[:], op0=ALU.mult, op1=ALU.mult), r=[xq_r[b], rstd_r, g_r], w=[out_sb_r])
            else:
                b2 = k % 2
                p.op("dve", lambda e: e.scalar_tensor_tensor(
                    out=uo[b2][:], in0=xq[b][:], scalar=gsb[:, k:k + 1],
                    in1=rstd[:], op0=ALU.mult, op1=ALU.mult), r=[xq_r[b], rstd_r, g_r], w=[uo_r[b2]])
                tok = p.dma("act", out_dram[k * 128:(k + 1) * 128, hs], uo[b2][:],
                            r=[uo_r[b2]], is_output=is_output)
                if out_fence is not None:
                    out_fence.add(tok)


def build_tok(glu, final):
    p = Prog()
    resT = p.dram("resT", [D, NT], F32, "ExternalInput")
    aT = p.dram("aT", [D, NT], BF16, "ExternalInput")
    wmix = p.dram("wmix", [D, 4096 if glu else 2048], F32, "ExternalInput")
    gff = p.dram("gff", [128, KC], F32, "ExternalInput")
    gn = p.dram("gn", [128, KC], F32, "ExternalInput")
    w1 = p.dram("w1", [D, 2 * DFF], F32, "ExternalInput")
    w2 = p.dram("w2", [DFF, D], F32, "ExternalInput")
    hmidT = p.dram("hmidT", [D, NT], F32, "ExternalOutput")
    hT = p.dram("hT", [D, NT], F32, "ExternalOutput")
    normT = p.dram("normT", [D, NT], F32, "ExternalOutput")

    def T(name, shape, dt=F32):
        return p.sbuf(name, shape, dt), p.res(name)

    ones_f, ones_r = T("ones", [128, 128])
    p.op("dve", lambda e: e.memset(ones_f[:], 1.0), w=[ones_r])
    gffs, gff_r = T("gffs", [128, KC])
    p.dma("sp", gffs[:], gff[:], w=[gff_r])
    gns, gn_r = T("gns", [128, KC])
    p.dma("sp", gns[:], gn[:], w=[gn_r])
    xq = [T("xq", [128, 512]) for _ in range(4)]
    st = ([t for t, _ in xq], [r for _, r in xq],
          [p.sbuf("sq", [128, 512], F32) for _ in range(2)], [p.res("sq") for _ in range(2)],
          p.psum("ss", [128, 512]), p.res("ss"),
          p.sbuf("rstd", [128, 512], F32), p.res("rstd"),
          [p.sbuf("uo", [128, 512], F32) for _ in range(2)], [p.res("uo") for _ in range(2)])
    big, big_r = T("big", [128, MGC * NT], BF16)
    a_sb = big[:, 0:KC * NT].rearrange("p (k t) -> p k t", k=KC)
    actT = big[:, :].rearrange("p (m t) -> p m t", m=MGC)
    hnT, hnT_r = T("hnT", [128, KC, NT], BF16)
    NS = 3
    wst = [T("wst", [128, KC * 256]) for _ in range(NS)]
    NBF = 4
    wbf = [T("wbf", [128, KC * 256], BF16) for _ in range(NBF)]
    ps = [p.psum("ps", [128, 512]) for _ in range(6)]
    ps_r = [p.res("ps") for _ in range(6)]
    xc = [T("xc", [128, 512]) for _ in range(3)]
    t1 = [T("t1", [128, 512]) for _ in range(2)]
    cnt = {"w": 0, "b": 0, "ps": 0, "x": 0, "t": 0}

    def load_w(wd, row0, kc, col0, ncol):
        i = cnt["w"] % NS
        cnt["w"] += 1
        j = cnt["b"] % NBF
        cnt["b"] += 1
        ws, ws_r = wst[i]
        wb, wb_r = wbf[j]
        sv = ws[:, 0:kc * ncol].rearrange("p (k c) -> p k c", k=kc)
        bv = wb[:, 0:kc * ncol].rearrange("p (k c) -> p k c", k=kc)
        src = wd[row0:row0 + kc * 128, col0:col0 + ncol].rearrange("(k p) c -> p k c", p=128)
        h = (kc + 1) // 2
        p.dma("sp", sv[:, 0:h, :], src[:, 0:h, :], w=[ws_r])
        p.dma("sp", sv[:, h:kc, :], src[:, h:kc, :], w=[ws_r])
        p.op("dve", lambda e: e.tensor_copy(out=wb[:, 0:kc * ncol], in_=ws[:, 0:kc * ncol]), r=[ws_r], w=[wb_r])
        return bv, wb_r

    def mm(wv, w_r, inT, in_r, kc, half):
        pb = cnt["ps"] % 6
        cnt["ps"] += 1
        for k in range(kc):
            p.op("pe", lambda e: e.matmul(ps[pb][:], wv[:, k, :], inT[:, k, half * 512:(half + 1) * 512],
                                          start=(k == 0), stop=(k == kc - 1)), r=[w_r, in_r], w=[ps_r[pb]])
        return pb

    def prefetched(reqs):
        nxt = load_w(*reqs[0]) if reqs else None
        for n_ in range(len(reqs)):
            cur = nxt
            nxt = load_w(*reqs[n_ + 1]) if n_ + 1 < len(reqs) else None
            yield cur

    srcA = aT.rearrange("(k p) t -> p k t", p=128)
    for kk in range(0, KC, 4):
        p.dma("sp", a_sb[:, kk:kk + 4, :], srcA[:, kk:kk + 4, :], w=[big_r])
    f_mid = Fence()
    reqs = []
    for f in range(KC):
        reqs.append((wmix, 0, KC, f * 128, 128))
        if glu:
            reqs.append((wmix, 0, KC, 2048 + f * 128, 128))
    it = prefetched(reqs)
    for f in range(KC):
        wv, w_r = next(it)
        if glu:
            wv2, w2_r = next(it)
        for half in range(2):
            hs = slice(half * 512, (half + 1) * 512)
            xb = cnt["x"] % 3
            cnt["x"] += 1
            xcb, xcb_r = xc[xb]
            p.dma("act", xcb[:], resT[f * 128:(f + 1) * 128, hs], w=[xcb_r])
            pb = mm(wv, w_r, a_sb, big_r, KC, half)
            if glu:
                pb2 = mm(wv2, w2_r, a_sb, big_r, KC, half)
                tb = cnt["t"] % 2
                cnt["t"] += 1
                tt, tt_r = t1[tb]
                p.op("act", lambda e: e.activation(out=tt[:], in_=ps[pb2][:], func=AF.Sigmoid), r=[ps_r[pb2]], w=[tt_r])
                p.op("dve", lambda e: e.tensor_tensor(out=tt[:], in0=ps[pb][:], in1=tt[:], op=ALU.mult),
                     r=[ps_r[pb], tt_r], w=[tt_r])
                p.op("dve", lambda e: e.tensor_tensor(out=xcb[:], in0=xcb[:], in1=tt[:], op=ALU.add),
                     r=[tt_r, xcb_r], w=[xcb_r])
            else:
                p.op("dve", lambda e: e.tensor_tensor(out=xcb[:], in0=ps[pb][:], in1=xcb[:], op=ALU.add),
                     r=[ps_r[pb], xcb_r], w=[xcb_r])
            tok = p.dma("act", hmidT[f * 128:(f + 1) * 128, hs], xcb[:], r=[xcb_r], is_output=True)
            f_mid.add(tok)
    rmsnorm_T2(p, hmidT, f_mid, gffs, gff_r, ones_f, ones_r, st, out_sb=hnT, out_sb_r=hnT_r)
    src_res, src_fence = hmidT, f_mid
    for mg in range(MG):
        reqs = []
        for m2 in range(0, MGC, 2):
            m = mg * MGC + m2
            reqs.append((w1, 0, KC, m * 128, 256))
            reqs.append((w1, 0, KC, DFF + m * 128, 256))
        it = prefetched(reqs)
        for m2 in range(0, MGC, 2):
            wa, wa_r = next(it)
            wb_, wb_r = next(it)
            for mm_ in range(2):
                ml = m2 + mm_
                for half in range(2):
                    pa = mm(wa[:, :, mm_ * 128:(mm_ + 1) * 128], wa_r, hnT, hnT_r, KC, half)
                    pbb = mm(wb_[:, :, mm_ * 128:(mm_ + 1) * 128], wb_r, hnT, hnT_r, KC, half)
                    tb = cnt["t"] % 2
                    cnt["t"] += 1
                    tt, tt_r = t1[tb]
                    p.op("act", lambda e: e.activation(out=tt[:], in_=ps[pa][:], func=AF.Silu), r=[ps_r[pa]], w=[tt_r])
                    p.op("dve", lambda e: e.tensor_tensor(out=actT[:, ml, half * 512:(half + 1) * 512], in0=ps[pbb][:],
                                                          in1=tt[:], op=ALU.mult), r=[ps_r[pbb], tt_r], w=[big_r])
        f_new = Fence()
        reqs = [(w2, mg * MGC * 128, MGC, f * 128, 128) for f in range(KC)]
        it = prefetched(reqs)
        for f in range(KC):
            wo, wo_r = next(it)
            for half in range(2):
                hs = slice(half * 512, (half + 1) * 512)
                pb = cnt["ps"] % 6
                cnt["ps"] += 1
                for ml in range(MGC):
                    p.op("pe", lambda e: e.matmul(ps[pb][:], wo[:, ml, :], actT[:, ml, hs], start=(ml == 0), stop=(ml == MGC - 1)),
                         r=[wo_r, big_r], w=[ps_r[pb]])
                xb = cnt["x"] % 3
                cnt["x"] += 1
                xcb, xcb_r = xc[xb]
                src_fence.wait(p, "act")
                p.dma("act", xcb[:], src_res[f * 128:(f + 1) * 128, hs], w=[xcb_r])
                p.op("dve", lambda e: e.tensor_tensor(out=xcb[:], in0=ps[pb][:], in1=xcb[:], op=ALU.add),
                     r=[ps_r[pb], xcb_r], w=[xcb_r])
                dst = hT if mg == MG - 1 else normT
                tok = p.dma("act", dst[f * 128:(f + 1) * 128, hs], xcb[:], r=[xcb_r], is_output=True)
                f_new.add(tok)
        src_res, src_fence = (hT if mg == MG - 1 else normT), f_new
    rmsnorm_T2(p, hT, src_fence, gns, gn_r, ones_f, ones_r, st, out_dram=normT, is_output=True)
    p.finish()
    p.close()
    return p.nc


TWO_PI = 2.0 * math.pi


I32 = mybir.dt.int32


def sincos(p, T, ang, ang_r, s_out, c_out, out_r, tmp, tmp_r, shape_sl, n):
    sl = shape_sl
    tl = (slice(None), slice(0, n))
    if not hasattr(p, "_sc_tmp"):
        p._sc_tmp = (T("sc_ki", [128, 512], I32), T("sc_kf", [128, 512]), T("sc_y", [128, 512]), T("sc_m", [128, 512]))
    (ki, ki_r), (kf, kf_r), (y, y_r), (m, m_r) = p._sc_tmp
    p.op("dve", lambda e: e.tensor_scalar(out=kf[tl], in0=ang[sl], scalar1=1.0 / TWO_PI, scalar2=None, op0=ALU.mult),
         r=[ang_r], w=[kf_r])
    p.op("dve", lambda e: e.tensor_copy(out=ki[tl], in_=kf[tl]), r=[kf_r], w=[ki_r])
    p.op("dve", lambda e: e.tensor_copy(out=kf[tl], in_=ki[tl]), r=[ki_r], w=[kf_r])
    p.op("dve", lambda e: e.scalar_tensor_tensor(out=y[tl], in0=kf[tl], scalar=-TWO_PI, in1=ang[sl], op0=ALU.mult, op1=ALU.add),
         r=[kf_r, ang_r], w=[y_r])

    def fold(v, v_r):
        p.op("dve", lambda e: e.tensor_scalar(out=m[tl], in0=v[tl], scalar1=math.pi, scalar2=None, op0=ALU.is_gt), r=[v_r], w=[m_r])
        p.op("dve", lambda e: e.scalar_tensor_tensor(out=v[tl], in0=m[tl], scalar=-TWO_PI, in1=v[tl], op0=ALU.mult, op1=ALU.add),
             r=[m_r, v_r], w=[v_r])
        p.op("dve", lambda e: e.tensor_scalar(out=m[tl], in0=v[tl], scalar1=-math.pi, scalar2=None, op0=ALU.is_lt), r=[v_r], w=[m_r])
        p.op("dve", lambda e: e.scalar_tensor_tensor(out=v[tl], in0=m[tl], scalar=TWO_PI, in1=v[tl], op0=ALU.mult, op1=ALU.add),
             r=[m_r, v_r], w=[v_r])

    fold(y, y_r)
    p.op("act", lambda e: e.activation(out=s_out[sl], in_=y[tl], func=AF.Sin), r=[y_r], w=[out_r])
    p.op("dve", lambda e: e.tensor_scalar(out=y[tl], in0=y[tl], scalar1=0.5 * math.pi, scalar2=None, op0=ALU.add), r=[y_r], w=[y_r])
    fold(y, y_r)
    p.op("act", lambda e: e.activation(out=c_out[sl], in_=y[tl], func=AF.Sin), r=[y_r], w=[out_r])


def build_s5prep():
    p = Prog()
    A_re = p.dram("A_re", [128, 64], F32, "ExternalInput")
    A_im = p.dram("A_im", [128, 64], F32, "ExternalInput")
    log_dt = p.dram("log_dt", [128, 1], F32, "ExternalInput")
    B_re = p.dram("B_re", [128, 1024], F32, "ExternalInput")
    B_im = p.dram("B_im", [128, 1024], F32, "ExternalInput")
    o_r = p.dram("o_r", [128, 64], F32, "ExternalOutput")
    o_th = p.dram("o_th", [128, 64], F32, "ExternalOutput")
    o_bbre = p.dram("o_bbre", [128, 1024], F32, "ExternalOutput")
    o_bbim = p.dram("o_bbim", [128, 1024], F32, "ExternalOutput")

    def T(name, shape, dt=F32):
        return p.sbuf(name, shape, dt), p.res(name)

    are, are_r = T("are", [128, 64])
    aim, aim_r = T("aim", [128, 64])
    ldt, ldt_r = T("ldt", [128, 1])
    bre, bre_r = T("bre", [128, 64, 16])
    bim, bim_r = T("bim", [128, 64, 16])
    p.dma("sp", are[:], A_re[:], w=[are_r])
    p.dma("sp", aim[:], A_im[:], w=[aim_r])
    p.dma("sp", ldt[:], log_dt[:], w=[ldt_r])
    p.dma("sp", bre[:].rearrange("p n c -> p (n c)"), B_re[:], w=[bre_r])
    p.dma("sp", bim[:].rearrange("p n c -> p (n c)"), B_im[:], w=[bim_r])
    dt, dt_r = T("dt", [128, 1])
    lre, lre_r = T("lre", [128, 64])
    th, th_r = T("th", [128, 64])
    rr, rr_r = T("rr", [128, 64])
    sn, sc_r = T("sn", [128, 64])
    cs, _ = T("cs", [128, 64])
    tmp, tmp_r = T("tmp", [128, 64])
    abre, abre_r = T("abre", [128, 64])
    abim, abim_r = T("abim", [128, 64])
    den, den_r = T("den", [128, 64])
    t2, t2_r = T("t2", [128, 64])
    cfre, cfre_r = T("cfre", [128, 64])
    cfim, cfim_r = T("cfim", [128, 64])
    obr, obr_r = T("obr", [128, 64, 16])
    obi, obi_r = T("obi", [128, 64, 16])
    t3, t3_r = T("t3", [128, 64, 16])
    p.op("act", lambda e: e.activation(out=dt[:], in_=ldt[:], func=AF.Exp), r=[ldt_r], w=[dt_r])
    p.op("dve", lambda e: e.tensor_scalar(out=lre[:], in0=are[:], scalar1=dt[:, 0:1], scalar2=None, op0=ALU.mult),
         r=[are_r, dt_r], w=[lre_r])
    p.op("dve", lambda e: e.tensor_scalar(out=th[:], in0=aim[:], scalar1=dt[:, 0:1], scalar2=None, op0=ALU.mult),
         r=[aim_r, dt_r], w=[th_r])
    p.op("act", lambda e: e.activation(out=rr[:], in_=lre[:], func=AF.Exp), r=[lre_r], w=[rr_r])
    sl = (slice(None), slice(None))
    sincos(p, T, th, th_r, sn, cs, sc_r, tmp, tmp_r, sl, 64)
    p.op("dve", lambda e: e.tensor_tensor(out=abre[:], in0=rr[:], in1=cs[:], op=ALU.mult), r=[rr_r, sc_r], w=[abre_r])
    p.op("dve", lambda e: e.tensor_tensor(out=abim[:], in0=rr[:], in1=sn[:], op=ALU.mult), r=[rr_r, sc_r], w=[abim_r])
    p.op("dve", lambda e: e.tensor_scalar(out=abre[:], in0=abre[:], scalar1=-1.0, scalar2=None, op0=ALU.add), r=[abre_r], w=[abre_r])
    p.op("dve", lambda e: e.tensor_tensor(out=den[:], in0=are[:], in1=are[:], op=ALU.mult), r=[are_r], w=[den_r])
    p.op("dve", lambda e: e.tensor_tensor(out=t2[:], in0=aim[:], in1=aim[:], op=ALU.mult), r=[aim_r], w=[t2_r])
    p.op("dve", lambda e: e.tensor_tensor(out=den[:], in0=den[:], in1=t2[:], op=ALU.add), r=[den_r, t2_r], w=[den_r])
    p.op("dve", lambda e: e.reciprocal(out=den[:], in_=den[:]), r=[den_r], w=[den_r])
    p.op("dve", lambda e: e.tensor_tensor(out=cfre[:], in0=abre[:], in1=are[:], op=ALU.mult), r=[abre_r, are_r], w=[cfre_r])
    p.op("dve", lambda e: e.tensor_tensor(out=t2[:], in0=abim[:], in1=aim[:], op=ALU.mult), r=[abim_r, aim_r], w=[t2_r])
    p.op("dve", lambda e: e.tensor_tensor(out=cfre[:], in0=cfre[:], in1=t2[:], op=ALU.add), r=[cfre_r, t2_r], w=[cfre_r])
    p.op("dve", lambda e: e.tensor_tensor(out=cfre[:], in0=cfre[:], in1=den[:], op=ALU.mult), r=[cfre_r, den_r], w=[cfre_r])
    p.op("dve", lambda e: e.tensor_tensor(out=cfim[:], in0=abim[:], in1=are[:], op=ALU.mult), r=[abim_r, are_r], w=[cfim_r])
    p.op("dve", lambda e: e.tensor_tensor(out=t2[:], in0=abre[:], in1=aim[:], op=ALU.mult), r=[abre_r, aim_r], w=[t2_r])
    p.op("dve", lambda e: e.tensor_tensor(out=cfim[:], in0=cfim[:], in1=t2[:], op=ALU.subtract), r=[cfim_r, t2_r], w=[cfim_r])
    p.op("dve", lambda e: e.tensor_tensor(out=cfim[:], in0=cfim[:], in1=den[:], op=ALU.mult), r=[cfim_r, den_r], w=[cfim_r])
    cre_b = cfre[:].unsqueeze(2).to_broadcast([128, 64, 16])
    cim_b = cfim[:].unsqueeze(2).to_broadcast([128, 64, 16])
    p.op("dve", lambda e: e.tensor_tensor(out=obr[:], in0=bre[:], in1=cre_b, op=ALU.mult), r=[bre_r, cfre_r], w=[obr_r])
    p.op("dve", lambda e: e.tensor_tensor(out=t3[:], in0=bim[:], in1=cim_b, op=ALU.mult), r=[bim_r, cfim_r], w=[t3_r])
    p.op("dve", lambda e: e.tensor_tensor(out=obr[:], in0=obr[:], in1=t3[:], op=ALU.subtract), r=[obr_r, t3_r], w=[obr_r])
    p.op("dve", lambda e: e.tensor_tensor(out=obi[:], in0=bim[:], in1=cre_b, op=ALU.mult), r=[bim_r, cfre_r], w=[obi_r])
    p.op("dve", lambda e: e.tensor_tensor(out=t3[:], in0=bre[:], in1=cim_b, op=ALU.mult), r=[bre_r, cfim_r, obr_r], w=[t3_r])
    p.op("dve", lambda e: e.tensor_tensor(out=obi[:], in0=obi[:], in1=t3[:], op=ALU.add), r=[obi_r, t3_r], w=[obi_r])
    p.dma("sp", o_r[:], rr[:], r=[rr_r], is_output=True)
    p.dma("sp", o_th[:], th[:], r=[th_r], is_output=True)
    p.dma("sp", o_bbre[:], obr[:].rearrange("p n c -> p (n c)"), r=[obr_r], is_output=True)
    p.dma("sp", o_bbim[:], obi[:].rearrange("p n c -> p (n c)"), r=[obi_r], is_output=True)
    p.finish()
    p.close()
    return p.nc


NPAIR = 8
NB = 16


def build_s5main(nblocks=NB):
    p = Prog()
    uT = p.dram("uT", [256, 8192], F32, "ExternalInput")
    BD = p.dram("BD", [128, NPAIR * 2 * 128], F32, "ExternalInput")
    CT = p.dram("CT", [128, NPAIR * 2 * 128], F32, "ExternalInput")
    rcol = p.dram("rcol", [128, NPAIR], F32, "ExternalInput")
    thcol = p.dram("thcol", [128, NPAIR], F32, "ExternalInput")
    iota = p.dram("iota", [128, 512], F32, "ExternalInput")
    Dcol = p.dram("Dcol", [128, 2], F32, "ExternalInput")
    identf = p.dram("identf", [128, 128], F32, "ExternalInput")
    yT = p.dram("yT", [256, 8192], BF16, "ExternalOutput")

    def T(name, shape, dt=F32):
        return p.sbuf(name, shape, dt), p.res(name)

    bd, bd_r = T("bd", [128, NPAIR, 2, 128])
    ct, ct_r = T("ct", [128, NPAIR, 2, 128])
    rc, rc_r = T("rc", [128, NPAIR])
    thc, thc_r = T("thc", [128, NPAIR])
    io, io_r = T("io", [128, 512])
    dc, dc_r = T("dc", [128, 2])
    p.dma("sp", bd[:].rearrange("p a b c -> p (a b c)"), BD[:], w=[bd_r])
    p.dma("sp", ct[:].rearrange("p a b c -> p (a b c)"), CT[:], w=[ct_r])
    p.dma("sp", rc[:], rcol[:], w=[rc_r])
    p.dma("sp", thc[:], thcol[:], w=[thc_r])
    p.dma("sp", io[:], iota[:], w=[io_r])
    p.dma("sp", dc[:], Dcol[:], w=[dc_r])
    cosT, tab_r = T("cosT", [128, NPAIR, 512])
    sinT, _ = T("sinT", [128, NPAIR, 512])
    nsinT, _ = T("nsinT", [128, NPAIR, 512])
    c512, c512_r = T("c512", [128, NPAIR])
    s512, _ = T("s512", [128, NPAIR])
    ang, ang_r = T("ang", [128, 512])
    tmp, tmp_r = T("tmp", [128, 512])
    for k in range(NPAIR):
        p.op("dve", lambda e: e.tensor_scalar(out=ang[:], in0=io[:], scalar1=thc[:, k:k + 1], scalar2=None, op0=ALU.mult),
             r=[io_r, thc_r], w=[ang_r])
        sincos(p, T, ang, ang_r, sinT[:, k, :], cosT[:, k, :], tab_r, tmp, tmp_r, (slice(None), slice(None)), 512)
        p.op("dve", lambda e: e.tensor_scalar(out=nsinT[:, k, :], in0=sinT[:, k, :], scalar1=-1.0, scalar2=None, op0=ALU.mult),
             r=[tab_r], w=[tab_r])
    p.op("dve", lambda e: e.tensor_scalar(out=ang[:, 0:NPAIR], in0=thc[:], scalar1=512.0, scalar2=None, op0=ALU.mult),
         r=[thc_r], w=[ang_r])
    sincos(p, T, ang, ang_r, s512, c512, c512_r, tmp, tmp_r, (slice(None), slice(0, NPAIR)), NPAIR)

    ctb, ctb_r = T("ctb", [128, NPAIR, 3, 128], BF16)
    p.op("dve", lambda e: e.tensor_copy(out=ctb[:, :, 0, :], in_=ct[:, :, 0, :]), r=[ct_r], w=[ctb_r])
    p.op("dve", lambda e: e.tensor_scalar(out=ctb[:, :, 1, :], in0=ct[:, :, 0, :], scalar1=-1.0, scalar2=None, op0=ALU.mult),
         r=[ct_r], w=[ctb_r])
    p.op("dve", lambda e: e.tensor_scalar(out=ctb[:, :, 2, :], in0=ct[:, :, 1, :], scalar1=-1.0, scalar2=None, op0=ALU.mult),
         r=[ct_r], w=[ctb_r])
    idf, idf_r = T("idf", [128, 128])
    p.dma("sp", idf[:], identf[:], w=[idf_r])
    vin_re, vin_r = T("vin_re", [128, NPAIR])
    vin_im, _ = T("vin_im", [128, NPAIR])
    p.op("dve", lambda e: e.memset(vin_re[:], 0.0), w=[vin_r])
    p.op("dve", lambda e: e.memset(vin_im[:], 0.0), w=[vin_r])
    ub = [T("ub", [128, 2, 512]) for _ in range(2)]
    psB = [p.psum("psB", [128, 512]) for _ in range(4)]
    psB_r = [p.res("psB") for _ in range(4)]
    psW = [p.psum("psW", [128, 512]) for _ in range(2)]
    psW_r = [p.res("psW") for _ in range(2)]
    psY = [p.psum("psY", [128, 512]) for _ in range(2)]
    psY_r = [p.res("psY") for _ in range(2)]
    tq = [[T("tq", [128, 512]) for _ in range(4)] for _ in range(2)]
    vre = [T("vre", [128, 512]) for _ in range(2)]
    vim = [T("vim", [128, 512]) for _ in range(2)]
    xq = [[T("xq", [128, 512], BF16) for _ in range(4)] for _ in range(2)]
    yg, yg_r = T("yg", [128, 512])
    tg, tg_r = T("tg", [128, 512])
    yo = [T("yo", [128, 512], BF16) for _ in range(2)]
    sm, sm_r = T("sm", [128, 4])
    steps = [(b, k) for b in range(nblocks) for k in range(NPAIR)]
    ubuf = {}

    def stageA(idx):
        b, k = steps[idx]
        if k == 0:
            u, u_r = ub[b % 2]
            p.dma("sp", u[:], uT[:, b * 512:(b + 1) * 512].rearrange("(k p) t -> p k t", p=128), w=[u_r])
            ubuf[b] = (u, u_r)
        u, u_r = ubuf[b]
        kc = k // 4
        i2 = idx % 2
        pre, pre_r = psB[2 * i2], psB_r[2 * i2]
        pim, pim_r = psB[2 * i2 + 1], psB_r[2 * i2 + 1]
        p.op("pe", lambda e: e.matmul(pre[:], bd[:, k, 0, :], u[:, kc, :], start=True, stop=True), r=[bd_r, u_r], w=[pre_r])
        p.op("pe", lambda e: e.matmul(pim[:], bd[:, k, 1, :], u[:, kc, :], start=True, stop=True), r=[bd_r, u_r], w=[pim_r])
        cT, sT, nsT = cosT[:, k, :], sinT[:, k, :], nsinT[:, k, :]
        tt = tq[i2]
        prods = [(pre, pre_r, cT), (pim, pim_r, sT), (pim, pim_r, cT), (pre, pre_r, nsT)]
        for j, (src_, src_r, tab) in enumerate(prods):
            p.op("dve", lambda e: e.tensor_tensor(out=tt[j][0][:], in0=src_[:], in1=tab, op=ALU.mult),
                 r=[src_r, tab_r], w=[tt[j][1]])
        wre_p, wre_r = psW[0], psW_r[0]
        wim_p, wim_r = psW[1], psW_r[1]
        p.op("pe", lambda e: e.matmul(wre_p[:], idf[:], tt[0][0][:], start=True, stop=False), r=[idf_r, tt[0][1]], w=[wre_r])
        p.op("pe", lambda e: e.matmul(wre_p[:], idf[:], tt[1][0][:], start=False, stop=True), r=[idf_r, tt[1][1]], w=[wre_r])
        p.op("pe", lambda e: e.matmul(wim_p[:], idf[:], tt[2][0][:], start=True, stop=False), r=[idf_r, tt[2][1]], w=[wim_r])
        p.op("pe", lambda e: e.matmul(wim_p[:], idf[:], tt[3][0][:], start=False, stop=True), r=[idf_r, tt[3][1]], w=[wim_r])

    def stageB(idx):
        b, k = steps[idx]
        u, u_r = ubuf[b]
        kc = k // 4
        i2 = idx % 2
        (vr, vr_r), (vi, vi_r) = vre[i2], vim[i2]
        cT, sT = cosT[:, k, :], sinT[:, k, :]
        wre_p, wre_r = psW[0], psW_r[0]
        wim_p, wim_r = psW[1], psW_r[1]
        rb = rc[:, k:k + 1].to_broadcast([128, 512])
        p.op("dve", lambda e: e.tensor_tensor_scan(out=vr[:], data0=rb, data1=wre_p[:], initial=vin_re[:, k:k + 1],
                                                   op0=ALU.mult, op1=ALU.add), r=[wre_r, rc_r, vin_r], w=[vr_r])
        p.op("dve", lambda e: e.tensor_tensor_scan(out=vi[:], data0=rb, data1=wim_p[:], initial=vin_im[:, k:k + 1],
                                                   op0=ALU.mult, op1=ALU.add), r=[wim_r, rc_r, vin_r], w=[vi_r])
        return (b, k, kc, i2, u, u_r, vr, vr_r, vi, vi_r, cT, sT)

    def stageC(ctx):
        b, k, kc, i2, u, u_r, vr, vr_r, vi, vi_r, cT, sT = ctx
        p.op("dve", lambda e: e.tensor_tensor(out=sm[:, 0:1], in0=vr[:, 511:512], in1=c512[:, k:k + 1], op=ALU.mult),
             r=[vr_r, c512_r], w=[sm_r])
        p.op("dve", lambda e: e.tensor_tensor(out=sm[:, 1:2], in0=vi[:, 511:512], in1=s512[:, k:k + 1], op=ALU.mult),
             r=[vi_r, c512_r], w=[sm_r])
        p.op("dve", lambda e: e.tensor_tensor(out=sm[:, 2:3], in0=vr[:, 511:512], in1=s512[:, k:k + 1], op=ALU.mult),
             r=[vr_r, c512_r], w=[sm_r])
        p.op("dve", lambda e: e.tensor_tensor(out=sm[:, 3:4], in0=vi[:, 511:512], in1=c512[:, k:k + 1], op=ALU.mult),
             r=[vi_r, c512_r], w=[sm_r])
        p.op("dve", lambda e: e.tensor_tensor(out=vin_re[:, k:k + 1], in0=sm[:, 0:1], in1=sm[:, 1:2], op=ALU.subtract),
             r=[sm_r], w=[vin_r])
        p.op("dve", lambda e: e.tensor_tensor(out=vin_im[:, k:k + 1], in0=sm[:, 2:3], in1=sm[:, 3:4], op=ALU.add),
             r=[sm_r], w=[vin_r])
        xx = xq[i2]
        posts = [("dve", vr, vr_r, cT), ("pool", vi, vi_r, sT), ("dve", vr, vr_r, sT), ("pool", vi, vi_r, cT)]
        for j, (eng_, src_, src_r, tab) in enumerate(posts):
            p.op(eng_, lambda e: e.tensor_tensor(out=xx[j][0][:], in0=src_[:], in1=tab, op=ALU.mult),
                 r=[src_r, tab_r], w=[xx[j][1]])
        py, py_r = psY[kc], psY_r[kc]
        kk = k % 4
        cv = [0, 1, 2, 2]
        for j in range(4):
            p.op("pe", lambda e: e.matmul(py[:], ctb[:, k, cv[j], :], xx[j][0][:], start=(kk == 0 and j == 0),
                                          stop=(kk == 3 and j == 3)), r=[ctb_r, xx[j][1]], w=[py_r])
        if kk == 3:
            yob, yob_r = yo[kc]
            p.op("dve", lambda e: e.scalar_tensor_tensor(out=yg[:], in0=u[:, kc, :], scalar=dc[:, kc:kc + 1], in1=py[:],
                                                         op0=ALU.mult, op1=ALU.add), r=[u_r, dc_r, py_r], w=[yg_r])
            p.op("pool", lambda e: e.tensor_tensor(out=tg[:], in0=yg[:], in1=yg[:], op=ALU.mult), r=[yg_r], w=[tg_r])
            p.op("pool", lambda e: e.tensor_scalar(out=tg[:], in0=tg[:], scalar1=0.044715, scalar2=1.0, op0=ALU.mult, op1=ALU.add),
                 r=[tg_r], w=[tg_r])
            p.op("pool", lambda e: e.tensor_tensor(out=tg[:], in0=tg[:], in1=yg[:], op=ALU.mult), r=[tg_r, yg_r], w=[tg_r])
            p.op("act", lambda e: e.activation(out=tg[:], in_=tg[:], func=AF.Sigmoid, scale=1.5957691216), r=[tg_r], w=[tg_r])
            p.op("pool", lambda e: e.tensor_tensor(out=yob[:], in0=tg[:], in1=yg[:], op=ALU.mult), r=[tg_r, yg_r], w=[yob_r])
            p.dma("sp", yT[kc * 128:(kc + 1) * 128, b * 512:(b + 1) * 512], yob[:], r=[yob_r], is_output=True)

    stageA(0)
    for idx in range(len(steps)):
        ctx = stageB(idx)
        if idx + 1 < len(steps):
            stageA(idx + 1)
        stageC(ctx)
    p.finish()
    p.close()
    return p.nc


def _run(nc, in_maps):
    res = run_bass_kernel_spmd(nc, in_maps, core_ids=list(range(8)))
    return res.results


def kernel(**inp):
    inp = {k: np.asarray(v) for k, v in inp.items()}
    x = inp["x"][0]
    idx = [own_idx(c) for c in range(8)]
    xT = [np.ascontiguousarray(x[idx[c]].T) for c in range(8)]
    w_in = np.ascontiguousarray(inp["nsa_w_in"][0])
    g0 = gcol(inp["mix_norm_g"][0])
    l1 = _run(build_p1(), [{"xT": xT[c], "gcol": g0, "w_in": w_in} for c in range(8)])
    shared = p2_shared(inp, l1)
    l2 = _run(build_p2(), [p2_inputs(inp, l1, c, shared) for c in range(8)])
    del shared
    m3 = [{"resT": xT[c], "aT": np.ascontiguousarray(np.asarray(l2[c]["o_out"]).T),
           "wmix": np.ascontiguousarray(inp["nsa_w_out"][0]), "gff": gcol(inp["ffn_norm_g"][0]),
           "gn": gcol(inp["mix_norm_g"][1]), "w1": np.ascontiguousarray(inp["ffn_w_in"][0]),
           "w2": np.ascontiguousarray(inp["ffn_w_out"][0])} for c in range(8)]
    l3 = _run(build_tok(False, False), m3)
    del m3
    prep = _run(build_s5prep(), [s5prep_inputs(inp) for _ in range(8)])[0]
    uT_full = np.zeros((2048, 8192), np.float32)
    for c in range(8):
        uT_full[:, idx[c]] = np.asarray(l3[c]["normT"])
    l4 = _run(build_s5main(), [s5main_inputs(inp, prep, uT_full, c) for c in range(8)])
    y_full = np.concatenate([np.asarray(l4[c]["yT"]) for c in range(8)], axis=0)
    m5 = [{"resT": np.ascontiguousarray(np.asarray(l3[c]["hT"])), "aT": np.ascontiguousarray(y_full[:, idx[c]]),
           "wmix": np.ascontiguousarray(inp["s5_w_glu"][0]), "gff": gcol(inp["ffn_norm_g"][1]),
           "gn": gcol(inp["final_norm_g"]), "w1": np.ascontiguousarray(inp["ffn_w_in"][1]),
           "w2": np.ascontiguousarray(inp["ffn_w_out"][1])} for c in range(8)]
    l5 = _run(build_tok(True, True), m5)
    out = np.zeros((1, 8192, 2048), np.float32)
    for c in range(8):
        out[0, idx[c]] = np.asarray(l5[c]["normT"]).T
    return out
```

```python
import math
from contextlib import ExitStack
import numpy as np
import ml_dtypes
import concourse.bass as bass
import concourse.mybir as mybir
from concourse.bass_utils import run_bass_kernel_spmd


F32 = mybir.dt.float32
BF16 = mybir.dt.bfloat16
AF = mybir.ActivationFunctionType
ALU = mybir.AluOpType
AX = mybir.AxisListType


NO_SELF_WAIT = ()


class Res:
    __slots__ = ("name", "last_w", "readers", "dsem", "dcnt")

    def __init__(self, name):
        self.name = name
        self.last_w = None
        self.readers = []
        self.dsem = None
        self.dcnt = 0


class Prog:
    def __init__(self, num_devices=None):
        self.nc = bass.Bass("TRN2", target_bir_lowering=False, num_devices=num_devices)
        self.es = ExitStack()
        nc = self.nc
        self.eng = {"pe": nc.tensor, "act": nc.scalar, "dve": nc.vector, "pool": nc.gpsimd, "sp": nc.sync}
        self.esem = {}
        self.ecnt = {}
        self.waited = {e: {} for e in self.eng}
        self.semid = {}
        for e in self.eng:
            s = self.es.enter_context(nc.semaphore("es_" + e))
            self.esem[e] = s
            self.ecnt[e] = 0
        self.nsem = len(self.eng)
        self.out_tokens = []
        self.no_self_wait = set(NO_SELF_WAIT)
        self.dma_toks = {}
        self._n = 0
        self._stacks = [self.es]

    def uname(self, base):
        self._n += 1
        return f"{base}_{self._n}"

    def sbuf(self, name, shape, dt):
        return self._stacks[-1].enter_context(self.nc.sbuf_tensor(self.uname(name), list(shape), dt))

    def push_scope(self):
        st = ExitStack()
        self._stacks.append(st)
        return st

    def pop_scope(self):
        self.barrier()
        st = self._stacks.pop()
        st.close()

    def barrier(self):
        for e in self.eng:
            for e2 in self.eng:
                if e2 != e and self.ecnt[e2] > 0:
                    self._wait(e, (self.esem[e2], self.ecnt[e2]))
            for tok in self.dma_toks.values():
                self._wait(e, tok)

    def psum(self, name, shape, dt=F32):
        return self.es.enter_context(self.nc.psum_tensor(self.uname(name), list(shape), dt))

    def dram(self, name, shape, dt, kind):
        return self.nc.dram_tensor(name, list(shape), dt, kind=kind).ap()

    def res(self, name="r"):
        return Res(name)

    def _wait(self, e, tok):
        if tok is None:
            return
        sem, val = tok
        if e == "pe" and sem is self.esem["pe"]:
            return
        if e in self.no_self_wait and sem is self.esem[e]:
            return
        k = id(sem)
        w = self.waited[e]
        if w.get(k, 0) >= val:
            return
        self.eng[e].wait_ge(sem, val)
        w[k] = val

    def _deps(self, e, r, w):
        for x in r:
            self._wait(e, x.last_w)
        for x in w:
            self._wait(e, x.last_w)
            for t in x.readers:
                self._wait(e, t)

    def _commit(self, tok, r, w):
        for x in r:
            x.readers.append(tok)
            if len(x.readers) > 64:
                best = {}
                for s, v in x.readers:
                    if id(s) not in best or best[id(s)][1] < v:
                        best[id(s)] = (s, v)
                x.readers = list(best.values())
        for x in w:
            x.last_w = tok
            x.readers = []

    def op(self, e, fn, r=(), w=()):
        self._deps(e, r, w)
        inst = fn(self.eng[e])
        inst.then_inc(self.esem[e], 1)
        self.ecnt[e] += 1
        tok = (self.esem[e], self.ecnt[e])
        self._commit(tok, r, w)
        return tok

    def dma(self, q, out, in_, r=(), w=(), sres=None, is_output=False, **kw):
        self._deps(q, r, w)
        if sres is None:
            sres = (list(w) + list(r))[0]
        if sres.dsem is None:
            sres.dsem = self.es.enter_context(self.nc.semaphore(self.uname("ds")))
            self.nsem += 1
        inst = self.eng[q].dma_start(out=out, in_=in_, **kw)
        inst.then_inc(sres.dsem, 16)
        sres.dcnt += 16
        tok = (sres.dsem, sres.dcnt)
        self.dma_toks[id(sres.dsem)] = tok
        self._commit(tok, r, w)
        if is_output:
            self.out_tokens.append(tok)
        return tok

    def finish(self):
        best = {}
        for s, v in self.out_tokens:
            if id(s) not in best or best[id(s)][1] < v:
                best[id(s)] = (s, v)
        for s, v in best.values():
            self.eng["sp"].wait_ge(s, v)
        return self.nc

    def close(self):
        self.es.close()


BF = ml_dtypes.bfloat16
NEG = -30000.0


def own_idx(c):
    return np.concatenate([np.arange((8 * i + c) * 128, (8 * i + c) * 128 + 128) for i in range(8)])


def gcol(g):
    return np.ascontiguousarray(np.asarray(g, np.float32).reshape(16, 128).T)


def rel_bucket_np(dist):
    n = np.maximum(dist, 0).astype(np.int64)
    max_exact = 16
    nf = np.maximum(n, 1).astype(np.float32)
    large = max_exact + (np.log(nf / np.float32(max_exact)) / np.float32(np.log(128 / 16)) * np.float32(16)).astype(np.int32)
    large = np.minimum(large, 31)
    return np.where(n < max_exact, n, large).astype(np.int64)


def p2_tables(rel_bias, c):
    rel_bias = np.asarray(rel_bias, np.float32)
    kl = np.arange(128)[:, None]
    ql = np.arange(128)[None, :]
    tb_bias = np.zeros((9, 128, 16, 128), np.float32)
    tb_mask = np.zeros((128, 9, 128), np.float32)
    for jj in range(9):
        j = jj - 1
        r = c - j
        dist = r * 128 + ql - kl
        b = rel_bias[rel_bucket_np(dist)]
        tb_bias[jj] = b.transpose(0, 2, 1)
        tb_mask[:, jj, :] = np.where(dist < 0, NEG, 0.0)
    win_mask = np.zeros((128, 12, 128), np.float32)
    for jp in range(12):
        r = c + 4 - jp
        dist = r * 128 + ql - kl
        win_mask[:, jp, :] = np.where((dist < 0) | (dist >= 512), NEG, 0.0)
    qcol = np.arange(128)[:, None]
    e = np.arange(72)[None, :]
    dist_c = 128 * c + qcol - 16 * (e - 9) - 31
    ec_bias = rel_bias[rel_bucket_np(dist_c)].transpose(0, 2, 1)
    ec_mask = np.where(dist_c < 0, NEG, 0.0).astype(np.float32)
    fq = np.zeros((128, 8, 128), np.float32)
    blk = np.arange(128)[None, :]
    for i in range(8):
        t = (8 * i + c) * 128 + np.arange(128)[:, None]
        tb = t // 64
        f = np.zeros((128, 128), np.float32)
        f = np.where(blk > tb, -1e30, f)
        f = np.where(blk == tb - 1, 1e9, f)
        f = np.where(blk == tb, 2e9, f)
        f = np.where(blk == 0, 3e9, f)
        fq[:, i, :] = f
    return {
        "tb_bias": np.ascontiguousarray(tb_bias.reshape(9, 128, 2048)),
        "tb_mask": np.ascontiguousarray(tb_mask.reshape(128, 9 * 128)),
        "b31": np.ascontiguousarray(np.broadcast_to(rel_bias[31][None, :], (128, 16))),
        "win_mask": np.ascontiguousarray(win_mask.reshape(128, 12 * 128)),
        "ec_bias": np.ascontiguousarray(ec_bias.reshape(128, 16 * 72)),
        "ec_mask": ec_mask,
        "fq": np.ascontiguousarray(fq.reshape(128, 8 * 128)),
    }


def p2_consts():
    key = np.arange(8192)[None, :]
    j = np.arange(128)[:, None]
    R = (key // 64 == j).astype(np.float32).astype(BF)
    ident = np.eye(128, dtype=np.float32).astype(BF)
    return {"Rm": R, "ident": ident}


def p2_inputs(inp, l1, c, shared):
    r = l1[c]
    qT = np.asarray(r["qT"])
    q_l = qT.reshape(4, 4, 128, 8, 128).transpose(3, 0, 2, 1, 4).reshape(8, 4, 128, 512)
    gates = np.asarray(r["gates"]).reshape(8, 128, 48).transpose(1, 0, 2).reshape(128, 8 * 48)
    d = {"q_l": np.ascontiguousarray(q_l), "gates_l": np.ascontiguousarray(gates)}
    d.update(shared)
    d.update(p2_tables(inp["rel_bias"], c))
    return d


def p2_shared(inp, l1):
    sh = {}
    for nm in ("kcT", "vcT", "ksT", "kwT"):
        full = np.zeros((512, 8192), BF)
        for c in range(8):
            full[:, own_idx(c)] = np.asarray(l1[c][nm])
        sh[nm] = np.ascontiguousarray(full.reshape(4, 128, 8192))
    for nm, out in (("vs", "vs1"), ("vw", "vw1")):
        full = np.zeros((8192, 512), BF)
        for c in range(8):
            full[own_idx(c)] = np.asarray(l1[c][nm])
        v = full.reshape(64, 128, 4, 128).transpose(2, 1, 0, 3)
        v1 = np.ones((4, 128, 64, 129), BF)
        v1[..., :128] = v
        sh[out] = np.ascontiguousarray(v1.reshape(4, 128, 64 * 129))
    for kv in ("k", "v"):
        w1 = np.asarray(inp[f"cmp_w1_{kv}"][0], np.float32)
        sh[f"w1{kv}"] = np.ascontiguousarray(w1.reshape(32, 128, 256).transpose(1, 0, 2).reshape(128, 32 * 256))
        w2 = np.asarray(inp[f"cmp_w2_{kv}"][0], np.float32)
        sh[f"w2{kv}"] = np.ascontiguousarray(w2.reshape(2, 128, 128).transpose(1, 0, 2).reshape(128, 256))
        sh[f"posT{kv}"] = np.ascontiguousarray(np.asarray(inp[f"cmp_pos_{kv}"][0], np.float32).T)
    sh.update(p2_consts())
    return sh


def s5prep_inputs(inp):
    return {"A_re": np.ascontiguousarray(inp["s5_A_re"][0]), "A_im": np.ascontiguousarray(inp["s5_A_im"][0]),
            "log_dt": np.ascontiguousarray(inp["s5_log_dt"][0].reshape(128, 1)),
            "B_re": np.ascontiguousarray(inp["s5_B_re"][0].reshape(128, 1024)),
            "B_im": np.ascontiguousarray(inp["s5_B_im"][0].reshape(128, 1024))}


def s5main_inputs(inp, prep, uT_full, c):
    r = np.asarray(prep["o_r"]); th = np.asarray(prep["o_th"])
    bbre = np.asarray(prep["o_bbre"]).reshape(128, 64, 16); bbim = np.asarray(prep["o_bbim"]).reshape(128, 64, 16)
    C_re = np.asarray(inp["s5_C_re"][0]); C_im = np.asarray(inp["s5_C_im"][0])
    D = np.asarray(inp["s5_D"][0])
    BD = np.zeros((128, 8, 2, 128), np.float32)
    CT = np.zeros((128, 8, 2, 128), np.float32)
    rcol = np.zeros((128, 8), np.float32); thcol = np.zeros((128, 8), np.float32)
    for k in range(8):
        kg = 8 * c + k
        for gg in range(2):
            g = 2 * kg + gg
            ch0 = 32 * (k % 4) + 16 * gg
            st0 = 64 * gg
            BD[ch0:ch0 + 16, k, 0, st0:st0 + 64] = bbre[g].T
            BD[ch0:ch0 + 16, k, 1, st0:st0 + 64] = bbim[g].T
            CT[st0:st0 + 64, k, 0, ch0:ch0 + 16] = C_re[g].T
            CT[st0:st0 + 64, k, 1, ch0:ch0 + 16] = C_im[g].T
            rcol[st0:st0 + 64, k] = r[g]
            thcol[st0:st0 + 64, k] = th[g]
    Dcol = np.ascontiguousarray(D[256 * c:256 * c + 256].reshape(2, 128).T)
    iota = np.ascontiguousarray(np.broadcast_to(np.arange(512, dtype=np.float32)[None, :], (128, 512)))
    return {"uT": np.ascontiguousarray(uT_full[256 * c:256 * c + 256]), "BD": BD.reshape(128, -1), "CT": CT.reshape(128, -1),
            "rcol": rcol, "thcol": thcol, "iota": iota, "Dcol": Dcol, "identf": np.eye(128, dtype=np.float32)}


D = 2048
NT = 1024
KC = 16
EPS = 1e-6


def rmsnorm_T(p, xT_dram, gcol_sb, gcol_res, hnT, hnT_res, ones_f, ones_res, out_dt_note=""):
    nc = p.nc
    xs = p.sbuf("xs", [128, KC, 512], F32)
    xs_r = p.res("xs")
    sq = [p.sbuf("sq", [128, 512], F32) for _ in range(2)]
    sq_r = [p.res("sq") for _ in range(2)]
    ss = p.psum("ss", [128, 512])
    ss_r = p.res("ss")
    rstd = p.sbuf("rstd", [128, 512], F32)
    rstd_r = p.res("rstd")
    for half in range(NT // 512):
        src = xT_dram[:, half * 512:(half + 1) * 512].rearrange("(k p) t -> p k t", p=128)
        for kk in range(0, KC, 4):
            p.dma("sp", xs[:, kk:kk + 4, :], src[:, kk:kk + 4, :], w=[xs_r])
        for k in range(KC):
            b = k % 2
            p.op("act", lambda e: e.activation(out=sq[b][:], in_=xs[:, k, :], func=AF.Square),
                 r=[xs_r], w=[sq_r[b]])
            p.op("pe", lambda e: e.matmul(ss[:], ones_f[:], sq[b][:], start=(k == 0), stop=(k == KC - 1)),
                 r=[sq_r[b], ones_res], w=[ss_r])
        p.op("dve", lambda e: e.tensor_scalar(out=rstd[:], in0=ss[:], scalar1=1.0 / D, scalar2=EPS,
                                              op0=ALU.mult, op1=ALU.add), r=[ss_r], w=[rstd_r])
        p.op("act", lambda e: e.activation(out=rstd[:], in_=rstd[:], func=AF.Sqrt), r=[rstd_r], w=[rstd_r])
        p.op("dve", lambda e: e.reciprocal(out=rstd[:], in_=rstd[:]), r=[rstd_r], w=[rstd_r])
        for k in range(KC):
            p.op("dve", lambda e: e.scalar_tensor_tensor(
                out=hnT[:, k, half * 512:(half + 1) * 512], in0=xs[:, k, :], scalar=gcol_sb[:, k:k + 1],
                in1=rstd[:], op0=ALU.mult, op1=ALU.mult), r=[xs_r, rstd_r, gcol_res], w=[hnT_res])


def build_p1(only=None):
    p = Prog()
    nc = p.nc
    xT = p.dram("xT", [D, NT], F32, "ExternalInput")
    gcol = p.dram("gcol", [128, KC], F32, "ExternalInput")
    w_in = p.dram("w_in", [D, 5168], F32, "ExternalInput")
    qT = p.dram("qT", [2048, NT], BF16, "ExternalOutput")
    kT = {n: p.dram(n, [512, NT], BF16, "ExternalOutput") for n in ("kcT", "vcT", "ksT", "kwT")}
    vv = {n: p.dram(n, [NT, 512], BF16, "ExternalOutput") for n in ("vs", "vw")}
    gates = p.dram("gates", [NT, 48], F32, "ExternalOutput")

    ones_f = p.sbuf("ones", [128, 128], F32)
    ones_r = p.res("ones")
    p.op("pool", lambda e: e.memset(ones_f[:], 1.0), w=[ones_r])
    gsb = p.sbuf("gsb", [128, KC], F32)
    g_r = p.res("g")
    p.dma("sp", gsb[:], gcol[:], w=[g_r])
    hnT = p.sbuf("hnT", [128, KC, NT], BF16)
    hnT_r = p.res("hnT")
    rmsnorm_T(p, xT, gsb, g_r, hnT, hnT_r, ones_f, ones_r)

    wst = [p.sbuf("wst", [128, KC, 512], F32) for _ in range(2)]
    wst_r = [p.res("wst") for _ in range(2)]
    wb = [p.sbuf("wb", [128, KC, 512], BF16) for _ in range(2)]
    wb_r = [p.res("wb") for _ in range(2)]
    ps = [p.psum("ps", [128, 512]) for _ in range(4)]
    ps_r = [p.res("ps") for _ in range(4)]
    ot = [p.sbuf("ot", [128, 512], BF16) for _ in range(4)]
    ot_r = [p.res("ot") for _ in range(4)]
    og = p.sbuf("og", [128, 48], F32)
    og_r = p.res("og")
    chunks = [("qT", 0), ("qT", 1), ("qT", 2), ("qT", 3), ("kcT", 0), ("vcT", 0), ("ksT", 0), ("vs", 0),
              ("kwT", 0), ("vw", 0), ("gates", 0)]
    pi = 0
    for ci, (nm, sub) in enumerate(chunks):
        if only is not None and ci not in only:
            continue
        c0 = ci * 512
        ncol = 512 if nm != "gates" else 48
        b = ci % 2
        src = w_in[:, c0:c0 + ncol].rearrange("(k p) c -> p k c", p=128)
        for kk in range(0, KC, 4):
            p.dma("sp", wst[b][:, kk:kk + 4, :ncol], src[:, kk:kk + 4, :], w=[wst_r[b]])
        for kk in range(0, KC, 8):
            p.op("dve", lambda e: e.tensor_copy(out=wb[b][:, kk:kk + 8, :ncol], in_=wst[b][:, kk:kk + 8, :ncol]),
                 r=[wst_r[b]], w=[wb_r[b]])
        if nm in ("qT", "kcT", "vcT", "ksT", "kwT"):
            dst = qT if nm == "qT" else kT[nm]
            scale = 128 ** -0.5 if nm == "qT" else 1.0
            for m in range(4):
                for n in range(NT // 512):
                    pb = pi % 4
                    pi += 1
                    for k in range(KC):
                        p.op("pe", lambda e: e.matmul(ps[pb][:], wb[b][:, k, m * 128:(m + 1) * 128],
                                                      hnT[:, k, n * 512:(n + 1) * 512], start=(k == 0), stop=(k == KC - 1)),
                             r=[wb_r[b], hnT_r], w=[ps_r[pb]])
                    p.op("act", lambda e: e.activation(out=ot[pb][:], in_=ps[pb][:], func=AF.Copy, scale=scale),
                         r=[ps_r[pb]], w=[ot_r[pb]])
                    row0 = sub * 512 + m * 128
                    p.dma("act", dst[row0:row0 + 128, n * 512:(n + 1) * 512], ot[pb][:], r=[ot_r[pb]], is_output=True)
        else:
            for t in range(NT // 128):
                pb = pi % 4
                pi += 1
                for k in range(KC):
                    p.op("pe", lambda e: e.matmul(ps[pb][:, :ncol], hnT[:, k, t * 128:(t + 1) * 128],
                                                  wb[b][:, k, :ncol], start=(k == 0), stop=(k == KC - 1)),
                         r=[wb_r[b], hnT_r], w=[ps_r[pb]])
                if nm == "gates":
                    p.op("act", lambda e: e.activation(out=og[:], in_=ps[pb][:, :48], func=AF.Sigmoid),
                         r=[ps_r[pb]], w=[og_r])
                    p.dma("act", gates[t * 128:(t + 1) * 128, :], og[:], r=[og_r], is_output=True)
                else:
                    p.op("act", lambda e: e.activation(out=ot[pb][:], in_=ps[pb][:], func=AF.Copy),
                         r=[ps_r[pb]], w=[ot_r[pb]])
                    p.dma("act", vv[nm][t * 128:(t + 1) * 128, :], ot[pb][:], r=[ot_r[pb]], is_output=True)
    p.finish()
    p.close()
    return p.nc


NI = 8
G = 4
NCMP = 511


def build_p2(only_groups=None, do_sel=True, do_win=True):
    p = Prog()
    q_l = p.dram("q_l", [NI, G, 128, 512], BF16, "ExternalInput")
    gates_l = p.dram("gates_l", [128, NI * 48], F32, "ExternalInput")
    kcT = p.dram("kcT", [G, 128, 8192], BF16, "ExternalInput")
    vcT = p.dram("vcT", [G, 128, 8192], BF16, "ExternalInput")
    ksT = p.dram("ksT", [G, 128, 8192], BF16, "ExternalInput")
    kwT = p.dram("kwT", [G, 128, 8192], BF16, "ExternalInput")
    vs1 = p.dram("vs1", [G, 128, 64 * 129], BF16, "ExternalInput")
    vw1 = p.dram("vw1", [G, 128, 64 * 129], BF16, "ExternalInput")
    w1 = {"k": p.dram("w1k", [128, 32 * 256], F32, "ExternalInput"), "v": p.dram("w1v", [128, 32 * 256], F32, "ExternalInput")}
    w2 = {"k": p.dram("w2k", [128, 2 * 128], F32, "ExternalInput"), "v": p.dram("w2v", [128, 2 * 128], F32, "ExternalInput")}
    posT = {"k": p.dram("posTk", [128, 32], F32, "ExternalInput"), "v": p.dram("posTv", [128, 32], F32, "ExternalInput")}
    tb_bias = p.dram("tb_bias", [9, 128, 16 * 128], F32, "ExternalInput")
    tb_mask = p.dram("tb_mask", [128, 9 * 128], F32, "ExternalInput")
    b31 = p.dram("b31", [128, 16], F32, "ExternalInput")
    win_mask = p.dram("win_mask", [128, 12 * 128], F32, "ExternalInput")
    ec_bias = p.dram("ec_bias", [128, 16 * 72], F32, "ExternalInput")
    ec_mask = p.dram("ec_mask", [128, 72], F32, "ExternalInput")
    fq = p.dram("fq", [128, NI * 128], F32, "ExternalInput")
    Rm = p.dram("Rm", [128, 8192], BF16, "ExternalInput")
    ident = p.dram("ident", [128, 128], BF16, "ExternalInput")
    o_out = p.dram("o_out", [NI * 128, 2048], BF16, "ExternalOutput")

    def T(name, shape, dt):
        return p.sbuf(name, shape, dt), p.res(name)

    kccT, kccT_r = T("kccT", [128, G, 512], BF16)
    vcc, vcc_r = T("vcc", [128, G, 4, 128], BF16)
    R_sb, R_r = T("R", [128, 8192], BF16)
    id_sb, id_r = T("ident", [128, 128], BF16)
    Tt, Tt_r = T("Tt", [128, 9, 16 * 128], BF16)
    Wm4, Wm4_r = T("Wm4", [128, 12, 4, 128], BF16)
    Fq, Fq_r = T("Fq", [128, NI, 128], F32)
    Ec, Ec_r = T("Ec", [128, 16, 72], F32)
    b31s, b31_r = T("b31", [128, 16], F32)
    gat, gat_r = T("gat", [128, NI, 48], F32)
    psS = [p.psum("psS", [128, 512]) for _ in range(2)]
    psS_r = [p.res("psS") for _ in range(2)]
    psO = [p.psum("psO", [128, 512]) for _ in range(4)]
    psO_r = [p.res("psO") for _ in range(4)]
    psC = psS
    psC_r = psS_r
    psT = p.psum("psT", [128, 1024], BF16)
    psT_r = p.res("psT")
    psX = p.psum("psX", [128, 512])
    psX_r = p.res("psX")

    p.dma("sp", R_sb[:], Rm[:], w=[R_r])
    p.dma("sp", id_sb[:], ident[:], w=[id_r])
    p.dma("sp", Fq[:].rearrange("p i b -> p (i b)"), fq[:], w=[Fq_r])
    p.dma("sp", b31s[:], b31[:], w=[b31_r])
    p.dma("sp", gat[:].rearrange("p i c -> p (i c)"), gates_l[:], w=[gat_r])

    p.push_scope()
    stg = [T("stg", [128, 2048], F32) for _ in range(2)]
    msk, msk_r = T("msk", [128, 12 * 128], F32)
    p.dma("sp", msk[:, 0:9 * 128], tb_mask[:], w=[msk_r])
    for j in range(9):
        sb, sr = stg[j % 2]
        p.dma("sp", sb[:], tb_bias[j], w=[sr])
        v3 = sb[:].rearrange("p (h q) -> p h q", h=16)
        p.op("dve", lambda e: e.tensor_tensor(out=v3, in0=v3, in1=b31s[:].unsqueeze(2).to_broadcast([128, 16, 128]),
                                              op=ALU.subtract), r=[sr, b31_r], w=[sr])
        p.op("dve", lambda e: e.tensor_tensor(out=Tt[:, j, :].rearrange("p (h q) -> p h q", h=16), in0=v3,
                                              in1=msk[:, j * 128:(j + 1) * 128].unsqueeze(1).to_broadcast([128, 16, 128]),
                                              op=ALU.add), r=[sr, msk_r], w=[Tt_r])
    p.dma("sp", msk[:], win_mask[:], w=[msk_r])
    for j in range(12):
        p.op("dve", lambda e: e.tensor_copy(out=Wm4[:, j, :, :],
                                            in_=msk[:, j * 128:(j + 1) * 128].unsqueeze(1).to_broadcast([128, 4, 128])),
             r=[msk_r], w=[Wm4_r])
    sb, sr = stg[0]
    p.dma("sp", sb[:, 0:16 * 72], ec_bias[:], w=[sr])
    p.dma("sp", msk[:, 0:72], ec_mask[:], w=[msk_r])
    v3 = sb[:, 0:16 * 72].rearrange("p (h e) -> p h e", h=16)
    p.op("dve", lambda e: e.tensor_tensor(out=v3, in0=v3, in1=b31s[:].unsqueeze(2).to_broadcast([128, 16, 72]),
                                          op=ALU.subtract), r=[sr, b31_r], w=[sr])
    p.op("dve", lambda e: e.tensor_tensor(out=Ec[:], in0=v3, in1=msk[:, 0:72].unsqueeze(1).to_broadcast([128, 16, 72]),
                                          op=ALU.add), r=[sr, msk_r], w=[Ec_r])
    w1b, w1b_r = T("w1b", [128, 32, 256], BF16)
    w2b, w2b_r = T("w2b", [128, 2, 128], BF16)
    posb, posb_r = T("posb", [128, 32], BF16)
    pw1, pw1_r = T("pw1", [128, 2], F32)
    kvT, kvT_r = T("kvT", [128, 8192], BF16)
    hidT, hidT_r = T("hidT", [128, 2, 512], BF16)
    xg, xg_r = T("xg", [128, 512], F32)
    tg, tg_r = T("tg", [128, 512], F32)
    p.op("dve", lambda e: e.memset(vcc[:], 0.0), w=[vcc_r])
    p.op("dve", lambda e: e.memset(kccT[:], 0.0), w=[kccT_r])
    for kv in ("k", "v"):
        for q4 in range(4):
            sb, sr = stg[q4 % 2]
            p.dma("sp", sb[:], w1[kv][:, q4 * 2048:(q4 + 1) * 2048], w=[sr])
            p.op("dve", lambda e: e.tensor_copy(out=w1b[:, q4 * 8:(q4 + 1) * 8, :].rearrange("p j c -> p (j c)"), in_=sb[:]),
                 r=[sr], w=[w1b_r])
        sb, sr = stg[0]
        p.dma("sp", sb[:, 0:256], w2[kv][:], w=[sr])
        p.op("dve", lambda e: e.tensor_copy(out=w2b[:].rearrange("p a b -> p (a b)"), in_=sb[:, 0:256]), r=[sr], w=[w2b_r])
        sb, sr = stg[1]
        p.dma("sp", sb[:, 0:32], posT[kv][:], w=[sr])
        p.op("dve", lambda e: e.tensor_copy(out=posb[:], in_=sb[:, 0:32]), r=[sr], w=[posb_r])
        for hc in range(2):
            for j in range(32):
                p.op("pe", lambda e: e.matmul(psX[:, 0:1], w1b[:, j, hc * 128:(hc + 1) * 128], posb[:, j:j + 1],
                                              start=(j == 0), stop=(j == 31)), r=[w1b_r, posb_r], w=[psX_r])
            p.op("dve", lambda e: e.tensor_copy(out=pw1[:, hc:hc + 1], in_=psX[:, 0:1]), r=[psX_r], w=[pw1_r])
        src = kcT if kv == "k" else vcT
        for g in range(G):
            p.dma("sp", kvT[:], src[g], w=[kvT_r])
            for hc in range(2):
                pc = psC[hc]
                for j in range(32):
                    p.op("pe", lambda e: e.matmul(pc[:, 0:NCMP], w1b[:, j, hc * 128:(hc + 1) * 128],
                                                  kvT[:, j:j + 16 * (NCMP - 1) + 1:16], start=(j == 0), stop=(j == 31)),
                         r=[w1b_r, kvT_r], w=[psC_r[hc]])
                p.op("act", lambda e: e.activation(out=xg[:, 0:NCMP], in_=pc[:, 0:NCMP], func=AF.Identity,
                                                   bias=pw1[:, hc:hc + 1]), r=[psC_r[hc], pw1_r], w=[xg_r])
                p.op("dve", lambda e: e.tensor_tensor(out=tg[:, 0:NCMP], in0=xg[:, 0:NCMP], in1=xg[:, 0:NCMP], op=ALU.mult),
                     r=[xg_r], w=[tg_r])
                p.op("dve", lambda e: e.tensor_scalar(out=tg[:, 0:NCMP], in0=tg[:, 0:NCMP], scalar1=0.044715, scalar2=1.0,
                                                      op0=ALU.mult, op1=ALU.add), r=[tg_r], w=[tg_r])
                p.op("dve", lambda e: e.tensor_tensor(out=tg[:, 0:NCMP], in0=tg[:, 0:NCMP], in1=xg[:, 0:NCMP], op=ALU.mult),
                     r=[tg_r, xg_r], w=[tg_r])
                p.op("act", lambda e: e.activation(out=tg[:, 0:NCMP], in_=tg[:, 0:NCMP], func=AF.Sigmoid, scale=1.5957691216),
                     r=[tg_r], w=[tg_r])
                p.op("dve", lambda e: e.tensor_tensor(out=hidT[:, hc, 0:NCMP], in0=tg[:, 0:NCMP], in1=xg[:, 0:NCMP], op=ALU.mult),
                     r=[tg_r, xg_r], w=[hidT_r])
            if kv == "k":
                for hc in range(2):
                    p.op("pe", lambda e: e.matmul(psX[:, 0:NCMP], w2b[:, hc, :], hidT[:, hc, 0:NCMP],
                                                  start=(hc == 0), stop=(hc == 1)), r=[w2b_r, hidT_r], w=[psX_r])
                p.op("act", lambda e: e.activation(out=kccT[:, g, 0:NCMP], in_=psX[:, 0:NCMP], func=AF.Copy),
                     r=[psX_r], w=[kccT_r])
            else:
                for cb in range(4):
                    n = min(NCMP, (cb + 1) * 128) - cb * 128
                    for hc in range(2):
                        p.op("pe", lambda e: e.matmul(psX[0:n, cb * 128:(cb + 1) * 128], hidT[:, hc, cb * 128:cb * 128 + n],
                                                      w2b[:, hc, :], start=(hc == 0), stop=(hc == 1)),
                             r=[w2b_r, hidT_r], w=[psX_r])
                    p.op("act", lambda e: e.activation(out=vcc[0:n, g, cb, :], in_=psX[0:n, cb * 128:(cb + 1) * 128], func=AF.Copy),
                         r=[psX_r], w=[vcc_r])
    p.pop_scope()

    ksT_s, ksT_r = T("ksT", [128, 8192], BF16)
    kwT_s, kwT_r = T("kwT", [128, 8192], BF16)
    vs_s, vs_r = T("vs1", [128, 64, 129], BF16)
    vw_s, vw_r = T("vw1", [128, 64, 129], BF16)
    qg = [T("qg", [128, 512], BF16) for _ in range(2)]
    s4, s4_r = T("s4", [128, 4, 512], F32)
    e4, e4_r = T("e4", [128, 4, 512], F32)
    pb4, pb4_r = T("pb4", [128, 4, 512], BF16)
    pT4, pT4_r = T("pT4", [128, 4, 4, 128], BF16)
    st4, st4_r = T("st4", [128, 4, 8], F32)
    imp, imp_r = T("imp", [128, 520], F32)
    sc, sc_r = T("sc", [128, 128], F32)
    wk, wk_r = T("wk", [128, 128], F32)
    m8, m8_r = T("m8", [128, 16], F32)
    st1, st1_r = T("st1", [128, 8], F32)
    negm, negm_r = T("negm", [128, 128], BF16)
    nmT2 = [T("nmT", [128, 4, 128], BF16) for _ in range(2)]
    PT = [T("PT", [128, 512], BF16) for _ in range(3)]
    OA = [T("oacc", [128, 512], F32) for _ in range(2)]
    obf = [T("obf", [128, 512], BF16) for _ in range(2)]
    coef, coef_r = T("coef", [128, 8], F32)
    cnt = {"S": 0, "P": 0, "q": 0, "c": 0}

    def attend(i, g, qv, q_r, kts, K_s, K_r, V_s, V_r, extra, gate_col0, init_acc, oacc, oacc_r, filler=None, per_iter=0):
        n = len(kts)
        slots = {}

        def qk(idx):
            kt = kts[idx]
            sb_ = cnt["S"] % 2
            cnt["S"] += 1
            slots[idx] = sb_
            mms = [(K_s[:, kt * 128:(kt + 1) * 128], K_r, qv[:], q_r)] + extra(kt)
            for mi, (lt, lr, rh, rr) in enumerate(mms):
                p.op("pe", lambda e: e.matmul(psS[sb_][:], lt, rh, start=(mi == 0), stop=(mi == len(mms) - 1)),
                     r=[lr, rr], w=[psS_r[sb_]])

        qk(0)
        for idx, kt in enumerate(kts):
            if idx + 1 < n:
                qk(idx + 1)
            sb_ = slots.pop(idx)
            pb_ = cnt["P"] % 3
            cnt["P"] += 1
            Pt, Pt_r = PT[pb_]
            p.op("act", lambda e: e.activation(out=Pt[:], in_=psS[sb_][:], func=AF.Exp), r=[psS_r[sb_]], w=[Pt_r])
            for h in range(4):
                bank = h
                c0 = 0
                p.op("pe", lambda e: e.matmul(psO[bank][:, c0:c0 + 129], Pt[:, h * 128:(h + 1) * 128], V_s[:, kt, :],
                                              start=(idx == 0), stop=(idx == n - 1)), r=[Pt_r, V_r], w=[psO_r[bank]])
            if filler is not None:
                for _ in range(per_iter):
                    next(filler, None)
        for h in range(4):
            bank = h
            c0 = 0
            head = g * 4 + h
            p.op("dve", lambda e: e.reciprocal(out=coef[:, h:h + 1], in_=psO[bank][:, c0 + 128:c0 + 129]),
                 r=[psO_r[bank]], w=[coef_r])
            p.op("dve", lambda e: e.tensor_tensor(out=coef[:, h:h + 1], in0=coef[:, h:h + 1],
                                                  in1=gat[:, i, gate_col0 + head:gate_col0 + head + 1], op=ALU.mult),
                 r=[coef_r, gat_r], w=[coef_r])
            dst = oacc[:, h * 128:(h + 1) * 128]
            if init_acc:
                p.op("dve", lambda e: e.tensor_scalar(out=dst, in0=psO[bank][:, c0:c0 + 128], scalar1=coef[:, h:h + 1],
                                                      scalar2=None, op0=ALU.mult), r=[psO_r[bank], coef_r], w=[oacc_r])
            else:
                p.op("dve", lambda e: e.scalar_tensor_tensor(out=dst, in0=psO[bank][:, c0:c0 + 128], scalar=coef[:, h:h + 1],
                                                             in1=dst, op0=ALU.mult, op1=ALU.add),
                     r=[psO_r[bank], coef_r, oacc_r], w=[oacc_r])


    groups = list(range(G)) if only_groups is None else only_groups
    items = [(g, i) for g in groups for i in range(NI)]

    def prep_item(n):
        g, i = items[n]
        qv, q_r = qg[n % 2]
        p.dma("sp", qv[:], q_l[i, g], w=[q_r])
        oacc, oacc_r = OA[n % 2]
        nmT, nmT_r = nmT2[n % 2]
        ncv = min(NCMP, 64 * i + 63)
        e_lo = 64 * i - 9
        return dict(n=n, g=g, i=i, qv=qv, q_r=q_r, oacc=oacc, oacc_r=oacc_r, nmT=nmT, nmT_r=nmT_r, ncv=ncv, e_lo=e_lo,
                    c_lo=max(0, e_lo), c_hi=min(ncv, 64 * i + 63), ncb=(ncv + 127) // 128, gsl=slice(g * 4, (g + 1) * 4))

    def comp_topk_gen(c):
        g, i, qv, q_r, oacc, oacc_r = c["g"], c["i"], c["qv"], c["q_r"], c["oacc"], c["oacc_r"]
        ncv, e_lo, c_lo, c_hi, ncb, gsl = c["ncv"], c["e_lo"], c["c_lo"], c["c_hi"], c["ncb"], c["gsl"]
        nmT, nmT_r = c["nmT"], c["nmT_r"]
        for h in range(4):
            p.op("pe", lambda e: e.matmul(psX[:, 0:ncv], qv[:, h * 128:(h + 1) * 128], kccT[:, g, 0:ncv], start=True, stop=True),
                 r=[q_r, kccT_r], w=[psX_r])
            yield
            p.op("act", lambda e: e.activation(out=s4[:, h, 0:ncv], in_=psX[:, 0:ncv], func=AF.Copy), r=[psX_r], w=[s4_r])
            yield
        p.op("dve", lambda e: e.tensor_tensor(out=s4[:, :, c_lo:c_hi], in0=s4[:, :, c_lo:c_hi],
                                              in1=Ec[:, gsl, c_lo - e_lo:c_hi - e_lo], op=ALU.add), r=[s4_r, Ec_r], w=[s4_r])
        yield
        p.op("dve", lambda e: e.tensor_reduce(out=st4[:, :, 0], in_=s4[:, :, 0:ncv], axis=AX.X, op=ALU.max), r=[s4_r], w=[st4_r])
        yield
        p.op("dve", lambda e: e.tensor_scalar(out=st4[:, :, 1], in0=st4[:, :, 0], scalar1=-1000.0, scalar2=-1.0,
                                              op0=ALU.max, op1=ALU.mult), r=[st4_r], w=[st4_r])
        p.op("dve", lambda e: e.memset(st4[:, :, 2], 0.0), w=[st4_r])
        yield
        for h in range(4):
            p.op("act", lambda e: e.activation(out=e4[:, h, 0:ncv], in_=s4[:, h, 0:ncv], func=AF.Exp, bias=st4[:, h, 1:2],
                                               accum_out=st4[:, h, 2:3]), r=[s4_r, st4_r], w=[e4_r, st4_r])
            yield
        p.op("dve", lambda e: e.tensor_scalar(out=st4[:, :, 3], in0=st4[:, :, 2], scalar1=1e-30, scalar2=None, op0=ALU.add),
             r=[st4_r], w=[st4_r])
        p.op("dve", lambda e: e.reciprocal(out=st4[:, :, 4], in_=st4[:, :, 3]), r=[st4_r], w=[st4_r])
        p.op("dve", lambda e: e.tensor_tensor(out=st4[:, :, 5], in0=st4[:, :, 4], in1=gat[:, i, gsl], op=ALU.mult),
             r=[st4_r, gat_r], w=[st4_r])
        yield
        p.op("dve", lambda e: e.tensor_tensor(out=s4[:, :, 0:ncv], in0=e4[:, :, 0:ncv],
                                              in1=st4[:, :, 4:5].to_broadcast([128, 4, ncv]), op=ALU.mult),
             r=[e4_r, st4_r], w=[s4_r])
        yield
        p.op("pool", lambda e: e.memset(imp[:], 0.0), w=[imp_r])
        p.op("dve", lambda e: e.tensor_reduce(out=imp[:, 1:1 + ncv], in_=s4[:, :, 0:ncv].rearrange("p h c -> p c h"),
                                              axis=AX.X, op=ALU.add), r=[s4_r], w=[imp_r])
        yield
        p.op("pool", lambda e: e.memset(pb4[:], 0.0), w=[pb4_r])
        p.op("dve", lambda e: e.tensor_tensor(out=pb4[:, :, 0:ncv], in0=e4[:, :, 0:ncv],
                                              in1=st4[:, :, 5:6].to_broadcast([128, 4, ncv]), op=ALU.mult),
             r=[e4_r, st4_r], w=[pb4_r])
        yield
        if do_sel:
            iv = imp[:, 0:512].rearrange("p (j f) -> p j f", f=4)
            p.op("dve", lambda e: e.tensor_reduce(out=sc[:], in_=iv, axis=AX.X, op=ALU.add), r=[imp_r], w=[sc_r])
            yield
            p.op("dve", lambda e: e.tensor_tensor(out=sc[:], in0=sc[:], in1=imp[:, 4:516].rearrange("p (j f) -> p j f", f=4)[:, :, 0],
                                                  op=ALU.add), r=[sc_r, imp_r], w=[sc_r])
            p.op("dve", lambda e: e.tensor_tensor(out=sc[:], in0=sc[:], in1=Fq[:, i, :], op=ALU.add), r=[sc_r, Fq_r], w=[sc_r])
            yield
            p.op("dve", lambda e: e.max(out=m8[:, 0:8], in_=sc[:]), r=[sc_r], w=[m8_r])
            p.op("dve", lambda e: e.match_replace(out=wk[:], in_to_replace=m8[:, 0:8], in_values=sc[:], imm_value=-3.0e38),
                 r=[sc_r, m8_r], w=[wk_r])
            yield
            p.op("dve", lambda e: e.max(out=m8[:, 8:16], in_=wk[:]), r=[wk_r], w=[m8_r])
            p.op("dve", lambda e: e.tensor_scalar(out=wk[:], in0=sc[:], scalar1=m8[:, 15:16], scalar2=None, op0=ALU.is_ge),
                 r=[sc_r, m8_r], w=[wk_r])
            p.op("dve", lambda e: e.tensor_scalar(out=negm[:], in0=wk[:], scalar1=1.0, scalar2=-NEG, op0=ALU.subtract, op1=ALU.mult),
                 r=[wk_r], w=[negm_r])
            yield
        for hp in range(2):
            for h in (2 * hp, 2 * hp + 1):
                for cb in range(ncb):
                    c0 = (h % 2) * 512 + cb * 128
                    p.op("pe", lambda e: e.transpose(psT[:, c0:c0 + 128], pb4[:, h, cb * 128:(cb + 1) * 128], id_sb[:]),
                         r=[pb4_r, id_r], w=[psT_r])
                yield
            for h in (2 * hp, 2 * hp + 1):
                c0 = (h % 2) * 512
                p.op("act", lambda e: e.activation(out=pT4[:, h, 0:ncb, :].rearrange("p a b -> p (a b)"),
                                                   in_=psT[:, c0:c0 + ncb * 128], func=AF.Copy), r=[psT_r], w=[pT4_r])
                yield
        for h in range(4):
            for cb in range(ncb):
                p.op("pe", lambda e: e.matmul(psX[:, 0:128], pT4[:, h, cb, :], vcc[:, g, cb, :], start=(cb == 0), stop=(cb == ncb - 1)),
                     r=[pT4_r, vcc_r], w=[psX_r])
            yield
            p.op("act", lambda e: e.activation(out=oacc[:, h * 128:(h + 1) * 128], in_=psX[:, 0:128], func=AF.Copy),
                 r=[psX_r], w=[oacc_r])
            yield
        if do_sel:
            p.op("pe", lambda e: e.transpose(psT[:, 0:128], negm[:], id_sb[:]), r=[negm_r, id_r], w=[psT_r])
            yield
            p.op("act", lambda e: e.activation(out=nmT[:], in_=psT[:, 0:128].unsqueeze(1).to_broadcast([128, 4, 128]), func=AF.Copy),
                 r=[psT_r], w=[nmT_r])
            yield

    N_STEPS = 46

    def run_attends(c, filler):
        g, i, qv, q_r, oacc, oacc_r, nmT, nmT_r = c["g"], c["i"], c["qv"], c["q_r"], c["oacc"], c["oacc_r"], c["nmT"], c["nmT_r"]
        kts_w = [kt for kt in range(8 * i - 4, 8 * i + 8) if kt >= 0] if do_win else []
        kts_s = list(range(0, 8 * i + 8)) if do_sel else []
        tot = max(1, len(kts_w) + len(kts_s))
        per_iter = (N_STEPS + tot - 1) // tot if filler is not None else 0
        if do_sel:
            def extra_sel(kt):
                ex = [(R_sb[:, kt * 128:(kt + 1) * 128], R_r, nmT[:].rearrange("p a b -> p (a b)"), nmT_r)]
                j = kt - 8 * i
                if j >= -1:
                    ex.append((id_sb[:], id_r, Tt[:, j + 1, g * 512:(g + 1) * 512], Tt_r))
                return ex
            attend(i, g, qv, q_r, kts_s, ksT_s, ksT_r, vs_s, vs_r, extra_sel, 16, False, oacc, oacc_r, filler=filler, per_iter=per_iter)
        if do_win:
            def extra_win(kt):
                jp = kt - (8 * i - 4)
                ex = [(id_sb[:], id_r, Wm4[:, jp, :, :].rearrange("p a b -> p (a b)"), Wm4_r)]
                if jp >= 3:
                    ex.append((id_sb[:], id_r, Tt[:, jp - 3, g * 512:(g + 1) * 512], Tt_r))
                return ex
            attend(i, g, qv, q_r, kts_w, kwT_s, kwT_r, vw_s, vw_r, extra_win, 32, False, oacc, oacc_r, filler=filler, per_iter=per_iter)
        if filler is not None:
            for _ in filler:
                pass
        ob, ob_r = obf[c["n"] % 2]
        p.op("act", lambda e: e.activation(out=ob[:], in_=oacc[:], func=AF.Copy), r=[oacc_r], w=[ob_r])
        p.dma("sp", o_out[i * 128:(i + 1) * 128, g * 512:(g + 1) * 512], ob[:], r=[ob_r], is_output=True)

    cur = prep_item(0)
    for _ in comp_topk_gen(cur):
        pass
    for n in range(len(items)):
        g, i = items[n]
        if i == 0:
            p.dma("sp", ksT_s[:], ksT[g], w=[ksT_r])
            p.dma("sp", kwT_s[:], kwT[g], w=[kwT_r])
            p.dma("sp", vs_s[:].rearrange("p k d -> p (k d)"), vs1[g], w=[vs_r])
            p.dma("sp", vw_s[:].rearrange("p k d -> p (k d)"), vw1[g], w=[vw_r])
        if n + 1 < len(items):
            nxt = prep_item(n + 1)
            gen = comp_topk_gen(nxt)
        else:
            nxt, gen = None, None
        run_attends(cur, gen)
        cur = nxt
    p.finish()
    p.close()
    return p.nc


DFF = 5632
MC = DFF // 128
MG = 2
MGC = MC // MG


def max_tokens(toks):
    best = {}
    for s, v in toks:
        if id(s) not in best or best[id(s)][1] < v:
            best[id(s)] = (s, v)
    return list(best.values())


class Fence:
    def __init__(self):
        self.toks = []

    def add(self, tok):
        self.toks.append(tok)
        if len(self.toks) > 256:
            self.toks = max_tokens(self.toks)

    def wait(self, p, e):
        for t in max_tokens(self.toks):
            p._wait(e, t)


def rmsnorm_T2(p, src_dram, fence, gsb, g_r, ones_f, ones_r, st, out_sb=None, out_sb_r=None, out_dram=None,
               out_fence=None, is_output=False):
    xq, xq_r, sq, sq_r, ss, ss_r, rstd, rstd_r, uo, uo_r = st
    n = 0
    for half in range(NT // 512):
        if fence is not None:
            fence.wait(p, "act")
        hs = slice(half * 512, (half + 1) * 512)
        for k in range(KC):
            b = n % 4
            n += 1
            p.dma("act", xq[b][:], src_dram[k * 128:(k + 1) * 128, hs], w=[xq_r[b]])
            b2 = k % 2
            p.op("act", lambda e: e.activation(out=sq[b2][:], in_=xq[b][:], func=AF.Square), r=[xq_r[b]], w=[sq_r[b2]])
            p.op("pe", lambda e: e.matmul(ss[:], ones_f[:], sq[b2][:], start=(k == 0), stop=(k == KC - 1)),
                 r=[sq_r[b2], ones_r], w=[ss_r])
        p.op("dve", lambda e: e.tensor_scalar(out=rstd[:], in0=ss[:], scalar1=1.0 / D, scalar2=EPS,
                                              op0=ALU.mult, op1=ALU.add), r=[ss_r], w=[rstd_r])
        p.op("act", lambda e: e.activation(out=rstd[:], in_=rstd[:], func=AF.Sqrt), r=[rstd_r], w=[rstd_r])
        p.op("dve", lambda e: e.reciprocal(out=rstd[:], in_=rstd[:]), r=[rstd_r], w=[rstd_r])
        for k in range(KC):
            b = n % 4
            n += 1
            p.dma("act", xq[b][:], src_dram[k * 128:(k + 1) * 128, hs], w=[xq_r[b]])
            if out_sb is not None:
                p.op("dve", lambda e: e.scalar_tensor_tensor(
                    out=out_sb[:, k, hs], in0=xq[b][:], scalar=gsb[:, k:k + 1],
                    in1=rstd[:], op0=ALU.mult, op1=ALU.mult), r=[xq_r[b], rstd_r, g_r], w=[out_sb_r])
            else:
                b2 = k % 2
                p.op("dve", lambda e: e.scalar_tensor_tensor(
                    out=uo[b2][:], in0=xq[b][:], scalar=gsb[:, k:k + 1],
                    in1=rstd[:], op0=ALU.mult, op1=ALU.mult), r=[xq_r[b], rstd_r, g_r], w=[uo_r[b2]])
                tok = p.dma("act", out_dram[k * 128:(k + 1) * 128, hs], uo[b2][:],
                            r=[uo_r[b2]], is_output=is_output)
                if out_fence is not None:
                    out_fence.add(tok)


def build_tok(glu, final):
    p = Prog()
    resT = p.dram("resT", [D, NT], F32, "ExternalInput")
    aT = p.dram("aT", [D, NT], BF16, "ExternalInput")
    wmix = p.dram("wmix", [D, 4096 if glu else 2048], F32, "ExternalInput")
    gff = p.dram("gff", [128, KC], F32, "ExternalInput")
    gn = p.dram("gn", [128, KC], F32, "ExternalInput")
    w1 = p.dram("w1", [D, 2 * DFF], F32, "ExternalInput")
    w2 = p.dram("w2", [DFF, D], F32, "ExternalInput")
    hmidT = p.dram("hmidT", [D, NT], F32, "ExternalOutput")
    hT = p.dram("hT", [D, NT], F32, "ExternalOutput")
    normT = p.dram("normT", [D, NT], F32, "ExternalOutput")

    def T(name, shape, dt=F32):
        return p.sbuf(name, shape, dt), p.res(name)

    ones_f, ones_r = T("ones", [128, 128])
    p.op("dve", lambda e: e.memset(ones_f[:], 1.0), w=[ones_r])
    gffs, gff_r = T("gffs", [128, KC])
    p.dma("sp", gffs[:], gff[:], w=[gff_r])
    gns, gn_r = T("gns", [128, KC])
    p.dma("sp", gns[:], gn[:], w=[gn_r])
    xq = [T("xq", [128, 512]) for _ in range(4)]
    st = ([t for t, _ in xq], [r for _, r in xq],
          [p.sbuf("sq", [128, 512], F32) for _ in range(2)], [p.res("sq") for _ in range(2)],
          p.psum("ss", [128, 512]), p.res("ss"),
          p.sbuf("rstd", [128, 512], F32), p.res("rstd"),
          [p.sbuf("uo", [128, 512], F32) for _ in range(2)], [p.res("uo") for _ in range(2)])
    big, big_r = T("big", [128, MGC * NT], BF16)
    a_sb = big[:, 0:KC * NT].rearrange("p (k t) -> p k t", k=KC)
    actT = big[:, :].rearrange("p (m t) -> p m t", m=MGC)
    hnT, hnT_r = T("hnT", [128, KC, NT], BF16)
    NS = 3
    wst = [T("wst", [128, KC * 256]) for _ in range(NS)]
    NBF = 4
    wbf = [T("wbf", [128, KC * 256], BF16) for _ in range(NBF)]
    ps = [p.psum("ps", [128, 512]) for _ in range(6)]
    ps_r = [p.res("ps") for _ in range(6)]
    xc = [T("xc", [128, 512]) for _ in range(3)]
    t1 = [T("t1", [128, 512]) for _ in range(2)]
    cnt = {"w": 0, "b": 0, "ps": 0, "x": 0, "t": 0}

    def load_w(wd, row0, kc, col0, ncol):
        i = cnt["w"] % NS
        cnt["w"] += 1
        j = cnt["b"] % NBF
        cnt["b"] += 1
        ws, ws_r = wst[i]
        wb, wb_r = wbf[j]
        sv = ws[:, 0:kc * ncol].rearrange("p (k c) -> p k c", k=kc)
        bv = wb[:, 0:kc * ncol].rearrange("p (k c) -> p k c", k=kc)
        src = wd[row0:row0 + kc * 128, col0:col0 + ncol].rearrange("(k p) c -> p k c", p=128)
        h = (kc + 1) // 2
        p.dma("sp", sv[:, 0:h, :], src[:, 0:h, :], w=[ws_r])
        p.dma("sp", sv[:, h:kc, :], src[:, h:kc, :], w=[ws_r])
        p.op("dve", lambda e: e.tensor_copy(out=wb[:, 0:kc * ncol], in_=ws[:, 0:kc * ncol]), r=[ws_r], w=[wb_r])
        return bv, wb_r

    def mm(wv, w_r, inT, in_r, kc, half):
        pb = cnt["ps"] % 6
        cnt["ps"] += 1
        for k in range(kc):
            p.op("pe", lambda e: e.matmul(ps[pb][:], wv[:, k, :], inT[:, k, half * 512:(half + 1) * 512],
                                          start=(k == 0), stop=(k == kc - 1)), r=[w_r, in_r], w=[ps_r[pb]])
        return pb

    def prefetched(reqs):
        nxt = load_w(*reqs[0]) if reqs else None
        for n_ in range(len(reqs)):
            cur = nxt
            nxt = load_w(*reqs[n_ + 1]) if n_ + 1 < len(reqs) else None
            yield cur

    srcA = aT.rearrange("(k p) t -> p k t", p=128)
    for kk in range(0, KC, 4):
        p.dma("sp", a_sb[:, kk:kk + 4, :], srcA[:, kk:kk + 4, :], w=[big_r])
    f_mid = Fence()
    reqs = []
    for f in range(KC):
        reqs.append((wmix, 0, KC, f * 128, 128))
        if glu:
            reqs.append((wmix, 0, KC, 2048 + f * 128, 128))
    it = prefetched(reqs)
    for f in range(KC):
        wv, w_r = next(it)
        if glu:
            wv2, w2_r = next(it)
        for half in range(2):
            hs = slice(half * 512, (half + 1) * 512)
            xb = cnt["x"] % 3
            cnt["x"] += 1
            xcb, xcb_r = xc[xb]
            p.dma("act", xcb[:], resT[f * 128:(f + 1) * 128, hs], w=[xcb_r])
            pb = mm(wv, w_r, a_sb, big_r, KC, half)
            if glu:
                pb2 = mm(wv2, w2_r, a_sb, big_r, KC, half)
                tb = cnt["t"] % 2
                cnt["t"] += 1
                tt, tt_r = t1[tb]
                p.op("act", lambda e: e.activation(out=tt[:], in_=ps[pb2][:], func=AF.Sigmoid), r=[ps_r[pb2]], w=[tt_r])
                p.op("dve", lambda e: e.tensor_tensor(out=tt[:], in0=ps[pb][:], in1=tt[:], op=ALU.mult),
                     r=[ps_r[pb], tt_r], w=[tt_r])
                p.op("dve", lambda e: e.tensor_tensor(out=xcb[:], in0=xcb[:], in1=tt[:], op=ALU.add),
                     r=[tt_r, xcb_r], w=[xcb_r])
            else:
                p.op("dve", lambda e: e.tensor_tensor(out=xcb[:], in0=ps[pb][:], in1=xcb[:], op=ALU.add),
                     r=[ps_r[pb], xcb_r], w=[xcb_r])
            tok = p.dma("act", hmidT[f * 128:(f + 1) * 128, hs], xcb[:], r=[xcb_r], is_output=True)
            f_mid.add(tok)
    rmsnorm_T2(p, hmidT, f_mid, gffs, gff_r, ones_f, ones_r, st, out_sb=hnT, out_sb_r=hnT_r)
    src_res, src_fence = hmidT, f_mid
    for mg in range(MG):
        reqs = []
        for m2 in range(0, MGC, 2):
            m = mg * MGC + m2
            reqs.append((w1, 0, KC, m * 128, 256))
            reqs.append((w1, 0, KC, DFF + m * 128, 256))
        it = prefetched(reqs)
        for m2 in range(0, MGC, 2):
            wa, wa_r = next(it)
            wb_, wb_r = next(it)
            for mm_ in range(2):
                ml = m2 + mm_
                for half in range(2):
                    pa = mm(wa[:, :, mm_ * 128:(mm_ + 1) * 128], wa_r, hnT, hnT_r, KC, half)
                    pbb = mm(wb_[:, :, mm_ * 128:(mm_ + 1) * 128], wb_r, hnT, hnT_r, KC, half)
                    tb = cnt["t"] % 2
                    cnt["t"] += 1
                    tt, tt_r = t1[tb]
                    p.op("act", lambda e: e.activation(out=tt[:], in_=ps[pa][:], func=AF.Silu), r=[ps_r[pa]], w=[tt_r])
                    p.op("dve", lambda e: e.tensor_tensor(out=actT[:, ml, half * 512:(half + 1) * 512], in0=ps[pbb][:],
                                                          in1=tt[:], op=ALU.mult), r=[ps_r[pbb], tt_r], w=[big_r])
        f_new = Fence()
        reqs = [(w2, mg * MGC * 128, MGC, f * 128, 128) for f in range(KC)]
        it = prefetched(reqs)
        for f in range(KC):
            wo, wo_r = next(it)
            for half in range(2):
                hs = slice(half * 512, (half + 1) * 512)
                pb = cnt["ps"] % 6
                cnt["ps"] += 1
                for ml in range(MGC):
                    p.op("pe", lambda e: e.matmul(ps[pb][:], wo[:, ml, :], actT[:, ml, hs], start=(ml == 0), stop=(ml == MGC - 1)),
                         r=[wo_r, big_r], w=[ps_r[pb]])
                xb = cnt["x"] % 3
                cnt["x"] += 1
                xcb, xcb_r = xc[xb]
                src_fence.wait(p, "act")
                p.dma("act", xcb[:], src_res[f * 128:(f + 1) * 128, hs], w=[xcb_r])
                p.op("dve", lambda e: e.tensor_tensor(out=xcb[:], in0=ps[pb][:], in1=xcb[:], op=ALU.add),
                     r=[ps_r[pb], xcb_r], w=[xcb_r])
                dst = hT if mg == MG - 1 else normT
                tok = p.dma("act", dst[f * 128:(f + 1) * 128, hs], xcb[:], r=[xcb_r], is_output=True)
                f_new.add(tok)
        src_res, src_fence = (hT if mg == MG - 1 else normT), f_new
    rmsnorm_T2(p, hT, src_fence, gns, gn_r, ones_f, ones_r, st, out_dram=normT, is_output=True)
    p.finish()
    p.close()
    return p.nc


TWO_PI = 2.0 * math.pi


I32 = mybir.dt.int32


def sincos(p, T, ang, ang_r, s_out, c_out, out_r, tmp, tmp_r, shape_sl, n):
    sl = shape_sl
    tl = (slice(None), slice(0, n))
    if not hasattr(p, "_sc_tmp"):
        p._sc_tmp = (T("sc_ki", [128, 512], I32), T("sc_kf", [128, 512]), T("sc_y", [128, 512]), T("sc_m", [128, 512]))
    (ki, ki_r), (kf, kf_r), (y, y_r), (m, m_r) = p._sc_tmp
    p.op("dve", lambda e: e.tensor_scalar(out=kf[tl], in0=ang[sl], scalar1=1.0 / TWO_PI, scalar2=None, op0=ALU.mult),
         r=[ang_r], w=[kf_r])
    p.op("dve", lambda e: e.tensor_copy(out=ki[tl], in_=kf[tl]), r=[kf_r], w=[ki_r])
    p.op("dve", lambda e: e.tensor_copy(out=kf[tl], in_=ki[tl]), r=[ki_r], w=[kf_r])
    p.op("dve", lambda e: e.scalar_tensor_tensor(out=y[tl], in0=kf[tl], scalar=-TWO_PI, in1=ang[sl], op0=ALU.mult, op1=ALU.add),
         r=[kf_r, ang_r], w=[y_r])

    def fold(v, v_r):
        p.op("dve", lambda e: e.tensor_scalar(out=m[tl], in0=v[tl], scalar1=math.pi, scalar2=None, op0=ALU.is_gt), r=[v_r], w=[m_r])
        p.op("dve", lambda e: e.scalar_tensor_tensor(out=v[tl], in0=m[tl], scalar=-TWO_PI, in1=v[tl], op0=ALU.mult, op1=ALU.add),
             r=[m_r, v_r], w=[v_r])
        p.op("dve", lambda e: e.tensor_scalar(out=m[tl], in0=v[tl], scalar1=-math.pi, scalar2=None, op0=ALU.is_lt), r=[v_r], w=[m_r])
        p.op("dve", lambda e: e.scalar_tensor_tensor(out=v[tl], in0=m[tl], scalar=TWO_PI, in1=v[tl], op0=ALU.mult, op1=ALU.add),
             r=[m_r, v_r], w=[v_r])

    fold(y, y_r)
    p.op("act", lambda e: e.activation(out=s_out[sl], in_=y[tl], func=AF.Sin), r=[y_r], w=[out_r])
    p.op("dve", lambda e: e.tensor_scalar(out=y[tl], in0=y[tl], scalar1=0.5 * math.pi, scalar2=None, op0=ALU.add), r=[y_r], w=[y_r])
    fold(y, y_r)
    p.op("act", lambda e: e.activation(out=c_out[sl], in_=y[tl], func=AF.Sin), r=[y_r], w=[out_r])


def build_s5prep():
    p = Prog()
    A_re = p.dram("A_re", [128, 64], F32, "ExternalInput")
    A_im = p.dram("A_im", [128, 64], F32, "ExternalInput")
    log_dt = p.dram("log_dt", [128, 1], F32, "ExternalInput")
    B_re = p.dram("B_re", [128, 1024], F32, "ExternalInput")
    B_im = p.dram("B_im", [128, 1024], F32, "ExternalInput")
    o_r = p.dram("o_r", [128, 64], F32, "ExternalOutput")
    o_th = p.dram("o_th", [128, 64], F32, "ExternalOutput")
    o_bbre = p.dram("o_bbre", [128, 1024], F32, "ExternalOutput")
    o_bbim = p.dram("o_bbim", [128, 1024], F32, "ExternalOutput")

    def T(name, shape, dt=F32):
        return p.sbuf(name, shape, dt), p.res(name)

    are, are_r = T("are", [128, 64])
    aim, aim_r = T("aim", [128, 64])
    ldt, ldt_r = T("ldt", [128, 1])
    bre, bre_r = T("bre", [128, 64, 16])
    bim, bim_r = T("bim", [128, 64, 16])
    p.dma("sp", are[:], A_re[:], w=[are_r])
    p.dma("sp", aim[:], A_im[:], w=[aim_r])
    p.dma("sp", ldt[:], log_dt[:], w=[ldt_r])
    p.dma("sp", bre[:].rearrange("p n c -> p (n c)"), B_re[:], w=[bre_r])
    p.dma("sp", bim[:].rearrange("p n c -> p (n c)"), B_im[:], w=[bim_r])
    dt, dt_r = T("dt", [128, 1])
    lre, lre_r = T("lre", [128, 64])
    th, th_r = T("th", [128, 64])
    rr, rr_r = T("rr", [128, 64])
    sn, sc_r = T("sn", [128, 64])
    cs, _ = T("cs", [128, 64])
    tmp, tmp_r = T("tmp", [128, 64])
    abre, abre_r = T("abre", [128, 64])
    abim, abim_r = T("abim", [128, 64])
    den, den_r = T("den", [128, 64])
    t2, t2_r = T("t2", [128, 64])
    cfre, cfre_r = T("cfre", [128, 64])
    cfim, cfim_r = T("cfim", [128, 64])
    obr, obr_r = T("obr", [128, 64, 16])
    obi, obi_r = T("obi", [128, 64, 16])
    t3, t3_r = T("t3", [128, 64, 16])
    p.op("act", lambda e: e.activation(out=dt[:], in_=ldt[:], func=AF.Exp), r=[ldt_r], w=[dt_r])
    p.op("dve", lambda e: e.tensor_scalar(out=lre[:], in0=are[:], scalar1=dt[:, 0:1], scalar2=None, op0=ALU.mult),
         r=[are_r, dt_r], w=[lre_r])
    p.op("dve", lambda e: e.tensor_scalar(out=th[:], in0=aim[:], scalar1=dt[:, 0:1], scalar2=None, op0=ALU.mult),
         r=[aim_r, dt_r], w=[th_r])
    p.op("act", lambda e: e.activation(out=rr[:], in_=lre[:], func=AF.Exp), r=[lre_r], w=[rr_r])
    sl = (slice(None), slice(None))
    sincos(p, T, th, th_r, sn, cs, sc_r, tmp, tmp_r, sl, 64)
    p.op("dve", lambda e: e.tensor_tensor(out=abre[:], in0=rr[:], in1=cs[:], op=ALU.mult), r=[rr_r, sc_r], w=[abre_r])
    p.op("dve", lambda e: e.tensor_tensor(out=abim[:], in0=rr[:], in1=sn[:], op=ALU.mult), r=[rr_r, sc_r], w=[abim_r])
    p.op("dve", lambda e: e.tensor_scalar(out=abre[:], in0=abre[:], scalar1=-1.0, scalar2=None, op0=ALU.add), r=[abre_r], w=[abre_r])
    p.op("dve", lambda e: e.tensor_tensor(out=den[:], in0=are[:], in1=are[:], op=ALU.mult), r=[are_r], w=[den_r])
    p.op("dve", lambda e: e.tensor_tensor(out=t2[:], in0=aim[:], in1=aim[:], op=ALU.mult), r=[aim_r], w=[t2_r])
    p.op("dve", lambda e: e.tensor_tensor(out=den[:], in0=den[:], in1=t2[:], op=ALU.add), r=[den_r, t2_r], w=[den_r])
    p.op("dve", lambda e: e.reciprocal(out=den[:], in_=den[:]), r=[den_r], w=[den_r])
    p.op("dve", lambda e: e.tensor_tensor(out=cfre[:], in0=abre[:], in1=are[:], op=ALU.mult), r=[abre_r, are_r], w=[cfre_r])
    p.op("dve", lambda e: e.tensor_tensor(out=t2[:], in0=abim[:], in1=aim[:], op=ALU.mult), r=[abim_r, aim_r], w=[t2_r])
    p.op("dve", lambda e: e.tensor_tensor(out=cfre[:], in0=cfre[:], in1=t2[:], op=ALU.add), r=[cfre_r, t2_r], w=[cfre_r])
    p.op("dve", lambda e: e.tensor_tensor(out=cfre[:], in0=cfre[:], in1=den[:], op=ALU.mult), r=[cfre_r, den_r], w=[cfre_r])
    p.op("dve", lambda e: e.tensor_tensor(out=cfim[:], in0=abim[:], in1=are[:], op=ALU.mult), r=[abim_r, are_r], w=[cfim_r])
    p.op("dve", lambda e: e.tensor_tensor(out=t2[:], in0=abre[:], in1=aim[:], op=ALU.mult), r=[abre_r, aim_r], w=[t2_r])
    p.op("dve", lambda e: e.tensor_tensor(out=cfim[:], in0=cfim[:], in1=t2[:], op=ALU.subtract), r=[cfim_r, t2_r], w=[cfim_r])
    p.op("dve", lambda e: e.tensor_tensor(out=cfim[:], in0=cfim[:], in1=den[:], op=ALU.mult), r=[cfim_r, den_r], w=[cfim_r])
    cre_b = cfre[:].unsqueeze(2).to_broadcast([128, 64, 16])
    cim_b = cfim[:].unsqueeze(2).to_broadcast([128, 64, 16])
    p.op("dve", lambda e: e.tensor_tensor(out=obr[:], in0=bre[:], in1=cre_b, op=ALU.mult), r=[bre_r, cfre_r], w=[obr_r])
    p.op("dve", lambda e: e.tensor_tensor(out=t3[:], in0=bim[:], in1=cim_b, op=ALU.mult), r=[bim_r, cfim_r], w=[t3_r])
    p.op("dve", lambda e: e.tensor_tensor(out=obr[:], in0=obr[:], in1=t3[:], op=ALU.subtract), r=[obr_r, t3_r], w=[obr_r])
    p.op("dve", lambda e: e.tensor_tensor(out=obi[:], in0=bim[:], in1=cre_b, op=ALU.mult), r=[bim_r, cfre_r], w=[obi_r])
    p.op("dve", lambda e: e.tensor_tensor(out=t3[:], in0=bre[:], in1=cim_b, op=ALU.mult), r=[bre_r, cfim_r, obr_r], w=[t3_r])
    p.op("dve", lambda e: e.tensor_tensor(out=obi[:], in0=obi[:], in1=t3[:], op=ALU.add), r=[obi_r, t3_r], w=[obi_r])
    p.dma("sp", o_r[:], rr[:], r=[rr_r], is_output=True)
    p.dma("sp", o_th[:], th[:], r=[th_r], is_output=True)
    p.dma("sp", o_bbre[:], obr[:].rearrange("p n c -> p (n c)"), r=[obr_r], is_output=True)
    p.dma("sp", o_bbim[:], obi[:].rearrange("p n c -> p (n c)"), r=[obi_r], is_output=True)
    p.finish()
    p.close()
    return p.nc


NPAIR = 8
NB = 16


def build_s5main(nblocks=NB):
    p = Prog()
    uT = p.dram("uT", [256, 8192], F32, "ExternalInput")
    BD = p.dram("BD", [128, NPAIR * 2 * 128], F32, "ExternalInput")
    CT = p.dram("CT", [128, NPAIR * 2 * 128], F32, "ExternalInput")
    rcol = p.dram("rcol", [128, NPAIR], F32, "ExternalInput")
    thcol = p.dram("thcol", [128, NPAIR], F32, "ExternalInput")
    iota = p.dram("iota", [128, 512], F32, "ExternalInput")
    Dcol = p.dram("Dcol", [128, 2], F32, "ExternalInput")
    identf = p.dram("identf", [128, 128], F32, "ExternalInput")
    yT = p.dram("yT", [256, 8192], BF16, "ExternalOutput")

    def T(name, shape, dt=F32):
        return p.sbuf(name, shape, dt), p.res(name)

    bd, bd_r = T("bd", [128, NPAIR, 2, 128])
    ct, ct_r = T("ct", [128, NPAIR, 2, 128])
    rc, rc_r = T("rc", [128, NPAIR])
    thc, thc_r = T("thc", [128, NPAIR])
    io, io_r = T("io", [128, 512])
    dc, dc_r = T("dc", [128, 2])
    p.dma("sp", bd[:].rearrange("p a b c -> p (a b c)"), BD[:], w=[bd_r])
    p.dma("sp", ct[:].rearrange("p a b c -> p (a b c)"), CT[:], w=[ct_r])
    p.dma("sp", rc[:], rcol[:], w=[rc_r])
    p.dma("sp", thc[:], thcol[:], w=[thc_r])
    p.dma("sp", io[:], iota[:], w=[io_r])
    p.dma("sp", dc[:], Dcol[:], w=[dc_r])
    cosT, tab_r = T("cosT", [128, NPAIR, 512])
    sinT, _ = T("sinT", [128, NPAIR, 512])
    nsinT, _ = T("nsinT", [128, NPAIR, 512])
    c512, c512_r = T("c512", [128, NPAIR])
    s512, _ = T("s512", [128, NPAIR])
    ang, ang_r = T("ang", [128, 512])
    tmp, tmp_r = T("tmp", [128, 512])
    for k in range(NPAIR):
        p.op("dve", lambda e: e.tensor_scalar(out=ang[:], in0=io[:], scalar1=thc[:, k:k + 1], scalar2=None, op0=ALU.mult),
             r=[io_r, thc_r], w=[ang_r])
        sincos(p, T, ang, ang_r, sinT[:, k, :], cosT[:, k, :], tab_r, tmp, tmp_r, (slice(None), slice(None)), 512)
        p.op("dve", lambda e: e.tensor_scalar(out=nsinT[:, k, :], in0=sinT[:, k, :], scalar1=-1.0, scalar2=None, op0=ALU.mult),
             r=[tab_r], w=[tab_r])
    p.op("dve", lambda e: e.tensor_scalar(out=ang[:, 0:NPAIR], in0=thc[:], scalar1=512.0, scalar2=None, op0=ALU.mult),
         r=[thc_r], w=[ang_r])
    sincos(p, T, ang, ang_r, s512, c512, c512_r, tmp, tmp_r, (slice(None), slice(0, NPAIR)), NPAIR)

    ctb, ctb_r = T("ctb", [128, NPAIR, 3, 128], BF16)
    p.op("dve", lambda e: e.tensor_copy(out=ctb[:, :, 0, :], in_=ct[:, :, 0, :]), r=[ct_r], w=[ctb_r])
    p.op("dve", lambda e: e.tensor_scalar(out=ctb[:, :, 1, :], in0=ct[:, :, 0, :], scalar1=-1.0, scalar2=None, op0=ALU.mult),
         r=[ct_r], w=[ctb_r])
    p.op("dve", lambda e: e.tensor_scalar(out=ctb[:, :, 2, :], in0=ct[:, :, 1, :], scalar1=-1.0, scalar2=None, op0=ALU.mult),
         r=[ct_r], w=[ctb_r])
    idf, idf_r = T("idf", [128, 128])
    p.dma("sp", idf[:], identf[:], w=[idf_r])
    vin_re, vin_r = T("vin_re", [128, NPAIR])
    vin_im, _ = T("vin_im", [128, NPAIR])
    p.op("dve", lambda e: e.memset(vin_re[:], 0.0), w=[vin_r])
    p.op("dve", lambda e: e.memset(vin_im[:], 0.0), w=[vin_r])
    ub = [T("ub", [128, 2, 512]) for _ in range(2)]
    psB = [p.psum("psB", [128, 512]) for _ in range(4)]
    psB_r = [p.res("psB") for _ in range(4)]
    psW = [p.psum("psW", [128, 512]) for _ in range(2)]
    psW_r = [p.res("psW") for _ in range(2)]
    psY = [p.psum("psY", [128, 512]) for _ in range(2)]
    psY_r = [p.res("psY") for _ in range(2)]
    tq = [[T("tq", [128, 512]) for _ in range(4)] for _ in range(2)]
    vre = [T("vre", [128, 512]) for _ in range(2)]
    vim = [T("vim", [128, 512]) for _ in range(2)]
    xq = [[T("xq", [128, 512], BF16) for _ in range(4)] for _ in range(2)]
    yg, yg_r = T("yg", [128, 512])
    tg, tg_r = T("tg", [128, 512])
    yo = [T("yo", [128, 512], BF16) for _ in range(2)]
    sm, sm_r = T("sm", [128, 4])
    steps = [(b, k) for b in range(nblocks) for k in range(NPAIR)]
    ubuf = {}

    def stageA(idx):
        b, k = steps[idx]
        if k == 0:
            u, u_r = ub[b % 2]
            p.dma("sp", u[:], uT[:, b * 512:(b + 1) * 512].rearrange("(k p) t -> p k t", p=128), w=[u_r])
            ubuf[b] = (u, u_r)
        u, u_r = ubuf[b]
        kc = k // 4
        i2 = idx % 2
        pre, pre_r = psB[2 * i2], psB_r[2 * i2]
        pim, pim_r = psB[2 * i2 + 1], psB_r[2 * i2 + 1]
        p.op("pe", lambda e: e.matmul(pre[:], bd[:, k, 0, :], u[:, kc, :], start=True, stop=True), r=[bd_r, u_r], w=[pre_r])
        p.op("pe", lambda e: e.matmul(pim[:], bd[:, k, 1, :], u[:, kc, :], start=True, stop=True), r=[bd_r, u_r], w=[pim_r])
        cT, sT, nsT = cosT[:, k, :], sinT[:, k, :], nsinT[:, k, :]
        tt = tq[i2]
        prods = [(pre, pre_r, cT), (pim, pim_r, sT), (pim, pim_r, cT), (pre, pre_r, nsT)]
        for j, (src_, src_r, tab) in enumerate(prods):
            p.op("dve", lambda e: e.tensor_tensor(out=tt[j][0][:], in0=src_[:], in1=tab, op=ALU.mult),
                 r=[src_r, tab_r], w=[tt[j][1]])
        wre_p, wre_r = psW[0], psW_r[0]
        wim_p, wim_r = psW[1], psW_r[1]
        p.op("pe", lambda e: e.matmul(wre_p[:], idf[:], tt[0][0][:], start=True, stop=False), r=[idf_r, tt[0][1]], w=[wre_r])
        p.op("pe", lambda e: e.matmul(wre_p[:], idf[:], tt[1][0][:], start=False, stop=True), r=[idf_r, tt[1][1]], w=[wre_r])
        p.op("pe", lambda e: e.matmul(wim_p[:], idf[:], tt[2][0][:], start=True, stop=False), r=[idf_r, tt[2][1]], w=[wim_r])
        p.op("pe", lambda e: e.matmul(wim_p[:], idf[:], tt[3][0][:], start=False, stop=True), r=[idf_r, tt[3][1]], w=[wim_r])

    def stageB(idx):
        b, k = steps[idx]
        u, u_r = ubuf[b]
        kc = k // 4
        i2 = idx % 2
        (vr, vr_r), (vi, vi_r) = vre[i2], vim[i2]
        cT, sT = cosT[:, k, :], sinT[:, k, :]
        wre_p, wre_r = psW[0], psW_r[0]
        wim_p, wim_r = psW[1], psW_r[1]
        rb = rc[:, k:k + 1].to_broadcast([128, 512])
        p.op("dve", lambda e: e.tensor_tensor_scan(out=vr[:], data0=rb, data1=wre_p[:], initial=vin_re[:, k:k + 1],
                                                   op0=ALU.mult, op1=ALU.add), r=[wre_r, rc_r, vin_r], w=[vr_r])
        p.op("dve", lambda e: e.tensor_tensor_scan(out=vi[:], data0=rb, data1=wim_p[:], initial=vin_im[:, k:k + 1],
                                                   op0=ALU.mult, op1=ALU.add), r=[wim_r, rc_r, vin_r], w=[vi_r])
        return (b, k, kc, i2, u, u_r, vr, vr_r, vi, vi_r, cT, sT)

    def stageC(ctx):
        b, k, kc, i2, u, u_r, vr, vr_r, vi, vi_r, cT, sT = ctx
        p.op("dve", lambda e: e.tensor_tensor(out=sm[:, 0:1], in0=vr[:, 511:512], in1=c512[:, k:k + 1], op=ALU.mult),
             r=[vr_r, c512_r], w=[sm_r])
        p.op("dve", lambda e: e.tensor_tensor(out=sm[:, 1:2], in0=vi[:, 511:512], in1=s512[:, k:k + 1], op=ALU.mult),
             r=[vi_r, c512_r], w=[sm_r])
        p.op("dve", lambda e: e.tensor_tensor(out=sm[:, 2:3], in0=vr[:, 511:512], in1=s512[:, k:k + 1], op=ALU.mult),
             r=[vr_r, c512_r], w=[sm_r])
        p.op("dve", lambda e: e.tensor_tensor(out=sm[:, 3:4], in0=vi[:, 511:512], in1=c512[:, k:k + 1], op=ALU.mult),
             r=[vi_r, c512_r], w=[sm_r])
        p.op("dve", lambda e: e.tensor_tensor(out=vin_re[:, k:k + 1], in0=sm[:, 0:1], in1=sm[:, 1:2], op=ALU.subtract),
             r=[sm_r], w=[vin_r])
        p.op("dve", lambda e: e.tensor_tensor(out=vin_im[:, k:k + 1], in0=sm[:, 2:3], in1=sm[:, 3:4], op=ALU.add),
             r=[sm_r], w=[vin_r])
        xx = xq[i2]
        posts = [("dve", vr, vr_r, cT), ("pool", vi, vi_r, sT), ("dve", vr, vr_r, sT), ("pool", vi, vi_r, cT)]
        for j, (eng_, src_, src_r, tab) in enumerate(posts):
            p.op(eng_, lambda e: e.tensor_tensor(out=xx[j][0][:], in0=src_[:], in1=tab, op=ALU.mult),
                 r=[src_r, tab_r], w=[xx[j][1]])
        py, py_r = psY[kc], psY_r[kc]
        kk = k % 4
        cv = [0, 1, 2, 2]
        for j in range(4):
            p.op("pe", lambda e: e.matmul(py[:], ctb[:, k, cv[j], :], xx[j][0][:], start=(kk == 0 and j == 0),
                                          stop=(kk == 3 and j == 3)), r=[ctb_r, xx[j][1]], w=[py_r])
        if kk == 3:
            yob, yob_r = yo[kc]
            p.op("dve", lambda e: e.scalar_tensor_tensor(out=yg[:], in0=u[:, kc, :], scalar=dc[:, kc:kc + 1], in1=py[:],
                                                         op0=ALU.mult, op1=ALU.add), r=[u_r, dc_r, py_r], w=[yg_r])
            p.op("pool", lambda e: e.tensor_tensor(out=tg[:], in0=yg[:], in1=yg[:], op=ALU.mult), r=[yg_r], w=[tg_r])
            p.op("pool", lambda e: e.tensor_scalar(out=tg[:], in0=tg[:], scalar1=0.044715, scalar2=1.0, op0=ALU.mult, op1=ALU.add),
                 r=[tg_r], w=[tg_r])
            p.op("pool", lambda e: e.tensor_tensor(out=tg[:], in0=tg[:], in1=yg[:], op=ALU.mult), r=[tg_r, yg_r], w=[tg_r])
            p.op("act", lambda e: e.activation(out=tg[:], in_=tg[:], func=AF.Sigmoid, scale=1.5957691216), r=[tg_r], w=[tg_r])
            p.op("pool", lambda e: e.tensor_tensor(out=yob[:], in0=tg[:], in1=yg[:], op=ALU.mult), r=[tg_r, yg_r], w=[yob_r])
            p.dma("sp", yT[kc * 128:(kc + 1) * 128, b * 512:(b + 1) * 512], yob[:], r=[yob_r], is_output=True)

    stageA(0)
    for idx in range(len(steps)):
        ctx = stageB(idx)
        if idx + 1 < len(steps):
            stageA(idx + 1)
        stageC(ctx)
    p.finish()
    p.close()
    return p.nc


def _run(nc, in_maps):
    res = run_bass_kernel_spmd(nc, in_maps, core_ids=list(range(8)))
    return res.results


def kernel(**inp):
    inp = {k: np.asarray(v) for k, v in inp.items()}
    x = inp["x"][0]
    idx = [own_idx(c) for c in range(8)]
    xT = [np.ascontiguousarray(x[idx[c]].T) for c in range(8)]
    w_in = np.ascontiguousarray(inp["nsa_w_in"][0])
    g0 = gcol(inp["mix_norm_g"][0])
    l1 = _run(build_p1(), [{"xT": xT[c], "gcol": g0, "w_in": w_in} for c in range(8)])
    shared = p2_shared(inp, l1)
    l2 = _run(build_p2(), [p2_inputs(inp, l1, c, shared) for c in range(8)])
    del shared
    m3 = [{"resT": xT[c], "aT": np.ascontiguousarray(np.asarray(l2[c]["o_out"]).T),
           "wmix": np.ascontiguousarray(inp["nsa_w_out"][0]), "gff": gcol(inp["ffn_norm_g"][0]),
           "gn": gcol(inp["mix_norm_g"][1]), "w1": np.ascontiguousarray(inp["ffn_w_in"][0]),
           "w2": np.ascontiguousarray(inp["ffn_w_out"][0])} for c in range(8)]
    l3 = _run(build_tok(False, False), m3)
    del m3
    prep = _run(build_s5prep(), [s5prep_inputs(inp) for _ in range(8)])[0]
    uT_full = np.zeros((2048, 8192), np.float32)
    for c in range(8):
        uT_full[:, idx[c]] = np.asarray(l3[c]["normT"])
    l4 = _run(build_s5main(), [s5main_inputs(inp, prep, uT_full, c) for c in range(8)])
    y_full = np.concatenate([np.asarray(l4[c]["yT"]) for c in range(8)], axis=0)
    m5 = [{"resT": np.ascontiguousarray(np.asarray(l3[c]["hT"])), "aT": np.ascontiguousarray(y_full[:, idx[c]]),
           "wmix": np.ascontiguousarray(inp["s5_w_glu"][0]), "gff": gcol(inp["ffn_norm_g"][1]),
           "gn": gcol(inp["final_norm_g"]), "w1": np.ascontiguousarray(inp["ffn_w_in"][1]),
           "w2": np.ascontiguousarray(inp["ffn_w_out"][1])} for c in range(8)]
    l5 = _run(build_tok(True, True), m5)
    out = np.zeros((1, 8192, 2048), np.float32)
    for c in range(8):
        out[0, idx[c]] = np.asarray(l5[c]["normT"]).T
    return out
```

```python
import math
from contextlib import ExitStack
import numpy as np
import ml_dtypes
import concourse.bass as bass
import concourse.mybir as mybir
from concourse.bass_utils import run_bass_kernel_spmd


F32 = mybir.dt.float32
BF16 = mybir.dt.bfloat16
AF = mybir.ActivationFunctionType
ALU = mybir.AluOpType
AX = mybir.AxisListType


NO_SELF_WAIT = ()


class Res:
    __slots__ = ("name", "last_w", "readers", "dsem", "dcnt")

    def __init__(self, name):
        self.name = name
        self.last_w = None
        self.readers = []
        self.dsem = None
        self.dcnt = 0


class Prog:
    def __init__(self, num_devices=None):
        self.nc = bass.Bass("TRN2", target_bir_lowering=False, num_devices=num_devices)
        self.es = ExitStack()
        nc = self.nc
        self.eng = {"pe": nc.tensor, "act": nc.scalar, "dve": nc.vector, "pool": nc.gpsimd, "sp": nc.sync}
        self.esem = {}
        self.ecnt = {}
        self.waited = {e: {} for e in self.eng}
        self.semid = {}
        for e in self.eng:
            s = self.es.enter_context(nc.semaphore("es_" + e))
            self.esem[e] = s
            self.ecnt[e] = 0
        self.nsem = len(self.eng)
        self.out_tokens = []
        self.no_self_wait = set(NO_SELF_WAIT)
        self.dma_toks = {}
        self._n = 0
        self._stacks = [self.es]

    def uname(self, base):
        self._n += 1
        return f"{base}_{self._n}"

    def sbuf(self, name, shape, dt):
        return self._stacks[-1].enter_context(self.nc.sbuf_tensor(self.uname(name), list(shape), dt))

    def push_scope(self):
        st = ExitStack()
        self._stacks.append(st)
        return st

    def pop_scope(self):
        self.barrier()
        st = self._stacks.pop()
        st.close()

    def barrier(self):
        for e in self.eng:
            for e2 in self.eng:
                if e2 != e and self.ecnt[e2] > 0:
                    self._wait(e, (self.esem[e2], self.ecnt[e2]))
            for tok in self.dma_toks.values():
                self._wait(e, tok)

    def psum(self, name, shape, dt=F32):
        return self.es.enter_context(self.nc.psum_tensor(self.uname(name), list(shape), dt))

    def dram(self, name, shape, dt, kind):
        return self.nc.dram_tensor(name, list(shape), dt, kind=kind).ap()

    def res(self, name="r"):
        return Res(name)

    def _wait(self, e, tok):
        if tok is None:
            return
        sem, val = tok
        if e == "pe" and sem is self.esem["pe"]:
            return
        if e in self.no_self_wait and sem is self.esem[e]:
            return
        k = id(sem)
        w = self.waited[e]
        if w.get(k, 0) >= val:
            return
        self.eng[e].wait_ge(sem, val)
        w[k] = val

    def _deps(self, e, r, w):
        for x in r:
            self._wait(e, x.last_w)
        for x in w:
            self._wait(e, x.last_w)
            for t in x.readers:
                self._wait(e, t)

    def _commit(self, tok, r, w):
        for x in r:
            x.readers.append(tok)
            if len(x.readers) > 64:
                best = {}
                for s, v in x.readers:
                    if id(s) not in best or best[id(s)][1] < v:
                        best[id(s)] = (s, v)
                x.readers = list(best.values())
        for x in w:
            x.last_w = tok
            x.readers = []

    def op(self, e, fn, r=(), w=()):
        self._deps(e, r, w)
        inst = fn(self.eng[e])
        inst.then_inc(self.esem[e], 1)
        self.ecnt[e] += 1
        tok = (self.esem[e], self.ecnt[e])
        self._commit(tok, r, w)
        return tok

    def dma(self, q, out, in_, r=(), w=(), sres=None, is_output=False, **kw):
        self._deps(q, r, w)
        if sres is None:
            sres = (list(w) + list(r))[0]
        if sres.dsem is None:
            sres.dsem = self.es.enter_context(self.nc.semaphore(self.uname("ds")))
            self.nsem += 1
        inst = self.eng[q].dma_start(out=out, in_=in_, **kw)
        inst.then_inc(sres.dsem, 16)
        sres.dcnt += 16
        tok = (sres.dsem, sres.dcnt)
        self.dma_toks[id(sres.dsem)] = tok
        self._commit(tok, r, w)
        if is_output:
            self.out_tokens.append(tok)
        return tok

    def finish(self):
        best = {}
        for s, v in self.out_tokens:
            if id(s) not in best or best[id(s)][1] < v:
                best[id(s)] = (s, v)
        for s, v in best.values():
            self.eng["sp"].wait_ge(s, v)
        return self.nc

    def close(self):
        self.es.close()


BF = ml_dtypes.bfloat16
NEG = -30000.0


def own_idx(c):
    return np.concatenate([np.arange((8 * i + c) * 128, (8 * i + c) * 128 + 128) for i in range(8)])


def gcol(g):
    return np.ascontiguousarray(np.asarray(g, np.float32).reshape(16, 128).T)


def rel_bucket_np(dist):
    n = np.maximum(dist, 0).astype(np.int64)
    max_exact = 16
    nf = np.maximum(n, 1).astype(np.float32)
    large = max_exact + (np.log(nf / np.float32(max_exact)) / np.float32(np.log(128 / 16)) * np.float32(16)).astype(np.int32)
    large = np.minimum(large, 31)
    return np.where(n < max_exact, n, large).astype(np.int64)


def p2_tables(rel_bias, c):
    rel_bias = np.asarray(rel_bias, np.float32)
    kl = np.arange(128)[:, None]
    ql = np.arange(128)[None, :]
    tb_bias = np.zeros((9, 128, 16, 128), np.float32)
    tb_mask = np.zeros((128, 9, 128), np.float32)
    for jj in range(9):
        j = jj - 1
        r = c - j
        dist = r * 128 + ql - kl
        b = rel_bias[rel_bucket_np(dist)]
        tb_bias[jj] = b.transpose(0, 2, 1)
        tb_mask[:, jj, :] = np.where(dist < 0, NEG, 0.0)
    win_mask = np.zeros((128, 12, 128), np.float32)
    for jp in range(12):
        r = c + 4 - jp
        dist = r * 128 + ql - kl
        win_mask[:, jp, :] = np.where((dist < 0) | (dist >= 512), NEG, 0.0)
    qcol = np.arange(128)[:, None]
    e = np.arange(72)[None, :]
    dist_c = 128 * c + qcol - 16 * (e - 9) - 31
    ec_bias = rel_bias[rel_bucket_np(dist_c)].transpose(0, 2, 1)
    ec_mask = np.where(dist_c < 0, NEG, 0.0).astype(np.float32)
    fq = np.zeros((128, 8, 128), np.float32)
    blk = np.arange(128)[None, :]
    for i in range(8):
        t = (8 * i + c) * 128 + np.arange(128)[:, None]
        tb = t // 64
        f = np.zeros((128, 128), np.float32)
        f = np.where(blk > tb, -1e30, f)
        f = np.where(blk == tb - 1, 1e9, f)
        f = np.where(blk == tb, 2e9, f)
        f = np.where(blk == 0, 3e9, f)
        fq[:, i, :] = f
    return {
        "tb_bias": np.ascontiguousarray(tb_bias.reshape(9, 128, 2048)),
        "tb_mask": np.ascontiguousarray(tb_mask.reshape(128, 9 * 128)),
        "b31": np.ascontiguousarray(np.broadcast_to(rel_bias[31][None, :], (128, 16))),
        "win_mask": np.ascontiguousarray(win_mask.reshape(128, 12 * 128)),
        "ec_bias": np.ascontiguousarray(ec_bias.reshape(128, 16 * 72)),
        "ec_mask": ec_mask,
        "fq": np.ascontiguousarray(fq.reshape(128, 8 * 128)),
    }


def p2_consts():
    key = np.arange(8192)[None, :]
    j = np.arange(128)[:, None]
    R = (key // 64 == j).astype(np.float32).astype(BF)
    ident = np.eye(128, dtype=np.float32).astype(BF)
    return {"Rm": R, "ident": ident}


def p2_inputs(inp, l1, c, shared):
    r = l1[c]
    qT = np.asarray(r["qT"])
    q_l = qT.reshape(4, 4, 128, 8, 128).transpose(3, 0, 2, 1, 4).reshape(8, 4, 128, 512)
    gates = np.asarray(r["gates"]).reshape(8, 128, 48).transpose(1, 0, 2).reshape(128, 8 * 48)
    d = {"q_l": np.ascontiguousarray(q_l), "gates_l": np.ascontiguousarray(gates)}
    d.update(shared)
    d.update(p2_tables(inp["rel_bias"], c))
    return d


def p2_shared(inp, l1):
    sh = {}
    for nm in ("kcT", "vcT", "ksT", "kwT"):
        full = np.zeros((512, 8192), BF)
        for c in range(8):
            full[:, own_idx(c)] = np.asarray(l1[c][nm])
        sh[nm] = np.ascontiguousarray(full.reshape(4, 128, 8192))
    for nm, out in (("vs", "vs1"), ("vw", "vw1")):
        full = np.zeros((8192, 512), BF)
        for c in range(8):
            full[own_idx(c)] = np.asarray(l1[c][nm])
        v = full.reshape(64, 128, 4, 128).transpose(2, 1, 0, 3)
        v1 = np.ones((4, 128, 64, 129), BF)
        v1[..., :128] = v
        sh[out] = np.ascontiguousarray(v1.reshape(4, 128, 64 * 129))
    for kv in ("k", "v"):
        w1 = np.asarray(inp[f"cmp_w1_{kv}"][0], np.float32)
        sh[f"w1{kv}"] = np.ascontiguousarray(w1.reshape(32, 128, 256).transpose(1, 0, 2).reshape(128, 32 * 256))
        w2 = np.asarray(inp[f"cmp_w2_{kv}"][0], np.float32)
        sh[f"w2{kv}"] = np.ascontiguousarray(w2.reshape(2, 128, 128).transpose(1, 0, 2).reshape(128, 256))
        sh[f"posT{kv}"] = np.ascontiguousarray(np.asarray(inp[f"cmp_pos_{kv}"][0], np.float32).T)
    sh.update(p2_consts())
    return sh


def s5prep_inputs(inp):
    return {"A_re": np.ascontiguousarray(inp["s5_A_re"][0]), "A_im": np.ascontiguousarray(inp["s5_A_im"][0]),
            "log_dt": np.ascontiguousarray(inp["s5_log_dt"][0].reshape(128, 1)),
            "B_re": np.ascontiguousarray(inp["s5_B_re"][0].reshape(128, 1024)),
            "B_im": np.ascontiguousarray(inp["s5_B_im"][0].reshape(128, 1024))}


def s5main_inputs(inp, prep, uT_full, c):
    r = np.asarray(prep["o_r"]); th = np.asarray(prep["o_th"])
    bbre = np.asarray(prep["o_bbre"]).reshape(128, 64, 16); bbim = np.asarray(prep["o_bbim"]).reshape(128, 64, 16)
    C_re = np.asarray(inp["s5_C_re"][0]); C_im = np.asarray(inp["s5_C_im"][0])
    D = np.asarray(inp["s5_D"][0])
    BD = np.zeros((128, 8, 2, 128), np.float32)
    CT = np.zeros((128, 8, 2, 128), np.float32)
    rcol = np.zeros((128, 8), np.float32); thcol = np.zeros((128, 8), np.float32)
    for k in range(8):
        kg = 8 * c + k
        for gg in range(2):
            g = 2 * kg + gg
            ch0 = 32 * (k % 4) + 16 * gg
            st0 = 64 * gg
            BD[ch0:ch0 + 16, k, 0, st0:st0 + 64] = bbre[g].T
            BD[ch0:ch0 + 16, k, 1, st0:st0 + 64] = bbim[g].T
            CT[st0:st0 + 64, k, 0, ch0:ch0 + 16] = C_re[g].T
            CT[st0:st0 + 64, k, 1, ch0:ch0 + 16] = C_im[g].T
            rcol[st0:st0 + 64, k] = r[g]
            thcol[st0:st0 + 64, k] = th[g]
    Dcol = np.ascontiguousarray(D[256 * c:256 * c + 256].reshape(2, 128).T)
    iota = np.ascontiguousarray(np.broadcast_to(np.arange(512, dtype=np.float32)[None, :], (128, 512)))
    return {"uT": np.ascontiguousarray(uT_full[256 * c:256 * c + 256]), "BD": BD.reshape(128, -1), "CT": CT.reshape(128, -1),
            "rcol": rcol, "thcol": thcol, "iota": iota, "Dcol": Dcol, "identf": np.eye(128, dtype=np.float32)}


D = 2048
NT = 1024
KC = 16
EPS = 1e-6


def rmsnorm_T(p, xT_dram, gcol_sb, gcol_res, hnT, hnT_res, ones_f, ones_res, out_dt_note=""):
    nc = p.nc
    xs = p.sbuf("xs", [128, KC, 512], F32)
    xs_r = p.res("xs")
    sq = [p.sbuf("sq", [128, 512], F32) for _ in range(2)]
    sq_r = [p.res("sq") for _ in range(2)]
    ss = p.psum("ss", [128, 512])
    ss_r = p.res("ss")
    rstd = p.sbuf("rstd", [128, 512], F32)
    rstd_r = p.res("rstd")
    for half in range(NT // 512):
        src = xT_dram[:, half * 512:(half + 1) * 512].rearrange("(k p) t -> p k t", p=128)
        for kk in range(0, KC, 4):
            p.dma("sp", xs[:, kk:kk + 4, :], src[:, kk:kk + 4, :], w=[xs_r])
        for k in range(KC):
            b = k % 2
            p.op("act", lambda e: e.activation(out=sq[b][:], in_=xs[:, k, :], func=AF.Square),
                 r=[xs_r], w=[sq_r[b]])
            p.op("pe", lambda e: e.matmul(ss[:], ones_f[:], sq[b][:], start=(k == 0), stop=(k == KC - 1)),
                 r=[sq_r[b], ones_res], w=[ss_r])
        p.op("dve", lambda e: e.tensor_scalar(out=rstd[:], in0=ss[:], scalar1=1.0 / D, scalar2=EPS,
                                              op0=ALU.mult, op1=ALU.add), r=[ss_r], w=[rstd_r])
        p.op("act", lambda e: e.activation(out=rstd[:], in_=rstd[:], func=AF.Sqrt), r=[rstd_r], w=[rstd_r])
        p.op("dve", lambda e: e.reciprocal(out=rstd[:], in_=rstd[:]), r=[rstd_r], w=[rstd_r])
        for k in range(KC):
            p.op("dve", lambda e: e.scalar_tensor_tensor(
                out=hnT[:, k, half * 512:(half + 1) * 512], in0=xs[:, k, :], scalar=gcol_sb[:, k:k + 1],
                in1=rstd[:], op0=ALU.mult, op1=ALU.mult), r=[xs_r, rstd_r, gcol_res], w=[hnT_res])


def build_p1(only=None):
    p = Prog()
    nc = p.nc
    xT = p.dram("xT", [D, NT], F32, "ExternalInput")
    gcol = p.dram("gcol", [128, KC], F32, "ExternalInput")
    w_in = p.dram("w_in", [D, 5168], F32, "ExternalInput")
    qT = p.dram("qT", [2048, NT], BF16, "ExternalOutput")
    kT = {n: p.dram(n, [512, NT], BF16, "ExternalOutput") for n in ("kcT", "vcT", "ksT", "kwT")}
    vv = {n: p.dram(n, [NT, 512], BF16, "ExternalOutput") for n in ("vs", "vw")}
    gates = p.dram("gates", [NT, 48], F32, "ExternalOutput")

    ones_f = p.sbuf("ones", [128, 128], F32)
    ones_r = p.res("ones")
    p.op("pool", lambda e: e.memset(ones_f[:], 1.0), w=[ones_r])
    gsb = p.sbuf("gsb", [128, KC], F32)
    g_r = p.res("g")
    p.dma("sp", gsb[:], gcol[:], w=[g_r])
    hnT = p.sbuf("hnT", [128, KC, NT], BF16)
    hnT_r = p.res("hnT")
    rmsnorm_T(p, xT, gsb, g_r, hnT, hnT_r, ones_f, ones_r)

    wst = [p.sbuf("wst", [128, KC, 512], F32) for _ in range(2)]
    wst_r = [p.res("wst") for _ in range(2)]
    wb = [p.sbuf("wb", [128, KC, 512], BF16) for _ in range(2)]
    wb_r = [p.res("wb") for _ in range(2)]
    ps = [p.psum("ps", [128, 512]) for _ in range(4)]
    ps_r = [p.res("ps") for _ in range(4)]
    ot = [p.sbuf("ot", [128, 512], BF16) for _ in range(4)]
    ot_r = [p.res("ot") for _ in range(4)]
    og = p.sbuf("og", [128, 48], F32)
    og_r = p.res("og")
    chunks = [("qT", 0), ("qT", 1), ("qT", 2), ("qT", 3), ("kcT", 0), ("vcT", 0), ("ksT", 0), ("vs", 0),
              ("kwT", 0), ("vw", 0), ("gates", 0)]
    pi = 0
    for ci, (nm, sub) in enumerate(chunks):
        if only is not None and ci not in only:
            continue
        c0 = ci * 512
        ncol = 512 if nm != "gates" else 48
        b = ci % 2
        src = w_in[:, c0:c0 + ncol].rearrange("(k p) c -> p k c", p=128)
        for kk in range(0, KC, 4):
            p.dma("sp", wst[b][:, kk:kk + 4, :ncol], src[:, kk:kk + 4, :], w=[wst_r[b]])
        for kk in range(0, KC, 8):
            p.op("dve", lambda e: e.tensor_copy(out=wb[b][:, kk:kk + 8, :ncol], in_=wst[b][:, kk:kk + 8, :ncol]),
                 r=[wst_r[b]], w=[wb_r[b]])
        if nm in ("qT", "kcT", "vcT", "ksT", "kwT"):
            dst = qT if nm == "qT" else kT[nm]
            scale = 128 ** -0.5 if nm == "qT" else 1.0
            for m in range(4):
                for n in range(NT // 512):
                    pb = pi % 4
                    pi += 1
                    for k in range(KC):
                        p.op("pe", lambda e: e.matmul(ps[pb][:], wb[b][:, k, m * 128:(m + 1) * 128],
                                                      hnT[:, k, n * 512:(n + 1) * 512], start=(k == 0), stop=(k == KC - 1)),
                             r=[wb_r[b], hnT_r], w=[ps_r[pb]])
                    p.op("act", lambda e: e.activation(out=ot[pb][:], in_=ps[pb][:], func=AF.Copy, scale=scale),
                         r=[ps_r[pb]], w=[ot_r[pb]])
                    row0 = sub * 512 + m * 128
                    p.dma("act", dst[row0:row0 + 128, n * 512:(n + 1) * 512], ot[pb][:], r=[ot_r[pb]], is_output=True)
        else:
            for t in range(NT // 128):
                pb = pi % 4
                pi += 1
                for k in range(KC):
                    p.op("pe", lambda e: e.matmul(ps[pb][:, :ncol], hnT[:, k, t * 128:(t + 1) * 128],
                                                  wb[b][:, k, :ncol], start=(k == 0), stop=(k == KC - 1)),
                         r=[wb_r[b], hnT_r], w=[ps_r[pb]])
                if nm == "gates":
                    p.op("act", lambda e: e.activation(out=og[:], in_=ps[pb][:, :48], func=AF.Sigmoid),
                         r=[ps_r[pb]], w=[og_r])
                    p.dma("act", gates[t * 128:(t + 1) * 128, :], og[:], r=[og_r], is_output=True)
                else:
                    p.op("act", lambda e: e.activation(out=ot[pb][:], in_=ps[pb][:], func=AF.Copy),
                         r=[ps_r[pb]], w=[ot_r[pb]])
                    p.dma("act", vv[nm][t * 128:(t + 1) * 128, :], ot[pb][:], r=[ot_r[pb]], is_output=True)
    p.finish()
    p.close()
    return p.nc


NI = 8
G = 4
NCMP = 511


def build_p2(only_groups=None, do_sel=True, do_win=True):
    p = Prog()
    q_l = p.dram("q_l", [NI, G, 128, 512], BF16, "ExternalInput")
    gates_l = p.dram("gates_l", [128, NI * 48], F32, "ExternalInput")
    kcT = p.dram("kcT", [G, 128, 8192], BF16, "ExternalInput")
    vcT = p.dram("vcT", [G, 128, 8192], BF16, "ExternalInput")
    ksT = p.dram("ksT", [G, 128, 8192], BF16, "ExternalInput")
    kwT = p.dram("kwT", [G, 128, 8192], BF16, "ExternalInput")
    vs1 = p.dram("vs1", [G, 128, 64 * 129], BF16, "ExternalInput")
    vw1 = p.dram("vw1", [G, 128, 64 * 129], BF16, "ExternalInput")
    w1 = {"k": p.dram("w1k", [128, 32 * 256], F32, "ExternalInput"), "v": p.dram("w1v", [128, 32 * 256], F32, "ExternalInput")}
    w2 = {"k": p.dram("w2k", [128, 2 * 128], F32, "ExternalInput"), "v": p.dram("w2v", [128, 2 * 128], F32, "ExternalInput")}
    posT = {"k": p.dram("posTk", [128, 32], F32, "ExternalInput"), "v": p.dram("posTv", [128, 32], F32, "ExternalInput")}
    tb_bias = p.dram("tb_bias", [9, 128, 16 * 128], F32, "ExternalInput")
    tb_mask = p.dram("tb_mask", [128, 9 * 128], F32, "ExternalInput")
    b31 = p.dram("b31", [128, 16], F32, "ExternalInput")
    win_mask = p.dram("win_mask", [128, 12 * 128], F32, "ExternalInput")
    ec_bias = p.dram("ec_bias", [128, 16 * 72], F32, "ExternalInput")
    ec_mask = p.dram("ec_mask", [128, 72], F32, "ExternalInput")
    fq = p.dram("fq", [128, NI * 128], F32, "ExternalInput")
    Rm = p.dram("Rm", [128, 8192], BF16, "ExternalInput")
    ident = p.dram("ident", [128, 128], BF16, "ExternalInput")
    o_out = p.dram("o_out", [NI * 128, 2048], BF16, "ExternalOutput")

    def T(name, shape, dt):
        return p.sbuf(name, shape, dt), p.res(name)

    kccT, kccT_r = T("kccT", [128, G, 512], BF16)
    vcc, vcc_r = T("vcc", [128, G, 4, 128], BF16)
    R_sb, R_r = T("R", [128, 8192], BF16)
    id_sb, id_r = T("ident", [128, 128], BF16)
    Tt, Tt_r = T("Tt", [128, 9, 16 * 128], BF16)
    Wm4, Wm4_r = T("Wm4", [128, 12, 4, 128], BF16)
    Fq, Fq_r = T("Fq", [128, NI, 128], F32)
    Ec, Ec_r = T("Ec", [128, 16, 72], F32)
    b31s, b31_r = T("b31", [128, 16], F32)
    gat, gat_r = T("gat", [128, NI, 48], F32)
    psS = [p.psum("psS", [128, 512]) for _ in range(2)]
    psS_r = [p.res("psS") for _ in range(2)]
    psO = [p.psum("psO", [128, 512]) for _ in range(4)]
    psO_r = [p.res("psO") for _ in range(4)]
    psC = psS
    psC_r = psS_r
    psT = p.psum("psT", [128, 1024], BF16)
    psT_r = p.res("psT")
    psX = p.psum("psX", [128, 512])
    psX_r = p.res("psX")

    p.dma("sp", R_sb[:], Rm[:], w=[R_r])
    p.dma("sp", id_sb[:], ident[:], w=[id_r])
    p.dma("sp", Fq[:].rearrange("p i b -> p (i b)"), fq[:], w=[Fq_r])
    p.dma("sp", b31s[:], b31[:], w=[b31_r])
    p.dma("sp", gat[:].rearrange("p i c -> p (i c)"), gates_l[:], w=[gat_r])

    p.push_scope()
    stg = [T("stg", [128, 2048], F32) for _ in range(2)]
    msk, msk_r = T("msk", [128, 12 * 128], F32)
    p.dma("sp", msk[:, 0:9 * 128], tb_mask[:], w=[msk_r])
    for j in range(9):
        sb, sr = stg[j % 2]
        p.dma("sp", sb[:], tb_bias[j], w=[sr])
        v3 = sb[:].rearrange("p (h q) -> p h q", h=16)
        p.op("dve", lambda e: e.tensor_tensor(out=v3, in0=v3, in1=b31s[:].unsqueeze(2).to_broadcast([128, 16, 128]),
                                              op=ALU.subtract), r=[sr, b31_r], w=[sr])
        p.op("dve", lambda e: e.tensor_tensor(out=Tt[:, j, :].rearrange("p (h q) -> p h q", h=16), in0=v3,
                                              in1=msk[:, j * 128:(j + 1) * 128].unsqueeze(1).to_broadcast([128, 16, 128]),
                                              op=ALU.add), r=[sr, msk_r], w=[Tt_r])
    p.dma("sp", msk[:], win_mask[:], w=[msk_r])
    for j in range(12):
        p.op("dve", lambda e: e.tensor_copy(out=Wm4[:, j, :, :],
                                            in_=msk[:, j * 128:(j + 1) * 128].unsqueeze(1).to_broadcast([128, 4, 128])),
             r=[msk_r], w=[Wm4_r])
    sb, sr = stg[0]
    p.dma("sp", sb[:, 0:16 * 72], ec_bias[:], w=[sr])
    p.dma("sp", msk[:, 0:72], ec_mask[:], w=[msk_r])
    v3 = sb[:, 0:16 * 72].rearrange("p (h e) -> p h e", h=16)
    p.op("dve", lambda e: e.tensor_tensor(out=v3, in0=v3, in1=b31s[:].unsqueeze(2).to_broadcast([128, 16, 72]),
                                          op=ALU.subtract), r=[sr, b31_r], w=[sr])
    p.op("dve", lambda e: e.tensor_tensor(out=Ec[:], in0=v3, in1=msk[:, 0:72].unsqueeze(1).to_broadcast([128, 16, 72]),
                                          op=ALU.add), r=[sr, msk_r], w=[Ec_r])
    w1b, w1b_r = T("w1b", [128, 32, 256], BF16)
    w2b, w2b_r = T("w2b", [128, 2, 128], BF16)
    posb, posb_r = T("posb", [128, 32], BF16)
    pw1, pw1_r = T("pw1", [128, 2], F32)
    kvT, kvT_r = T("kvT", [128, 8192], BF16)
    hidT, hidT_r = T("hidT", [128, 2, 512], BF16)
    xg, xg_r = T("xg", [128, 512], F32)
    tg, tg_r = T("tg", [128, 512], F32)
    p.op("dve", lambda e: e.memset(vcc[:], 0.0), w=[vcc_r])
    p.op("dve", lambda e: e.memset(kccT[:], 0.0), w=[kccT_r])
    for kv in ("k", "v"):
        for q4 in range(4):
            sb, sr = stg[q4 % 2]
            p.dma("sp", sb[:], w1[kv][:, q4 * 2048:(q4 + 1) * 2048], w=[sr])
            p.op("dve", lambda e: e.tensor_copy(out=w1b[:, q4 * 8:(q4 + 1) * 8, :].rearrange("p j c -> p (j c)"), in_=sb[:]),
                 r=[sr], w=[w1b_r])
        sb, sr = stg[0]
        p.dma("sp", sb[:, 0:256], w2[kv][:], w=[sr])
        p.op("dve", lambda e: e.tensor_copy(out=w2b[:].rearrange("p a b -> p (a b)"), in_=sb[:, 0:256]), r=[sr], w=[w2b_r])
        sb, sr = stg[1]
        p.dma("sp", sb[:, 0:32], posT[kv][:], w=[sr])
        p.op("dve", lambda e: e.tensor_copy(out=posb[:], in_=sb[:, 0:32]), r=[sr], w=[posb_r])
        for hc in range(2):
            for j in range(32):
                p.op("pe", lambda e: e.matmul(psX[:, 0:1], w1b[:, j, hc * 128:(hc + 1) * 128], posb[:, j:j + 1],
                                              start=(j == 0), stop=(j == 31)), r=[w1b_r, posb_r], w=[psX_r])
            p.op("dve", lambda e: e.tensor_copy(out=pw1[:, hc:hc + 1], in_=psX[:, 0:1]), r=[psX_r], w=[pw1_r])
        src = kcT if kv == "k" else vcT
        for g in range(G):
            p.dma("sp", kvT[:], src[g], w=[kvT_r])
            for hc in range(2):
                pc = psC[hc]
                for j in range(32):
                    p.op("pe", lambda e: e.matmul(pc[:, 0:NCMP], w1b[:, j, hc * 128:(hc + 1) * 128],
                                                  kvT[:, j:j + 16 * (NCMP - 1) + 1:16], start=(j == 0), stop=(j == 31)),
                         r=[w1b_r, kvT_r], w=[psC_r[hc]])
                p.op("act", lambda e: e.activation(out=xg[:, 0:NCMP], in_=pc[:, 0:NCMP], func=AF.Identity,
                                                   bias=pw1[:, hc:hc + 1]), r=[psC_r[hc], pw1_r], w=[xg_r])
                p.op("dve", lambda e: e.tensor_tensor(out=tg[:, 0:NCMP], in0=xg[:, 0:NCMP], in1=xg[:, 0:NCMP], op=ALU.mult),
                     r=[xg_r], w=[tg_r])
                p.op("dve", lambda e: e.tensor_scalar(out=tg[:, 0:NCMP], in0=tg[:, 0:NCMP], scalar1=0.044715, scalar2=1.0,
                                                      op0=ALU.mult, op1=ALU.add), r=[tg_r], w=[tg_r])
                p.op("dve", lambda e: e.tensor_tensor(out=tg[:, 0:NCMP], in0=tg[:, 0:NCMP], in1=xg[:, 0:NCMP], op=ALU.mult),
                     r=[tg_r, xg_r], w=[tg_r])
                p.op("act", lambda e: e.activation(out=tg[:, 0:NCMP], in_=tg[:, 0:NCMP], func=AF.Sigmoid, scale=1.5957691216),
                     r=[tg_r], w=[tg_r])
                p.op("dve", lambda e: e.tensor_tensor(out=hidT[:, hc, 0:NCMP], in0=tg[:, 0:NCMP], in1=xg[:, 0:NCMP], op=ALU.mult),
                     r=[tg_r, xg_r], w=[hidT_r])
            if kv == "k":
                for hc in range(2):
                    p.op("pe", lambda e: e.matmul(psX[:, 0:NCMP], w2b[:, hc, :], hidT[:, hc, 0:NCMP],
                                                  start=(hc == 0), stop=(hc == 1)), r=[w2b_r, hidT_r], w=[psX_r])
                p.op("act", lambda e: e.activation(out=kccT[:, g, 0:NCMP], in_=psX[:, 0:NCMP], func=AF.Copy),
                     r=[psX_r], w=[kccT_r])
            else:
                for cb in range(4):
                    n = min(NCMP, (cb + 1) * 128) - cb * 128
                    for hc in range(2):
                        p.op("pe", lambda e: e.matmul(psX[0:n, cb * 128:(cb + 1) * 128], hidT[:, hc, cb * 128:cb * 128 + n],
                                                      w2b[:, hc, :], start=(hc == 0), stop=(hc == 1)),
                             r=[w2b_r, hidT_r], w=[psX_r])
                    p.op("act", lambda e: e.activation(out=vcc[0:n, g, cb, :], in_=psX[0:n, cb * 128:(cb + 1) * 128], func=AF.Copy),
                         r=[psX_r], w=[vcc_r])
    p.pop_scope()

    ksT_s, ksT_r = T("ksT", [128, 8192], BF16)
    kwT_s, kwT_r = T("kwT", [128, 8192], BF16)
    vs_s, vs_r = T("vs1", [128, 64, 129], BF16)
    vw_s, vw_r = T("vw1", [128, 64, 129], BF16)
    qg = [T("qg", [128, 512], BF16) for _ in range(2)]
    s4, s4_r = T("s4", [128, 4, 512], F32)
    e4, e4_r = T("e4", [128, 4, 512], F32)
    pb4, pb4_r = T("pb4", [128, 4, 512], BF16)
    pT4, pT4_r = T("pT4", [128, 4, 4, 128], BF16)
    st4, st4_r = T("st4", [128, 4, 8], F32)
    imp, imp_r = T("imp", [128, 520], F32)
    sc, sc_r = T("sc", [128, 128], F32)
    wk, wk_r = T("wk", [128, 128], F32)
    m8, m8_r = T("m8", [128, 16], F32)
    st1, st1_r = T("st1", [128, 8], F32)
    negm, negm_r = T("negm", [128, 128], BF16)
    nmT2 = [T("nmT", [128, 4, 128], BF16) for _ in range(2)]
    PT = [T("PT", [128, 512], BF16) for _ in range(3)]
    OA = [T("oacc", [128, 512], F32) for _ in range(2)]
    obf = [T("obf", [128, 512], BF16) for _ in range(2)]
    coef, coef_r = T("coef", [128, 8], F32)
    cnt = {"S": 0, "P": 0, "q": 0, "c": 0}

    def attend(i, g, qv, q_r, kts, K_s, K_r, V_s, V_r, extra, gate_col0, init_acc, oacc, oacc_r, filler=None, per_iter=0):
        n = len(kts)
        slots = {}

        def qk(idx):
            kt = kts[idx]
            sb_ = cnt["S"] % 2
            cnt["S"] += 1
            slots[idx] = sb_
            mms = [(K_s[:, kt * 128:(kt + 1) * 128], K_r, qv[:], q_r)] + extra(kt)
            for mi, (lt, lr, rh, rr) in enumerate(mms):
                p.op("pe", lambda e: e.matmul(psS[sb_][:], lt, rh, start=(mi == 0), stop=(mi == len(mms) - 1)),
                     r=[lr, rr], w=[psS_r[sb_]])

        qk(0)
        for idx, kt in enumerate(kts):
            if idx + 1 < n:
                qk(idx + 1)
            sb_ = slots.pop(idx)
            pb_ = cnt["P"] % 3
            cnt["P"] += 1
            Pt, Pt_r = PT[pb_]
            p.op("act", lambda e: e.activation(out=Pt[:], in_=psS[sb_][:], func=AF.Exp), r=[psS_r[sb_]], w=[Pt_r])
            for h in range(4):
                bank = h
                c0 = 0
                p.op("pe", lambda e: e.matmul(psO[bank][:, c0:c0 + 129], Pt[:, h * 128:(h + 1) * 128], V_s[:, kt, :],
                                              start=(idx == 0), stop=(idx == n - 1)), r=[Pt_r, V_r], w=[psO_r[bank]])
            if filler is not None:
                for _ in range(per_iter):
                    next(filler, None)
        for h in range(4):
            bank = h
            c0 = 0
            head = g * 4 + h
            p.op("dve", lambda e: e.reciprocal(out=coef[:, h:h + 1], in_=psO[bank][:, c0 + 128:c0 + 129]),
                 r=[psO_r[bank]], w=[coef_r])
            p.op("dve", lambda e: e.tensor_tensor(out=coef[:, h:h + 1], in0=coef[:, h:h + 1],
                                                  in1=gat[:, i, gate_col0 + head:gate_col0 + head + 1], op=ALU.mult),
                 r=[coef_r, gat_r], w=[coef_r])
            dst = oacc[:, h * 128:(h + 1) * 128]
            if init_acc:
                p.op("dve", lambda e: e.tensor_scalar(out=dst, in0=psO[bank][:, c0:c0 + 128], scalar1=coef[:, h:h + 1],
                                                      scalar2=None, op0=ALU.mult), r=[psO_r[bank], coef_r], w=[oacc_r])
            else:
                p.op("dve", lambda e: e.scalar_tensor_tensor(out=dst, in0=psO[bank][:, c0:c0 + 128], scalar=coef[:, h:h + 1],
                                                             in1=dst, op0=ALU.mult, op1=ALU.add),
                     r=[psO_r[bank], coef_r, oacc_r], w=[oacc_r])


    groups = list(range(G)) if only_groups is None else only_groups
    items = [(g, i) for g in groups for i in range(NI)]

    def prep_item(n):
        g, i = items[n]
        qv, q_r = qg[n % 2]
        p.dma("sp", qv[:], q_l[i, g], w=[q_r])
        oacc, oacc_r = OA[n % 2]
        nmT, nmT_r = nmT2[n % 2]
        ncv = min(NCMP, 64 * i + 63)
        e_lo = 64 * i - 9
        return dict(n=n, g=g, i=i, qv=qv, q_r=q_r, oacc=oacc, oacc_r=oacc_r, nmT=nmT, nmT_r=nmT_r, ncv=ncv, e_lo=e_lo,
                    c_lo=max(0, e_lo), c_hi=min(ncv, 64 * i + 63), ncb=(ncv + 127) // 128, gsl=slice(g * 4, (g + 1) * 4))

    def comp_topk_gen(c):
        g, i, qv, q_r, oacc, oacc_r = c["g"], c["i"], c["qv"], c["q_r"], c["oacc"], c["oacc_r"]
        ncv, e_lo, c_lo, c_hi, ncb, gsl = c["ncv"], c["e_lo"], c["c_lo"], c["c_hi"], c["ncb"], c["gsl"]
        nmT, nmT_r = c["nmT"], c["nmT_r"]
        for h in range(4):
            p.op("pe", lambda e: e.matmul(psX[:, 0:ncv], qv[:, h * 128:(h + 1) * 128], kccT[:, g, 0:ncv], start=True, stop=True),
                 r=[q_r, kccT_r], w=[psX_r])
            yield
            p.op("act", lambda e: e.activation(out=s4[:, h, 0:ncv], in_=psX[:, 0:ncv], func=AF.Copy), r=[psX_r], w=[s4_r])
            yield
        p.op("dve", lambda e: e.tensor_tensor(out=s4[:, :, c_lo:c_hi], in0=s4[:, :, c_lo:c_hi],
                                              in1=Ec[:, gsl, c_lo - e_lo:c_hi - e_lo], op=ALU.add), r=[s4_r, Ec_r], w=[s4_r])
        yield
        p.op("dve", lambda e: e.tensor_reduce(out=st4[:, :, 0], in_=s4[:, :, 0:ncv], axis=AX.X, op=ALU.max), r=[s4_r], w=[st4_r])
        yield
        p.op("dve", lambda e: e.tensor_scalar(out=st4[:, :, 1], in0=st4[:, :, 0], scalar1=-1000.0, scalar2=-1.0,
                                              op0=ALU.max, op1=ALU.mult), r=[st4_r], w=[st4_r])
        p.op("dve", lambda e: e.memset(st4[:, :, 2], 0.0), w=[st4_r])
        yield
        for h in range(4):
            p.op("act", lambda e: e.activation(out=e4[:, h, 0:ncv], in_=s4[:, h, 0:ncv], func=AF.Exp, bias=st4[:, h, 1:2],
                                               accum_out=st4[:, h, 2:3]), r=[s4_r, st4_r], w=[e4_r, st4_r])
            yield
        p.op("dve", lambda e: e.tensor_scalar(out=st4[:, :, 3], in0=st4[:, :, 2], scalar1=1e-30, scalar2=None, op0=ALU.add),
             r=[st4_r], w=[st4_r])
        p.op("dve", lambda e: e.reciprocal(out=st4[:, :, 4], in_=st4[:, :, 3]), r=[st4_r], w=[st4_r])
        p.op("dve", lambda e: e.tensor_tensor(out=st4[:, :, 5], in0=st4[:, :, 4], in1=gat[:, i, gsl], op=ALU.mult),
             r=[st4_r, gat_r], w=[st4_r])
        yield
        p.op("dve", lambda e: e.tensor_tensor(out=s4[:, :, 0:ncv], in0=e4[:, :, 0:ncv],
                                              in1=st4[:, :, 4:5].to_broadcast([128, 4, ncv]), op=ALU.mult),
             r=[e4_r, st4_r], w=[s4_r])
        yield
        p.op("pool", lambda e: e.memset(imp[:], 0.0), w=[imp_r])
        p.op("dve", lambda e: e.tensor_reduce(out=imp[:, 1:1 + ncv], in_=s4[:, :, 0:ncv].rearrange("p h c -> p c h"),
                                              axis=AX.X, op=ALU.add), r=[s4_r], w=[imp_r])
        yield
        p.op("pool", lambda e: e.memset(pb4[:], 0.0), w=[pb4_r])
        p.op("dve", lambda e: e.tensor_tensor(out=pb4[:, :, 0:ncv], in0=e4[:, :, 0:ncv],
                                              in1=st4[:, :, 5:6].to_broadcast([128, 4, ncv]), op=ALU.mult),
             r=[e4_r, st4_r], w=[pb4_r])
        yield
        if do_sel:
            iv = imp[:, 0:512].rearrange("p (j f) -> p j f", f=4)
            p.op("dve", lambda e: e.tensor_reduce(out=sc[:], in_=iv, axis=AX.X, op=ALU.add), r=[imp_r], w=[sc_r])
            yield
            p.op("dve", lambda e: e.tensor_tensor(out=sc[:], in0=sc[:], in1=imp[:, 4:516].rearrange("p (j f) -> p j f", f=4)[:, :, 0],
                                                  op=ALU.add), r=[sc_r, imp_r], w=[sc_r])
            p.op("dve", lambda e: e.tensor_tensor(out=sc[:], in0=sc[:], in1=Fq[:, i, :], op=ALU.add), r=[sc_r, Fq_r], w=[sc_r])
            yield
            p.op("dve", lambda e: e.max(out=m8[:, 0:8], in_=sc[:]), r=[sc_r], w=[m8_r])
            p.op("dve", lambda e: e.match_replace(out=wk[:], in_to_replace=m8[:, 0:8], in_values=sc[:], imm_value=-3.0e38),
                 r=[sc_r, m8_r], w=[wk_r])
            yield
            p.op("dve", lambda e: e.max(out=m8[:, 8:16], in_=wk[:]), r=[wk_r], w=[m8_r])
            p.op("dve", lambda e: e.tensor_scalar(out=wk[:], in0=sc[:], scalar1=m8[:, 15:16], scalar2=None, op0=ALU.is_ge),
                 r=[sc_r, m8_r], w=[wk_r])
            p.op("dve", lambda e: e.tensor_scalar(out=negm[:], in0=wk[:], scalar1=1.0, scalar2=-NEG, op0=ALU.subtract, op1=ALU.mult),
                 r=[wk_r], w=[negm_r])
            yield
        for hp in range(2):
            for h in (2 * hp, 2 * hp + 1):
                for cb in range(ncb):
                    c0 = (h % 2) * 512 + cb * 128
                    p.op("pe", lambda e: e.transpose(psT[:, c0:c0 + 128], pb4[:, h, cb * 128:(cb + 1) * 128], id_sb[:]),
                         r=[pb4_r, id_r], w=[psT_r])
                yield
            for h in (2 * hp, 2 * hp + 1):
                c0 = (h % 2) * 512
                p.op("act", lambda e: e.activation(out=pT4[:, h, 0:ncb, :].rearrange("p a b -> p (a b)"),
                                                   in_=psT[:, c0:c0 + ncb * 128], func=AF.Copy), r=[psT_r], w=[pT4_r])
                yield
        for h in range(4):
            for cb in range(ncb):
                p.op("pe", lambda e: e.matmul(psX[:, 0:128], pT4[:, h, cb, :], vcc[:, g, cb, :], start=(cb == 0), stop=(cb == ncb - 1)),
                     r=[pT4_r, vcc_r], w=[psX_r])
            yield
            p.op("act", lambda e: e.activation(out=oacc[:, h * 128:(h + 1) * 128], in_=psX[:, 0:128], func=AF.Copy),
                 r=[psX_r], w=[oacc_r])
            yield
        if do_sel:
            p.op("pe", lambda e: e.transpose(psT[:, 0:128], negm[:], id_sb[:]), r=[negm_r, id_r], w=[psT_r])
            yield
            p.op("act", lambda e: e.activation(out=nmT[:], in_=psT[:, 0:128].unsqueeze(1).to_broadcast([128, 4, 128]), func=AF.Copy),
                 r=[psT_r], w=[nmT_r])
            yield

    N_STEPS = 46

    def run_attends(c, filler):
        g, i, qv, q_r, oacc, oacc_r, nmT, nmT_r = c["g"], c["i"], c["qv"], c["q_r"], c["oacc"], c["oacc_r"], c["nmT"], c["nmT_r"]
        kts_w = [kt for kt in range(8 * i - 4, 8 * i + 8) if kt >= 0] if do_win else []
        kts_s = list(range(0, 8 * i + 8)) if do_sel else []
        tot = max(1, len(kts_w) + len(kts_s))
        per_iter = (N_STEPS + tot - 1) // tot if filler is not None else 0
        if do_sel:
            def extra_sel(kt):
                ex = [(R_sb[:, kt * 128:(kt + 1) * 128], R_r, nmT[:].rearrange("p a b -> p (a b)"), nmT_r)]
                j = kt - 8 * i
                if j >= -1:
                    ex.append((id_sb[:], id_r, Tt[:, j + 1, g * 512:(g + 1) * 512], Tt_r))
                return ex
            attend(i, g, qv, q_r, kts_s, ksT_s, ksT_r, vs_s, vs_r, extra_sel, 16, False, oacc, oacc_r, filler=filler, per_iter=per_iter)
        if do_win:
            def extra_win(kt):
                jp = kt - (8 * i - 4)
                ex = [(id_sb[:], id_r, Wm4[:, jp, :, :].rearrange("p a b -> p (a b)"), Wm4_r)]
                if jp >= 3:
                    ex.append((id_sb[:], id_r, Tt[:, jp - 3, g * 512:(g + 1) * 512], Tt_r))
                return ex
            attend(i, g, qv, q_r, kts_w, kwT_s, kwT_r, vw_s, vw_r, extra_win, 32, False, oacc, oacc_r, filler=filler, per_iter=per_iter)
        if filler is not None:
            for _ in filler:
                pass
        ob, ob_r = obf[c["n"] % 2]
        p.op("act", lambda e: e.activation(out=ob[:], in_=oacc[:], func=AF.Copy), r=[oacc_r], w=[ob_r])
        p.dma("sp", o_out[i * 128:(i + 1) * 128, g * 512:(g + 1) * 512], ob[:], r=[ob_r], is_output=True)

    cur = prep_item(0)
    for _ in comp_topk_gen(cur):
        pass
    for n in range(len(items)):
        g, i = items[n]
        if i == 0:
            p.dma("sp", ksT_s[:], ksT[g], w=[ksT_r])
            p.dma("sp", kwT_s[:], kwT[g], w=[kwT_r])
            p.dma("sp", vs_s[:].rearrange("p k d -> p (k d)"), vs1[g], w=[vs_r])
            p.dma("sp", vw_s[:].rearrange("p k d -> p (k d)"), vw1[g], w=[vw_r])
        if n + 1 < len(items):
            nxt = prep_item(n + 1)
            gen = comp_topk_gen(nxt)
        else:
            nxt, gen = None, None
        run_attends(cur, gen)
        cur = nxt
    p.finish()
    p.close()
    return p.nc


DFF = 5632
MC = DFF // 128
MG = 2
MGC = MC // MG


def max_tokens(toks):
    best = {}
    for s, v in toks:
        if id(s) not in best or best[id(s)][1] < v:
            best[id(s)] = (s, v)
    return list(best.values())


class Fence:
    def __init__(self):
        self.toks = []

    def add(self, tok):
        self.toks.append(tok)
        if len(self.toks) > 256:
            self.toks = max_tokens(self.toks)

    def wait(self, p, e):
        for t in max_tokens(self.toks):
            p._wait(e, t)


def rmsnorm_T2(p, src_dram, fence, gsb, g_r, ones_f, ones_r, st, out_sb=None, out_sb_r=None, out_dram=None,
               out_fence=None, is_output=False):
    xq, xq_r, sq, sq_r, ss, ss_r, rstd, rstd_r, uo, uo_r = st
    n = 0
    for half in range(NT // 512):
        if fence is not None:
            fence.wait(p, "sp")
        hs = slice(half * 512, (half + 1) * 512)
        for k in range(KC):
            b = n % 6
            n += 1
            p.dma("sp", xq[b][:], src_dram[k * 128:(k + 1) * 128, hs], w=[xq_r[b]])
            b2 = k % 2
            p.op("act", lambda e: e.activation(out=sq[b2][:], in_=xq[b][:], func=AF.Square), r=[xq_r[b]], w=[sq_r[b2]])
            p.op("pe", lambda e: e.matmul(ss[:], ones_f[:], sq[b2][:], start=(k == 0), stop=(k == KC - 1)),
                 r=[sq_r[b2], ones_r], w=[ss_r])
        p.op("dve", lambda e: e.tensor_scalar(out=rstd[:], in0=ss[:], scalar1=1.0 / D, scalar2=EPS,
                                              op0=ALU.mult, op1=ALU.add), r=[ss_r], w=[rstd_r])
        p.op("act", lambda e: e.activation(out=rstd[:], in_=rstd[:], func=AF.Sqrt), r=[rstd_r], w=[rstd_r])
        p.op("dve", lambda e: e.reciprocal(out=rstd[:], in_=rstd[:]), r=[rstd_r], w=[rstd_r])
        for k in range(KC):
            b = n % 6
            n += 1
            p.dma("sp", xq[b][:], src_dram[k * 128:(k + 1) * 128, hs], w=[xq_r[b]])
            if out_sb is not None:
                p.op("dve", lambda e: e.scalar_tensor_tensor(
                    out=out_sb[:, k, hs], in0=xq[b][:], scalar=gsb[:, k:k + 1],
                    in1=rstd[:], op0=ALU.mult, op1=ALU.mult), r=[xq_r[b], rstd_r, g_r], w=[out_sb_r])
            else:
                b2 = k % 2
                p.op("dve", lambda e: e.scalar_tensor_tensor(
                    out=uo[b2][:], in0=xq[b][:], scalar=gsb[:, k:k + 1],
                    in1=rstd[:], op0=ALU.mult, op1=ALU.mult), r=[xq_r[b], rstd_r, g_r], w=[uo_r[b2]])
                tok = p.dma("act", out_dram[k * 128:(k + 1) * 128, hs], uo[b2][:],
                            r=[uo_r[b2]], is_output=is_output)
                if out_fence is not None:
                    out_fence.add(tok)


def build_tok(glu, final):
    p = Prog()
    resT = p.dram("resT", [D, NT], F32, "ExternalInput")
    aT = p.dram("aT", [D, NT], BF16, "ExternalInput")
    wmix = p.dram("wmix", [D, 4096 if glu else 2048], F32, "ExternalInput")
    gff = p.dram("gff", [128, KC], F32, "ExternalInput")
    gn = p.dram("gn", [128, KC], F32, "ExternalInput")
    w1 = p.dram("w1", [D, 2 * DFF], F32, "ExternalInput")
    w2 = p.dram("w2", [DFF, D], F32, "ExternalInput")
    hmidT = p.dram("hmidT", [D, NT], F32, "ExternalOutput")
    hT = p.dram("hT", [D, NT], F32, "ExternalOutput")
    normT = p.dram("normT", [D, NT], F32, "ExternalOutput")

    def T(name, shape, dt=F32):
        return p.sbuf(name, shape, dt), p.res(name)

    ones_f, ones_r = T("ones", [128, 128])
    p.op("dve", lambda e: e.memset(ones_f[:], 1.0), w=[ones_r])
    gffs, gff_r = T("gffs", [128, KC])
    p.dma("sp", gffs[:], gff[:], w=[gff_r])
    gns, gn_r = T("gns", [128, KC])
    p.dma("sp", gns[:], gn[:], w=[gn_r])
    xq = [T("xq", [128, 512]) for _ in range(6)]
    st = ([t for t, _ in xq], [r for _, r in xq],
          [p.sbuf("sq", [128, 512], F32) for _ in range(2)], [p.res("sq") for _ in range(2)],
          p.psum("ss", [128, 512]), p.res("ss"),
          p.sbuf("rstd", [128, 512], F32), p.res("rstd"),
          [p.sbuf("uo", [128, 512], F32) for _ in range(2)], [p.res("uo") for _ in range(2)])
    big, big_r = T("big", [128, MGC * NT], BF16)
    a_sb = big[:, 0:KC * NT].rearrange("p (k t) -> p k t", k=KC)
    actT = big[:, :].rearrange("p (m t) -> p m t", m=MGC)
    hnT, hnT_r = T("hnT", [128, KC, NT], BF16)
    NS = 3
    wst = [T("wst", [128, KC * 256]) for _ in range(NS)]
    NBF = 4
    wbf = [T("wbf", [128, KC * 256], BF16) for _ in range(NBF)]
    ps = [p.psum("ps", [128, 512]) for _ in range(6)]
    ps_r = [p.res("ps") for _ in range(6)]
    xc = [T("xc", [128, 512]) for _ in range(3)]
    t1 = [T("t1", [128, 512]) for _ in range(2)]
    cnt = {"w": 0, "b": 0, "ps": 0, "x": 0, "t": 0}

    def load_w(wd, row0, kc, col0, ncol):
        i = cnt["w"] % NS
        cnt["w"] += 1
        j = cnt["b"] % NBF
        cnt["b"] += 1
        ws, ws_r = wst[i]
        wb, wb_r = wbf[j]
        sv = ws[:, 0:kc * ncol].rearrange("p (k c) -> p k c", k=kc)
        bv = wb[:, 0:kc * ncol].rearrange("p (k c) -> p k c", k=kc)
        src = wd[row0:row0 + kc * 128, col0:col0 + ncol].rearrange("(k p) c -> p k c", p=128)
        h = (kc + 1) // 2
        p.dma("sp", sv[:, 0:h, :], src[:, 0:h, :], w=[ws_r])
        p.dma("sp", sv[:, h:kc, :], src[:, h:kc, :], w=[ws_r])
        p.op("dve", lambda e: e.tensor_copy(out=wb[:, 0:kc * ncol], in_=ws[:, 0:kc * ncol]), r=[ws_r], w=[wb_r])
        return bv, wb_r

    def mm(wv, w_r, inT, in_r, kc, half):
        pb = cnt["ps"] % 6
        cnt["ps"] += 1
        for k in range(kc):
            p.op("pe", lambda e: e.matmul(ps[pb][:], wv[:, k, :], inT[:, k, half * 512:(half + 1) * 512],
                                          start=(k == 0), stop=(k == kc - 1)), r=[w_r, in_r], w=[ps_r[pb]])
        return pb

    def prefetched(reqs):
        nxt = load_w(*reqs[0]) if reqs else None
        for n_ in range(len(reqs)):
            cur = nxt
            nxt = load_w(*reqs[n_ + 1]) if n_ + 1 < len(reqs) else None
            yield cur

    srcA = aT.rearrange("(k p) t -> p k t", p=128)
    for kk in range(0, KC, 4):
        p.dma("sp", a_sb[:, kk:kk + 4, :], srcA[:, kk:kk + 4, :], w=[big_r])
    f_mid = Fence()
    reqs = []
    for f in range(KC):
        reqs.append((wmix, 0, KC, f * 128, 128))
        if glu:
            reqs.append((wmix, 0, KC, 2048 + f * 128, 128))
    it = prefetched(reqs)
    for f in range(KC):
        wv, w_r = next(it)
        if glu:
            wv2, w2_r = next(it)
        for half in range(2):
            hs = slice(half * 512, (half + 1) * 512)
            xb = cnt["x"] % 3
            cnt["x"] += 1
            xcb, xcb_r = xc[xb]
            p.dma("act", xcb[:], resT[f * 128:(f + 1) * 128, hs], w=[xcb_r])
            pb = mm(wv, w_r, a_sb, big_r, KC, half)
            if glu:
                pb2 = mm(wv2, w2_r, a_sb, big_r, KC, half)
                tb = cnt["t"] % 2
                cnt["t"] += 1
                tt, tt_r = t1[tb]
                p.op("act", lambda e: e.activation(out=tt[:], in_=ps[pb2][:], func=AF.Sigmoid), r=[ps_r[pb2]], w=[tt_r])
                p.op("dve", lambda e: e.tensor_tensor(out=tt[:], in0=ps[pb][:], in1=tt[:], op=ALU.mult),
                     r=[ps_r[pb], tt_r], w=[tt_r])
                p.op("dve", lambda e: e.tensor_tensor(out=xcb[:], in0=xcb[:], in1=tt[:], op=ALU.add),
                     r=[tt_r, xcb_r], w=[xcb_r])
            else:
                p.op("dve", lambda e: e.tensor_tensor(out=xcb[:], in0=ps[pb][:], in1=xcb[:], op=ALU.add),
                     r=[ps_r[pb], xcb_r], w=[xcb_r])
            tok = p.dma("act", hmidT[f * 128:(f + 1) * 128, hs], xcb[:], r=[xcb_r], is_output=True)
            f_mid.add(tok)
    rmsnorm_T2(p, hmidT, f_mid, gffs, gff_r, ones_f, ones_r, st, out_sb=hnT, out_sb_r=hnT_r)
    src_res, src_fence = hmidT, f_mid
    for mg in range(MG):
        reqs = []
        for m2 in range(0, MGC, 2):
            m = mg * MGC + m2
            reqs.append((w1, 0, KC, m * 128, 256))
            reqs.append((w1, 0, KC, DFF + m * 128, 256))
        it = prefetched(reqs)
        for m2 in range(0, MGC, 2):
            wa, wa_r = next(it)
            wb_, wb_r = next(it)
            for mm_ in range(2):
                ml = m2 + mm_
                for half in range(2):
                    pa = mm(wa[:, :, mm_ * 128:(mm_ + 1) * 128], wa_r, hnT, hnT_r, KC, half)
                    pbb = mm(wb_[:, :, mm_ * 128:(mm_ + 1) * 128], wb_r, hnT, hnT_r, KC, half)
                    tb = cnt["t"] % 2
                    cnt["t"] += 1
                    tt, tt_r = t1[tb]
                    p.op("act", lambda e: e.activation(out=tt[:], in_=ps[pa][:], func=AF.Silu), r=[ps_r[pa]], w=[tt_r])
                    p.op("dve", lambda e: e.tensor_tensor(out=actT[:, ml, half * 512:(half + 1) * 512], in0=ps[pbb][:],
                                                          in1=tt[:], op=ALU.mult), r=[ps_r[pbb], tt_r], w=[big_r])
        f_new = Fence()
        reqs = [(w2, mg * MGC * 128, MGC, f * 128, 128) for f in range(KC)]
        it = prefetched(reqs)
        for f in range(KC):
            wo, wo_r = next(it)
            for half in range(2):
                hs = slice(half * 512, (half + 1) * 512)
                pb = cnt["ps"] % 6
                cnt["ps"] += 1
                for ml in range(MGC):
                    p.op("pe", lambda e: e.matmul(ps[pb][:], wo[:, ml, :], actT[:, ml, hs], start=(ml == 0), stop=(ml == MGC - 1)),
                         r=[wo_r, big_r], w=[ps_r[pb]])
                xb = cnt["x"] % 3
                cnt["x"] += 1
                xcb, xcb_r = xc[xb]
                src_fence.wait(p, "act")
                p.dma("act", xcb[:], src_res[f * 128:(f + 1) * 128, hs], w=[xcb_r])
                p.op("dve", lambda e: e.tensor_tensor(out=xcb[:], in0=ps[pb][:], in1=xcb[:], op=ALU.add),
                     r=[ps_r[pb], xcb_r], w=[xcb_r])
                dst = hT if mg == MG - 1 else normT
                tok = p.dma("act", dst[f * 128:(f + 1) * 128, hs], xcb[:], r=[xcb_r], is_output=True)
                f_new.add(tok)
        src_res, src_fence = (hT if mg == MG - 1 else normT), f_new
    rmsnorm_T2(p, hT, src_fence, gns, gn_r, ones_f, ones_r, st, out_dram=normT, is_output=True)
    p.finish()
    p.close()
    return p.nc


TWO_PI = 2.0 * math.pi


I32 = mybir.dt.int32


def sincos(p, T, ang, ang_r, s_out, c_out, out_r, tmp, tmp_r, shape_sl, n):
    sl = shape_sl
    tl = (slice(None), slice(0, n))
    if not hasattr(p, "_sc_tmp"):
        p._sc_tmp = (T("sc_ki", [128, 512], I32), T("sc_kf", [128, 512]), T("sc_y", [128, 512]), T("sc_m", [128, 512]))
    (ki, ki_r), (kf, kf_r), (y, y_r), (m, m_r) = p._sc_tmp
    p.op("dve", lambda e: e.tensor_scalar(out=kf[tl], in0=ang[sl], scalar1=1.0 / TWO_PI, scalar2=None, op0=ALU.mult),
         r=[ang_r], w=[kf_r])
    p.op("dve", lambda e: e.tensor_copy(out=ki[tl], in_=kf[tl]), r=[kf_r], w=[ki_r])
    p.op("dve", lambda e: e.tensor_copy(out=kf[tl], in_=ki[tl]), r=[ki_r], w=[kf_r])
    p.op("dve", lambda e: e.scalar_tensor_tensor(out=y[tl], in0=kf[tl], scalar=-TWO_PI, in1=ang[sl], op0=ALU.mult, op1=ALU.add),
         r=[kf_r, ang_r], w=[y_r])

    def fold(v, v_r):
        p.op("dve", lambda e: e.tensor_scalar(out=m[tl], in0=v[tl], scalar1=math.pi, scalar2=None, op0=ALU.is_gt), r=[v_r], w=[m_r])
        p.op("dve", lambda e: e.scalar_tensor_tensor(out=v[tl], in0=m[tl], scalar=-TWO_PI, in1=v[tl], op0=ALU.mult, op1=ALU.add),
             r=[m_r, v_r], w=[v_r])
        p.op("dve", lambda e: e.tensor_scalar(out=m[tl], in0=v[tl], scalar1=-math.pi, scalar2=None, op0=ALU.is_lt), r=[v_r], w=[m_r])
        p.op("dve", lambda e: e.scalar_tensor_tensor(out=v[tl], in0=m[tl], scalar=TWO_PI, in1=v[tl], op0=ALU.mult, op1=ALU.add),
             r=[m_r, v_r], w=[v_r])

    fold(y, y_r)
    p.op("act", lambda e: e.activation(out=s_out[sl], in_=y[tl], func=AF.Sin), r=[y_r], w=[out_r])
    p.op("dve", lambda e: e.tensor_scalar(out=y[tl], in0=y[tl], scalar1=0.5 * math.pi, scalar2=None, op0=ALU.add), r=[y_r], w=[y_r])
    fold(y, y_r)
    p.op("act", lambda e: e.activation(out=c_out[sl], in_=y[tl], func=AF.Sin), r=[y_r], w=[out_r])


def build_s5prep():
    p = Prog()
    A_re = p.dram("A_re", [128, 64], F32, "ExternalInput")
    A_im = p.dram("A_im", [128, 64], F32, "ExternalInput")
    log_dt = p.dram("log_dt", [128, 1], F32, "ExternalInput")
    B_re = p.dram("B_re", [128, 1024], F32, "ExternalInput")
    B_im = p.dram("B_im", [128, 1024], F32, "ExternalInput")
    o_r = p.dram("o_r", [128, 64], F32, "ExternalOutput")
    o_th = p.dram("o_th", [128, 64], F32, "ExternalOutput")
    o_bbre = p.dram("o_bbre", [128, 1024], F32, "ExternalOutput")
    o_bbim = p.dram("o_bbim", [128, 1024], F32, "ExternalOutput")

    def T(name, shape, dt=F32):
        return p.sbuf(name, shape, dt), p.res(name)

    are, are_r = T("are", [128, 64])
    aim, aim_r = T("aim", [128, 64])
    ldt, ldt_r = T("ldt", [128, 1])
    bre, bre_r = T("bre", [128, 64, 16])
    bim, bim_r = T("bim", [128, 64, 16])
    p.dma("sp", are[:], A_re[:], w=[are_r])
    p.dma("sp", aim[:], A_im[:], w=[aim_r])
    p.dma("sp", ldt[:], log_dt[:], w=[ldt_r])
    p.dma("sp", bre[:].rearrange("p n c -> p (n c)"), B_re[:], w=[bre_r])
    p.dma("sp", bim[:].rearrange("p n c -> p (n c)"), B_im[:], w=[bim_r])
    dt, dt_r = T("dt", [128, 1])
    lre, lre_r = T("lre", [128, 64])
    th, th_r = T("th", [128, 64])
    rr, rr_r = T("rr", [128, 64])
    sn, sc_r = T("sn", [128, 64])
    cs, _ = T("cs", [128, 64])
    tmp, tmp_r = T("tmp", [128, 64])
    abre, abre_r = T("abre", [128, 64])
    abim, abim_r = T("abim", [128, 64])
    den, den_r = T("den", [128, 64])
    t2, t2_r = T("t2", [128, 64])
    cfre, cfre_r = T("cfre", [128, 64])
    cfim, cfim_r = T("cfim", [128, 64])
    obr, obr_r = T("obr", [128, 64, 16])
    obi, obi_r = T("obi", [128, 64, 16])
    t3, t3_r = T("t3", [128, 64, 16])
    p.op("act", lambda e: e.activation(out=dt[:], in_=ldt[:], func=AF.Exp), r=[ldt_r], w=[dt_r])
    p.op("dve", lambda e: e.tensor_scalar(out=lre[:], in0=are[:], scalar1=dt[:, 0:1], scalar2=None, op0=ALU.mult),
         r=[are_r, dt_r], w=[lre_r])
    p.op("dve", lambda e: e.tensor_scalar(out=th[:], in0=aim[:], scalar1=dt[:, 0:1], scalar2=None, op0=ALU.mult),
         r=[aim_r, dt_r], w=[th_r])
    p.op("act", lambda e: e.activation(out=rr[:], in_=lre[:], func=AF.Exp), r=[lre_r], w=[rr_r])
    sl = (slice(None), slice(None))
    sincos(p, T, th, th_r, sn, cs, sc_r, tmp, tmp_r, sl, 64)
    p.op("dve", lambda e: e.tensor_tensor(out=abre[:], in0=rr[:], in1=cs[:], op=ALU.mult), r=[rr_r, sc_r], w=[abre_r])
    p.op("dve", lambda e: e.tensor_tensor(out=abim[:], in0=rr[:], in1=sn[:], op=ALU.mult), r=[rr_r, sc_r], w=[abim_r])
    p.op("dve", lambda e: e.tensor_scalar(out=abre[:], in0=abre[:], scalar1=-1.0, scalar2=None, op0=ALU.add), r=[abre_r], w=[abre_r])
    p.op("dve", lambda e: e.tensor_tensor(out=den[:], in0=are[:], in1=are[:], op=ALU.mult), r=[are_r], w=[den_r])
    p.op("dve", lambda e: e.tensor_tensor(out=t2[:], in0=aim[:], in1=aim[:], op=ALU.mult), r=[aim_r], w=[t2_r])
    p.op("dve", lambda e: e.tensor_tensor(out=den[:], in0=den[:], in1=t2[:], op=ALU.add), r=[den_r, t2_r], w=[den_r])
    p.op("dve", lambda e: e.reciprocal(out=den[:], in_=den[:]), r=[den_r], w=[den_r])
    p.op("dve", lambda e: e.tensor_tensor(out=cfre[:], in0=abre[:], in1=are[:], op=ALU.mult), r=[abre_r, are_r], w=[cfre_r])
    p.op("dve", lambda e: e.tensor_tensor(out=t2[:], in0=abim[:], in1=aim[:], op=ALU.mult), r=[abim_r, aim_r], w=[t2_r])
    p.op("dve", lambda e: e.tensor_tensor(out=cfre[:], in0=cfre[:], in1=t2[:], op=ALU.add), r=[cfre_r, t2_r], w=[cfre_r])
    p.op("dve", lambda e: e.tensor_tensor(out=cfre[:], in0=cfre[:], in1=den[:], op=ALU.mult), r=[cfre_r, den_r], w=[cfre_r])
    p.op("dve", lambda e: e.tensor_tensor(out=cfim[:], in0=abim[:], in1=are[:], op=ALU.mult), r=[abim_r, are_r], w=[cfim_r])
    p.op("dve", lambda e: e.tensor_tensor(out=t2[:], in0=abre[:], in1=aim[:], op=ALU.mult), r=[abre_r, aim_r], w=[t2_r])
    p.op("dve", lambda e: e.tensor_tensor(out=cfim[:], in0=cfim[:], in1=t2[:], op=ALU.subtract), r=[cfim_r, t2_r], w=[cfim_r])
    p.op("dve", lambda e: e.tensor_tensor(out=cfim[:], in0=cfim[:], in1=den[:], op=ALU.mult), r=[cfim_r, den_r], w=[cfim_r])
    cre_b = cfre[:].unsqueeze(2).to_broadcast([128, 64, 16])
    cim_b = cfim[:].unsqueeze(2).to_broadcast([128, 64, 16])
    p.op("dve", lambda e: e.tensor_tensor(out=obr[:], in0=bre[:], in1=cre_b, op=ALU.mult), r=[bre_r, cfre_r], w=[obr_r])
    p.op("dve", lambda e: e.tensor_tensor(out=t3[:], in0=bim[:], in1=cim_b, op=ALU.mult), r=[bim_r, cfim_r], w=[t3_r])
    p.op("dve", lambda e: e.tensor_tensor(out=obr[:], in0=obr[:], in1=t3[:], op=ALU.subtract), r=[obr_r, t3_r], w=[obr_r])
    p.op("dve", lambda e: e.tensor_tensor(out=obi[:], in0=bim[:], in1=cre_b, op=ALU.mult), r=[bim_r, cfre_r], w=[obi_r])
    p.op("dve", lambda e: e.tensor_tensor(out=t3[:], in0=bre[:], in1=cim_b, op=ALU.mult), r=[bre_r, cfim_r, obr_r], w=[t3_r])
    p.op("dve", lambda e: e.tensor_tensor(out=obi[:], in0=obi[:], in1=t3[:], op=ALU.add), r=[obi_r, t3_r], w=[obi_r])
    p.dma("sp", o_r[:], rr[:], r=[rr_r], is_output=True)
    p.dma("sp", o_th[:], th[:], r=[th_r], is_output=True)
    p.dma("sp", o_bbre[:], obr[:].rearrange("p n c -> p (n c)"), r=[obr_r], is_output=True)
    p.dma("sp", o_bbim[:], obi[:].rearrange("p n c -> p (n c)"), r=[obi_r], is_output=True)
    p.finish()
    p.close()
    return p.nc


NPAIR = 8
NB = 16


def build_s5main(nblocks=NB):
    p = Prog()
    uT = p.dram("uT", [256, 8192], F32, "ExternalInput")
    BD = p.dram("BD", [128, NPAIR * 2 * 128], F32, "ExternalInput")
    CT = p.dram("CT", [128, NPAIR * 2 * 128], F32, "ExternalInput")
    rcol = p.dram("rcol", [128, NPAIR], F32, "ExternalInput")
    thcol = p.dram("thcol", [128, NPAIR], F32, "ExternalInput")
    iota = p.dram("iota", [128, 512], F32, "ExternalInput")
    Dcol = p.dram("Dcol", [128, 2], F32, "ExternalInput")
    identf = p.dram("identf", [128, 128], F32, "ExternalInput")
    yT = p.dram("yT", [256, 8192], BF16, "ExternalOutput")

    def T(name, shape, dt=F32):
        return p.sbuf(name, shape, dt), p.res(name)

    bd, bd_r = T("bd", [128, NPAIR, 2, 128])
    ct, ct_r = T("ct", [128, NPAIR, 2, 128])
    rc, rc_r = T("rc", [128, NPAIR])
    thc, thc_r = T("thc", [128, NPAIR])
    io, io_r = T("io", [128, 512])
    dc, dc_r = T("dc", [128, 2])
    p.dma("sp", bd[:].rearrange("p a b c -> p (a b c)"), BD[:], w=[bd_r])
    p.dma("sp", ct[:].rearrange("p a b c -> p (a b c)"), CT[:], w=[ct_r])
    p.dma("sp", rc[:], rcol[:], w=[rc_r])
    p.dma("sp", thc[:], thcol[:], w=[thc_r])
    p.dma("sp", io[:], iota[:], w=[io_r])
    p.dma("sp", dc[:], Dcol[:], w=[dc_r])
    cosT, tab_r = T("cosT", [128, NPAIR, 512])
    sinT, _ = T("sinT", [128, NPAIR, 512])
    nsinT, _ = T("nsinT", [128, NPAIR, 512])
    c512, c512_r = T("c512", [128, NPAIR])
    s512, _ = T("s512", [128, NPAIR])
    ang, ang_r = T("ang", [128, 512])
    tmp, tmp_r = T("tmp", [128, 512])
    for k in range(NPAIR):
        p.op("dve", lambda e: e.tensor_scalar(out=ang[:], in0=io[:], scalar1=thc[:, k:k + 1], scalar2=None, op0=ALU.mult),
             r=[io_r, thc_r], w=[ang_r])
        sincos(p, T, ang, ang_r, sinT[:, k, :], cosT[:, k, :], tab_r, tmp, tmp_r, (slice(None), slice(None)), 512)
        p.op("dve", lambda e: e.tensor_scalar(out=nsinT[:, k, :], in0=sinT[:, k, :], scalar1=-1.0, scalar2=None, op0=ALU.mult),
             r=[tab_r], w=[tab_r])
    p.op("dve", lambda e: e.tensor_scalar(out=ang[:, 0:NPAIR], in0=thc[:], scalar1=512.0, scalar2=None, op0=ALU.mult),
         r=[thc_r], w=[ang_r])
    sincos(p, T, ang, ang_r, s512, c512, c512_r, tmp, tmp_r, (slice(None), slice(0, NPAIR)), NPAIR)

    ctb, ctb_r = T("ctb", [128, NPAIR, 3, 128], BF16)
    p.op("dve", lambda e: e.tensor_copy(out=ctb[:, :, 0, :], in_=ct[:, :, 0, :]), r=[ct_r], w=[ctb_r])
    p.op("dve", lambda e: e.tensor_scalar(out=ctb[:, :, 1, :], in0=ct[:, :, 0, :], scalar1=-1.0, scalar2=None, op0=ALU.mult),
         r=[ct_r], w=[ctb_r])
    p.op("dve", lambda e: e.tensor_scalar(out=ctb[:, :, 2, :], in0=ct[:, :, 1, :], scalar1=-1.0, scalar2=None, op0=ALU.mult),
         r=[ct_r], w=[ctb_r])
    idf, idf_r = T("idf", [128, 128])
    p.dma("sp", idf[:], identf[:], w=[idf_r])
    vin_re, vin_r = T("vin_re", [128, NPAIR])
    vin_im, _ = T("vin_im", [128, NPAIR])
    p.op("dve", lambda e: e.memset(vin_re[:], 0.0), w=[vin_r])
    p.op("dve", lambda e: e.memset(vin_im[:], 0.0), w=[vin_r])
    ub = [T("ub", [128, 2, 512]) for _ in range(2)]
    psB = [p.psum("psB", [128, 512]) for _ in range(4)]
    psB_r = [p.res("psB") for _ in range(4)]
    psW = [p.psum("psW", [128, 512]) for _ in range(2)]
    psW_r = [p.res("psW") for _ in range(2)]
    psY = [p.psum("psY", [128, 512]) for _ in range(2)]
    psY_r = [p.res("psY") for _ in range(2)]
    tq = [[T("tq", [128, 512]) for _ in range(4)] for _ in range(2)]
    vre = [T("vre", [128, 512]) for _ in range(2)]
    vim = [T("vim", [128, 512]) for _ in range(2)]
    xq = [[T("xq", [128, 512], BF16) for _ in range(4)] for _ in range(2)]
    yg, yg_r = T("yg", [128, 512])
    tg, tg_r = T("tg", [128, 512])
    yo = [T("yo", [128, 512], BF16) for _ in range(2)]
    sm, sm_r = T("sm", [128, 4])
    steps = [(b, k) for b in range(nblocks) for k in range(NPAIR)]
    ubuf = {}

    def stageA(idx):
        b, k = steps[idx]
        if k == 0:
            u, u_r = ub[b % 2]
            p.dma("sp", u[:], uT[:, b * 512:(b + 1) * 512].rearrange("(k p) t -> p k t", p=128), w=[u_r])
            ubuf[b] = (u, u_r)
        u, u_r = ubuf[b]
        kc = k // 4
        i2 = idx % 2
        pre, pre_r = psB[2 * i2], psB_r[2 * i2]
        pim, pim_r = psB[2 * i2 + 1], psB_r[2 * i2 + 1]
        p.op("pe", lambda e: e.matmul(pre[:], bd[:, k, 0, :], u[:, kc, :], start=True, stop=True), r=[bd_r, u_r], w=[pre_r])
        p.op("pe", lambda e: e.matmul(pim[:], bd[:, k, 1, :], u[:, kc, :], start=True, stop=True), r=[bd_r, u_r], w=[pim_r])
        cT, sT, nsT = cosT[:, k, :], sinT[:, k, :], nsinT[:, k, :]
        tt = tq[i2]
        prods = [(pre, pre_r, cT), (pim, pim_r, sT), (pim, pim_r, cT), (pre, pre_r, nsT)]
        for j, (src_, src_r, tab) in enumerate(prods):
            p.op("dve", lambda e: e.tensor_tensor(out=tt[j][0][:], in0=src_[:], in1=tab, op=ALU.mult),
                 r=[src_r, tab_r], w=[tt[j][1]])
        wre_p, wre_r = psW[0], psW_r[0]
        wim_p, wim_r = psW[1], psW_r[1]
        p.op("pe", lambda e: e.matmul(wre_p[:], idf[:], tt[0][0][:], start=True, stop=False), r=[idf_r, tt[0][1]], w=[wre_r])
        p.op("pe", lambda e: e.matmul(wre_p[:], idf[:], tt[1][0][:], start=False, stop=True), r=[idf_r, tt[1][1]], w=[wre_r])
        p.op("pe", lambda e: e.matmul(wim_p[:], idf[:], tt[2][0][:], start=True, stop=False), r=[idf_r, tt[2][1]], w=[wim_r])
        p.op("pe", lambda e: e.matmul(wim_p[:], idf[:], tt[3][0][:], start=False, stop=True), r=[idf_r, tt[3][1]], w=[wim_r])

    def stageB(idx):
        b, k = steps[idx]
        u, u_r = ubuf[b]
        kc = k // 4
        i2 = idx % 2
        (vr, vr_r), (vi, vi_r) = vre[i2], vim[i2]
        cT, sT = cosT[:, k, :], sinT[:, k, :]
        wre_p, wre_r = psW[0], psW_r[0]
        wim_p, wim_r = psW[1], psW_r[1]
        rb = rc[:, k:k + 1].to_broadcast([128, 512])
        p.op("dve", lambda e: e.tensor_tensor_scan(out=vr[:], data0=rb, data1=wre_p[:], initial=vin_re[:, k:k + 1],
                                                   op0=ALU.mult, op1=ALU.add), r=[wre_r, rc_r, vin_r], w=[vr_r])
        p.op("dve", lambda e: e.tensor_tensor_scan(out=vi[:], data0=rb, data1=wim_p[:], initial=vin_im[:, k:k + 1],
                                                   op0=ALU.mult, op1=ALU.add), r=[wim_r, rc_r, vin_r], w=[vi_r])
        return (b, k, kc, i2, u, u_r, vr, vr_r, vi, vi_r, cT, sT)

    def stageC(ctx):
        b, k, kc, i2, u, u_r, vr, vr_r, vi, vi_r, cT, sT = ctx
        p.op("dve", lambda e: e.tensor_tensor(out=sm[:, 0:1], in0=vr[:, 511:512], in1=c512[:, k:k + 1], op=ALU.mult),
             r=[vr_r, c512_r], w=[sm_r])
        p.op("dve", lambda e: e.tensor_tensor(out=sm[:, 1:2], in0=vi[:, 511:512], in1=s512[:, k:k + 1], op=ALU.mult),
             r=[vi_r, c512_r], w=[sm_r])
        p.op("dve", lambda e: e.tensor_tensor(out=sm[:, 2:3], in0=vr[:, 511:512], in1=s512[:, k:k + 1], op=ALU.mult),
             r=[vr_r, c512_r], w=[sm_r])
        p.op("dve", lambda e: e.tensor_tensor(out=sm[:, 3:4], in0=vi[:, 511:512], in1=c512[:, k:k + 1], op=ALU.mult),
             r=[vi_r, c512_r], w=[sm_r])
        p.op("dve", lambda e: e.tensor_tensor(out=vin_re[:, k:k + 1], in0=sm[:, 0:1], in1=sm[:, 1:2], op=ALU.subtract),
             r=[sm_r], w=[vin_r])
        p.op("dve", lambda e: e.tensor_tensor(out=vin_im[:, k:k + 1], in0=sm[:, 2:3], in1=sm[:, 3:4], op=ALU.add),
             r=[sm_r], w=[vin_r])
        xx = xq[i2]
        posts = [("dve", vr, vr_r, cT), ("pool", vi, vi_r, sT), ("dve", vr, vr_r, sT), ("pool", vi, vi_r, cT)]
        for j, (eng_, src_, src_r, tab) in enumerate(posts):
            p.op(eng_, lambda e: e.tensor_tensor(out=xx[j][0][:], in0=src_[:], in1=tab, op=ALU.mult),
                 r=[src_r, tab_r], w=[xx[j][1]])
        py, py_r = psY[kc], psY_r[kc]
        kk = k % 4
        cv = [0, 1, 2, 2]
        for j in range(4):
            p.op("pe", lambda e: e.matmul(py[:], ctb[:, k, cv[j], :], xx[j][0][:], start=(kk == 0 and j == 0),
                                          stop=(kk == 3 and j == 3)), r=[ctb_r, xx[j][1]], w=[py_r])
        if kk == 3:
            yob, yob_r = yo[kc]
            p.op("dve", lambda e: e.scalar_tensor_tensor(out=yg[:], in0=u[:, kc, :], scalar=dc[:, kc:kc + 1], in1=py[:],
                                                         op0=ALU.mult, op1=ALU.add), r=[u_r, dc_r, py_r], w=[yg_r])
            p.op("pool", lambda e: e.tensor_tensor(out=tg[:], in0=yg[:], in1=yg[:], op=ALU.mult), r=[yg_r], w=[tg_r])
            p.op("pool", lambda e: e.tensor_scalar(out=tg[:], in0=tg[:], scalar1=0.044715, scalar2=1.0, op0=ALU.mult, op1=ALU.add),
                 r=[tg_r], w=[tg_r])
            p.op("pool", lambda e: e.tensor_tensor(out=tg[:], in0=tg[:], in1=yg[:], op=ALU.mult), r=[tg_r, yg_r], w=[tg_r])
            p.op("act", lambda e: e.activation(out=tg[:], in_=tg[:], func=AF.Sigmoid, scale=1.5957691216), r=[tg_r], w=[tg_r])
            p.op("pool", lambda e: e.tensor_tensor(out=yob[:], in0=tg[:], in1=yg[:], op=ALU.mult), r=[tg_r, yg_r], w=[yob_r])
            p.dma("sp", yT[kc * 128:(kc + 1) * 128, b * 512:(b + 1) * 512], yob[:], r=[yob_r], is_output=True)

    stageA(0)
    for idx in range(len(steps)):
        ctx = stageB(idx)
        if idx + 1 < len(steps):
            stageA(idx + 1)
        stageC(ctx)
    p.finish()
    p.close()
    return p.nc


def _run(nc, in_maps):
    res = run_bass_kernel_spmd(nc, in_maps, core_ids=list(range(8)))
    return res.results


def kernel(**inp):
    inp = {k: np.asarray(v) for k, v in inp.items()}
    x = inp["x"][0]
    idx = [own_idx(c) for c in range(8)]
    xT = [np.ascontiguousarray(x[idx[c]].T) for c in range(8)]
    w_in = np.ascontiguousarray(inp["nsa_w_in"][0])
    g0 = gcol(inp["mix_norm_g"][0])
    l1 = _run(build_p1(), [{"xT": xT[c], "gcol": g0, "w_in": w_in} for c in range(8)])
    shared = p2_shared(inp, l1)
    l2 = _run(build_p2(), [p2_inputs(inp, l1, c, shared) for c in range(8)])
    del shared
    m3 = [{"resT": xT[c], "aT": np.ascontiguousarray(np.asarray(l2[c]["o_out"]).T),
           "wmix": np.ascontiguousarray(inp["nsa_w_out"][0]), "gff": gcol(inp["ffn_norm_g"][0]),
           "gn": gcol(inp["mix_norm_g"][1]), "w1": np.ascontiguousarray(inp["ffn_w_in"][0]),
           "w2": np.ascontiguousarray(inp["ffn_w_out"][0])} for c in range(8)]
    l3 = _run(build_tok(False, False), m3)
    del m3
    prep = _run(build_s5prep(), [s5prep_inputs(inp) for _ in range(8)])[0]
    uT_full = np.zeros((2048, 8192), np.float32)
    for c in range(8):
        uT_full[:, idx[c]] = np.asarray(l3[c]["normT"])
    l4 = _run(build_s5main(), [s5main_inputs(inp, prep, uT_full, c) for c in range(8)])
    y_full = np.concatenate([np.asarray(l4[c]["yT"]) for c in range(8)], axis=0)
    m5 = [{"resT": np.ascontiguousarray(np.asarray(l3[c]["hT"])), "aT": np.ascontiguousarray(y_full[:, idx[c]]),
           "wmix": np.ascontiguousarray(inp["s5_w_glu"][0]), "gff": gcol(inp["ffn_norm_g"][1]),
           "gn": gcol(inp["final_norm_g"]), "w1": np.ascontiguousarray(inp["ffn_w_in"][1]),
           "w2": np.ascontiguousarray(inp["ffn_w_out"][1])} for c in range(8)]
    l5 = _run(build_tok(True, True), m5)
    out = np.zeros((1, 8192, 2048), np.float32)
    for c in range(8):
        out[0, idx[c]] = np.asarray(l5[c]["normT"]).T
    return out
```
